# Optimizing a Trainium2 kernel written in Bass

```python
import math
import jax, jax.numpy as jnp
from jax import lax
import numpy as np

D_MODEL = 1024
BATCH = 4
SEQ = 4096
DEPTH = 4
DEC_BATCH = 128
DEC_SEQ = 8
PAST_LEN = 2048
PAGE_SIZE = 128

N_MIXERS = 3
N_CONV_LAYERS = (DEPTH + 2) // 3
N_NSA_LAYERS = (DEPTH + 1) // 3
N_POOL_LAYERS = DEPTH // 3

CONV_WIDTH = 31
CONV_DIM = D_MODEL

N_HEADS = 16
HEAD_DIM = 64
N_KV_HEADS = 4
GROUP = N_HEADS // N_KV_HEADS
CMP_BLOCK = 64
SEL_BLOCK = 64
N_SEL = 16
WINDOW = 512
CMP_HIDDEN = 256
N_BRANCH = 3
Q_COLS = N_HEADS * HEAD_DIM
KV_COLS = 2 * N_KV_HEADS * HEAD_DIM
NSA_IN_COLS = Q_COLS + N_BRANCH * KV_COLS + N_BRANCH * N_HEADS
WIN_QBLOCK = 128
ATTN_SCALE = HEAD_DIM ** -0.5

POOL_WINDOWS = (2, 4, 8, 16)
N_POOL_GROUPS = len(POOL_WINDOWS)
POOL_GROUP_DIM = D_MODEL // N_POOL_GROUPS
POOL_STATE = max(POOL_WINDOWS) - 1

D_FF = 2816
FFN_CONV_WIDTH = 3

EPS = 1e-6
NEG = -1e30
FORCE_SCORE = 1e4
F32 = jnp.float32

kernel_name = "hybrid_conv_nsa_pool_adaln_decode_step"


def rms_norm(x, g):
    xf = x.astype(F32)
    y = xf * lax.rsqrt(jnp.mean(xf * xf, -1, keepdims=True) + EPS)
    return (y * g.astype(F32)).astype(x.dtype)


def layer_norm(x, g, b):
    xf = x.astype(F32)
    mu = jnp.mean(xf, -1, keepdims=True)
    var = jnp.mean(jnp.square(xf - mu), -1, keepdims=True)
    return ((xf - mu) * lax.rsqrt(var + EPS) * g.astype(F32) + b.astype(F32)).astype(x.dtype)


def modulate(h, shift, scale):
    return h * (1 + scale[:, None, :]) + shift[:, None, :]


def depthwise_causal(xp, w):
    return lax.conv_general_dilated(xp, w[:, None, :].astype(xp.dtype), (1,), 'VALID',
                                    dimension_numbers=('NWC', 'WIO', 'NWC'),
                                    feature_group_count=xp.shape[-1])


def conformer_conv(h, prefix, w_in, w_dw, ln_g, ln_b, w_out):
    a, g = jnp.split(h @ w_in, 2, axis=-1)
    u = a * jax.nn.sigmoid(g)
    up = jnp.concatenate([prefix.astype(u.dtype), u], axis=1)
    y = depthwise_causal(up, w_dw)
    y = jax.nn.silu(layer_norm(y, ln_g, ln_b))
    return y @ w_out, up[:, -(CONV_WIDTH - 1):]


def pool_mixer(h, prefix, start_pos, w_grp, ls):
    B, T, D = h.shape
    hcat = jnp.concatenate([prefix.astype(h.dtype), h], axis=1)
    hp = hcat.astype(F32)
    cs = jnp.concatenate([jnp.zeros((B, 1, D), F32), jnp.cumsum(hp, axis=1)], axis=1)
    pos = start_pos + jnp.arange(T)
    means = []
    for gi, w in enumerate(POOL_WINDOWS):
        sl = slice(gi * POOL_GROUP_DIM, (gi + 1) * POOL_GROUP_DIM)
        end = cs[:, POOL_STATE + 1:POOL_STATE + 1 + T, sl]
        begin = cs[:, POOL_STATE + 1 - w:POOL_STATE + 1 - w + T, sl]
        cnt = jnp.minimum(w, pos + 1).astype(F32)
        means.append((end - begin) / cnt[None, :, None])
    pooled = (jnp.concatenate(means, -1) - hp[:, POOL_STATE:]).astype(h.dtype)
    d = pooled.reshape(B, T, N_POOL_GROUPS, POOL_GROUP_DIM)
    y = jnp.einsum('btgc,gcd->btgd', d, w_grp).reshape(B, T, D)
    return y * ls, hcat[:, -POOL_STATE:]


def conv_ffn(h, prefix, w_up, w_dw, w_down):
    a, b = jnp.split(h @ w_up, 2, axis=-1)
    ap = jnp.concatenate([prefix.astype(a.dtype), a], axis=1)
    a_c = depthwise_causal(ap, w_dw)
    return (jax.nn.silu(a_c) * b) @ w_down, ap[:, -(FFN_CONV_WIDTH - 1):]


def nsa_project(h, w_in):
    B, T, _ = h.shape
    z = h @ w_in
    q = z[..., :Q_COLS].reshape(B, T, N_KV_HEADS, GROUP, HEAD_DIM)
    kv = z[..., Q_COLS:Q_COLS + N_BRANCH * KV_COLS].reshape(B, T, N_BRANCH, 2, N_KV_HEADS, HEAD_DIM)
    gates = jax.nn.sigmoid(z[..., Q_COLS + N_BRANCH * KV_COLS:].astype(F32)).astype(h.dtype)
    gates = gates.reshape(B, T, N_BRANCH, N_KV_HEADS, GROUP)
    rows = kv[:, :, :2].reshape(B, T, 4, N_KV_HEADS, HEAD_DIM)
    return q, rows, kv[:, :, 2], gates


def compress(kv, pe, w1, w2):
    B, L, G, Dh = kv.shape
    nc = L // CMP_BLOCK
    blk = kv[:, :nc * CMP_BLOCK].reshape(B, nc, CMP_BLOCK, G, Dh) + pe[None, None, :, None, :]
    flat = jnp.moveaxis(blk, 3, 2).reshape(B, nc, G, CMP_BLOCK * Dh)
    return jax.nn.silu(flat @ w1) @ w2


def nsa_global(q, q_pos, rows, cmp_pe, cmp_w1, cmp_w2):
    B, Tq = q.shape[:2]
    L = rows.shape[1]
    ck = compress(rows[:, :, 0], cmp_pe[0], cmp_w1[0], cmp_w2[0])
    cv = compress(rows[:, :, 1], cmp_pe[1], cmp_w1[1], cmp_w2[1])
    nc = ck.shape[1]
    ns = -(-L // SEL_BLOCK)
    pad = ns * SEL_BLOCK - L
    padw = ((0, 0), (0, pad), (0, 0), (0, 0))
    kb = jnp.pad(rows[:, :, 2], padw).reshape(B, ns, SEL_BLOCK, N_KV_HEADS, HEAD_DIM).transpose(0, 3, 1, 2, 4)
    vb = jnp.pad(rows[:, :, 3], padw).reshape(B, ns, SEL_BLOCK, N_KV_HEADS, HEAD_DIM).transpose(0, 3, 1, 2, 4)
    n_sel = min(N_SEL, ns)
    cmp_end = (jnp.arange(nc) + 1) * CMP_BLOCK - 1
    blk_ids = jnp.arange(ns)
    b_ix = jnp.arange(B)[:, None, None, None]
    g_ix = jnp.arange(N_KV_HEADS)[None, None, :, None]

    def chunk(args):
        qc, pc = args
        C = qc.shape[1]
        s = jnp.einsum('bqkgd,bnkd->bqkgn', qc, ck).astype(F32) * ATTN_SCALE
        valid = cmp_end[None, :] <= pc[:, None]
        s = jnp.where(valid[None, :, None, None, :], s, NEG)
        p = jax.nn.softmax(s, axis=-1) * jnp.any(valid, -1).astype(F32)[None, :, None, None, None]
        o_cmp = jnp.einsum('bqkgn,bnkd->bqkgd', p.astype(cv.dtype), cv)
        imp = jnp.pad(jnp.sum(p, axis=3), ((0, 0), (0, 0), (0, 0), (0, ns - nc)))
        cur = pc // SEL_BLOCK
        forced = (blk_ids[None, :] == 0) | (blk_ids[None, :] == cur[:, None]) | (blk_ids[None, :] == cur[:, None] - 1)
        future = blk_ids[None, :] > cur[:, None]
        imp = jnp.where(forced[None, :, None, :], FORCE_SCORE, imp)
        imp = jnp.where(future[None, :, None, :], NEG, imp)
        _, idx = lax.top_k(imp, n_sel)
        ksel = kb[b_ix, g_ix, idx].reshape(B, C, N_KV_HEADS, n_sel * SEL_BLOCK, HEAD_DIM)
        vsel = vb[b_ix, g_ix, idx].reshape(B, C, N_KV_HEADS, n_sel * SEL_BLOCK, HEAD_DIM)
        kpos = (idx[..., None] * SEL_BLOCK + jnp.arange(SEL_BLOCK)).reshape(B, C, N_KV_HEADS, n_sel * SEL_BLOCK)
        s2 = jnp.einsum('bqkgd,bqksd->bqkgs', qc, ksel).astype(F32) * ATTN_SCALE
        s2 = jnp.where((kpos <= pc[None, :, None, None])[:, :, :, None, :], s2, NEG)
        p2 = jax.nn.softmax(s2, axis=-1)
        o_sel = jnp.einsum('bqkgs,bqksd->bqkgd', p2.astype(vsel.dtype), vsel)
        return o_cmp, o_sel

    C = math.gcd(Tq, max(1, 256 // B))
    nchunk = Tq // C
    qs = q.reshape(B, nchunk, C, N_KV_HEADS, GROUP, HEAD_DIM).swapaxes(0, 1)
    ps = q_pos.reshape(nchunk, C)
    o_cmp, o_sel = lax.map(chunk, (qs, ps))
    shp = (B, Tq, N_KV_HEADS, GROUP, HEAD_DIM)
    return o_cmp.swapaxes(0, 1).reshape(shp), o_sel.swapaxes(0, 1).reshape(shp)


def window_attend(q, q_pos, k, v, k_pos):
    s = jnp.einsum('bqkgd,bskd->bqkgs', q, k).astype(F32) * ATTN_SCALE
    m = (k_pos[None, :] <= q_pos[:, None]) & (k_pos[None, :] >= q_pos[:, None] - WINDOW) & (k_pos[None, :] >= 0)
    s = jnp.where(m[None, :, None, None, :], s, NEG)
    p = jax.nn.softmax(s, axis=-1)
    return jnp.einsum('bqkgs,bskd->bqkgd', p.astype(v.dtype), v)


def window_prompt(q, k, v):
    B, T = q.shape[:2]
    qb = min(WIN_QBLOCK, T)
    nb = T // qb
    span = WINDOW + qb
    padw = ((0, 0), (WINDOW, 0), (0, 0), (0, 0))
    kp = jnp.pad(k, padw)
    vp = jnp.pad(v, padw)
    qs = q.reshape(B, nb, qb, N_KV_HEADS, GROUP, HEAD_DIM).swapaxes(0, 1)
    starts = jnp.arange(nb) * qb

    def blk(args):
        qc, st = args
        kc = lax.dynamic_slice_in_dim(kp, st, span, axis=1)
        vc = lax.dynamic_slice_in_dim(vp, st, span, axis=1)
        return window_attend(qc, st + jnp.arange(qb), kc, vc, st - WINDOW + jnp.arange(span))

    o = lax.map(blk, (qs, starts))
    return o.swapaxes(0, 1).reshape(B, T, N_KV_HEADS, GROUP, HEAD_DIM)


def nsa_combine(o_cmp, o_sel, o_win, gates, w_out):
    B, T = o_cmp.shape[:2]
    o = (gates[:, :, 0, ..., None] * o_cmp + gates[:, :, 1, ..., None] * o_sel
         + gates[:, :, 2, ..., None] * o_win)
    return o.reshape(B, T, N_HEADS * HEAD_DIM) @ w_out


def nsa_prompt(h, w_in, cmp_pe, cmp_w1, cmp_w2, w_out):
    T = h.shape[1]
    q, rows, win, gates = nsa_project(h, w_in)
    o_cmp, o_sel = nsa_global(q, jnp.arange(T), rows, cmp_pe, cmp_w1, cmp_w2)
    o_win = window_prompt(q, win[:, :, 0], win[:, :, 1])
    return nsa_combine(o_cmp, o_sel, o_win, gates, w_out), rows, win[:, -min(WINDOW, T):]


def nsa_sample(h, pool, page_table, win_buf, w_in, cmp_pe, cmp_w1, cmp_w2, w_out):
    B, T, _ = h.shape
    past_len = page_table.shape[1] * pool.shape[1]
    q, rows, win, gates = nsa_project(h, w_in)
    past = pool[page_table].reshape(B, past_len, 4, N_KV_HEADS, HEAD_DIM)
    full = jnp.concatenate([past.astype(rows.dtype), rows], axis=1)
    q_pos = past_len + jnp.arange(T)
    o_cmp, o_sel = nsa_global(q, q_pos, full, cmp_pe, cmp_w1, cmp_w2)
    wb = win_buf.shape[1]
    wk = jnp.concatenate([win_buf.astype(win.dtype), win], axis=1)
    k_pos = past_len - wb + jnp.arange(wb + T)
    o_win = window_attend(q, q_pos, wk[:, :, 0], wk[:, :, 1], k_pos)
    new_win = wk[:, -min(WINDOW, past_len + T):]
    return nsa_combine(o_cmp, o_sel, o_win, gates, w_out), rows, new_win


def trunk(x, c, start_pos, conv_prefix, pool_prefix, ffn_prefix, nsa_attn, weights):
    (ada_w, ada_b, norm1_g, norm2_g, final_g, conv_w_in, conv_w_dw, conv_ln_g, conv_ln_b, conv_w_out,
     pool_w, pool_scale, ffn_w_up, ffn_w_dw, ffn_w_down) = weights
    cmod = jax.nn.silu(c)
    new_conv, new_rows, new_win, new_pool, new_ffn = [], [], [], [], []
    for i in range(DEPTH):
        mod = cmod @ ada_w[i] + ada_b[i]
        sh1, sc1, g1, sh2, sc2, g2 = jnp.split(mod, 6, axis=-1)
        h = modulate(rms_norm(x, norm1_g[i]), sh1, sc1)
        kind, j = i % N_MIXERS, i // N_MIXERS
        if kind == 0:
            y, st = conformer_conv(h, conv_prefix[j], conv_w_in[j], conv_w_dw[j], conv_ln_g[j], conv_ln_b[j], conv_w_out[j])
            new_conv.append(st)
        elif kind == 1:
            y, rows, win = nsa_attn(j, h)
            new_rows.append(rows)
            new_win.append(win)
        else:
            y, st = pool_mixer(h, pool_prefix[j], start_pos, pool_w[j], pool_scale[j])
            new_pool.append(st)
        x = x + g1[:, None, :] * y
        h = modulate(rms_norm(x, norm2_g[i]), sh2, sc2)
        y, st = conv_ffn(h, ffn_prefix[i], ffn_w_up[i], ffn_w_dw[i], ffn_w_down[i])
        new_ffn.append(st)
        x = x + g2[:, None, :] * y
    y_out = rms_norm(x, final_g)
    return (y_out, jnp.stack(new_conv), jnp.stack(new_rows), jnp.stack(new_win),
            jnp.stack(new_pool), jnp.stack(new_ffn))


def setup_inputs(seed: int = 0) -> dict:
    key = jax.random.key(seed)
    ks = list(jax.random.split(key, 40))
    cnt = [0]

    def nk():
        cnt[0] += 1
        return ks[cnt[0] - 1]

    def nrm(shape, s):
        return jax.random.normal(nk(), shape, F32) * s

    D = D_MODEL
    n_pages = PAST_LEN // PAGE_SIZE
    n_phys = (5 * DEC_BATCH * n_pages) // 4
    perm = jax.random.permutation(nk(), n_phys)
    page_table = perm[:DEC_BATCH * n_pages].reshape(DEC_BATCH, n_pages).astype(jnp.int32)
    win_buf = min(WINDOW, PAST_LEN)
    return {
        "x_prompt": nrm((BATCH, SEQ, D), 1.0),
        "x_sample": nrm((DEC_BATCH, DEC_SEQ, D), 1.0),
        "cache_nsa_kv": nrm((N_NSA_LAYERS, n_phys, PAGE_SIZE, 4, N_KV_HEADS, HEAD_DIM), 1.0),
        "state_nsa_win": nrm((N_NSA_LAYERS, DEC_BATCH, win_buf, 2, N_KV_HEADS, HEAD_DIM), 1.0),
        "state_conv": nrm((N_CONV_LAYERS, DEC_BATCH, CONV_WIDTH - 1, CONV_DIM), 0.5),
        "state_pool": nrm((N_POOL_LAYERS, DEC_BATCH, POOL_STATE, D), 1.0),
        "state_ffn": nrm((DEPTH, DEC_BATCH, FFN_CONV_WIDTH - 1, D_FF), 0.5),
        "page_table": page_table,
        "c_prompt": nrm((BATCH, D), 1.0),
        "c_sample": nrm((DEC_BATCH, D), 1.0),
        "ada_w": nrm((DEPTH, D, 6 * D), 0.5 * D ** -0.5),
        "ada_b": nrm((DEPTH, 6 * D), 0.02),
        "norm1_g": 1.0 + nrm((DEPTH, D), 0.05),
        "norm2_g": 1.0 + nrm((DEPTH, D), 0.05),
        "final_g": 1.0 + nrm((D,), 0.05),
        "conv_w_in": nrm((N_CONV_LAYERS, D, 2 * CONV_DIM), D ** -0.5),
        "conv_w_dw": nrm((N_CONV_LAYERS, CONV_WIDTH, CONV_DIM), CONV_WIDTH ** -0.5),
        "conv_ln_g": 1.0 + nrm((N_CONV_LAYERS, CONV_DIM), 0.05),
        "conv_ln_b": nrm((N_CONV_LAYERS, CONV_DIM), 0.02),
        "conv_w_out": nrm((N_CONV_LAYERS, CONV_DIM, D), CONV_DIM ** -0.5),
        "nsa_w_in": nrm((N_NSA_LAYERS, D, NSA_IN_COLS), D ** -0.5),
        "nsa_cmp_pe": nrm((N_NSA_LAYERS, 2, CMP_BLOCK, HEAD_DIM), 0.1),
        "nsa_cmp_w1": nrm((N_NSA_LAYERS, 2, CMP_BLOCK * HEAD_DIM, CMP_HIDDEN), (CMP_BLOCK * HEAD_DIM) ** -0.5),
        "nsa_cmp_w2": nrm((N_NSA_LAYERS, 2, CMP_HIDDEN, HEAD_DIM), CMP_HIDDEN ** -0.5),
        "nsa_w_out": nrm((N_NSA_LAYERS, N_HEADS * HEAD_DIM, D), (N_HEADS * HEAD_DIM) ** -0.5),
        "pool_w": nrm((N_POOL_LAYERS, N_POOL_GROUPS, POOL_GROUP_DIM, POOL_GROUP_DIM), POOL_GROUP_DIM ** -0.5),
        "pool_scale": 1.0 + nrm((N_POOL_LAYERS, D), 0.1),
        "ffn_w_up": nrm((DEPTH, D, 2 * D_FF), D ** -0.5),
        "ffn_w_dw": nrm((DEPTH, FFN_CONV_WIDTH, D_FF), FFN_CONV_WIDTH ** -0.5),
        "ffn_w_down": nrm((DEPTH, D_FF, D), D_FF ** -0.5),
    }


def reference(x_prompt, x_sample, cache_nsa_kv, state_nsa_win, state_conv, state_pool, state_ffn, page_table,
              c_prompt, c_sample, ada_w, ada_b, norm1_g, norm2_g, final_g, conv_w_in, conv_w_dw, conv_ln_g,
              conv_ln_b, conv_w_out, nsa_w_in, nsa_cmp_pe, nsa_cmp_w1, nsa_cmp_w2, nsa_w_out, pool_w, pool_scale,
              ffn_w_up, ffn_w_dw, ffn_w_down):
    weights = (ada_w, ada_b, norm1_g, norm2_g, final_g, conv_w_in, conv_w_dw, conv_ln_g, conv_ln_b, conv_w_out,
               pool_w, pool_scale, ffn_w_up, ffn_w_dw, ffn_w_down)
    dt = x_prompt.dtype
    B = x_prompt.shape[0]
    zc = jnp.zeros((N_CONV_LAYERS, B, CONV_WIDTH - 1, CONV_DIM), dt)
    zp = jnp.zeros((N_POOL_LAYERS, B, POOL_STATE, D_MODEL), dt)
    zf = jnp.zeros((DEPTH, B, FFN_CONV_WIDTH - 1, D_FF), dt)
    prompt_attn = lambda j, h: nsa_prompt(h, nsa_w_in[j], nsa_cmp_pe[j], nsa_cmp_w1[j], nsa_cmp_w2[j], nsa_w_out[j])
    y_prompt, p_conv, p_rows, p_win, p_pool, p_ffn = trunk(x_prompt, c_prompt, 0, zc, zp, zf, prompt_attn, weights)
    past_len = page_table.shape[1] * cache_nsa_kv.shape[2]
    sample_attn = lambda j, h: nsa_sample(h, cache_nsa_kv[j], page_table, state_nsa_win[j], nsa_w_in[j],
                                          nsa_cmp_pe[j], nsa_cmp_w1[j], nsa_cmp_w2[j], nsa_w_out[j])
    y_sample, s_conv, s_rows, s_win, s_pool, s_ffn = trunk(x_sample, c_sample, past_len, state_conv, state_pool,
                                                           state_ffn, sample_attn, weights)
    return (y_prompt, y_sample, p_conv, p_rows, p_win, p_pool, p_ffn, s_conv, s_rows, s_win, s_pool, s_ffn)
```

```python
import os
from contextlib import ExitStack
import numpy as np
import concourse.bass as bass
import concourse.mybir as mybir
from concourse.bass_utils import run_bass_kernel_spmd

F32 = mybir.dt.float32
BF16 = mybir.dt.bfloat16
I32 = mybir.dt.int32
AF = mybir.ActivationFunctionType
ALU = mybir.AluOpType
AX = mybir.AxisListType

D = 1024
KC = 8
NPT = 2176
NS = 128
TOT = NPT + NS
DFF = 2816
FC = 22
EPS = 1e-6
NCORES = 8

V_N1, V_N2, V_FG, V_ADAB, V_CDW, V_CLNG, V_CLNB, V_PSC, V_FDW = 0, 32, 64, 72, 264, 760, 776, 792, 800
NV = 800 + 264


import types


def freeze(fn):
    if fn is None or fn.__closure__ is None:
        return fn
    cells = []
    for c in fn.__closure__:
        try:
            cells.append(types.CellType(c.cell_contents))
        except ValueError:
            cells.append(c)
    return types.FunctionType(fn.__code__, fn.__globals__, fn.__name__, fn.__defaults__, tuple(cells))


class Buf:
    __slots__ = ("w", "r")

    def __init__(self):
        self.w = None
        self.r = {}


class Sched:
    COMPUTE = ("pe", "act", "dve", "pool")

    def __init__(self, nc, stack, ndma=8):
        self.nc = nc
        self.ops = []
        self.cnt = {k: 0 for k in self.COMPUTE}
        self.known = {}
        self.ndma = ndma
        self.dma_i = {"sp": 0, "pool": 0}
        self.dma_cnt = {}
        self.bufs = {}
        self.sems = {}
        for k in self.COMPUTE:
            self.sems[k] = stack.enter_context(nc.semaphore("s_" + k))
        for q in ("sp", "pool"):
            for i in range(ndma):
                key = "d%s%d" % (q, i)
                self.sems[key] = stack.enter_context(nc.semaphore("s_" + key))
                self.dma_cnt[key] = 0

    def buf(self, *key):
        b = self.bufs.get(key)
        if b is None:
            b = self.bufs[key] = Buf()
        return b

    def _deps(self, queue, reads, writes):
        need = {}

        def add(sv):
            if sv is None:
                return
            k, v = sv
            if need.get(k, 0) < v:
                need[k] = v
        for b in reads:
            add(b.w)
        for b in writes:
            add(b.w)
            for k, v in b.r.items():
                add((k, v))
        waits = []
        for k, v in need.items():
            if queue == "pe" and k == "pe":
                continue
            if self.known.get((queue, k), 0) >= v:
                continue
            self.known[(queue, k)] = v
            waits.append((k, v))
        return waits

    def op(self, queue, fn, reads=(), writes=()):
        fn = freeze(fn)
        waits = self._deps(queue, reads, writes)
        self.cnt[queue] += 1
        v = self.cnt[queue]
        self.ops.append((queue, fn, waits, (queue, 1)))
        for b in reads:
            if b.r.get(queue, 0) < v:
                b.r[queue] = v
        for b in writes:
            b.w = (queue, v)
            b.r = {}

    def dma(self, queue, fn, reads=(), writes=()):
        i = self.dma_i[queue]
        self.dma_i[queue] += 1
        key = "d%s%d" % (queue, i % self.ndma)
        fn = freeze(fn)
        prev = self.dma_cnt[key]
        waits = self._deps(queue, reads, writes)
        if prev and self.known.get((queue, key), 0) < prev:
            self.known[(queue, key)] = prev
            waits.append((key, prev))
        v = prev + 16
        self.dma_cnt[key] = v
        self.ops.append((queue, fn, waits, (key, 16)))
        for b in reads:
            b.r[key] = v
        for b in writes:
            b.w = (key, v)
            b.r = {}

    def emit(self, final=False):
        nc = self.nc
        if final:
            waits = [(k, v) for k, v in self.dma_cnt.items() if v]
            waits += [(k, self.cnt[k]) for k in self.COMPUTE if self.cnt[k]]
            self.ops.append(("sp", None, waits, None))
        if not final:
            allw = [(k, v) for k, v in self.dma_cnt.items() if v] + [(k, self.cnt[k]) for k in self.COMPUTE if self.cnt[k]]
            for q in ("sp", "pe", "act", "dve", "pool"):
                ws = [(k, v) for (k, v) in allw if self.known.get((q, k), 0) < v]
                for (k, v) in ws:
                    self.known[(q, k)] = v
                self.ops.append((q, None, ws, None))
        byq = {q: [] for q in ("sp", "pe", "act", "dve", "pool")}
        for o in self.ops:
            byq[o[0]].append(o)
        self.ops = []
        sems = self.sems

        def run(eng, lst):
            for (_, fn, waits, inc) in lst:
                for (k, v) in waits:
                    eng.wait_ge(sems[k], v)
                if fn is not None:
                    fn(eng).then_inc(sems[inc[0]], inc[1])

        with nc.Block() as block:
            @block.sync
            def _(e):
                run(e, byq["sp"])

            @block.tensor
            def _(e):
                run(e, byq["pe"])

            @block.scalar
            def _(e):
                run(e, byq["act"])

            @block.vector
            def _(e):
                run(e, byq["dve"])

            @block.gpsimd
            def _(e):
                run(e, byq["pool"])


def bc(ap, shape):
    return ap.to_broadcast(list(shape))


class K:
    def __init__(self, stage):
        self.stage = stage
        nc = self.nc = bass.Bass("TRN2", target_bir_lowering=False)
        dt = nc.dram_tensor

        def inp(name, shape, dtype=F32):
            return dt(name, list(shape), dtype, kind="ExternalInput").ap()

        def outp(name, shape):
            return dt(name, list(shape), F32, kind="ExternalOutput").ap()
        self.d_xT = inp("xT", [D, TOT])
        self.d_xA = inp("xA", [D, 2048])
        self.d_cT = inp("cT", [D, 17])
        self.d_flag = inp("flag", [128, 1])
        self.d_vecs = inp("vecs", [128, NV])
        self.d_ident = inp("ident", [128, 128])
        self.d_ada_w = inp("ada_w", [4, D, 6 * D])
        self.d_conv_w_in = inp("conv_w_in", [2, D, 2 * D])
        self.d_conv_w_out = inp("conv_w_out", [2, D, D])
        self.d_ffn_w_up = inp("ffn_w_up", [4, D, 2 * DFF])
        self.d_ffn_w_down = inp("ffn_w_down", [4, DFF, D])
        self.d_pool_w = inp("pool_w", [4, 256, 256])
        self.d_sconvT = inp("sconvT", [2, D, 16 * 30])
        self.d_spoolT = inp("spoolT", [D, 16 * 15])
        self.d_sffnT = inp("sffnT", [4, DFF, 16 * 2])
        self.d_invcnt = inp("invcnt", [128, 4 * 128])
        self.d_nsa_w_in = inp("nsa_w_in", [D, 2608])
        self.d_nsa_w_out = inp("nsa_w_out", [D, D])
        self.d_w1 = inp("cmp_w1", [2, 4096, 256])
        self.d_w2 = inp("cmp_w2", [2, 256, 64])
        self.d_pe2 = inp("pe2", [64, 256])
        self.d_onehot = inp("onehot", [64, 4096])
        self.d_tri = inp("tri", [128, 256])
        self.d_masks = inp("masks", [18, 128, 256])
        self.d_cache = inp("cache", [2560 * 128, 1024])
        self.d_pt = inp("ptab", [1, 256], I32)
        self.d_iota = inp("iota", [128, 1])
        self.d_swin = inp("swin", [16, 512, 512])
        self.d_vbs = inp("vbs", [128, 512])
        self.d_ohnew = inp("ohnew", [64, 128])
        self.d_mnew = inp("mnew", [128, 128])
        self.sc_sKs = dt("sc_sKs", [16, 64, 4, 2048], BF16).ap()
        self.sc_sVs = dt("sc_sVs", [16, 16, 128, 256], BF16).ap()
        self.sc_sVc = dt("sc_sVc", [16, 64, 4, 2048], BF16).ap()
        self.o_swinp = outp("o_swinp", [16, 504, 512])
        self.sc_Ks = dt("sc_Ks", [64, 4, 4096], BF16).ap()
        self.sc_Vs = dt("sc_Vs", [32, 128, 256], BF16).ap()
        self.sc_wK = dt("sc_wK", [32, 64, 512], BF16).ap()
        self.sc_wV = dt("sc_wV", [32, 128, 256], BF16).ap()
        self.o_rows = outp("o_rows", [2048 + 128, 1024])
        self.o_win = outp("o_win", [512 + 128, 512])
        self.o_yT = outp("o_yT", [D, TOT])
        self.o_convT = outp("o_convT", [2, D, 160 + 352])
        self.o_ffnT = outp("o_ffnT", [4, DFF, 2 + 32])
        self.o_poolT = outp("o_poolT", [D, 144 + 112])

    def build(self):
        nc = self.nc
        with ExitStack() as st:
            E = st.enter_context
            self.S = S = Sched(nc, st)
            self.B = S.buf
            sb = lambda n, s, d=F32: E(nc.sbuf_tensor(self.un(n), list(s), d))
            self.xT = sb("xT_s", [128, KC, TOT])
            self.modT = sb("modT", [128, 2, 48, 17])
            self.vecs = sb("vecs_s", [128, NV])
            self.ident = sb("ident_s", [128, 128])
            self.ones = sb("ones_s", [128, 128], BF16)
            self.flag = sb("flag_s", [128, 1])
            self.cmT = sb("cmT", [128, KC, 17], BF16)
            self.coef = sb("coef", [128, 6, KC])
            self.As = sb("As", [128, KC, NS])
            self.ckT = sb("ckT", [64, 4, 64], BF16)
            self.cv = sb("cv", [64, 4, 64], BF16)
            self.ps = E(nc.psum_tensor("ps", [128, 8, 512], F32))
            self.setup()
            S.emit()
            self.main()
            S.emit(final=True)
        return nc

    def setup(self):
        S, B = self.S, self.B
        xT, vecs, ident, flag = self.xT, self.vecs, self.ident, self.flag
        if self.stage >= 2:
            for c in range(KC):
                S.dma("sp", lambda e, c=c: e.dma_start(out=xT[:, c, 0:2048], in_=self.d_xA[c * 128:(c + 1) * 128, :]),
                      writes=[B("x", c, t) for t in range(16)])
        else:
            self.load_x()
        S.dma("sp", lambda e: e.dma_start(out=vecs[:], in_=self.d_vecs), writes=[B("vecs")])
        S.dma("sp", lambda e: e.dma_start(out=ident[:], in_=self.d_ident), writes=[B("ident")])
        S.dma("sp", lambda e: e.dma_start(out=flag[:], in_=self.d_flag), writes=[B("flag")])
        S.op("pool", lambda e: e.memset(self.ones[:], 1.0), writes=[B("ones")])
        with ExitStack() as ph:
            ct = ph.enter_context(self.nc.sbuf_tensor("ct", [128, KC, 17], F32))
            S.dma("sp", lambda e: e.dma_start(out=ct[:], in_=self.d_cT.rearrange("(c p) n -> p c n", p=128)), writes=[B("ct")])
            S.op("act", lambda e: e.activation(out=self.cmT[:], in_=ct[:], func=AF.Silu), reads=[B("ct")], writes=[B("cmT")])
            S.emit()

    def load_x(self):
        S, B = self.S, self.B
        for c in range(KC):
            S.dma("sp", lambda e, c=c: e.dma_start(out=self.xT[:, c, :], in_=self.d_xT[c * 128:(c + 1) * 128, :]),
                  writes=[B("x", c, t) for t in range(TOT // 128)])

    def un(self, n):
        self._uid = getattr(self, "_uid", 0) + 1
        return "%s_%d" % (n, self._uid)

    def xb(self, c, t0, w):
        return [self.B("x", c, t) for t in range(t0 // 128, (t0 + w + 127) // 128)]

    def ada(self, i):
        S, B, nc = self.S, self.B, self.nc
        slot = i % 2
        with ExitStack() as ph:
            wb = [ph.enter_context(nc.sbuf_tensor(self.un("adaw%d" % k), [128, KC, 1024], BF16)) for k in range(2)]
            for grp in range(6):
                w = wb[grp % 2]
                bw = B("adaw", grp % 2)
                S.dma("pool", lambda e, w=w, grp=grp: e.dma_start(
                    out=w[:], in_=self.d_ada_w[i, :, grp * 1024:(grp + 1) * 1024].rearrange("(c p) n -> p c n", p=128)), writes=[bw])
                for j in range(8):
                    bank = j % 2
                    bp = B("ps", bank)
                    for k in range(KC):
                        S.op("pe", lambda e, w=w, j=j, k=k, bank=bank: e.matmul(
                            self.ps[:, bank, 0:17], lhsT=w[:, k, j * 128:(j + 1) * 128], rhs=self.cmT[:, k, :],
                            start=(k == 0), stop=(k == KC - 1)), reads=[bw, B("cmT")], writes=[bp])
                    m = grp * 8 + j
                    S.op("act", lambda e, m=m, bank=bank: e.activation(
                        out=self.modT[:, slot, m, :], in_=self.ps[:, bank, 0:17], func=AF.Identity,
                        bias=self.vecs[:, V_ADAB + i * 48 + m:V_ADAB + i * 48 + m + 1], scale=1.0),
                        reads=[bp, B("vecs")], writes=[B("modT", slot)])
            S.emit()

    def coefs(self, i, sub):
        S, B = self.S, self.B
        slot = i % 2
        mod = self.modT
        ng = self.vecs[:, (V_N1 if sub == 0 else V_N2) + i * 8:(V_N1 if sub == 0 else V_N2) + i * 8 + 8]
        o = 3 * sub
        m0 = 24 * sub
        rd = [B("modT", slot), B("vecs")]
        S.op("dve", lambda e: e.scalar_tensor_tensor(out=self.coef[:, o, :], in0=mod[:, slot, m0 + 8:m0 + 16, 0], scalar=1.0,
                                                     in1=ng, op0=ALU.add, op1=ALU.mult), reads=rd, writes=[B("coef")])
        S.op("dve", lambda e: e.tensor_copy(out=self.coef[:, o + 1, :], in_=mod[:, slot, m0:m0 + 8, 0]), reads=rd, writes=[B("coef")])
        S.op("dve", lambda e: e.tensor_copy(out=self.coef[:, o + 2, :], in_=mod[:, slot, m0 + 16:m0 + 24, 0]), reads=rd, writes=[B("coef")])
        for c in range(KC):
            S.op("dve", lambda e, c=c: e.tensor_scalar(
                out=self.As[:, c, :].rearrange("p (s j) -> p s j", j=8),
                in0=bc(mod[:, slot, m0 + 8 + c, 1:17].unsqueeze(2), [128, 16, 8]), scalar1=1.0,
                scalar2=ng[:, c:c + 1], op0=ALU.add, op1=ALU.mult), reads=rd, writes=[B("As")])

    def mod_s(self, i, m):
        return bc(self.modT[:, i % 2, m * 8:(m + 1) * 8, 1:17].unsqueeze(3), [128, 8, 16, 8])

    def norm_mod(self, i, sub, t0, w, kind, hT, hcol, tmp, sq, rs, hname, fp32_out=None):
        S, B = self.S, self.B
        xT, ps = self.xT, self.ps
        bp = B("ps", 7)
        for c in range(KC):
            s_ = sq[c % 2]
            S.op("act", lambda e, c=c, s_=s_: e.activation(out=s_[:, 0:w], in_=xT[:, c, t0:t0 + w], func=AF.Square),
                 reads=self.xb(c, t0, w), writes=[B("sq", c % 2)])
            S.op("pe", lambda e, c=c, s_=s_: e.matmul(ps[:, 7, 0:w], lhsT=self.ones[:], rhs=s_[:, 0:w], start=(c == 0), stop=(c == KC - 1)),
                 reads=[B("sq", c % 2), B("ones")], writes=[bp])
        S.op("act", lambda e: e.activation(out=rs[:, 0:w], in_=ps[:, 7, 0:w], func=AF.Sqrt, bias=EPS, scale=1.0 / D), reads=[bp], writes=[B("rs")])
        S.op("dve", lambda e: e.reciprocal(out=rs[:, 0:w], in_=rs[:, 0:w]), reads=[B("rs")], writes=[B("rs")])
        o = 3 * sub
        for c in range(KC):
            t_ = tmp[c % 2]
            bt = B("nt", c % 2)
            hb = B(hname, c, hcol // 128) if hname else None
            S.op("dve", lambda e, c=c, t_=t_: e.tensor_tensor(out=t_[:, 0:w], in0=xT[:, c, t0:t0 + w], in1=rs[:, 0:w], op=ALU.mult),
                 reads=self.xb(c, t0, w) + [B("rs")], writes=[bt])
            wr = [B(hname, c, t) for t in range(hcol // 128, (hcol + w + 127) // 128)]
            if kind == "p":
                S.op("act", lambda e, c=c, t_=t_: e.activation(out=hT[:, c, hcol:hcol + w], in_=t_[:, 0:w], func=AF.Identity,
                                                              bias=self.coef[:, o + 1, c:c + 1], scale=self.coef[:, o, c:c + 1]),
                     reads=[bt, B("coef")], writes=wr)
                if fp32_out is not None:
                    S.op("dve", lambda e, c=c, t_=t_: e.tensor_scalar(out=fp32_out[:, c, hcol:hcol + w], in0=t_[:, 0:w],
                                                                       scalar1=self.coef[:, o, c:c + 1], scalar2=self.coef[:, o + 1, c:c + 1],
                                                                       op0=ALU.mult, op1=ALU.add), reads=[bt, B("coef")], writes=[B("h32", c)])
            else:
                S.op("dve", lambda e, c=c, t_=t_: e.tensor_tensor(out=t_[:, 0:w], in0=t_[:, 0:w], in1=self.As[:, c, :], op=ALU.mult),
                     reads=[bt, B("As")], writes=[bt])
                shs = self.modT[:, i % 2, 24 * sub + c, 1:17]
                S.op("dve", lambda e, c=c, t_=t_, shs=shs: e.tensor_tensor(
                    out=hT[:, c, hcol:hcol + w].rearrange("p (s j) -> p s j", j=8), in0=t_[:, 0:w].rearrange("p (s j) -> p s j", j=8),
                    in1=bc(shs.unsqueeze(2), [128, 16, 8]), op=ALU.add), reads=[bt, B("modT", i % 2)], writes=wr)
                if fp32_out is not None:
                    S.op("pool", lambda e, c=c, t_=t_, shs=shs: e.tensor_tensor(
                        out=fp32_out[:, c, hcol:hcol + w].rearrange("p (s j) -> p s j", j=8), in0=t_[:, 0:w].rearrange("p (s j) -> p s j", j=8),
                        in1=bc(shs.unsqueeze(2), [128, 16, 8]), op=ALU.add), reads=[bt, B("modT", i % 2)], writes=[B("h32", c)])

    def resid(self, i, sub, c, t0, w, kind, src_ap, rd, extra_scale=None):
        S, B = self.S, self.B
        xs = self.xT[:, c, t0:t0 + w]
        if kind == "p":
            g = extra_scale if extra_scale is not None else self.coef[:, 3 * sub + 2, c:c + 1]
            S.op("dve", lambda e: e.scalar_tensor_tensor(out=xs, in0=src_ap, scalar=g, in1=xs, op0=ALU.mult, op1=ALU.add),
                 reads=rd + [B("coef")] + self.xb(c, t0, w), writes=self.xb(c, t0, w))
        else:
            gs = self.modT[:, i % 2, 24 * sub + 16 + c, 1:17]
            tmp = self.rtmp
            S.op("dve", lambda e: e.tensor_tensor(out=tmp[:, 0:w].rearrange("p (s j) -> p s j", j=8), in0=src_ap.rearrange("p (s j) -> p s j", j=8),
                                                  in1=bc(gs.unsqueeze(2), [128, 16, 8]), op=ALU.mult),
                 reads=rd + [B("modT", i % 2)], writes=[B("rtmp")])
            if extra_scale is not None:
                S.op("dve", lambda e: e.tensor_scalar(out=tmp[:, 0:w], in0=tmp[:, 0:w], scalar1=extra_scale, scalar2=None, op0=ALU.mult),
                     reads=[B("rtmp")], writes=[B("rtmp")])
            S.op("dve", lambda e: e.tensor_tensor(out=xs, in0=xs, in1=tmp[:, 0:w], op=ALU.add),
                 reads=[B("rtmp")] + self.xb(c, t0, w), writes=self.xb(c, t0, w))

    def ffn(self, i, tiles):
        S, B, nc, ps = self.S, self.B, self.nc, self.ps
        G = 4
        with ExitStack() as ph:
            A = lambda n, s, d=F32: ph.enter_context(nc.sbuf_tensor(self.un(n), list(s), d))
            hT = A("f_hT", [128, KC, TOT], BF16)
            wa = [A("f_wa%d" % k, [128, KC, G * 128], BF16) for k in range(2)]
            wbb = [A("f_wb%d" % k, [128, KC, G * 128], BF16) for k in range(2)]
            wd = [A("f_wd%d" % k, [128, G, D], BF16) for k in range(2)]
            abuf = A("f_abuf", [128, G, 2 + 512])
            abs_ = A("f_abs", [128, G, 16, 10])
            ac = [A("f_ac%d" % k, [128, 512]) for k in range(2)]
            gb = [A("f_g%d" % k, [128, G, 512], BF16) for k in range(2)]
            bsb = [A("f_b%d" % k, [128, 512]) for k in range(2)]
            tmp = ac
            sq = [A("f_sq%d" % k, [128, 512], BF16) for k in range(2)]
            rs = A("f_rs", [128, 512])
            self.rtmp = A("f_rtmp", [128, 128])
            self.coefs(i, 1)
            for (t0, w, kind) in tiles:
                self.norm_mod(i, 1, t0, w, kind, hT, t0, tmp, sq, rs, "fh")
            fdw = lambda j, k: self.vecs[:, V_FDW + (i * FC + j) * 3 + k:V_FDW + (i * FC + j) * 3 + k + 1]
            npass = (FC + G - 1) // G
            gi = 0
            for p_ in range(npass):
                j0 = p_ * G
                ng = min(G, FC - j0)
                sl = p_ % 2
                bw = B("fw", sl)
                S.dma("pool", lambda e, sl=sl, j0=j0, ng=ng: e.dma_start(
                    out=wa[sl][:, :, 0:ng * 128], in_=self.d_ffn_w_up[i, :, j0 * 128:(j0 + ng) * 128].rearrange("(c p) n -> p c n", p=128)), writes=[bw])
                S.dma("pool", lambda e, sl=sl, j0=j0, ng=ng: e.dma_start(
                    out=wbb[sl][:, :, 0:ng * 128], in_=self.d_ffn_w_up[i, :, DFF + j0 * 128:DFF + (j0 + ng) * 128].rearrange("(c p) n -> p c n", p=128)), writes=[bw])
                S.dma("pool", lambda e, sl=sl, j0=j0, ng=ng: e.dma_start(
                    out=wd[sl][:, 0:ng, :], in_=self.d_ffn_w_down[i, j0 * 128:(j0 + ng) * 128, :].rearrange("(g p) n -> p g n", p=128)), writes=[bw])
                if any(k == "s" for (_, _, k) in tiles):
                    for g_ in range(ng):
                        S.dma("sp", lambda e, j0=j0, g_=g_: e.dma_start(
                            out=abs_[:, g_, :, 0:2], in_=self.d_sffnT[i, (j0 + g_) * 128:(j0 + g_ + 1) * 128, :].rearrange("p (s r) -> p s r", r=2)),
                            writes=[B("abs", g_)])
                pending = [None]

                def down(t0, w, kind, gsl, g_t):
                    for c in range(KC):
                        bank = 4 + (c % 2)
                        bp = B("ps", bank)
                        for jj in range(ng):
                            S.op("pe", lambda e, c=c, jj=jj, bank=bank, g_t=g_t: e.matmul(ps[:, bank, 0:w], lhsT=wd[sl][:, jj, c * 128:(c + 1) * 128],
                                                                                         rhs=g_t[:, jj, 0:w], start=(jj == 0), stop=(jj == ng - 1)),
                                 reads=[bw, B("g", gsl, jj)], writes=[bp])
                        self.resid(i, 1, c, t0, w, kind, ps[:, bank, 0:w], [bp])
                for ti, (t0, w, kind) in enumerate(tiles):
                    gsl = gi % 2
                    gi += 1
                    g_t = gb[gsl]
                    for jj in range(ng):
                        if jj == 1 and pending[0] is not None:
                            down(*pending[0])
                            pending[0] = None
                        j = j0 + jj
                        pa, pb = 2 * (jj % 2), 2 * (jj % 2) + 1
                        bpa, bpb = B("ps", pa), B("ps", pb)
                        hrd = [B("fh", k, t) for k in range(KC) for t in range(t0 // 128, (t0 + w + 127) // 128)]
                        for k in range(KC):
                            S.op("pe", lambda e, k=k, jj=jj, pa=pa: e.matmul(ps[:, pa, 0:w], lhsT=wa[sl][:, k, jj * 128:(jj + 1) * 128],
                                                                            rhs=hT[:, k, t0:t0 + w], start=(k == 0), stop=(k == KC - 1)),
                                 reads=[bw] + hrd, writes=[bpa])
                        for k in range(KC):
                            S.op("pe", lambda e, k=k, jj=jj, pb=pb: e.matmul(ps[:, pb, 0:w], lhsT=wbb[sl][:, k, jj * 128:(jj + 1) * 128],
                                                                            rhs=hT[:, k, t0:t0 + w], start=(k == 0), stop=(k == KC - 1)),
                                 reads=[bw] + hrd, writes=[bpb])
                        a_ = ac[jj % 2]
                        ba = B("ac", jj % 2)
                        b_ = bsb[jj % 2]
                        S.op("act", lambda e, b_=b_, pb=pb: e.activation(out=b_[:, 0:w], in_=ps[:, pb, 0:w], func=AF.Copy), reads=[bpb], writes=[B("bsb", jj % 2)])
                        if kind == "p":
                            bab = B("abuf", jj)
                            if ti == 0:
                                S.op("pool", lambda e, jj=jj: e.memset(abuf[:, jj, 0:2], 0.0), writes=[bab])
                            S.op("act", lambda e, jj=jj, pa=pa: e.activation(out=abuf[:, jj, 2:2 + w], in_=ps[:, pa, 0:w], func=AF.Copy),
                                 reads=[bpa], writes=[bab])
                            if ti == 0 and self.halo:
                                S.op("dve", lambda e, jj=jj: e.tensor_scalar(out=abuf[:, jj, 2:130], in0=abuf[:, jj, 2:130], scalar1=self.flag[:, 0:1],
                                                                            scalar2=None, op0=ALU.mult), reads=[bab, B("flag")], writes=[bab])
                            if t0 + w == NPT and self.halo:
                                S.dma("sp", lambda e, jj=jj, j=j: e.dma_start(out=self.o_ffnT[i, j * 128:(j + 1) * 128, 0:2], in_=abuf[:, jj, w:w + 2]), reads=[bab])
                            S.op("dve", lambda e, jj=jj, j=j, a_=a_: e.tensor_scalar(out=a_[:, 0:w], in0=abuf[:, jj, 0:w], scalar1=fdw(j, 0), scalar2=None, op0=ALU.mult),
                                 reads=[bab, B("vecs")], writes=[ba])
                            for k in (1, 2):
                                S.op("dve", lambda e, jj=jj, j=j, a_=a_, k=k: e.scalar_tensor_tensor(
                                    out=a_[:, 0:w], in0=abuf[:, jj, k:k + w], scalar=fdw(j, k), in1=a_[:, 0:w], op0=ALU.mult, op1=ALU.add),
                                    reads=[bab, B("vecs"), ba], writes=[ba])
                            S.op("pool", lambda e, jj=jj: e.tensor_copy(out=abuf[:, jj, 0:2], in_=abuf[:, jj, w:w + 2]), reads=[bab], writes=[bab])
                        else:
                            bab = B("abs", jj)
                            S.op("act", lambda e, jj=jj, pa=pa: e.activation(out=abs_[:, jj, :, 2:10], in_=ps[:, pa, 0:128].rearrange("p (s j) -> p s j", j=8), func=AF.Copy),
                                 reads=[bpa], writes=[bab])
                            S.dma("sp", lambda e, jj=jj, j=j: e.dma_start(
                                out=self.o_ffnT[i, j * 128:(j + 1) * 128, 2:34].rearrange("p (s r) -> p s r", r=2), in_=abs_[:, jj, :, 8:10]), reads=[bab])
                            a3 = a_[:, 0:128].rearrange("p (s j) -> p s j", j=8)
                            S.op("dve", lambda e, jj=jj, j=j, a3=a3: e.tensor_scalar(out=a3, in0=abs_[:, jj, :, 0:8], scalar1=fdw(j, 0), scalar2=None, op0=ALU.mult),
                                 reads=[bab, B("vecs")], writes=[ba])
                            for k in (1, 2):
                                S.op("dve", lambda e, jj=jj, j=j, a3=a3, k=k: e.scalar_tensor_tensor(
                                    out=a3, in0=abs_[:, jj, :, k:k + 8], scalar=fdw(j, k), in1=a3, op0=ALU.mult, op1=ALU.add),
                                    reads=[bab, B("vecs"), ba], writes=[ba])
                        S.op("act", lambda e, a_=a_: e.activation(out=a_[:, 0:w], in_=a_[:, 0:w], func=AF.Silu), reads=[ba], writes=[ba])
                        S.op("dve", lambda e, a_=a_, jj=jj, b_=b_, g_t=g_t: e.tensor_tensor(out=g_t[:, jj, 0:w], in0=a_[:, 0:w], in1=b_[:, 0:w], op=ALU.mult),
                             reads=[ba, B("bsb", jj % 2)], writes=[B("g", gsl, jj)])
                    if pending[0] is not None:
                        down(*pending[0])
                    pending[0] = (t0, w, kind, gsl, g_t)
                if pending[0] is not None:
                    down(*pending[0])
                    pending[0] = None
            S.emit()

    def conv(self, i, jl, tiles):
        S, B, nc, ps = self.S, self.B, self.nc, self.ps
        has_s = any(k == "s" for (_, _, k) in tiles)
        with ExitStack() as ph0:
            A0 = lambda n, s, d=F32: ph0.enter_context(nc.sbuf_tensor(self.un(n), list(s), d))
            ubuf = A0("c_ubuf", [128, KC, 30 + NPT], BF16)
            ubs = A0("c_ubs", [128, KC, 16, 38], BF16)
            self.rtmp = A0("c_rtmp", [128, 128])
            self.coefs(i, 0)
            with ExitStack() as ph:
                A = lambda n, s, d=F32: ph.enter_context(nc.sbuf_tensor(self.un(n), list(s), d))
                w_in = A("c_win", [128, KC, 2 * D], BF16)
                hT = A("c_hT", [128, KC, 512], BF16)
                tmp = [A("c_t%d" % k, [128, 512]) for k in range(2)]
                sq = [A("c_sq%d" % k, [128, 512], BF16) for k in range(2)]
                rs = A("c_rs", [128, 512])
                sg = [A("c_sg%d" % k, [128, 512]) for k in range(2)]
                u32 = [A("c_u%d" % k, [128, 512]) for k in range(2)]
                sst = [A("c_sst%d" % k, [128, 16, 30]) for k in range(2)]
                for hh in range(2):
                    S.dma("pool", lambda e, hh=hh: e.dma_start(out=w_in[:, :, hh * D:(hh + 1) * D],
                                                               in_=self.d_conv_w_in[jl, :, hh * D:(hh + 1) * D].rearrange("(c p) n -> p c n", p=128)),
                          writes=[B("cwin")])
                S.op("pool", lambda e: e.memset(ubuf[:, :, 0:30], 0.0), writes=[B("ub", c, 0) for c in range(KC)])
                if has_s:
                    for c in range(KC):
                        st_ = sst[c % 2]
                        S.dma("sp", lambda e, c=c, st_=st_: e.dma_start(out=st_[:], in_=self.d_sconvT[jl, c * 128:(c + 1) * 128, :].rearrange("p (s r) -> p s r", r=30)), writes=[B("sst", c % 2)])
                        S.op("pool", lambda e, c=c, st_=st_: e.tensor_copy(out=ubs[:, c, :, 0:30], in_=st_[:]), reads=[B("sst", c % 2)], writes=[B("ubs", c)])
                        S.dma("sp", lambda e, c=c, st_=st_: e.dma_start(out=self.o_convT[jl, c * 128:(c + 1) * 128, 160:512].rearrange("p (s r) -> p s r", r=22),
                                                                       in_=st_[:, :, 8:30]), reads=[B("sst", c % 2)])
                for (t0, w, kind) in tiles:
                    self.norm_mod(i, 0, t0, w, kind, hT, 0, tmp, sq, rs, "ch")
                    hrd = [B("ch", k, t) for k in range(KC) for t in range(0, (w + 127) // 128)]
                    for c in range(KC):
                        pa, pb = 2 * (c % 2), 2 * (c % 2) + 1
                        bpa, bpb = B("ps", pa), B("ps", pb)
                        for (pp, off) in ((pa, 0), (pb, D)):
                            for k in range(KC):
                                S.op("pe", lambda e, k=k, c=c, pp=pp, off=off: e.matmul(ps[:, pp, 0:w], lhsT=w_in[:, k, off + c * 128:off + (c + 1) * 128],
                                                                                       rhs=hT[:, k, 0:w], start=(k == 0), stop=(k == KC - 1)),
                                     reads=[B("cwin")] + hrd, writes=[B("ps", pp)])
                        s_ = sg[c % 2]
                        u_ = u32[c % 2]
                        S.op("act", lambda e, s_=s_, pb=pb: e.activation(out=s_[:, 0:w], in_=ps[:, pb, 0:w], func=AF.Sigmoid), reads=[bpb], writes=[B("sg", c % 2)])
                        S.op("dve", lambda e, s_=s_, u_=u_, pa=pa: e.tensor_tensor(out=u_[:, 0:w], in0=ps[:, pa, 0:w], in1=s_[:, 0:w], op=ALU.mult),
                             reads=[bpa, B("sg", c % 2)], writes=[B("u32", c % 2)])
                        if kind == "p":
                            ubw = [B("ub", c, t) for t in range((30 + t0) // 128, (30 + t0 + w + 127) // 128 + 1)]
                            if t0 == 0 and self.halo:
                                S.op("dve", lambda e, u_=u_: e.tensor_scalar(out=u_[:, 0:128], in0=u_[:, 0:128], scalar1=self.flag[:, 0:1], scalar2=None, op0=ALU.mult),
                                     reads=[B("u32", c % 2), B("flag")], writes=[B("u32", c % 2)])
                            S.op("pool", lambda e, u_=u_, c=c: e.tensor_copy(out=ubuf[:, c, 30 + t0:30 + t0 + w], in_=u_[:, 0:w]), reads=[B("u32", c % 2)], writes=ubw)
                            if t0 + w == NPT and self.halo:
                                S.dma("sp", lambda e, u_=u_, c=c: e.dma_start(out=self.o_convT[jl, c * 128:(c + 1) * 128, 0:32], in_=u_[:, w - 32:w]), reads=[B("u32", c % 2)])
                        else:
                            S.op("pool", lambda e, u_=u_, c=c: e.tensor_copy(out=ubs[:, c, :, 30:38], in_=u_[:, 0:128].rearrange("p (s j) -> p s j", j=8)),
                                 reads=[B("u32", c % 2)], writes=[B("ubs", c)])
                            S.dma("sp", lambda e, u_=u_, c=c: e.dma_start(out=self.o_convT[jl, c * 128:(c + 1) * 128, 32:160], in_=u_[:, 0:128]), reads=[B("u32", c % 2)])
                S.emit()
            if os.environ.get("KCUT") == "1":
                return
            with ExitStack() as ph:
                A = lambda n, s, d=F32: ph.enter_context(nc.sbuf_tensor(self.un(n), list(s), d))
                w_out = A("c_wout", [128, KC, D], BF16)
                identb = A("c_idb", [128, 128], BF16)
                diag = A("c_diag", [128, 2, 31, 128], BF16)
                W2 = 256
                y32 = A("c_y32", [128, KC, W2])
                ybf = [A("c_ybf%d" % k, [128, W2], BF16) for k in range(2)]
                ysq = [A("c_ysq%d" % k, [128, W2], BF16) for k in range(2)]
                mean = A("c_mean", [128, W2])
                rstd = A("c_rstd", [128, W2])
                zt = A("c_zt", [128, KC, W2], BF16)
                zt32 = [A("c_z32%d" % k, [128, W2]) for k in range(2)]
                uodd = A("c_uodd", [128, KC, W2 + 32], BF16)
                ubso = A("c_ubso", [128, KC, 16, 38], BF16)
                if has_s:
                    for c in range(KC):
                        S.op("pool", lambda e, c=c: e.tensor_copy(out=ubso[:, c, :, 0:36], in_=ubs[:, c, :, 1:37]), reads=[B("ubs", c)], writes=[B("ubso", c)])
                S.dma("pool", lambda e: e.dma_start(out=w_out[:], in_=self.d_conv_w_out[jl].rearrange("(c p) n -> p c n", p=128)), writes=[B("cwout")])
                S.op("pool", lambda e: e.tensor_copy(out=identb[:], in_=self.ident[:]), reads=[B("ident")], writes=[B("idb")])
                lng = lambda c: self.vecs[:, V_CLNG + jl * KC + c:V_CLNG + jl * KC + c + 1]
                lnb = lambda c: self.vecs[:, V_CLNB + jl * KC + c:V_CLNB + jl * KC + c + 1]
                sub = []
                for (t0, w, kind) in tiles:
                    for q0 in range(0, w, W2):
                        sub.append((t0 + q0, min(W2, w - q0), kind))
                for (t0, w, kind) in sub:
                    bps, bpq = B("ps", 6), B("ps", 7)
                    if kind == "p":
                        for c in range(KC):
                            S.op("pool", lambda e, c=c: e.tensor_copy(out=uodd[:, c, 0:w + 29], in_=ubuf[:, c, t0 + 1:t0 + w + 30]),
                                 reads=[B("ub", c, t) for t in range(t0 // 128, (t0 + w + 31 + 127) // 128 + 1)], writes=[B("uodd", c)])
                    for c in range(KC):
                        bank = c % 2
                        bp = B("ps", bank)
                        for k in range(31):
                            S.op("dve", lambda e, c=c, k=k: e.tensor_scalar(
                                out=diag[:, c % 2, k, :], in0=identb[:], scalar1=self.vecs[:, V_CDW + (jl * KC + c) * 31 + k:V_CDW + (jl * KC + c) * 31 + k + 1],
                                scalar2=None, op0=ALU.mult), reads=[B("idb"), B("vecs")], writes=[B("diag", c % 2, k)])
                        for k in range(31):
                            if kind == "p":
                                if k % 2 == 0:
                                    rhs = ubuf[:, c, t0 + k:t0 + k + w]
                                    rd = [B("ub", c, t) for t in range((t0 + k) // 128, (t0 + k + w + 127) // 128 + 1)]
                                else:
                                    rhs = uodd[:, c, k - 1:k - 1 + w]
                                    rd = [B("uodd", c)]
                            else:
                                if k % 2 == 0:
                                    rhs = ubs[:, c, :, k:k + 8]
                                    rd = [B("ubs", c)]
                                else:
                                    rhs = ubso[:, c, :, k - 1:k + 7]
                                    rd = [B("ubso", c)]
                            oap = ps[:, bank, 0:w] if kind == "p" else ps[:, bank, 0:128].rearrange("p (s j) -> p s j", j=8)
                            S.op("pe", lambda e, c=c, k=k, rhs=rhs, bank=bank, oap=oap: e.matmul(oap, lhsT=diag[:, c % 2, k, :], rhs=rhs, start=(k == 0), stop=(k == 30)),
                                 reads=rd + [B("diag", c % 2, k)], writes=[bp])
                        S.op("act", lambda e, c=c, bank=bank: e.activation(out=y32[:, c, 0:w], in_=ps[:, bank, 0:w], func=AF.Copy), reads=[bp], writes=[B("y32", c)])
                        S.op("dve", lambda e, c=c, bank=bank: e.tensor_copy(out=ybf[c % 2][:, 0:w], in_=y32[:, c, 0:w]), reads=[B("y32", c)], writes=[B("ybf", c % 2)])
                        S.op("act", lambda e, c=c, bank=bank: e.activation(out=ysq[c % 2][:, 0:w], in_=y32[:, c, 0:w], func=AF.Square), reads=[B("y32", c)], writes=[B("ysq", c % 2)])
                        S.op("pe", lambda e, c=c: e.matmul(ps[:, 6, 0:w], lhsT=self.ones[:], rhs=ybf[c % 2][:, 0:w], start=(c == 0), stop=(c == KC - 1)),
                             reads=[B("ybf", c % 2), B("ones")], writes=[bps])
                        S.op("pe", lambda e, c=c: e.matmul(ps[:, 7, 0:w], lhsT=self.ones[:], rhs=ysq[c % 2][:, 0:w], start=(c == 0), stop=(c == KC - 1)),
                             reads=[B("ysq", c % 2), B("ones")], writes=[bpq])
                    S.op("act", lambda e: e.activation(out=mean[:, 0:w], in_=ps[:, 6, 0:w], func=AF.Copy, scale=1.0 / D), reads=[bps], writes=[B("mean")])
                    S.op("dve", lambda e: e.tensor_tensor(out=rstd[:, 0:w], in0=mean[:, 0:w], in1=mean[:, 0:w], op=ALU.mult), reads=[B("mean")], writes=[B("rstd")])
                    S.op("dve", lambda e: e.scalar_tensor_tensor(out=rstd[:, 0:w], in0=ps[:, 7, 0:w], scalar=1.0 / D, in1=rstd[:, 0:w], op0=ALU.mult, op1=ALU.subtract),
                         reads=[bpq, B("rstd")], writes=[B("rstd")])
                    S.op("act", lambda e: e.activation(out=rstd[:, 0:w], in_=rstd[:, 0:w], func=AF.Sqrt, bias=EPS, scale=1.0), reads=[B("rstd")], writes=[B("rstd")])
                    S.op("dve", lambda e: e.reciprocal(out=rstd[:, 0:w], in_=rstd[:, 0:w]), reads=[B("rstd")], writes=[B("rstd")])
                    for c in range(KC):
                        z_ = zt32[c % 2]
                        bz = B("z32", c % 2)
                        S.op("dve", lambda e, c=c, z_=z_: e.tensor_tensor(out=z_[:, 0:w], in0=y32[:, c, 0:w], in1=mean[:, 0:w], op=ALU.subtract),
                             reads=[B("y32", c), B("mean")], writes=[bz])
                        S.op("dve", lambda e, c=c, z_=z_: e.tensor_tensor(out=z_[:, 0:w], in0=z_[:, 0:w], in1=rstd[:, 0:w], op=ALU.mult),
                             reads=[bz, B("rstd")], writes=[bz])
                        S.op("act", lambda e, c=c, z_=z_: e.activation(out=zt[:, c, 0:w], in_=z_[:, 0:w], func=AF.Silu, bias=lnb(c), scale=lng(c)),
                             reads=[bz, B("vecs")], writes=[B("zt", c)])
                    for c in range(KC):
                        bank = 2 + (c % 2)
                        bp = B("ps", bank)
                        for k in range(KC):
                            S.op("pe", lambda e, c=c, k=k, bank=bank: e.matmul(ps[:, bank, 0:w], lhsT=w_out[:, k, c * 128:(c + 1) * 128], rhs=zt[:, k, 0:w],
                                                                              start=(k == 0), stop=(k == KC - 1)), reads=[B("cwout"), B("zt", k)], writes=[bp])
                        self.resid(i, 0, c, t0, w, kind, ps[:, bank, 0:w], [bp])
                S.emit()


    def nsa_proj(self, i, ntiles, aux):
        S, B, nc, ps = self.S, self.B, self.nc, self.ps
        with ExitStack() as ph:
            A = lambda n, s, d=F32: ph.enter_context(nc.sbuf_tensor(self.un(n), list(s), d))
            w_kv = A("n_wkv", [128, KC, 1536], BF16)
            KcT = A("n_KcT", [64, 2, 4, 2048], BF16)
            w1 = A("n_w1", [64, 64, 256], BF16)
            w2 = A("n_w2", [128, 2, 2, 64], BF16)
            hT = A("n_hT", [128, KC, 128], BF16)
            rows = A("n_rows", [128, 1536])
            vst = A("n_vst", [128, 2, 256], BF16)
            kst = A("n_kst", [64, 2, 4, 128], BF16)
            pe2 = A("n_pe2", [64, 2, 128])
            HT = A("n_HT", [128, 2, 160], BF16)
            tmp = [A("n_t%d" % k, [128, 128]) for k in range(2)]
            sq = [A("n_sq%d" % k, [128, 128], BF16) for k in range(2)]
            rs = A("n_rs", [128, 128])
            self.coefs(i, 0)
            for hh in range(3):
                S.dma("pool", lambda e, hh=hh: e.dma_start(out=w_kv[:, :, hh * 512:(hh + 1) * 512],
                                                           in_=self.d_nsa_w_in[:, 1024 + hh * 512:1024 + (hh + 1) * 512].rearrange("(c p) n -> p c n", p=128)), writes=[B("wkv")])
            S.dma("sp", lambda e: e.dma_start(out=pe2[:], in_=self.d_pe2.rearrange("d (x t) -> d x t", x=2)), writes=[B("pe2")])
            for X in range(2):
                S.dma("pool", lambda e, X=X: e.dma_start(out=w2[:, X, :, :], in_=self.d_w2[X].rearrange("(c p) d -> p c d", p=128)), writes=[B("w2")])
            S.op("pool", lambda e: e.memset(HT[:], 0.0), writes=[B("HT")])
            for t in range(ntiles):
                l0 = t * 128 if aux else (t + 1) * 128
                pt = t if aux else 16 + t
                tok0 = t * 128
                self.norm_mod(i, 0, l0, 128, "p", hT, 0, tmp, sq, rs, "nh")
                hrd = [B("nh", k, 0) for k in range(KC)]
                for gq in range(3):
                    for k in range(KC):
                        S.op("pe", lambda e, gq=gq, k=k: e.matmul(ps[:, gq, 0:512], lhsT=hT[:, k, :], rhs=w_kv[:, k, gq * 512:(gq + 1) * 512],
                                                                 start=(k == 0), stop=(k == KC - 1)), reads=hrd + [B("wkv")], writes=[B("ps", gq)])
                    S.op("act", lambda e, gq=gq: e.activation(out=rows[:, gq * 512:(gq + 1) * 512], in_=ps[:, gq, 0:512], func=AF.Copy),
                         reads=[B("ps", gq)], writes=[B("rows", gq)])
                if not aux:
                    S.dma("sp", lambda e: e.dma_start(out=self.o_rows[tok0:tok0 + 128, :], in_=rows[:, 0:1024]), reads=[B("rows", 0), B("rows", 1)])
                    if t >= 12:
                        S.dma("sp", lambda e: e.dma_start(out=self.o_win[(t - 12) * 128:(t - 11) * 128, :], in_=rows[:, 1024:1536]), reads=[B("rows", 2)])
                S.op("dve", lambda e: e.tensor_copy(out=vst[:, 0, :], in_=rows[:, 768:1024]), reads=[B("rows", 1)], writes=[B("vst", 0)])
                S.op("dve", lambda e: e.tensor_copy(out=vst[:, 1, :], in_=rows[:, 1280:1536]), reads=[B("rows", 2)], writes=[B("vst", 1)])
                S.dma("sp", lambda e: e.dma_start(out=self.sc_Vs[pt], in_=vst[:, 0, :]), reads=[B("vst", 0)], writes=[B("scVs", pt)])
                S.dma("sp", lambda e: e.dma_start(out=self.sc_wV[pt], in_=vst[:, 1, :]), reads=[B("vst", 1)], writes=[B("scwV", pt)])
                for gi, off in enumerate((0, 256, 512, 1024)):
                    bank = 3 + gi % 2
                    for kvh in range(4):
                        for k in range(KC):
                            S.op("pe", lambda e, kvh=kvh, k=k, bank=bank, off=off: e.matmul(
                                ps[0:64, bank, kvh * 128:(kvh + 1) * 128], lhsT=w_kv[:, k, off + kvh * 64:off + kvh * 64 + 64], rhs=hT[:, k, :],
                                start=(k == 0), stop=(k == KC - 1)), reads=hrd + [B("wkv")], writes=[B("ps", bank)])
                    src3 = ps[0:64, bank, 0:512].rearrange("p (k t) -> p k t", k=4)
                    if gi < 2:
                        S.op("dve", lambda e, gi=gi, src3=src3: e.tensor_tensor(out=KcT[:, gi, :, tok0:tok0 + 128], in0=src3,
                                                                                 in1=bc(pe2[:, gi, :].unsqueeze(1), [64, 4, 128]), op=ALU.add),
                             reads=[B("ps", bank), B("pe2")], writes=[B("KcT", gi)])
                    else:
                        S.op("act", lambda e, gi=gi, src3=src3: e.activation(out=kst[:, gi - 2, :, :], in_=src3, func=AF.Copy),
                             reads=[B("ps", bank)], writes=[B("kst", gi - 2)])
                S.dma("sp", lambda e: e.dma_start(out=self.sc_Ks[:, :, pt * 128:(pt + 1) * 128], in_=kst[:, 0, :, :]), reads=[B("kst", 0)], writes=[B("scKs", pt)])
                S.dma("sp", lambda e: e.dma_start(out=self.sc_wK[pt].rearrange("d (k t) -> d k t", k=4), in_=kst[:, 1, :, :]), reads=[B("kst", 1)], writes=[B("scwK", pt)])
            slot0 = 0 if aux else 32
            for X in range(2):
                S.dma("pool", lambda e, X=X: e.dma_start(out=w1[:], in_=self.d_w1[X].rearrange("(i d) j -> d i j", d=64)), writes=[B("w1")])
                kv4 = KcT[:, X, :, :].rearrange("d k (n i) -> d k n i", i=64)
                for jc in range(2):
                    for ii in range(64):
                        S.op("pe", lambda e, jc=jc, ii=ii, kv4=kv4: e.matmul(
                            ps[:, 5, 0:128].rearrange("p (k n) -> p k n", k=4), lhsT=w1[:, ii, jc * 128:(jc + 1) * 128], rhs=kv4[:, :, :, ii],
                            start=(ii == 0), stop=(ii == 63)), reads=[B("w1"), B("KcT", X)], writes=[B("ps", 5)])
                    S.op("act", lambda e, jc=jc: e.activation(out=HT[:, jc, 32:160], in_=ps[:, 5, 0:128], func=AF.Silu), reads=[B("ps", 5)], writes=[B("HT")])
                if X == 0:
                    for jc in range(2):
                        S.op("pe", lambda e, jc=jc: e.matmul(ps[0:64, 6, 0:128], lhsT=w2[:, 0, jc, :], rhs=HT[:, jc, 32:160], start=(jc == 0), stop=(jc == 1)),
                             reads=[B("w2"), B("HT")], writes=[B("ps", 6)])
                    S.op("act", lambda e: e.activation(out=self.ckT[:, :, slot0:slot0 + 32], in_=ps[0:64, 6, 0:128].rearrange("p (k n) -> p k n", k=4), func=AF.Copy),
                         reads=[B("ps", 6)], writes=[B("ckT")])
                else:
                    for kvh in range(4):
                        for jc in range(2):
                            if slot0 == 0:
                                oap, lap = ps[0:32, 7, kvh * 64:(kvh + 1) * 64], HT[:, jc, 32 + kvh * 32:64 + kvh * 32]
                            else:
                                oap, lap = ps[0:64, 7, kvh * 64:(kvh + 1) * 64], HT[:, jc, kvh * 32:kvh * 32 + 64]
                            S.op("pe", lambda e, jc=jc, oap=oap, lap=lap: e.matmul(oap, lhsT=lap, rhs=w2[:, 1, jc, :], start=(jc == 0), stop=(jc == 1)),
                                 reads=[B("w2"), B("HT")], writes=[B("ps", 7)])
                    S.op("act", lambda e: e.activation(out=self.cv[slot0:slot0 + 32, :, :], in_=ps[slot0:slot0 + 32, 7, 0:256].rearrange("p (k d) -> p k d", k=4), func=AF.Copy),
                         reads=[B("ps", 7)], writes=[B("cv")])
            S.emit()

    def nsa_attn(self, i):
        S, B, nc, ps = self.S, self.B, self.nc, self.ps
        BIG = 240000.0
        with ExitStack() as ph:
            A = lambda n, s, d=F32: ph.enter_context(nc.sbuf_tensor(self.un(n), list(s), d))
            KsA = A("a_KsA", [128, 4, 4096], BF16)
            Vs = A("a_Vs", [128, 32, 4, 65], BF16)
            KwR = A("a_KwR", [64, 5, 4, 128], BF16)
            VwR = A("a_VwR", [128, 5, 4, 65], BF16)
            w_q = A("a_wq", [128, KC, 1072], BF16)
            w_o = A("a_wo", [128, KC, D], BF16)
            hT = A("a_hT", [128, KC, 128], BF16)
            QA = [A("a_QA%d" % k, [128, 4, 4, 128], BF16) for k in range(1)] * 2
            PT_ = [A("a_PT%d" % k, [128, 512], BF16) for k in range(2)]
            gates = [A("a_gates%d" % k, [128, 48]) for k in range(1)] * 2
            mk = A("a_mk", [128, 4, 64])
            tri = A("a_tri", [128, 2, 128], BF16)
            e4 = A("a_e4", [128, 16, 64])
            sm = A("a_sm", [128, 16])
            imp = A("a_imp", [128, 4, 64])
            imp2 = A("a_imp2", [128, 4, 64])
            wk = A("a_wk", [128, 4, 64])
            m8 = A("a_m8", [128, 4, 16])
            mbs = A("a_mbs", [128, 4, 128])
            pT = A("a_pT", [64, 4, 128], BF16)
            ocmp = [A("a_ocmp%d" % k, [128, 16, 64], BF16) for k in range(1)] * 2
            otot = A("a_otot", [128, 4, 4, 64])
            ototT = A("a_ototT", [128, KC, 128], BF16)
            ctmp = A("a_ctmp", [128, 4, 64])
            cf = A("a_cf", [128, 12])
            tmp = [imp2[:, 0:2, :].rearrange("p a b -> p (a b)"), wk[:, 0:2, :].rearrange("p a b -> p (a b)")]
            oTap = e4[0:65, 0:8, :].rearrange("p a b -> p (a b)")
            sq = [A("a_sq%d" % k, [128, 128], BF16) for k in range(2)]
            rs = A("a_rs", [128, 128])
            self.coefs(i, 0)
            S.dma("sp", lambda e: e.dma_start(out=KsA[0:64, :, :], in_=self.sc_Ks), reads=[B("scKs", t) for t in range(32)], writes=[B("KsA")])
            for kvh in range(4):
                S.dma("pool", lambda e, kvh=kvh: e.dma_start(out=KsA[64:128, kvh, :], in_=self.d_onehot), writes=[B("KsA")])
            S.op("pool", lambda e: e.memset(Vs[:], 1.0), writes=[B("Vs")])
            S.op("pool", lambda e: e.memset(VwR[:], 1.0), writes=[B("VwR", r_) for r_ in range(5)])
            for t in range(32):
                S.dma("sp", lambda e, t=t: e.dma_start(out=Vs[:, t, :, 0:64], in_=self.sc_Vs[t].rearrange("p (k d) -> p k d", k=4)), reads=[B("scVs", t)], writes=[B("Vs")])
            S.dma("pool", lambda e: e.dma_start(out=w_q[:, :, 0:1024], in_=self.d_nsa_w_in[:, 0:1024].rearrange("(c p) n -> p c n", p=128)), writes=[B("wq")])
            S.dma("pool", lambda e: e.dma_start(out=w_q[:, :, 1024:1072], in_=self.d_nsa_w_in[:, 2560:2608].rearrange("(c p) n -> p c n", p=128)), writes=[B("wq")])
            S.dma("pool", lambda e: e.dma_start(out=w_o[:], in_=self.d_nsa_w_out.rearrange("(c p) n -> p c n", p=128)), writes=[B("wo")])
            S.dma("pool", lambda e: e.dma_start(out=tri[:], in_=self.d_tri.rearrange("p (a t) -> p a t", a=2)), writes=[B("tri")])
            S.op("pool", lambda e: e.memset(mbs[:], 0.0), writes=[B("mbs")])

            def load_ring(pt):
                r_ = pt % 5
                S.dma("sp", lambda e: e.dma_start(out=KwR[:, r_, :, :], in_=self.sc_wK[pt].rearrange("d (k t) -> d k t", k=4)), reads=[B("scwK", pt)], writes=[B("KwR", r_)])
                S.dma("sp", lambda e: e.dma_start(out=VwR[:, r_, :, 0:64], in_=self.sc_wV[pt].rearrange("p (k d) -> p k d", k=4)), reads=[B("scwV", pt)], writes=[B("VwR", r_)])
            for pt in range(11, 15):
                load_ring(pt)

            def chain(qt):
                sl = 0
                l0 = qt * 128
                qa = QA[sl]
                bqa = B("QA", sl)
                load_ring(15 + qt)
                S.dma("sp", lambda e: e.dma_start(out=mk[:], in_=self.d_masks[qt].rearrange("p (a n) -> p a n", a=4)), writes=[B("mk")])
                self.norm_mod(i, 0, l0, 128, "p", hT, 0, tmp, sq, rs, "ah")
                hrd = [B("ah", k, 0) for k in range(KC)]
                for k in range(KC):
                    S.op("pe", lambda e, k=k: e.matmul(ps[:, 6, 0:48], lhsT=hT[:, k, :], rhs=w_q[:, k, 1024:1072], start=(k == 0), stop=(k == KC - 1)),
                         reads=hrd + [B("wq")], writes=[B("ps", 6)])
                S.op("act", lambda e: e.activation(out=gates[sl][:], in_=ps[:, 6, 0:48], func=AF.Sigmoid), reads=[B("ps", 6)], writes=[B("gates", sl)])
                for kvh in range(4):
                    bank = 4 + kvh % 2
                    for g in range(4):
                        hd = kvh * 4 + g
                        for k in range(KC):
                            S.op("pe", lambda e, g=g, k=k, hd=hd, bank=bank: e.matmul(ps[0:64, bank, g * 128:(g + 1) * 128], lhsT=w_q[:, k, hd * 64:(hd + 1) * 64], rhs=hT[:, k, :],
                                                                                     start=(k == 0), stop=(k == KC - 1)), reads=hrd + [B("wq")], writes=[B("ps", bank)])
                    S.op("act", lambda e, kvh=kvh, bank=bank: e.activation(out=qa[0:64, kvh, :, :], in_=ps[0:64, bank, 0:512].rearrange("p (g t) -> p g t", g=4), func=AF.Copy),
                         reads=[B("ps", bank)], writes=[bqa])
                for kvh in range(4):
                    for g in range(4):
                        hd = kvh * 4 + g
                        S.op("pe", lambda e, g=g, kvh=kvh, hd=hd: e.matmul(ps[:, 4 + hd // 8, (hd % 8) * 64:(hd % 8 + 1) * 64], lhsT=qa[0:64, kvh, g, :], rhs=self.ckT[:, kvh, :], start=True, stop=True),
                             reads=[bqa, B("ckT")], writes=[B("ps", 4 + hd // 8)])
                for hb in range(2):
                    S.op("dve", lambda e, hb=hb: e.scalar_tensor_tensor(out=e4[:, hb * 8:(hb + 1) * 8, :], in0=ps[:, 4 + hb, 0:512].rearrange("p (g n) -> p g n", g=8), scalar=0.125,
                                                                       in1=bc(mk[:, 0, :].unsqueeze(1), [128, 8, 64]), op0=ALU.mult, op1=ALU.add),
                         reads=[B("ps", 4 + hb), B("mk")], writes=[B("e4")])
                S.op("act", lambda e: e.activation(out=e4[:], in_=e4[:], func=AF.Exp), reads=[B("e4")], writes=[B("e4")])
                S.op("dve", lambda e: e.tensor_reduce(out=sm[:], in_=e4[:], axis=AX.X, op=ALU.add), reads=[B("e4")], writes=[B("sm")])
                S.op("dve", lambda e: e.tensor_scalar(out=sm[:], in0=sm[:], scalar1=1e-30, scalar2=None, op0=ALU.add), reads=[B("sm")], writes=[B("sm")])
                S.op("dve", lambda e: e.reciprocal(out=sm[:], in_=sm[:]), reads=[B("sm")], writes=[B("sm")])
                S.op("dve", lambda e: e.tensor_tensor(out=e4[:], in0=e4[:], in1=bc(sm[:].unsqueeze(2), [128, 16, 64]), op=ALU.mult),
                     reads=[B("sm"), B("e4")], writes=[B("e4")])
                for kvh in range(4):
                    S.op("dve", lambda e, kvh=kvh: e.tensor_reduce(out=imp[:, kvh, :], in_=e4[:, kvh * 4:(kvh + 1) * 4, :].rearrange("p g n -> p n g"), axis=AX.X, op=ALU.add),
                         reads=[B("e4")], writes=[B("imp")])
                S.op("dve", lambda e: e.tensor_tensor(out=imp[:], in0=imp[:], in1=bc(mk[:, 1, :].unsqueeze(1), [128, 4, 64]), op=ALU.mult), reads=[B("imp"), B("mk")], writes=[B("imp")])
                S.op("dve", lambda e: e.tensor_tensor(out=imp[:], in0=imp[:], in1=bc(mk[:, 2, :].unsqueeze(1), [128, 4, 64]), op=ALU.add), reads=[B("imp"), B("mk")], writes=[B("imp")])
                for kvh in range(4):
                    S.op("dve", lambda e, kvh=kvh: e.max(out=m8[:, kvh, 0:8], in_=imp[:, kvh, :]), reads=[B("imp")], writes=[B("m8")])
                    S.op("dve", lambda e, kvh=kvh: e.match_replace(out=wk[:, kvh, :], in_to_replace=m8[:, kvh, 0:8], in_values=imp[:, kvh, :], imm_value=-3e30), reads=[B("imp"), B("m8")], writes=[B("wk")])
                    S.op("dve", lambda e, kvh=kvh: e.max(out=m8[:, kvh, 8:16], in_=wk[:, kvh, :]), reads=[B("wk")], writes=[B("m8")])
                    S.op("dve", lambda e, kvh=kvh: e.match_replace(out=wk[:, kvh, :], in_to_replace=m8[:, kvh, 8:16], in_values=wk[:, kvh, :], imm_value=-3e30), reads=[B("wk"), B("m8")], writes=[B("wk")])
                S.op("dve", lambda e: e.tensor_tensor(out=imp2[:], in0=imp[:], in1=wk[:], op=ALU.subtract), reads=[B("wk"), B("imp")], writes=[B("imp2")])
                S.op("dve", lambda e: e.tensor_scalar(out=imp2[:], in0=imp2[:], scalar1=1.0, scalar2=None, op0=ALU.min), reads=[B("imp2")], writes=[B("imp2")])
                S.op("dve", lambda e: e.tensor_tensor(out=imp2[:], in0=imp2[:], in1=bc(mk[:, 3, :].unsqueeze(1), [128, 4, 64]), op=ALU.mult), reads=[B("imp2"), B("mk")], writes=[B("imp2")])
                S.op("dve", lambda e: e.tensor_scalar(out=mbs[:, :, 64:128], in0=imp2[:], scalar1=-1.0, scalar2=BIG, op0=ALU.add, op1=ALU.mult),
                     reads=[B("imp2")], writes=[B("mbs")])
                for kvh in range(4):
                    S.op("pe", lambda e, kvh=kvh: e.transpose(out=ps[:, 4, kvh * 128:(kvh + 1) * 128], in_=mbs[:, kvh, :], identity=self.ident[:]), reads=[B("mbs"), B("ident")], writes=[B("ps", 4)])
                for kvh in range(4):
                    S.op("act", lambda e, kvh=kvh: e.activation(out=qa[64:128, kvh, :, :], in_=bc(ps[64:128, 4, kvh * 128:(kvh + 1) * 128].unsqueeze(1), [64, 4, 128]), func=AF.Copy),
                         reads=[B("ps", 4)], writes=[bqa])
                for kvh in range(4):
                    for g in range(4):
                        S.op("pe", lambda e, g=g, kvh=kvh: e.transpose(out=ps[0:64, 5, g * 128:(g + 1) * 128], in_=e4[:, kvh * 4 + g, :], identity=self.ident[:]),
                             reads=[B("e4"), B("ident")], writes=[B("ps", 5)])
                    S.op("act", lambda e: e.activation(out=pT[:], in_=ps[0:64, 5, 0:512].rearrange("p (g t) -> p g t", g=4), func=AF.Copy), reads=[B("ps", 5)], writes=[B("pT")])
                    for g in range(4):
                        S.op("pe", lambda e, g=g, kvh=kvh: e.matmul(ps[:, 6, g * 64:(g + 1) * 64], lhsT=pT[:, g, :], rhs=self.cv[:, kvh, :], start=True, stop=True),
                             reads=[B("pT"), B("cv")], writes=[B("ps", 6)])
                    S.op("act", lambda e, kvh=kvh: e.activation(out=ocmp[sl][:, kvh * 4:(kvh + 1) * 4, :], in_=ps[:, 6, 0:256].rearrange("p (g d) -> p g d", g=4), func=AF.Copy),
                         reads=[B("ps", 6)], writes=[B("ocmp", sl)])

            tcnt = [0]

            def attend(qt):
                sl = 0
                l0 = qt * 128
                dk = 15 + qt
                qa_all = QA[sl]
                bqa = B("QA", sl)
                gt = gates[sl]
                tl = []
                for kvh in range(4):
                    for kt in range(dk + 1):
                        tl.append((kvh, 0, kt, kt == 0, kt == dk))
                    for wi, kt in enumerate(range(dk - 4, dk + 1)):
                        tl.append((kvh, 1, kt, wi == 0, wi == 4))

                def emit_S(n):
                    kvh, br, kt, first, last = tl[n]
                    sb_ = n % 2
                    if br == 0:
                        lhs, rhs, rk = KsA[:, kvh, kt * 128:(kt + 1) * 128], qa_all[:, kvh, :, :].rearrange("p g t -> p (g t)"), B("KsA")
                    else:
                        lhs, rhs, rk = KwR[:, kt % 5, kvh, :], qa_all[0:64, kvh, :, :].rearrange("p g t -> p (g t)"), B("KwR", kt % 5)
                    S.op("pe", lambda e, lhs=lhs, rhs=rhs, sb_=sb_: e.matmul(ps[:, sb_, 0:512], lhsT=lhs, rhs=rhs, start=True, stop=True), reads=[rk, bqa], writes=[B("ps", sb_)])

                def emit_rest(n):
                    kvh, br, kt, first, last = tl[n]
                    sb_ = n % 2
                    pt_ = PT_[sb_]
                    pt3 = pt_[:].rearrange("p (g t) -> p g t", g=4)
                    S.op("act", lambda e, sb_=sb_, pt_=pt_: e.activation(out=pt_[:], in_=ps[:, sb_, 0:512], func=AF.Exp, scale=0.125), reads=[B("ps", sb_)], writes=[B("PT", sb_)])
                    if (br == 0 and last) or (br == 1 and (first or last)):
                        ti_ = 1 if (br == 1 and first) else 0
                        S.op("dve", lambda e, pt3=pt3, ti_=ti_: e.tensor_tensor(out=pt3, in0=pt3, in1=bc(tri[:, ti_, :].unsqueeze(1), [128, 4, 128]), op=ALU.mult),
                             reads=[B("PT", sb_), B("tri")], writes=[B("PT", sb_)])
                    if br == 1 and kt <= 15:
                        S.op("dve", lambda e, pt_=pt_: e.tensor_scalar(out=pt_[:], in0=pt_[:], scalar1=self.flag[:, 0:1], scalar2=None, op0=ALU.mult),
                             reads=[B("PT", sb_), B("flag")], writes=[B("PT", sb_)])
                    if br == 0:
                        vv, rv = Vs[:, kt, kvh, :], B("Vs")
                    else:
                        vv, rv = VwR[:, kt % 5, kvh, :], B("VwR", kt % 5)
                    S.op("pe", lambda e, vv=vv, pt_=pt_, br=br, first=first, last=last: e.matmul(ps[0:65, 2 + br, 0:512], lhsT=vv, rhs=pt_[:], start=first, stop=last),
                         reads=[B("PT", sb_), rv], writes=[B("ps", 2 + br)])
                    if not last:
                        return
                    bk = 2 + br
                    S.op("act", lambda e, bk=bk: e.activation(out=oTap, in_=ps[0:65, bk, 0:512], func=AF.Copy), reads=[B("ps", bk)], writes=[B("e4")])
                    for g in range(4):
                        S.op("pe", lambda e, bk=bk, g=g: e.transpose(out=ps[:, bk + 2, g * 65:(g + 1) * 65], in_=oTap[:, g * 128:(g + 1) * 128], identity=self.ident[0:65, 0:65]),
                             reads=[B("e4"), B("ident")], writes=[B("ps", bk + 2)])
                    if br == 0:
                        return
                    osel = ps[:, 4, 0:260].rearrange("p (g d) -> p g d", g=4)
                    owin = ps[:, 5, 0:260].rearrange("p (g d) -> p g d", g=4)
                    oc = ocmp[sl][:, kvh * 4:(kvh + 1) * 4, :]
                    S.op("dve", lambda e, osel=osel: e.tensor_scalar(out=cf[:, 0:4], in0=osel[:, :, 64], scalar1=1e-30, scalar2=None, op0=ALU.add), reads=[B("ps", 4)], writes=[B("cf")])
                    S.op("dve", lambda e, owin=owin: e.tensor_scalar(out=cf[:, 4:8], in0=owin[:, :, 64], scalar1=1e-30, scalar2=None, op0=ALU.add), reads=[B("ps", 5)], writes=[B("cf")])
                    S.op("dve", lambda e: e.reciprocal(out=cf[:, 0:8], in_=cf[:, 0:8]), reads=[B("cf")], writes=[B("cf")])
                    S.op("dve", lambda e, kvh=kvh: e.tensor_tensor(out=cf[:, 0:4], in0=cf[:, 0:4], in1=gt[:, 16 + kvh * 4:20 + kvh * 4], op=ALU.mult), reads=[B("cf"), B("gates", sl)], writes=[B("cf")])
                    S.op("dve", lambda e, kvh=kvh: e.tensor_tensor(out=cf[:, 4:8], in0=cf[:, 4:8], in1=gt[:, 32 + kvh * 4:36 + kvh * 4], op=ALU.mult), reads=[B("cf"), B("gates", sl)], writes=[B("cf")])
                    ot = otot[:, kvh, :, :]
                    S.op("dve", lambda e, ot=ot, oc=oc, kvh=kvh: e.tensor_tensor(out=ot, in0=oc, in1=bc(gt[:, kvh * 4:kvh * 4 + 4].unsqueeze(2), [128, 4, 64]), op=ALU.mult),
                         reads=[B("ocmp", sl), B("gates", sl)], writes=[B("otot")])
                    for (src, c0, bk2) in ((osel, 0, 4), (owin, 4, 5)):
                        S.op("dve", lambda e, src=src, c0=c0: e.tensor_tensor(out=ctmp[:], in0=src[:, :, 0:64], in1=bc(cf[:, c0:c0 + 4].unsqueeze(2), [128, 4, 64]), op=ALU.mult),
                             reads=[B("ps", bk2), B("cf")], writes=[B("ctmp")])
                        S.op("dve", lambda e, ot=ot: e.tensor_tensor(out=ot, in0=ot, in1=ctmp[:], op=ALU.add), reads=[B("ctmp"), B("otot")], writes=[B("otot")])

                emit_S(0)
                for n in range(len(tl)):
                    if n + 1 < len(tl):
                        emit_S(n + 1)
                    emit_rest(n)
                of = otot[:].rearrange("p k g d -> p (k g d)")
                for hh in range(2):
                    for c4 in range(4):
                        c = hh * 4 + c4
                        S.op("pe", lambda e, c=c, c4=c4, hh=hh, of=of: e.transpose(out=ps[:, hh, c4 * 128:(c4 + 1) * 128], in_=of[:, c * 128:(c + 1) * 128], identity=self.ident[:]),
                             reads=[B("otot"), B("ident")], writes=[B("ps", hh)])
                    S.op("act", lambda e, hh=hh: e.activation(out=ototT[:, hh * 4:(hh + 1) * 4, :], in_=ps[:, hh, 0:512].rearrange("p (c t) -> p c t", c=4), func=AF.Copy),
                         reads=[B("ps", hh)], writes=[B("ototT")])
                for c in range(KC):
                    bank = 6 + (c % 2)
                    for k in range(KC):
                        S.op("pe", lambda e, c=c, k=k, bank=bank: e.matmul(ps[:, bank, 0:128], lhsT=w_o[:, k, c * 128:(c + 1) * 128], rhs=ototT[:, k, :], start=(k == 0), stop=(k == KC - 1)),
                             reads=[B("wo"), B("ototT")], writes=[B("ps", bank)])
                    self.resid(i, 0, c, l0, 128, "p", ps[:, bank, 0:128], [B("ps", bank)])

            for qt in range(17):
                chain(qt)
                attend(qt)
            S.emit()

    def nsa_sample_prep(self):
        S, B, nc, ps = self.S, self.B, self.nc, self.ps
        with ExitStack() as ph:
            A = lambda n, s, d=F32: ph.enter_context(nc.sbuf_tensor(self.un(n), list(s), d))
            w1 = A("s_w1", [64, 64, 256], BF16)
            w2 = A("s_w2", [128, 2, 2, 64], BF16)
            KcT = A("s_KcT", [64, 4, 2048], BF16)
            pgt = [A("s_pg%d" % k, [128, 1024]) for k in range(4)]
            kst = [A("s_kst%d" % k, [64, 4, 128], BF16) for k in range(2)]
            vcs = [A("s_vcs%d" % k, [64, 4, 128], BF16) for k in range(2)]
            vst = [A("s_vst%d" % k, [128, 256], BF16) for k in range(2)]
            pe2 = A("s_pe2", [64, 2, 128])
            HT = A("s_HT", [128, 2, 160], BF16)
            idx = A("s_idx", [128, 256], I32)
            iot = A("s_iot", [128, 1])
            with ExitStack() as ph1:
                ptb = ph1.enter_context(nc.sbuf_tensor(self.un("s_ptb"), [128, 256], I32))
                ptf = ph1.enter_context(nc.sbuf_tensor(self.un("s_ptf"), [128, 256], F32))
                S.dma("sp", lambda e: e.dma_start(out=ptb[:], in_=self.d_pt.partition_broadcast(128)), writes=[B("ptb")])
                S.dma("sp", lambda e: e.dma_start(out=iot[:], in_=self.d_iota), writes=[B("iot")])
                S.op("dve", lambda e: e.tensor_copy(out=ptf[:], in_=ptb[:]), reads=[B("ptb")], writes=[B("ptf")])
                S.op("dve", lambda e: e.tensor_scalar(out=ptf[:], in0=ptf[:], scalar1=128.0, scalar2=iot[:, 0:1], op0=ALU.mult, op1=ALU.add),
                     reads=[B("ptf"), B("iot")], writes=[B("ptf")])
                S.op("dve", lambda e: e.tensor_copy(out=idx[:], in_=ptf[:]), reads=[B("ptf")], writes=[B("idx")])
                S.emit()
            S.dma("sp", lambda e: e.dma_start(out=pe2[:], in_=self.d_pe2.rearrange("d (x t) -> d x t", x=2)), writes=[B("pe2")])
            for X in range(2):
                S.dma("pool", lambda e, X=X: e.dma_start(out=w2[:, X, :, :], in_=self.d_w2[X].rearrange("(c p) d -> p c d", p=128)), writes=[B("w2")])
            S.op("pool", lambda e: e.memset(HT[:], 0.0), writes=[B("HT")])
            S.dma("sp", lambda e: e.dma_start(out=self.o_swinp, in_=self.d_swin[:, 8:512, :]))

            def compress(X, sq_):
                r0 = 32 * (sq_ % 2)
                kv4 = KcT[:].rearrange("d k (n i) -> d k n i", i=64)
                for jc in range(2):
                    for ii in range(64):
                        S.op("pe", lambda e, jc=jc, ii=ii, kv4=kv4: e.matmul(
                            ps[:, 5, 0:128].rearrange("p (k n) -> p k n", k=4), lhsT=w1[:, ii, jc * 128:(jc + 1) * 128], rhs=kv4[:, :, :, ii],
                            start=(ii == 0), stop=(ii == 63)), reads=[B("w1"), B("KcT")], writes=[B("ps", 5)])
                    S.op("act", lambda e, jc=jc: e.activation(out=HT[:, jc, 32:160], in_=ps[:, 5, 0:128], func=AF.Silu), reads=[B("ps", 5)], writes=[B("HT")])
                if X == 0:
                    for jc in range(2):
                        S.op("pe", lambda e, jc=jc: e.matmul(ps[0:64, 6, 0:128], lhsT=w2[:, 0, jc, :], rhs=HT[:, jc, 32:160], start=(jc == 0), stop=(jc == 1)),
                             reads=[B("w2"), B("HT")], writes=[B("ps", 6)])
                    S.op("act", lambda e: e.activation(out=self.CK_all[:, :, sq_ * 32:(sq_ + 1) * 32], in_=ps[0:64, 6, 0:128].rearrange("p (k n) -> p k n", k=4), func=AF.Copy),
                         reads=[B("ps", 6)], writes=[B("CKall")])
                else:
                    for kvh in range(4):
                        for jc in range(2):
                            if r0 == 0:
                                oap, lap = ps[0:32, 7, kvh * 64:(kvh + 1) * 64], HT[:, jc, 32 + kvh * 32:64 + kvh * 32]
                            else:
                                oap, lap = ps[0:64, 7, kvh * 64:(kvh + 1) * 64], HT[:, jc, kvh * 32:kvh * 32 + 64]
                            S.op("pe", lambda e, jc=jc, oap=oap, lap=lap: e.matmul(oap, lhsT=lap, rhs=w2[:, 1, jc, :], start=(jc == 0), stop=(jc == 1)),
                                 reads=[B("w2"), B("HT")], writes=[B("ps", 7)])
                    S.op("act", lambda e: e.activation(out=self.CV_all[r0:r0 + 32, sq_ // 2, :, :], in_=ps[r0:r0 + 32, 7, 0:256].rearrange("p (k d) -> p k d", k=4), func=AF.Copy),
                         reads=[B("ps", 7)], writes=[B("CVall")])

            S.dma("pool", lambda e: e.dma_start(out=w1[:], in_=self.d_w1[0].rearrange("(i d) j -> d i j", d=64)), writes=[B("w1")])
            n = 0
            for sq_ in range(16):
                for pg in range(16):
                    pb = n % 4
                    kb = n % 2
                    n += 1
                    pt_ = pgt[pb]
                    col = sq_ * 16 + pg
                    S.dma("pool", lambda e, pt_=pt_, col=col: e.indirect_dma_start(out=pt_[:], out_offset=None, in_=self.d_cache,
                          in_offset=bass.IndirectOffsetOnAxis(ap=idx[:, col:col + 1], axis=0)), reads=[B("idx")], writes=[B("pg", pb)])
                    for X in range(3):
                        bank = X % 2
                        for kvh in range(4):
                            S.op("pe", lambda e, X=X, kvh=kvh, bank=bank, pt_=pt_: e.transpose(out=ps[0:64, bank, kvh * 128:(kvh + 1) * 128],
                                                                                              in_=pt_[:, X * 256 + kvh * 64:X * 256 + kvh * 64 + 64], identity=self.ident[:]),
                                 reads=[B("pg", pb), B("ident")], writes=[B("ps", bank)])
                        src3 = ps[0:64, bank, 0:512].rearrange("p (k t) -> p k t", k=4)
                        if X == 0:
                            S.op("dve", lambda e, src3=src3, pg=pg: e.tensor_tensor(out=KcT[:, :, pg * 128:(pg + 1) * 128], in0=src3,
                                                                                    in1=bc(pe2[:, 0, :].unsqueeze(1), [64, 4, 128]), op=ALU.add),
                                 reads=[B("ps", bank), B("pe2")], writes=[B("KcT")])
                        elif X == 1:
                            S.op("dve", lambda e, src3=src3, kb=kb: e.tensor_tensor(out=vcs[kb][:], in0=src3, in1=bc(pe2[:, 1, :].unsqueeze(1), [64, 4, 128]), op=ALU.add),
                                 reads=[B("ps", bank), B("pe2")], writes=[B("vcs", kb)])
                        else:
                            S.op("act", lambda e, src3=src3, kb=kb: e.activation(out=kst[kb][:], in_=src3, func=AF.Copy), reads=[B("ps", bank)], writes=[B("kst", kb)])
                    S.op("pool", lambda e, pt_=pt_, kb=kb: e.tensor_copy(out=vst[kb][:], in_=pt_[:, 768:1024]), reads=[B("pg", pb)], writes=[B("vst", kb)])
                    S.dma("sp", lambda e, kb=kb, sq_=sq_, pg=pg: e.dma_start(out=self.sc_sKs[sq_, :, :, pg * 128:(pg + 1) * 128], in_=kst[kb][:]), reads=[B("kst", kb)], writes=[B("scsK", sq_)])
                    S.dma("sp", lambda e, kb=kb, sq_=sq_, pg=pg: e.dma_start(out=self.sc_sVc[sq_, :, :, pg * 128:(pg + 1) * 128], in_=vcs[kb][:]), reads=[B("vcs", kb)], writes=[B("scsC", sq_)])
                    S.dma("sp", lambda e, kb=kb, sq_=sq_, pg=pg: e.dma_start(out=self.sc_sVs[sq_, pg], in_=vst[kb][:]), reads=[B("vst", kb)], writes=[B("scsV", sq_)])
                compress(0, sq_)
            S.dma("pool", lambda e: e.dma_start(out=w1[:], in_=self.d_w1[1].rearrange("(i d) j -> d i j", d=64)), writes=[B("w1")])
            for sq_ in range(16):
                S.dma("sp", lambda e, sq_=sq_: e.dma_start(out=KcT[:], in_=self.sc_sVc[sq_]), reads=[B("scsC", sq_)], writes=[B("KcT")])
                compress(1, sq_)
            S.emit()

    def nsa_sample_attn(self, i):
        S, B, nc, ps = self.S, self.B, self.nc, self.ps
        BIG = 240000.0
        l0 = NPT
        with ExitStack() as ph0:
            A0 = lambda n, s, d=F32: ph0.enter_context(nc.sbuf_tensor(self.un(n), list(s), d))
            QA = A0("b_QA", [128, 4, 4, 128], BF16)
            KnA = A0("b_KnA", [128, 4, 128], BF16)
            KwN = A0("b_KwN", [64, 4, 128], BF16)
            Vns = A0("b_Vns", [128, 4, 65], BF16)
            Vnw = A0("b_Vnw", [128, 4, 65], BF16)
            gates = A0("b_gates", [128, 48])
            self.rtmp = A0("b_rtmp", [128, 128])
            self.coefs(i, 0)
            with ExitStack() as ph:
                A = lambda n, s, d=F32: ph.enter_context(nc.sbuf_tensor(self.un(n), list(s), d))
                w_kv = A("b_wkv", [128, KC, 1536], BF16)
                w_q = A("b_wq", [128, KC, 1072], BF16)
                hT = A("b_hT", [128, KC, 128], BF16)
                rows = A("b_rows", [128, 1536])
                tmp = [A("b_t%d" % k, [128, 128]) for k in range(2)]
                sq = [A("b_sq%d" % k, [128, 128], BF16) for k in range(2)]
                rs = A("b_rs", [128, 128])
                for hh in range(3):
                    S.dma("pool", lambda e, hh=hh: e.dma_start(out=w_kv[:, :, hh * 512:(hh + 1) * 512],
                                                               in_=self.d_nsa_w_in[:, 1024 + hh * 512:1024 + (hh + 1) * 512].rearrange("(c p) n -> p c n", p=128)), writes=[B("wkv")])
                S.dma("pool", lambda e: e.dma_start(out=w_q[:, :, 0:1024], in_=self.d_nsa_w_in[:, 0:1024].rearrange("(c p) n -> p c n", p=128)), writes=[B("wq")])
                S.dma("pool", lambda e: e.dma_start(out=w_q[:, :, 1024:1072], in_=self.d_nsa_w_in[:, 2560:2608].rearrange("(c p) n -> p c n", p=128)), writes=[B("wq")])
                for kvh in range(4):
                    S.dma("pool", lambda e, kvh=kvh: e.dma_start(out=KnA[64:128, kvh, :], in_=self.d_ohnew), writes=[B("KnA")])
                S.op("pool", lambda e: e.memset(Vns[:], 1.0), writes=[B("Vns")])
                S.op("pool", lambda e: e.memset(Vnw[:], 1.0), writes=[B("Vnw")])
                self.norm_mod(i, 0, l0, 128, "s", hT, 0, tmp, sq, rs, "bh")
                hrd = [B("bh", k, 0) for k in range(KC)]
                for gq in range(3):
                    for k in range(KC):
                        S.op("pe", lambda e, gq=gq, k=k: e.matmul(ps[:, gq, 0:512], lhsT=hT[:, k, :], rhs=w_kv[:, k, gq * 512:(gq + 1) * 512],
                                                                 start=(k == 0), stop=(k == KC - 1)), reads=hrd + [B("wkv")], writes=[B("ps", gq)])
                    S.op("act", lambda e, gq=gq: e.activation(out=rows[:, gq * 512:(gq + 1) * 512], in_=ps[:, gq, 0:512], func=AF.Copy),
                         reads=[B("ps", gq)], writes=[B("rows", gq)])
                S.dma("sp", lambda e: e.dma_start(out=self.o_rows[2048:2176, :], in_=rows[:, 0:1024]), reads=[B("rows", 0), B("rows", 1)])
                S.dma("sp", lambda e: e.dma_start(out=self.o_win[512:640, :], in_=rows[:, 1024:1536]), reads=[B("rows", 2)])
                S.op("dve", lambda e: e.tensor_copy(out=Vns[:, :, 0:64], in_=rows[:, 768:1024].rearrange("p (k d) -> p k d", k=4)), reads=[B("rows", 1)], writes=[B("Vns")])
                S.op("dve", lambda e: e.tensor_copy(out=Vnw[:, :, 0:64], in_=rows[:, 1280:1536].rearrange("p (k d) -> p k d", k=4)), reads=[B("rows", 2)], writes=[B("Vnw")])
                for gi, off in enumerate((512, 1024)):
                    bank = 3 + gi
                    for kvh in range(4):
                        for k in range(KC):
                            S.op("pe", lambda e, kvh=kvh, k=k, bank=bank, off=off: e.matmul(
                                ps[0:64, bank, kvh * 128:(kvh + 1) * 128], lhsT=w_kv[:, k, off + kvh * 64:off + kvh * 64 + 64], rhs=hT[:, k, :],
                                start=(k == 0), stop=(k == KC - 1)), reads=hrd + [B("wkv")], writes=[B("ps", bank)])
                    src3 = ps[0:64, bank, 0:512].rearrange("p (k t) -> p k t", k=4)
                    dst = KnA[0:64, :, :] if gi == 0 else KwN[:]
                    S.op("act", lambda e, src3=src3, dst=dst: e.activation(out=dst, in_=src3, func=AF.Copy), reads=[B("ps", bank)], writes=[B("KnA") if gi == 0 else B("KwN")])
                for k in range(KC):
                    S.op("pe", lambda e, k=k: e.matmul(ps[:, 6, 0:48], lhsT=hT[:, k, :], rhs=w_q[:, k, 1024:1072], start=(k == 0), stop=(k == KC - 1)),
                         reads=hrd + [B("wq")], writes=[B("ps", 6)])
                S.op("act", lambda e: e.activation(out=gates[:], in_=ps[:, 6, 0:48], func=AF.Sigmoid), reads=[B("ps", 6)], writes=[B("gates")])
                for kvh in range(4):
                    bank = 5 + kvh % 2
                    for g in range(4):
                        hd = kvh * 4 + g
                        for k in range(KC):
                            S.op("pe", lambda e, g=g, k=k, hd=hd, bank=bank: e.matmul(ps[0:64, bank, g * 128:(g + 1) * 128], lhsT=w_q[:, k, hd * 64:(hd + 1) * 64], rhs=hT[:, k, :],
                                                                                     start=(k == 0), stop=(k == KC - 1)), reads=hrd + [B("wq")], writes=[B("ps", bank)])
                    S.op("act", lambda e, kvh=kvh, bank=bank: e.activation(out=QA[0:64, kvh, :, :], in_=ps[0:64, bank, 0:512].rearrange("p (g t) -> p g t", g=4), func=AF.Copy),
                         reads=[B("ps", bank)], writes=[B("QAs", kvh)])
                S.emit()
            with ExitStack() as ph:
                A = lambda n, s, d=F32: ph.enter_context(nc.sbuf_tensor(self.un(n), list(s), d))
                w_o = A("b_wo", [128, KC, D], BF16)
                KsA = A("b_KsA", [128, 4, 2048], BF16)
                Vs = A("b_Vs", [128, 16, 4, 65], BF16)
                KwS = A("b_KwS", [64, 4, 512], BF16)
                VwS = A("b_VwS", [128, 4, 4, 65], BF16)
                wst = A("b_wst", [128, 4, 512])
                e4s = A("b_e4s", [128, 4, 512])
                psg = A("b_psg", [128, 512])
                pTs = A("b_pTs", [64, 8, 128], BF16)
                pad = [A("b_pad%d" % k, [128, 4, 128], BF16) for k in range(2)]
                PTn = A("b_PTn", [128, 512], BF16)
                mnew = A("b_mnew", [128, 128], BF16)
                tri = A("b_tri", [128, 2, 128], BF16)
                vbs = A("b_vbs", [128, 512])
                mk = A("b_mk", [128, 4, 64])
                sm = A("b_sm", [128, 8])
                imp = A("b_imp", [128, 64])
                imp2 = A("b_imp2", [128, 64])
                wk = A("b_wk", [128, 64])
                m8 = A("b_m8", [128, 16])
                mbs = A("b_mbs", [128, 128])
                ocmp = A("b_ocmp", [128, 4, 4, 64])
                oacc = A("b_oacc", [128, 2, 4, 260])
                otot = A("b_otot", [128, 4, 4, 64])
                ototT = A("b_ototT", [128, KC, 128], BF16)
                ctmp = A("b_ctmp", [128, 4, 64])
                cf = A("b_cf", [128, 12])
                S.dma("pool", lambda e: e.dma_start(out=w_o[:], in_=self.d_nsa_w_out.rearrange("(c p) n -> p c n", p=128)), writes=[B("wo")])
                for kvh in range(4):
                    S.dma("pool", lambda e, kvh=kvh: e.dma_start(out=KsA[64:128, kvh, :], in_=self.d_onehot[:, 0:2048]), writes=[B("KsAs")])
                S.dma("pool", lambda e: e.dma_start(out=mnew[:], in_=self.d_mnew), writes=[B("mnew")])
                S.dma("pool", lambda e: e.dma_start(out=tri[:], in_=self.d_tri.rearrange("p (a t) -> p a t", a=2)), writes=[B("tri")])
                S.dma("sp", lambda e: e.dma_start(out=vbs[:], in_=self.d_vbs), writes=[B("vbs")])
                S.dma("sp", lambda e: e.dma_start(out=mk[:], in_=self.d_masks[17].rearrange("p (a n) -> p a n", a=4)), writes=[B("mk")])
                S.op("pool", lambda e: e.memset(Vs[:], 1.0), writes=[B("Vss")])
                S.op("pool", lambda e: e.memset(VwS[:], 1.0), writes=[B("VwS")])
                S.op("pool", lambda e: e.memset(mbs[:], 0.0), writes=[B("mbs")])
                S.op("pool", lambda e: e.memset(imp[:], 0.0), writes=[B("imp")])
                S.op("pool", lambda e: e.memset(oacc[:], 0.0), writes=[B("oacc")])
                for kvh in range(4):
                    bqa = B("QAs", kvh)
                    for g in range(4):
                        S.op("pe", lambda e, g=g, kvh=kvh: e.matmul(ps[:, g, 0:512], lhsT=QA[0:64, kvh, g, :], rhs=self.CK_all[:, kvh, :], start=True, stop=True),
                             reads=[bqa, B("CKall")], writes=[B("ps", g)])
                        S.op("dve", lambda e, g=g: e.scalar_tensor_tensor(out=e4s[:, g, :], in0=ps[:, g, 0:512], scalar=0.125, in1=vbs[:], op0=ALU.mult, op1=ALU.add),
                             reads=[B("ps", g), B("vbs")], writes=[B("e4s")])
                    S.op("act", lambda e: e.activation(out=e4s[:], in_=e4s[:], func=AF.Exp), reads=[B("e4s")], writes=[B("e4s")])
                    S.op("dve", lambda e: e.tensor_reduce(out=sm[:, 0:4], in_=e4s[:], axis=AX.X, op=ALU.add), reads=[B("e4s")], writes=[B("sm")])
                    S.op("dve", lambda e: e.tensor_scalar(out=sm[:, 0:4], in0=sm[:, 0:4], scalar1=1e-30, scalar2=None, op0=ALU.add), reads=[B("sm")], writes=[B("sm")])
                    S.op("dve", lambda e: e.reciprocal(out=sm[:, 0:4], in_=sm[:, 0:4]), reads=[B("sm")], writes=[B("sm")])
                    S.op("dve", lambda e: e.tensor_tensor(out=e4s[:], in0=e4s[:], in1=bc(sm[:, 0:4].unsqueeze(2), [128, 4, 512]), op=ALU.mult),
                         reads=[B("sm"), B("e4s")], writes=[B("e4s")])
                    S.op("dve", lambda e: e.tensor_tensor(out=psg[:], in0=e4s[:, 0, :], in1=e4s[:, 1, :], op=ALU.add), reads=[B("e4s")], writes=[B("psg")])
                    for g in (2, 3):
                        S.op("dve", lambda e, g=g: e.tensor_tensor(out=psg[:], in0=psg[:], in1=e4s[:, g, :], op=ALU.add), reads=[B("e4s"), B("psg")], writes=[B("psg")])
                    S.op("dve", lambda e: e.tensor_reduce(out=imp[:, 0:32], in_=psg[:].rearrange("p (s n) -> p n s", n=32), axis=AX.X, op=ALU.add), reads=[B("psg")], writes=[B("imp")])
                    S.op("dve", lambda e: e.tensor_tensor(out=imp2[:], in0=imp[:], in1=mk[:, 1, :], op=ALU.mult), reads=[B("imp"), B("mk")], writes=[B("imp2")])
                    S.op("dve", lambda e: e.tensor_tensor(out=imp2[:], in0=imp2[:], in1=mk[:, 2, :], op=ALU.add), reads=[B("imp2"), B("mk")], writes=[B("imp2")])
                    S.op("dve", lambda e: e.max(out=m8[:, 0:8], in_=imp2[:]), reads=[B("imp2")], writes=[B("m8")])
                    S.op("dve", lambda e: e.match_replace(out=wk[:], in_to_replace=m8[:, 0:8], in_values=imp2[:], imm_value=-3e30), reads=[B("imp2"), B("m8")], writes=[B("wk")])
                    S.op("dve", lambda e: e.max(out=m8[:, 8:16], in_=wk[:]), reads=[B("wk")], writes=[B("m8")])
                    S.op("dve", lambda e: e.match_replace(out=wk[:], in_to_replace=m8[:, 8:16], in_values=wk[:], imm_value=-3e30), reads=[B("wk"), B("m8")], writes=[B("wk")])
                    S.op("dve", lambda e: e.tensor_tensor(out=imp2[:], in0=imp2[:], in1=wk[:], op=ALU.subtract), reads=[B("wk"), B("imp2")], writes=[B("imp2")])
                    S.op("dve", lambda e: e.tensor_scalar(out=imp2[:], in0=imp2[:], scalar1=1.0, scalar2=None, op0=ALU.min), reads=[B("imp2")], writes=[B("imp2")])
                    S.op("dve", lambda e: e.tensor_tensor(out=imp2[:], in0=imp2[:], in1=mk[:, 3, :], op=ALU.mult), reads=[B("imp2"), B("mk")], writes=[B("imp2")])
                    S.op("dve", lambda e: e.tensor_scalar(out=mbs[:, 64:128], in0=imp2[:], scalar1=-1.0, scalar2=BIG, op0=ALU.add, op1=ALU.mult),
                         reads=[B("imp2")], writes=[B("mbs")])
                    S.op("pe", lambda e: e.transpose(out=ps[:, 4, 0:128], in_=mbs[:], identity=self.ident[:]), reads=[B("mbs"), B("ident")], writes=[B("ps", 4)])
                    S.op("act", lambda e, kvh=kvh: e.activation(out=QA[64:128, kvh, :, :], in_=bc(ps[64:128, 4, 0:128].unsqueeze(1), [64, 4, 128]), func=AF.Copy),
                         reads=[B("ps", 4)], writes=[bqa])
                    for g in range(4):
                        for hh in range(2):
                            for c4 in range(4):
                                ch = hh * 4 + c4
                                S.op("pe", lambda e, g=g, ch=ch, c4=c4, hh=hh: e.transpose(out=ps[0:64, 6 + hh, c4 * 128:(c4 + 1) * 128], in_=e4s[:, g, ch * 64:(ch + 1) * 64], identity=self.ident[:]),
                                     reads=[B("e4s"), B("ident")], writes=[B("ps", 6 + hh)])
                            S.op("act", lambda e, hh=hh: e.activation(out=pTs[:, hh * 4:(hh + 1) * 4, :], in_=ps[0:64, 6 + hh, 0:512].rearrange("p (c t) -> p c t", c=4), func=AF.Copy),
                                 reads=[B("ps", 6 + hh)], writes=[B("pTs")])
                        for ch in range(8):
                            S.op("pe", lambda e, g=g, ch=ch, kvh=kvh: e.matmul(ps[:, 5, g * 64:(g + 1) * 64], lhsT=pTs[:, ch, :], rhs=self.CV_all[:, ch, kvh, :], start=(ch == 0), stop=(ch == 7)),
                                 reads=[B("pTs"), B("CVall")], writes=[B("ps", 5)])
                    S.op("act", lambda e, kvh=kvh: e.activation(out=ocmp[:, kvh, :, :], in_=ps[:, 5, 0:256].rearrange("p (g d) -> p g d", g=4), func=AF.Copy), reads=[B("ps", 5)], writes=[B("ocmp")])
                ti = 0
                for kvh in range(4):
                    bqa = B("QAs", kvh)
                    for br in range(2):
                        sb_ = ti % 2
                        ti += 1
                        if br == 0:
                            lhs, rhs = KnA[:, kvh, :], QA[:, kvh, :, :].rearrange("p g t -> p (g t)")
                            vv = Vns
                        else:
                            lhs, rhs = KwN[:, kvh, :], QA[0:64, kvh, :, :].rearrange("p g t -> p (g t)")
                            vv = Vnw
                        S.op("pe", lambda e, lhs=lhs, rhs=rhs, sb_=sb_: e.matmul(ps[:, sb_, 0:512], lhsT=lhs, rhs=rhs, start=True, stop=True),
                             reads=[B("KnA"), B("KwN"), bqa], writes=[B("ps", sb_)])
                        S.op("act", lambda e, sb_=sb_: e.activation(out=PTn[:], in_=ps[:, sb_, 0:512], func=AF.Exp, scale=0.125), reads=[B("ps", sb_)], writes=[B("PTn")])
                        S.op("dve", lambda e: e.tensor_tensor(out=PTn[:].rearrange("p (g t) -> p g t", g=4), in0=PTn[:].rearrange("p (g t) -> p g t", g=4),
                                                              in1=bc(mnew[:].unsqueeze(1), [128, 4, 128]), op=ALU.mult), reads=[B("PTn"), B("mnew")], writes=[B("PTn")])
                        for g in range(4):
                            S.op("pe", lambda e, g=g, vv=vv, kvh=kvh, br=br: e.matmul(ps[:, 2 + br, g * 65:(g + 1) * 65], lhsT=PTn[:, g * 128:(g + 1) * 128], rhs=vv[:, kvh, :], start=True, stop=True),
                                 reads=[B("PTn"), B("Vns"), B("Vnw")], writes=[B("ps", 2 + br)])
                        S.op("dve", lambda e, kvh=kvh, br=br: e.tensor_tensor(out=oacc[:, br, kvh, :], in0=oacc[:, br, kvh, :], in1=ps[:, 2 + br, 0:260], op=ALU.add),
                             reads=[B("ps", 2 + br), B("oacc")], writes=[B("oacc")])
                for sq_ in range(16):
                    S.dma("sp", lambda e, sq_=sq_: e.dma_start(out=KsA[0:64, :, :], in_=self.sc_sKs[sq_]), reads=[B("scsK", sq_)], writes=[B("KsAs")])
                    for pg in range(16):
                        S.dma("sp", lambda e, sq_=sq_, pg=pg: e.dma_start(out=Vs[:, pg, :, 0:64], in_=self.sc_sVs[sq_, pg].rearrange("p (k d) -> p k d", k=4)), reads=[B("scsV", sq_)], writes=[B("Vss")])
                    S.dma("sp", lambda e, sq_=sq_: e.dma_start(out=wst[:], in_=self.d_swin[sq_].rearrange("(t p) c -> p t c", p=128)), writes=[B("wst")])
                    for t in range(4):
                        for kvh in range(4):
                            S.op("pe", lambda e, t=t, kvh=kvh: e.transpose(out=ps[0:64, 4, kvh * 128:(kvh + 1) * 128], in_=wst[:, t, kvh * 64:(kvh + 1) * 64], identity=self.ident[:]),
                                 reads=[B("wst"), B("ident")], writes=[B("ps", 4)])
                        S.op("act", lambda e, t=t: e.activation(out=KwS[:, :, t * 128:(t + 1) * 128], in_=ps[0:64, 4, 0:512].rearrange("p (k t) -> p k t", k=4), func=AF.Copy),
                             reads=[B("ps", 4)], writes=[B("KwS")])
                        S.op("pool", lambda e, t=t: e.tensor_copy(out=VwS[:, t, :, 0:64], in_=wst[:, t, 256:512].rearrange("p (k d) -> p k d", k=4)), reads=[B("wst")], writes=[B("VwS")])
                    for k_ in range(2):
                        S.op("pool", lambda e, k_=k_: e.memset(pad[k_][:], 0.0), writes=[B("pad", k_)])
                    qs = slice(sq_ * 8, (sq_ + 1) * 8)
                    tl = []
                    for kvh in range(4):
                        for br in range(2):
                            nkt = 16 if br == 0 else 4
                            for kt in range(nkt):
                                tl.append((kvh, br, kt, kt == 0, kt == nkt - 1))

                    def emit_S(n):
                        kvh, br, kt, first, last = tl[n]
                        sb_ = n % 2
                        if br == 0:
                            lhs, rhs, rdk = KsA[:, kvh, kt * 128:(kt + 1) * 128], QA[:, kvh, :, qs], B("KsAs")
                        else:
                            lhs, rhs, rdk = KwS[:, kvh, kt * 128:(kt + 1) * 128], QA[0:64, kvh, :, qs], B("KwS")
                        S.op("pe", lambda e, lhs=lhs, rhs=rhs, sb_=sb_: e.matmul(ps[:, sb_, 0:32].rearrange("p (g t) -> p g t", g=4), lhsT=lhs, rhs=rhs, start=True, stop=True),
                             reads=[rdk, B("QAs", kvh)], writes=[B("ps", sb_)])

                    def emit_rest(n):
                        kvh, br, kt, first, last = tl[n]
                        sb_ = n % 2
                        pd = pad[sb_]
                        if br == 0:
                            vv, rdv = Vs[:, kt, kvh, :], B("Vss")
                        else:
                            vv, rdv = VwS[:, kt, kvh, :], B("VwS")
                        S.op("act", lambda e, sb_=sb_, pd=pd: e.activation(out=pd[:, :, qs], in_=ps[:, sb_, 0:32].rearrange("p (g t) -> p g t", g=4), func=AF.Exp, scale=0.125),
                             reads=[B("ps", sb_)], writes=[B("pad", sb_)])
                        if br == 1 and kt == 0:
                            S.op("dve", lambda e, pd=pd: e.tensor_tensor(out=pd[:, :, qs], in0=pd[:, :, qs], in1=bc(tri[:, 1, 0:8].unsqueeze(1), [128, 4, 8]), op=ALU.mult),
                                 reads=[B("pad", sb_), B("tri")], writes=[B("pad", sb_)])
                        for g in range(4):
                            S.op("pe", lambda e, g=g, pd=pd, vv=vv, br=br, first=first, last=last: e.matmul(ps[:, 2 + br, g * 65:(g + 1) * 65], lhsT=pd[:, g, :], rhs=vv,
                                                                                                             start=first, stop=last), reads=[B("pad", sb_), rdv], writes=[B("ps", 2 + br)])
                        if last:
                            S.op("dve", lambda e, kvh=kvh, br=br: e.tensor_tensor(out=oacc[:, br, kvh, :], in0=oacc[:, br, kvh, :], in1=ps[:, 2 + br, 0:260], op=ALU.add),
                                 reads=[B("ps", 2 + br), B("oacc")], writes=[B("oacc")])

                    emit_S(0)
                    for n in range(len(tl)):
                        if n + 1 < len(tl):
                            emit_S(n + 1)
                        emit_rest(n)
                for kvh in range(4):
                    osel = oacc[:, 0, kvh, :].rearrange("p (g d) -> p g d", g=4)
                    owin = oacc[:, 1, kvh, :].rearrange("p (g d) -> p g d", g=4)
                    S.op("dve", lambda e, osel=osel: e.tensor_scalar(out=cf[:, 0:4], in0=osel[:, :, 64], scalar1=1e-30, scalar2=None, op0=ALU.add), reads=[B("oacc")], writes=[B("cf")])
                    S.op("dve", lambda e, owin=owin: e.tensor_scalar(out=cf[:, 4:8], in0=owin[:, :, 64], scalar1=1e-30, scalar2=None, op0=ALU.add), reads=[B("oacc")], writes=[B("cf")])
                    S.op("dve", lambda e: e.reciprocal(out=cf[:, 0:8], in_=cf[:, 0:8]), reads=[B("cf")], writes=[B("cf")])
                    S.op("dve", lambda e, kvh=kvh: e.tensor_tensor(out=cf[:, 0:4], in0=cf[:, 0:4], in1=gates[:, 16 + kvh * 4:20 + kvh * 4], op=ALU.mult), reads=[B("cf"), B("gates")], writes=[B("cf")])
                    S.op("dve", lambda e, kvh=kvh: e.tensor_tensor(out=cf[:, 4:8], in0=cf[:, 4:8], in1=gates[:, 32 + kvh * 4:36 + kvh * 4], op=ALU.mult), reads=[B("cf"), B("gates")], writes=[B("cf")])
                    ot = otot[:, kvh, :, :]
                    S.op("dve", lambda e, ot=ot, kvh=kvh: e.tensor_tensor(out=ot, in0=ocmp[:, kvh, :, :], in1=bc(gates[:, kvh * 4:kvh * 4 + 4].unsqueeze(2), [128, 4, 64]), op=ALU.mult),
                         reads=[B("ocmp"), B("gates")], writes=[B("otot")])
                    for (src, c0) in ((osel, 0), (owin, 4)):
                        S.op("dve", lambda e, src=src, c0=c0: e.tensor_tensor(out=ctmp[:], in0=src[:, :, 0:64], in1=bc(cf[:, c0:c0 + 4].unsqueeze(2), [128, 4, 64]), op=ALU.mult),
                             reads=[B("oacc"), B("cf")], writes=[B("ctmp")])
                        S.op("dve", lambda e, ot=ot: e.tensor_tensor(out=ot, in0=ot, in1=ctmp[:], op=ALU.add), reads=[B("ctmp"), B("otot")], writes=[B("otot")])
                of = otot[:].rearrange("p k g d -> p (k g d)")
                for hh in range(2):
                    for c4 in range(4):
                        c = hh * 4 + c4
                        S.op("pe", lambda e, c=c, c4=c4, hh=hh, of=of: e.transpose(out=ps[:, hh, c4 * 128:(c4 + 1) * 128], in_=of[:, c * 128:(c + 1) * 128], identity=self.ident[:]),
                             reads=[B("otot"), B("ident")], writes=[B("ps", hh)])
                    S.op("act", lambda e, hh=hh: e.activation(out=ototT[:, hh * 4:(hh + 1) * 4, :], in_=ps[:, hh, 0:512].rearrange("p (c t) -> p c t", c=4), func=AF.Copy),
                         reads=[B("ps", hh)], writes=[B("ototT")])
                for c in range(KC):
                    bank = 6 + (c % 2)
                    for k in range(KC):
                        S.op("pe", lambda e, c=c, k=k, bank=bank: e.matmul(ps[:, bank, 0:128], lhsT=w_o[:, k, c * 128:(c + 1) * 128], rhs=ototT[:, k, :], start=(k == 0), stop=(k == KC - 1)),
                             reads=[B("wo"), B("ototT")], writes=[B("ps", bank)])
                    self.resid(i, 0, c, l0, 128, "s", ps[:, bank, 0:128], [B("ps", bank)])
                S.emit()

    def pool(self, i, tiles):
        S, B, nc, ps = self.S, self.B, self.nc, self.ps
        with ExitStack() as ph:
            A = lambda n, s, d=F32: ph.enter_context(nc.sbuf_tensor(self.un(n), list(s), d))
            pw = A("p_w", [128, 4, 2, 256], BF16)
            hp = A("p_h", [128, KC, 15 + 512])
            hps = A("p_hs", [128, KC, 16, 23])
            sa = A("p_sa", [128, KC, 15 + 512])
            sb_ = A("p_sb", [128, KC, 15 + 512])
            dT = A("p_d", [128, KC, 512], BF16)
            icn = A("p_icn", [128, 4, 128])
            gl = A("p_gl", [128, KC])
            tmp = [A("p_t%d" % k, [128, 512]) for k in range(2)]
            sq = [A("p_sq%d" % k, [128, 512], BF16) for k in range(2)]
            rs = A("p_rs", [128, 512])
            hdummy = A("p_hd", [128, KC, 512], BF16)
            hs32 = A("p_hs32", [128, KC, 128])

            self.rtmp = A("p_rtmp", [128, 128])
            self.coefs(i, 0)
            S.dma("pool", lambda e: e.dma_start(out=pw[:], in_=self.d_pool_w.rearrange("g (k p) n -> p g k n", p=128)), writes=[B("pw")])
            S.dma("sp", lambda e: e.dma_start(out=icn[:], in_=self.d_invcnt.rearrange("p (g t) -> p g t", g=4)), writes=[B("icn")])
            for c in range(KC):
                S.dma("sp", lambda e, c=c: e.dma_start(out=hps[:, c, :, 0:15], in_=self.d_spoolT[c * 128:(c + 1) * 128, :].rearrange("p (s r) -> p s r", r=15)), writes=[B("hps")])
            for c in range(KC):
                S.dma("sp", lambda e, c=c: e.dma_start(out=self.o_poolT[c * 128:(c + 1) * 128, 144:256].rearrange("p (s r) -> p s r", r=7),
                                                       in_=hps[:, c, :, 8:15]), reads=[B("hps")])
            S.op("dve", lambda e: e.tensor_tensor(out=gl[:], in0=self.coef[:, 2, :], in1=self.vecs[:, V_PSC:V_PSC + 8], op=ALU.mult),
                 reads=[B("coef"), B("vecs")], writes=[B("gl")])
            S.op("pool", lambda e: e.memset(hp[:, :, 0:15], 0.0), writes=[B("h32", c) for c in range(KC)])
            wins = (2, 4, 8, 16)
            for (t0, w, kind) in tiles:
                h32w = [B("h32", c) for c in range(KC)]
                if kind == "p":
                    self.norm_mod(i, 0, t0, w, kind, hdummy, 0, tmp, sq, rs, "phd", fp32_out=hp[:, :, 15:15 + 512])
                    if t0 == 0 and self.halo:
                        S.op("dve", lambda e: e.tensor_scalar(out=hp[:, :, 15:143], in0=hp[:, :, 15:143], scalar1=self.flag[:, 0:1], scalar2=None, op0=ALU.mult),
                             reads=h32w + [B("flag")], writes=h32w)
                    if t0 + w == NPT and self.halo:
                        S.dma("sp", lambda e, w=w: e.dma_start(out=self.o_poolT[:, 0:16].rearrange("(c p) t -> p c t", p=128), in_=hp[:, :, 15 + w - 16:15 + w]), reads=h32w)
                    H = lambda a, b: hp[:, :, a:b]
                    L = 15 + w
                    cur = hp
                    stages = {}
                    src = hp
                    for si, sh in enumerate((1, 2, 4, 8)):
                        dst = sa if si % 2 == 0 else sb_
                        c0 = 2 * si
                        S.op("dve" if si % 2 == 0 else "pool", lambda e, src=src, dst=dst, sh=sh, c0=c0, L=L: e.tensor_tensor(
                            out=dst[:, c0:KC, sh:L], in0=src[:, c0:KC, sh:L], in1=src[:, c0:KC, 0:L - sh], op=ALU.add),
                            reads=h32w + [B("psa"), B("psb")], writes=[B("psa") if si % 2 == 0 else B("psb")])
                        src = dst
                        stages[si] = dst
                    for g in range(4):
                        stg = stages[g]
                        for cc in (2 * g, 2 * g + 1):
                            S.op("dve", lambda e, stg=stg, cc=cc, g=g, w=w: e.tensor_scalar(out=stg[:, cc, 15:15 + w], in0=stg[:, cc, 15:15 + w], scalar1=1.0 / wins[g],
                                                                                            scalar2=None, op0=ALU.mult), reads=[B("psa"), B("psb")], writes=[B("psa"), B("psb")])
                            if t0 == 0:
                                S.op("dve", lambda e, stg=stg, cc=cc, g=g: e.tensor_tensor(out=stg[:, cc, 143:271], in0=stg[:, cc, 143:271], in1=icn[:, g, :], op=ALU.mult),
                                     reads=[B("psa"), B("psb"), B("icn")], writes=[B("psa"), B("psb")])
                            S.op("dve", lambda e, stg=stg, cc=cc, w=w: e.tensor_tensor(out=dT[:, cc, 0:w], in0=stg[:, cc, 15:15 + w], in1=hp[:, cc, 15:15 + w], op=ALU.subtract),
                                 reads=[B("psa"), B("psb")] + h32w, writes=[B("pd", cc)])
                    S.op("pool", lambda e, w=w: e.tensor_copy(out=hp[:, :, 0:15], in_=hp[:, :, w:w + 15]), reads=h32w + [B("psa"), B("psb")], writes=h32w)
                else:
                    hview = hps[:, :, :, 15:23]
                    self.norm_mod(i, 0, t0, w, kind, hdummy, 0, tmp, sq, rs, "phd", fp32_out=hs32)
                    for c in range(KC):
                        S.op("pool", lambda e, c=c: e.tensor_copy(out=hps[:, c, :, 15:23], in_=hs32[:, c, :].rearrange("p (s j) -> p s j", j=8)), reads=h32w, writes=[B("hps")])
                    for c in range(KC):
                        S.dma("sp", lambda e, c=c: e.dma_start(out=self.o_poolT[c * 128:(c + 1) * 128, 16:144].rearrange("p (s j) -> p s j", j=8), in_=hps[:, c, :, 15:23]), reads=[B("hps")])
                    for g in range(4):
                        wn = wins[g]
                        for cc in (2 * g, 2 * g + 1):
                            acc = sb_[:, cc, 0:128].rearrange("p (s j) -> p s j", j=8)
                            S.op("dve", lambda e, cc=cc, acc=acc: e.tensor_tensor(out=acc, in0=hps[:, cc, :, 15:23], in1=hps[:, cc, :, 14:22], op=ALU.add),
                                 reads=[B("hps")], writes=[B("psb")])
                            for r in range(2, wn):
                                S.op("dve", lambda e, cc=cc, acc=acc, r=r: e.tensor_tensor(out=acc, in0=acc, in1=hps[:, cc, :, 15 - r:23 - r], op=ALU.add),
                                     reads=[B("hps"), B("psb")], writes=[B("psb")])
                            S.op("dve", lambda e, cc=cc, acc=acc, wn=wn: e.scalar_tensor_tensor(
                                out=dT[:, cc, 0:128].rearrange("p (s j) -> p s j", j=8), in0=acc, scalar=1.0 / wn, in1=hps[:, cc, :, 15:23], op0=ALU.mult, op1=ALU.subtract),
                                reads=[B("hps"), B("psb")], writes=[B("pd", cc)])
                for g in range(4):
                    for o in range(2):
                        c = 2 * g + o
                        bank = c % 2
                        bp = B("ps", bank)
                        for k in range(2):
                            S.op("pe", lambda e, g=g, o=o, k=k, bank=bank, w=w: e.matmul(ps[:, bank, 0:w], lhsT=pw[:, g, k, o * 128:(o + 1) * 128], rhs=dT[:, 2 * g + k, 0:w],
                                                                                        start=(k == 0), stop=(k == 1)), reads=[B("pw"), B("pd", 2 * g + k)], writes=[bp])
                        if kind == "p":
                            self.resid(i, 0, c, t0, w, kind, ps[:, bank, 0:w], [bp, B("gl")], extra_scale=gl[:, c:c + 1])
                        else:
                            self.resid(i, 0, c, t0, w, kind, ps[:, bank, 0:w], [bp, B("vecs")], extra_scale=self.vecs[:, V_PSC + c:V_PSC + c + 1])
            S.emit()

    def final(self, tiles):
        S, B, nc, ps = self.S, self.B, self.nc, self.ps
        with ExitStack() as ph:
            A = lambda n, s, d=F32: ph.enter_context(nc.sbuf_tensor(self.un(n), list(s), d))
            sq = [A("y_sq%d" % k, [128, 512], BF16) for k in range(2)]
            rs = A("y_rs", [128, 512])
            yo = [A("y_o%d" % k, [128, 512]) for k in range(2)]
            for (t0, w, kind) in tiles:
                bp = B("ps", 7)
                for c in range(KC):
                    s_ = sq[c % 2]
                    S.op("act", lambda e, c=c, s_=s_: e.activation(out=s_[:, 0:w], in_=self.xT[:, c, t0:t0 + w], func=AF.Square),
                         reads=self.xb(c, t0, w), writes=[B("sq", c % 2)])
                    S.op("pe", lambda e, c=c, s_=s_: e.matmul(ps[:, 7, 0:w], lhsT=self.ones[:], rhs=s_[:, 0:w], start=(c == 0), stop=(c == KC - 1)),
                         reads=[B("sq", c % 2), B("ones")], writes=[bp])
                S.op("act", lambda e: e.activation(out=rs[:, 0:w], in_=ps[:, 7, 0:w], func=AF.Sqrt, bias=EPS, scale=1.0 / D), reads=[bp], writes=[B("rs")])
                S.op("dve", lambda e: e.reciprocal(out=rs[:, 0:w], in_=rs[:, 0:w]), reads=[B("rs")], writes=[B("rs")])
                for c in range(KC):
                    y_ = yo[c % 2]
                    S.op("dve", lambda e, c=c, y_=y_: e.scalar_tensor_tensor(out=y_[:, 0:w], in0=self.xT[:, c, t0:t0 + w], scalar=self.vecs[:, V_FG + c:V_FG + c + 1],
                                                                              in1=rs[:, 0:w], op0=ALU.mult, op1=ALU.mult),
                         reads=self.xb(c, t0, w) + [B("rs"), B("vecs")], writes=[B("yo", c % 2)])
                    S.dma("sp", lambda e, c=c, y_=y_: e.dma_start(out=self.o_yT[c * 128:(c + 1) * 128, t0:t0 + w], in_=y_[:, 0:w]), reads=[B("yo", c % 2)])
            S.emit()

    def main(self):
        PT = [(0, 512, "p"), (512, 512, "p"), (1024, 512, "p"), (1536, 512, "p"), (2048, 128, "p"), (2176, 128, "s")]
        self.halo = True
        st = self.stage
        if os.environ.get("KNOS"):
            PT = PT[:int(os.environ["KNOS"])]
        if st >= 2:
            AT = [(0, 512, "p"), (512, 512, "p"), (1024, 512, "p"), (1536, 512, "p")]
            self.halo = False
            self.ada(0)
            self.conv(0, 0, AT)
            self.ffn(0, AT)
            self.ada(1)
            self.nsa_proj(1, 16, True)
            self.load_x()
            self.halo = True
        if st >= -1 and st < 2:
            self.ada(0)
        if st >= 0:
            self.conv(0, 0, PT)
        if st >= 1:
            self.ffn(0, PT)
        if st >= 2:
            self.nsa_proj(1, 16, False)
            self.nsa_attn(1)
            if not os.environ.get("KNOSAMP"):
                with ExitStack() as phs:
                    self.CK_all = phs.enter_context(self.nc.sbuf_tensor(self.un("CK_all"), [64, 4, 512], BF16))
                    self.CV_all = phs.enter_context(self.nc.sbuf_tensor(self.un("CV_all"), [64, 8, 4, 64], BF16))
                    self.nsa_sample_prep()
                    self.nsa_sample_attn(1)
            self.ffn(1, PT)
        if st >= 3:
            self.ada(2)
            self.pool(2, PT)
            self.ffn(2, PT)
            self.ada(3)
            self.conv(3, 1, PT)
            self.ffn(3, PT)
        self.final(PT)


_CACHE = {}


def kernel(x_prompt, x_sample, cache_nsa_kv, state_nsa_win, state_conv, state_pool, state_ffn, page_table,
           c_prompt, c_sample, ada_w, ada_b, norm1_g, norm2_g, final_g, conv_w_in, conv_w_dw, conv_ln_g,
           conv_ln_b, conv_w_out, nsa_w_in, nsa_cmp_pe, nsa_cmp_w1, nsa_cmp_w2, nsa_w_out, pool_w, pool_scale,
           ffn_w_up, ffn_w_dw, ffn_w_down):
    stage = int(os.environ.get("KSTAGE", "3"))
    f32 = np.float32
    A = lambda a: np.ascontiguousarray(np.asarray(a), dtype=f32)
    x_prompt, x_sample = A(x_prompt), A(x_sample)
    vecs = np.zeros((128, NV), f32)

    def fm(v):
        v = A(v)
        sh = v.shape[:-1]
        n = v.shape[-1] // 128
        return np.moveaxis(v.reshape(sh + (n, 128)), -1, 0)
    vecs[:, V_N1:V_N1 + 32] = fm(norm1_g).reshape(128, 32)
    vecs[:, V_N2:V_N2 + 32] = fm(norm2_g).reshape(128, 32)
    vecs[:, V_FG:V_FG + 8] = fm(final_g).reshape(128, 8)
    vecs[:, V_ADAB:V_ADAB + 192] = fm(ada_b).reshape(128, 192)
    vecs[:, V_CDW:V_CDW + 496] = np.transpose(fm(conv_w_dw), (0, 1, 3, 2)).reshape(128, 496)
    vecs[:, V_CLNG:V_CLNG + 16] = fm(conv_ln_g).reshape(128, 16)
    vecs[:, V_CLNB:V_CLNB + 16] = fm(conv_ln_b).reshape(128, 16)
    vecs[:, V_PSC:V_PSC + 8] = fm(pool_scale).reshape(128, 8)
    vecs[:, V_FDW:V_FDW + 264] = np.transpose(fm(ffn_w_dw), (0, 1, 3, 2)).reshape(128, 264)
    ident = np.eye(128, dtype=f32)
    pe = A(nsa_cmp_pe)[0]
    pe2 = np.concatenate([np.tile(pe[X].T, (1, 2)) for X in range(2)], axis=1)
    onehot = (np.arange(4096)[None, :] // 64 == np.arange(64)[:, None]).astype(f32)
    kk, qq = np.arange(128)[:, None], np.arange(128)[None, :]
    tri = np.concatenate([(kk <= qq), (kk >= qq)], axis=1).astype(f32)

    def masks_for(half):
        off = 32 * (1 - half)
        p0 = half * 2048 - 128
        M = np.zeros((18, 128, 4, 64), f32)
        nn = (np.arange(64) - off)[None, :]
        for qt in range(17):
            posc = (p0 + 128 * qt + np.arange(128))[:, None]
            validb = (nn >= 0) & (posc >= 0)
            cur = posc // 64
            cmpvalid = validb & (64 * (nn + 1) - 1 <= posc)
            future = (~validb) | (nn > cur)
            forced = validb & ((nn == 0) | (nn == cur) | (nn == cur - 1)) & ~future
            M[qt, :, 0] = np.where(cmpvalid, 0.0, -1e30)
            M[qt, :, 1] = (~forced & ~future)
            M[qt, :, 2] = forced * 1e4 + future * (-1e30)
            M[qt, :, 3] = ~future
        return M.reshape(18, 128, 256)
    mask_h = [masks_for(0), masks_for(1)]
    sl = np.arange(64)
    for M_ in mask_h:
        Ms = M_.reshape(18, 128, 4, 64)
        Ms[17, :, 0] = 0.0
        Ms[17, :, 1] = ((sl >= 1) & (sl <= 30))[None, :]
        Ms[17, :, 2] = (np.isin(sl, (0, 31, 32)) * 1e4 + (sl >= 33) * (-1e30))[None, :]
        Ms[17, :, 3] = (sl <= 32)[None, :]
    qs_, qj_ = np.arange(128) // 8, np.arange(128) % 8
    vbs = np.where(qs_[:, None] == (np.arange(512) // 32)[None, :], 0.0, -1e30).astype(f32)
    mnew = ((qs_[:, None] == qs_[None, :]) & (qj_[:, None] <= qj_[None, :])).astype(f32)
    ohnew = np.zeros((64, 128), f32)
    ohnew[32, :] = 1.0
    cache2d = A(cache_nsa_kv)[0].reshape(2560 * 128, 1024)
    iota = np.arange(128, dtype=f32).reshape(128, 1)
    swin_all = A(state_nsa_win)[0].reshape(128, 512, 512)
    ptab = np.ascontiguousarray(np.asarray(page_table), dtype=np.int32)
    shared = dict(vecs=vecs, ident=ident, ada_w=A(ada_w), conv_w_in=A(conv_w_in), conv_w_out=A(conv_w_out),
                  ffn_w_up=A(ffn_w_up), ffn_w_down=A(ffn_w_down), pool_w=A(pool_w)[0],
                  nsa_w_in=A(nsa_w_in)[0], nsa_w_out=A(nsa_w_out)[0], cmp_w1=A(nsa_cmp_w1)[0], cmp_w2=A(nsa_cmp_w2)[0],
                  pe2=np.ascontiguousarray(pe2), onehot=onehot, tri=tri, cache=cache2d, iota=iota, vbs=vbs, mnew=mnew, ohnew=ohnew)
    in_maps = []
    for c in range(NCORES):
        b, half = c // 2, c % 2
        xT = np.zeros((D, TOT), f32)
        p0 = half * 2048 - 128
        lo = max(p0, 0)
        xT[:, lo - p0:NPT] = x_prompt[b, lo:p0 + NPT].T
        ss = slice(16 * c, 16 * c + 16)
        xT[:, NPT:] = x_sample[ss].reshape(128, D).T
        xA = np.ascontiguousarray(x_prompt[b, 0:2048].T) if half == 1 else np.zeros((D, 2048), f32)
        cT = np.concatenate([A(c_prompt)[b][:, None], A(c_sample)[ss].T], axis=1)
        invcnt = np.zeros((128, 4, 128), f32)
        for g, wn in enumerate((2, 4, 8, 16)):
            pos = (p0 + 128 + np.arange(128)).astype(f32)
            invcnt[:, g, :] = (wn / np.minimum(wn, np.maximum(pos, 0) + 1))[None, :]
        m = dict(shared)
        m.update(xT=xT, xA=xA, cT=np.ascontiguousarray(cT), flag=np.full((128, 1), float(half), f32),
                 sconvT=np.ascontiguousarray(np.transpose(A(state_conv)[:, ss], (0, 3, 1, 2)).reshape(2, D, 480)),
                 spoolT=np.ascontiguousarray(np.transpose(A(state_pool)[0, ss], (2, 0, 1)).reshape(D, 240)),
                 sffnT=np.ascontiguousarray(np.transpose(A(state_ffn)[:, ss], (0, 3, 1, 2)).reshape(4, DFF, 32)),
                 invcnt=invcnt.reshape(128, 512), masks=mask_h[half],
                 ptab=np.ascontiguousarray(ptab[ss].reshape(1, 256)), swin=np.ascontiguousarray(swin_all[ss]))
        in_maps.append(m)
    if stage not in _CACHE:
        _CACHE[stage] = K(stage).build()
    nc = _CACHE[stage]
    res = run_bass_kernel_spmd(nc, in_maps, core_ids=list(range(NCORES)))
    R = res.results
    y_prompt = np.zeros((4, 4096, D), f32)
    y_sample = np.zeros((128, 8, D), f32)
    p_conv = np.zeros((2, 4, 30, D), f32)
    s_conv = np.zeros((2, 128, 30, D), f32)
    p_pool = np.zeros((1, 4, 15, D), f32)
    s_pool = np.zeros((1, 128, 15, D), f32)
    p_ffn = np.zeros((4, 4, 2, DFF), f32)
    s_ffn = np.zeros((4, 128, 2, DFF), f32)
    p_rows = np.zeros((1, 4, 4096, 4, 4, 64), f32)
    p_win = np.zeros((1, 4, 512, 2, 4, 64), f32)
    s_rows = np.zeros((1, 128, 8, 4, 4, 64), f32)
    s_win = np.zeros((1, 128, 512, 2, 4, 64), f32)
    sc_in, sp_in = A(state_conv), A(state_pool)
    for c in range(NCORES):
        b, half = c // 2, c % 2
        r = R[c]
        yT = r["o_yT"]
        y_prompt[b, half * 2048:(half + 1) * 2048] = yT[:, 128:NPT].T
        ss = slice(16 * c, 16 * c + 16)
        y_sample[ss] = yT[:, NPT:].T.reshape(16, 8, D)
        p_rows[0, b, half * 2048:(half + 1) * 2048] = r["o_rows"][0:2048].reshape(2048, 4, 4, 64)
        if half == 1:
            p_win[0, b] = r["o_win"][0:512].reshape(512, 2, 4, 64)
        s_rows[0, ss] = r["o_rows"][2048:2176].reshape(16, 8, 4, 4, 64)
        s_win[0, ss, 0:504] = r["o_swinp"].reshape(16, 504, 2, 4, 64)
        s_win[0, ss, 504:512] = r["o_win"][512:640].reshape(16, 8, 2, 4, 64)
        cv = r["o_convT"]
        fv = r["o_ffnT"]
        pv = r["o_poolT"]
        if half == 1:
            p_conv[:, b] = np.transpose(cv[:, :, 2:32], (0, 2, 1))
            p_ffn[:, b] = np.transpose(fv[:, :, 0:2], (0, 2, 1))
            p_pool[0, b] = pv[:, 1:16].T
        s_conv[:, ss, 0:22] = np.transpose(cv[:, :, 160:512].reshape(2, D, 16, 22), (0, 2, 3, 1))
        s_conv[:, ss, 22:30] = np.transpose(cv[:, :, 32:160].reshape(2, D, 16, 8), (0, 2, 3, 1))
        s_ffn[:, ss] = np.transpose(fv[:, :, 2:34].reshape(4, DFF, 16, 2), (0, 2, 3, 1))
        s_pool[0, ss, 0:7] = np.transpose(pv[:, 144:256].reshape(D, 16, 7), (1, 2, 0))
        s_pool[0, ss, 7:15] = np.transpose(pv[:, 16:144].reshape(D, 16, 8), (1, 2, 0))
    return (y_prompt, y_sample, p_conv, p_rows, p_win, p_pool, p_ffn, s_conv, s_rows, s_win, s_pool, s_ffn)
```

```python
import os
from contextlib import ExitStack
import numpy as np
import concourse.bass as bass
import concourse.mybir as mybir
from concourse.bass_utils import run_bass_kernel_spmd

F32 = mybir.dt.float32
BF16 = mybir.dt.bfloat16
I32 = mybir.dt.int32
AF = mybir.ActivationFunctionType
ALU = mybir.AluOpType
AX = mybir.AxisListType

D = 1024
KC = 8
NPT = 2176
NS = 128
TOT = NPT + NS
DFF = 2816
FC = 22
EPS = 1e-6
NCORES = 8

V_N1, V_N2, V_FG, V_ADAB, V_CDW, V_CLNG, V_CLNB, V_PSC, V_FDW = 0, 32, 64, 72, 264, 760, 776, 792, 800
NV = 800 + 264


import types


def freeze(fn):
    if fn is None or fn.__closure__ is None:
        return fn
    cells = []
    for c in fn.__closure__:
        try:
            cells.append(types.CellType(c.cell_contents))
        except ValueError:
            cells.append(c)
    return types.FunctionType(fn.__code__, fn.__globals__, fn.__name__, fn.__defaults__, tuple(cells))


class Buf:
    __slots__ = ("w", "r")

    def __init__(self):
        self.w = None
        self.r = {}


class Sched:
    COMPUTE = ("pe", "act", "dve", "pool")

    def __init__(self, nc, stack, ndma=8):
        self.nc = nc
        self.ops = []
        self.cnt = {k: 0 for k in self.COMPUTE}
        self.known = {}
        self.ndma = ndma
        self.dma_i = {"sp": 0, "pool": 0}
        self.dma_cnt = {}
        self.bufs = {}
        self.sems = {}
        for k in self.COMPUTE:
            self.sems[k] = stack.enter_context(nc.semaphore("s_" + k))
        for q in ("sp", "pool"):
            for i in range(ndma):
                key = "d%s%d" % (q, i)
                self.sems[key] = stack.enter_context(nc.semaphore("s_" + key))
                self.dma_cnt[key] = 0

    def buf(self, *key):
        b = self.bufs.get(key)
        if b is None:
            b = self.bufs[key] = Buf()
        return b

    def _deps(self, queue, reads, writes):
        need = {}

        def add(sv):
            if sv is None:
                return
            k, v = sv
            if need.get(k, 0) < v:
                need[k] = v
        for b in reads:
            add(b.w)
        for b in writes:
            add(b.w)
            for k, v in b.r.items():
                add((k, v))
        waits = []
        for k, v in need.items():
            if queue == "pe" and k == "pe":
                continue
            if self.known.get((queue, k), 0) >= v:
                continue
            self.known[(queue, k)] = v
            waits.append((k, v))
        return waits

    def op(self, queue, fn, reads=(), writes=()):
        fn = freeze(fn)
        waits = self._deps(queue, reads, writes)
        self.cnt[queue] += 1
        v = self.cnt[queue]
        self.ops.append((queue, fn, waits, (queue, 1)))
        for b in reads:
            if b.r.get(queue, 0) < v:
                b.r[queue] = v
        for b in writes:
            b.w = (queue, v)
            b.r = {}

    def dma(self, queue, fn, reads=(), writes=()):
        i = self.dma_i[queue]
        self.dma_i[queue] += 1
        key = "d%s%d" % (queue, i % self.ndma)
        fn = freeze(fn)
        prev = self.dma_cnt[key]
        waits = self._deps(queue, reads, writes)
        if prev and self.known.get((queue, key), 0) < prev:
            self.known[(queue, key)] = prev
            waits.append((key, prev))
        v = prev + 16
        self.dma_cnt[key] = v
        self.ops.append((queue, fn, waits, (key, 16)))
        for b in reads:
            b.r[key] = v
        for b in writes:
            b.w = (key, v)
            b.r = {}

    def emit(self, final=False):
        nc = self.nc
        if final:
            waits = [(k, v) for k, v in self.dma_cnt.items() if v]
            waits += [(k, self.cnt[k]) for k in self.COMPUTE if self.cnt[k]]
            self.ops.append(("sp", None, waits, None))
        if not final:
            allw = [(k, v) for k, v in self.dma_cnt.items() if v] + [(k, self.cnt[k]) for k in self.COMPUTE if self.cnt[k]]
            for q in ("sp", "pe", "act", "dve", "pool"):
                ws = [(k, v) for (k, v) in allw if self.known.get((q, k), 0) < v]
                for (k, v) in ws:
                    self.known[(q, k)] = v
                self.ops.append((q, None, ws, None))
        byq = {q: [] for q in ("sp", "pe", "act", "dve", "pool")}
        for o in self.ops:
            byq[o[0]].append(o)
        self.ops = []
        sems = self.sems

        def run(eng, lst):
            for (_, fn, waits, inc) in lst:
                for (k, v) in waits:
                    eng.wait_ge(sems[k], v)
                if fn is not None:
                    fn(eng).then_inc(sems[inc[0]], inc[1])

        with nc.Block() as block:
            @block.sync
            def _(e):
                run(e, byq["sp"])

            @block.tensor
            def _(e):
                run(e, byq["pe"])

            @block.scalar
            def _(e):
                run(e, byq["act"])

            @block.vector
            def _(e):
                run(e, byq["dve"])

            @block.gpsimd
            def _(e):
                run(e, byq["pool"])


def bc(ap, shape):
    return ap.to_broadcast(list(shape))


class K:
    def __init__(self, stage):
        self.stage = stage
        nc = self.nc = bass.Bass("TRN2", target_bir_lowering=False)
        dt = nc.dram_tensor

        def inp(name, shape, dtype=F32):
            return dt(name, list(shape), dtype, kind="ExternalInput").ap()

        def outp(name, shape):
            return dt(name, list(shape), F32, kind="ExternalOutput").ap()
        self.d_xT = inp("xT", [D, TOT])
        self.d_xA = inp("xA", [D, 2048])
        self.d_cT = inp("cT", [D, 17])
        self.d_flag = inp("flag", [128, 1])
        self.d_vecs = inp("vecs", [128, NV])
        self.d_ident = inp("ident", [128, 128])
        self.d_ada_w = inp("ada_w", [4, D, 6 * D])
        self.d_conv_w_in = inp("conv_w_in", [2, D, 2 * D])
        self.d_conv_w_out = inp("conv_w_out", [2, D, D])
        self.d_ffn_w_up = inp("ffn_w_up", [4, D, 2 * DFF])
        self.d_ffn_w_down = inp("ffn_w_down", [4, DFF, D])
        self.d_pool_w = inp("pool_w", [4, 256, 256])
        self.d_sconvT = inp("sconvT", [2, D, 16 * 30])
        self.d_spoolT = inp("spoolT", [D, 16 * 15])
        self.d_sffnT = inp("sffnT", [4, DFF, 16 * 2])
        self.d_invcnt = inp("invcnt", [128, 4 * 128])
        self.d_nsa_w_in = inp("nsa_w_in", [D, 2608])
        self.d_nsa_w_out = inp("nsa_w_out", [D, D])
        self.d_w1 = inp("cmp_w1", [2, 4096, 256])
        self.d_w2 = inp("cmp_w2", [2, 256, 64])
        self.d_pe2 = inp("pe2", [64, 256])
        self.d_onehot = inp("onehot", [64, 4096])
        self.d_tri = inp("tri", [128, 256])
        self.d_masks = inp("masks", [18, 128, 256])
        self.d_cache = inp("cache", [2560 * 128, 1024])
        self.d_pt = inp("ptab", [1, 256], I32)
        self.d_iota = inp("iota", [128, 1])
        self.d_swin = inp("swin", [16, 512, 512])
        self.d_vbs = inp("vbs", [128, 512])
        self.d_ohnew = inp("ohnew", [64, 128])
        self.d_mnew = inp("mnew", [128, 128])
        self.sc_sKs = dt("sc_sKs", [16, 64, 4, 2048], BF16).ap()
        self.sc_sVs = dt("sc_sVs", [16, 16, 128, 256], BF16).ap()
        self.sc_sVc = dt("sc_sVc", [16, 64, 4, 2048], BF16).ap()
        self.o_swinp = outp("o_swinp", [16, 504, 512])
        self.sc_Ks = dt("sc_Ks", [64, 4, 4096], BF16).ap()
        self.sc_Vs = dt("sc_Vs", [32, 128, 256], BF16).ap()
        self.sc_wK = dt("sc_wK", [32, 64, 512], BF16).ap()
        self.sc_wV = dt("sc_wV", [32, 128, 256], BF16).ap()
        self.o_rows = outp("o_rows", [2048 + 128, 1024])
        self.o_win = outp("o_win", [512 + 128, 512])
        self.o_yT = outp("o_yT", [D, TOT])
        self.o_convT = outp("o_convT", [2, D, 160 + 352])
        self.o_ffnT = outp("o_ffnT", [4, DFF, 2 + 32])
        self.o_poolT = outp("o_poolT", [D, 144 + 112])

    def build(self):
        nc = self.nc
        with ExitStack() as st:
            E = st.enter_context
            self.S = S = Sched(nc, st)
            self.B = S.buf
            sb = lambda n, s, d=F32: E(nc.sbuf_tensor(self.un(n), list(s), d))
            self.xT = sb("xT_s", [128, KC, TOT])
            self.modT = sb("modT", [128, 2, 48, 17])
            self.vecs = sb("vecs_s", [128, NV])
            self.ident = sb("ident_s", [128, 128])
            self.ones = sb("ones_s", [128, 128], BF16)
            self.flag = sb("flag_s", [128, 1])
            self.cmT = sb("cmT", [128, KC, 17], BF16)
            self.coef = sb("coef", [128, 6, KC])
            self.As = sb("As", [128, KC, NS])
            self.ckT = sb("ckT", [64, 4, 64], BF16)
            self.cv = sb("cv", [64, 4, 64], BF16)
            self.ps = E(nc.psum_tensor("ps", [128, 8, 512], F32))
            self.setup()
            S.emit()
            self.main()
            S.emit(final=True)
        return nc

    def setup(self):
        S, B = self.S, self.B
        xT, vecs, ident, flag = self.xT, self.vecs, self.ident, self.flag
        if self.stage >= 2:
            for c in range(KC):
                S.dma("sp", lambda e, c=c: e.dma_start(out=xT[:, c, 0:2048], in_=self.d_xA[c * 128:(c + 1) * 128, :]),
                      writes=[B("x", c, t) for t in range(16)])
        else:
            self.load_x()
        S.dma("sp", lambda e: e.dma_start(out=vecs[:], in_=self.d_vecs), writes=[B("vecs")])
        S.dma("sp", lambda e: e.dma_start(out=ident[:], in_=self.d_ident), writes=[B("ident")])
        S.dma("sp", lambda e: e.dma_start(out=flag[:], in_=self.d_flag), writes=[B("flag")])
        S.op("pool", lambda e: e.memset(self.ones[:], 1.0), writes=[B("ones")])
        with ExitStack() as ph:
            ct = ph.enter_context(self.nc.sbuf_tensor("ct", [128, KC, 17], F32))
            S.dma("sp", lambda e: e.dma_start(out=ct[:], in_=self.d_cT.rearrange("(c p) n -> p c n", p=128)), writes=[B("ct")])
            S.op("act", lambda e: e.activation(out=self.cmT[:], in_=ct[:], func=AF.Silu), reads=[B("ct")], writes=[B("cmT")])
            S.emit()

    def load_x(self):
        S, B = self.S, self.B
        for c in range(KC):
            S.dma("sp", lambda e, c=c: e.dma_start(out=self.xT[:, c, :], in_=self.d_xT[c * 128:(c + 1) * 128, :]),
                  writes=[B("x", c, t) for t in range(TOT // 128)])

    def un(self, n):
        self._uid = getattr(self, "_uid", 0) + 1
        return "%s_%d" % (n, self._uid)

    def xb(self, c, t0, w):
        return [self.B("x", c, t) for t in range(t0 // 128, (t0 + w + 127) // 128)]

    def ada(self, i):
        S, B, nc = self.S, self.B, self.nc
        slot = i % 2
        with ExitStack() as ph:
            wb = [ph.enter_context(nc.sbuf_tensor(self.un("adaw%d" % k), [128, KC, 1024], BF16)) for k in range(2)]
            for grp in range(6):
                w = wb[grp % 2]
                bw = B("adaw", grp % 2)
                S.dma("pool", lambda e, w=w, grp=grp: e.dma_start(
                    out=w[:], in_=self.d_ada_w[i, :, grp * 1024:(grp + 1) * 1024].rearrange("(c p) n -> p c n", p=128)), writes=[bw])
                for j in range(8):
                    bank = j % 2
                    bp = B("ps", bank)
                    for k in range(KC):
                        S.op("pe", lambda e, w=w, j=j, k=k, bank=bank: e.matmul(
                            self.ps[:, bank, 0:17], lhsT=w[:, k, j * 128:(j + 1) * 128], rhs=self.cmT[:, k, :],
                            start=(k == 0), stop=(k == KC - 1)), reads=[bw, B("cmT")], writes=[bp])
                    m = grp * 8 + j
                    S.op("act", lambda e, m=m, bank=bank: e.activation(
                        out=self.modT[:, slot, m, :], in_=self.ps[:, bank, 0:17], func=AF.Identity,
                        bias=self.vecs[:, V_ADAB + i * 48 + m:V_ADAB + i * 48 + m + 1], scale=1.0),
                        reads=[bp, B("vecs")], writes=[B("modT", slot)])
            S.emit()

    def coefs(self, i, sub):
        S, B = self.S, self.B
        slot = i % 2
        mod = self.modT
        ng = self.vecs[:, (V_N1 if sub == 0 else V_N2) + i * 8:(V_N1 if sub == 0 else V_N2) + i * 8 + 8]
        o = 3 * sub
        m0 = 24 * sub
        rd = [B("modT", slot), B("vecs")]
        S.op("dve", lambda e: e.scalar_tensor_tensor(out=self.coef[:, o, :], in0=mod[:, slot, m0 + 8:m0 + 16, 0], scalar=1.0,
                                                     in1=ng, op0=ALU.add, op1=ALU.mult), reads=rd, writes=[B("coef")])
        S.op("dve", lambda e: e.tensor_copy(out=self.coef[:, o + 1, :], in_=mod[:, slot, m0:m0 + 8, 0]), reads=rd, writes=[B("coef")])
        S.op("dve", lambda e: e.tensor_copy(out=self.coef[:, o + 2, :], in_=mod[:, slot, m0 + 16:m0 + 24, 0]), reads=rd, writes=[B("coef")])
        for c in range(KC):
            S.op("dve", lambda e, c=c: e.tensor_scalar(
                out=self.As[:, c, :].rearrange("p (s j) -> p s j", j=8),
                in0=bc(mod[:, slot, m0 + 8 + c, 1:17].unsqueeze(2), [128, 16, 8]), scalar1=1.0,
                scalar2=ng[:, c:c + 1], op0=ALU.add, op1=ALU.mult), reads=rd, writes=[B("As")])

    def mod_s(self, i, m):
        return bc(self.modT[:, i % 2, m * 8:(m + 1) * 8, 1:17].unsqueeze(3), [128, 8, 16, 8])

    def norm_mod(self, i, sub, t0, w, kind, hT, hcol, tmp, sq, rs, hname, fp32_out=None):
        S, B = self.S, self.B
        xT, ps = self.xT, self.ps
        bp = B("ps", 7)
        for c in range(KC):
            s_ = sq[c % 2]
            S.op("act", lambda e, c=c, s_=s_: e.activation(out=s_[:, 0:w], in_=xT[:, c, t0:t0 + w], func=AF.Square),
                 reads=self.xb(c, t0, w), writes=[B("sq", c % 2)])
            S.op("pe", lambda e, c=c, s_=s_: e.matmul(ps[:, 7, 0:w], lhsT=self.ones[:], rhs=s_[:, 0:w], start=(c == 0), stop=(c == KC - 1)),
                 reads=[B("sq", c % 2), B("ones")], writes=[bp])
        S.op("act", lambda e: e.activation(out=rs[:, 0:w], in_=ps[:, 7, 0:w], func=AF.Sqrt, bias=EPS, scale=1.0 / D), reads=[bp], writes=[B("rs")])
        S.op("dve", lambda e: e.reciprocal(out=rs[:, 0:w], in_=rs[:, 0:w]), reads=[B("rs")], writes=[B("rs")])
        o = 3 * sub
        for c in range(KC):
            t_ = tmp[c % 2]
            bt = B("nt", c % 2)
            hb = B(hname, c, hcol // 128) if hname else None
            S.op("dve", lambda e, c=c, t_=t_: e.tensor_tensor(out=t_[:, 0:w], in0=xT[:, c, t0:t0 + w], in1=rs[:, 0:w], op=ALU.mult),
                 reads=self.xb(c, t0, w) + [B("rs")], writes=[bt])
            wr = [B(hname, c, t) for t in range(hcol // 128, (hcol + w + 127) // 128)]
            if kind == "p":
                S.op("act", lambda e, c=c, t_=t_: e.activation(out=hT[:, c, hcol:hcol + w], in_=t_[:, 0:w], func=AF.Identity,
                                                              bias=self.coef[:, o + 1, c:c + 1], scale=self.coef[:, o, c:c + 1]),
                     reads=[bt, B("coef")], writes=wr)
                if fp32_out is not None:
                    S.op("dve", lambda e, c=c, t_=t_: e.tensor_scalar(out=fp32_out[:, c, hcol:hcol + w], in0=t_[:, 0:w],
                                                                       scalar1=self.coef[:, o, c:c + 1], scalar2=self.coef[:, o + 1, c:c + 1],
                                                                       op0=ALU.mult, op1=ALU.add), reads=[bt, B("coef")], writes=[B("h32", c)])
            else:
                S.op("dve", lambda e, c=c, t_=t_: e.tensor_tensor(out=t_[:, 0:w], in0=t_[:, 0:w], in1=self.As[:, c, :], op=ALU.mult),
                     reads=[bt, B("As")], writes=[bt])
                shs = self.modT[:, i % 2, 24 * sub + c, 1:17]
                S.op("dve", lambda e, c=c, t_=t_, shs=shs: e.tensor_tensor(
                    out=hT[:, c, hcol:hcol + w].rearrange("p (s j) -> p s j", j=8), in0=t_[:, 0:w].rearrange("p (s j) -> p s j", j=8),
                    in1=bc(shs.unsqueeze(2), [128, 16, 8]), op=ALU.add), reads=[bt, B("modT", i % 2)], writes=wr)
                if fp32_out is not None:
                    S.op("pool", lambda e, c=c, t_=t_, shs=shs: e.tensor_tensor(
                        out=fp32_out[:, c, hcol:hcol + w].rearrange("p (s j) -> p s j", j=8), in0=t_[:, 0:w].rearrange("p (s j) -> p s j", j=8),
                        in1=bc(shs.unsqueeze(2), [128, 16, 8]), op=ALU.add), reads=[bt, B("modT", i % 2)], writes=[B("h32", c)])

    def resid(self, i, sub, c, t0, w, kind, src_ap, rd, extra_scale=None):
        S, B = self.S, self.B
        xs = self.xT[:, c, t0:t0 + w]
        if kind == "p":
            g = extra_scale if extra_scale is not None else self.coef[:, 3 * sub + 2, c:c + 1]
            S.op("dve", lambda e: e.scalar_tensor_tensor(out=xs, in0=src_ap, scalar=g, in1=xs, op0=ALU.mult, op1=ALU.add),
                 reads=rd + [B("coef")] + self.xb(c, t0, w), writes=self.xb(c, t0, w))
        else:
            gs = self.modT[:, i % 2, 24 * sub + 16 + c, 1:17]
            tmp = self.rtmp
            S.op("dve", lambda e: e.tensor_tensor(out=tmp[:, 0:w].rearrange("p (s j) -> p s j", j=8), in0=src_ap.rearrange("p (s j) -> p s j", j=8),
                                                  in1=bc(gs.unsqueeze(2), [128, 16, 8]), op=ALU.mult),
                 reads=rd + [B("modT", i % 2)], writes=[B("rtmp")])
            if extra_scale is not None:
                S.op("dve", lambda e: e.tensor_scalar(out=tmp[:, 0:w], in0=tmp[:, 0:w], scalar1=extra_scale, scalar2=None, op0=ALU.mult),
                     reads=[B("rtmp")], writes=[B("rtmp")])
            S.op("dve", lambda e: e.tensor_tensor(out=xs, in0=xs, in1=tmp[:, 0:w], op=ALU.add),
                 reads=[B("rtmp")] + self.xb(c, t0, w), writes=self.xb(c, t0, w))

    def ffn(self, i, tiles):
        S, B, nc, ps = self.S, self.B, self.nc, self.ps
        G = 4
        with ExitStack() as ph:
            A = lambda n, s, d=F32: ph.enter_context(nc.sbuf_tensor(self.un(n), list(s), d))
            hT = A("f_hT", [128, KC, TOT], BF16)
            wa = [A("f_wa%d" % k, [128, KC, G * 128], BF16) for k in range(2)]
            wbb = [A("f_wb%d" % k, [128, KC, G * 128], BF16) for k in range(2)]
            wd = [A("f_wd%d" % k, [128, G, D], BF16) for k in range(2)]
            abuf = A("f_abuf", [128, G, 2 + 512])
            abs_ = A("f_abs", [128, G, 16, 10])
            ac = [A("f_ac%d" % k, [128, 512]) for k in range(2)]
            gb = [A("f_g%d" % k, [128, G, 512], BF16) for k in range(2)]
            tmp = ac
            sq = [A("f_sq%d" % k, [128, 512], BF16) for k in range(2)]
            rs = A("f_rs", [128, 512])
            self.rtmp = A("f_rtmp", [128, 128])
            self.coefs(i, 1)
            for (t0, w, kind) in tiles:
                self.norm_mod(i, 1, t0, w, kind, hT, t0, tmp, sq, rs, "fh")
            fdw = lambda j, k: self.vecs[:, V_FDW + (i * FC + j) * 3 + k:V_FDW + (i * FC + j) * 3 + k + 1]
            npass = (FC + G - 1) // G
            gi = 0
            for p_ in range(npass):
                j0 = p_ * G
                ng = min(G, FC - j0)
                sl = p_ % 2
                bw = B("fw", sl)
                S.dma("pool", lambda e, sl=sl, j0=j0, ng=ng: e.dma_start(
                    out=wa[sl][:, :, 0:ng * 128], in_=self.d_ffn_w_up[i, :, j0 * 128:(j0 + ng) * 128].rearrange("(c p) n -> p c n", p=128)), writes=[bw])
                S.dma("pool", lambda e, sl=sl, j0=j0, ng=ng: e.dma_start(
                    out=wbb[sl][:, :, 0:ng * 128], in_=self.d_ffn_w_up[i, :, DFF + j0 * 128:DFF + (j0 + ng) * 128].rearrange("(c p) n -> p c n", p=128)), writes=[bw])
                S.dma("pool", lambda e, sl=sl, j0=j0, ng=ng: e.dma_start(
                    out=wd[sl][:, 0:ng, :], in_=self.d_ffn_w_down[i, j0 * 128:(j0 + ng) * 128, :].rearrange("(g p) n -> p g n", p=128)), writes=[bw])
                if any(k == "s" for (_, _, k) in tiles):
                    for g_ in range(ng):
                        S.dma("sp", lambda e, j0=j0, g_=g_: e.dma_start(
                            out=abs_[:, g_, :, 0:2], in_=self.d_sffnT[i, (j0 + g_) * 128:(j0 + g_ + 1) * 128, :].rearrange("p (s r) -> p s r", r=2)),
                            writes=[B("abs", g_)])
                pending = [None]

                def down(t0, w, kind, gsl, g_t):
                    for c in range(KC):
                        bank = 4 + (c % 2)
                        bp = B("ps", bank)
                        for jj in range(ng):
                            S.op("pe", lambda e, c=c, jj=jj, bank=bank, g_t=g_t: e.matmul(ps[:, bank, 0:w], lhsT=wd[sl][:, jj, c * 128:(c + 1) * 128],
                                                                                         rhs=g_t[:, jj, 0:w], start=(jj == 0), stop=(jj == ng - 1)),
                                 reads=[bw, B("g", gsl, jj)], writes=[bp])
                        self.resid(i, 1, c, t0, w, kind, ps[:, bank, 0:w], [bp])
                for ti, (t0, w, kind) in enumerate(tiles):
                    gsl = gi % 2
                    gi += 1
                    g_t = gb[gsl]
                    for jj in range(ng):
                        if jj == 1 and pending[0] is not None:
                            down(*pending[0])
                            pending[0] = None
                        j = j0 + jj
                        pa, pb = 2 * (jj % 2), 2 * (jj % 2) + 1
                        bpa, bpb = B("ps", pa), B("ps", pb)
                        hrd = [B("fh", k, t) for k in range(KC) for t in range(t0 // 128, (t0 + w + 127) // 128)]
                        for k in range(KC):
                            S.op("pe", lambda e, k=k, jj=jj, pa=pa: e.matmul(ps[:, pa, 0:w], lhsT=wa[sl][:, k, jj * 128:(jj + 1) * 128],
                                                                            rhs=hT[:, k, t0:t0 + w], start=(k == 0), stop=(k == KC - 1)),
                                 reads=[bw] + hrd, writes=[bpa])
                        for k in range(KC):
                            S.op("pe", lambda e, k=k, jj=jj, pb=pb: e.matmul(ps[:, pb, 0:w], lhsT=wbb[sl][:, k, jj * 128:(jj + 1) * 128],
                                                                            rhs=hT[:, k, t0:t0 + w], start=(k == 0), stop=(k == KC - 1)),
                                 reads=[bw] + hrd, writes=[bpb])
                        a_ = ac[jj % 2]
                        ba = B("ac", jj % 2)
                        if kind == "p":
                            bab = B("abuf", jj)
                            if ti == 0:
                                S.op("pool", lambda e, jj=jj: e.memset(abuf[:, jj, 0:2], 0.0), writes=[bab])
                            S.op("act", lambda e, jj=jj, pa=pa: e.activation(out=abuf[:, jj, 2:2 + w], in_=ps[:, pa, 0:w], func=AF.Copy),
                                 reads=[bpa], writes=[bab])
                            if ti == 0 and self.halo:
                                S.op("dve", lambda e, jj=jj: e.tensor_scalar(out=abuf[:, jj, 2:130], in0=abuf[:, jj, 2:130], scalar1=self.flag[:, 0:1],
                                                                            scalar2=None, op0=ALU.mult), reads=[bab, B("flag")], writes=[bab])
                            if t0 + w == NPT and self.halo:
                                S.dma("sp", lambda e, jj=jj, j=j: e.dma_start(out=self.o_ffnT[i, j * 128:(j + 1) * 128, 0:2], in_=abuf[:, jj, w:w + 2]), reads=[bab])
                            S.op("act", lambda e, jj=jj, j=j, a_=a_: e.activation(out=a_[:, 0:w], in_=abuf[:, jj, 0:w], func=AF.Identity, scale=fdw(j, 0)),
                                 reads=[bab, B("vecs")], writes=[ba])
                            for k in (1, 2):
                                S.op("dve", lambda e, jj=jj, j=j, a_=a_, k=k: e.scalar_tensor_tensor(
                                    out=a_[:, 0:w], in0=abuf[:, jj, k:k + w], scalar=fdw(j, k), in1=a_[:, 0:w], op0=ALU.mult, op1=ALU.add),
                                    reads=[bab, B("vecs"), ba], writes=[ba])
                            S.op("pool", lambda e, jj=jj: e.tensor_copy(out=abuf[:, jj, 0:2], in_=abuf[:, jj, w:w + 2]), reads=[bab], writes=[bab])
                        else:
                            bab = B("abs", jj)
                            S.op("act", lambda e, jj=jj, pa=pa: e.activation(out=abs_[:, jj, :, 2:10], in_=ps[:, pa, 0:128].rearrange("p (s j) -> p s j", j=8), func=AF.Copy),
                                 reads=[bpa], writes=[bab])
                            S.dma("sp", lambda e, jj=jj, j=j: e.dma_start(
                                out=self.o_ffnT[i, j * 128:(j + 1) * 128, 2:34].rearrange("p (s r) -> p s r", r=2), in_=abs_[:, jj, :, 8:10]), reads=[bab])
                            a3 = a_[:, 0:128].rearrange("p (s j) -> p s j", j=8)
                            S.op("dve", lambda e, jj=jj, j=j, a3=a3: e.tensor_scalar(out=a3, in0=abs_[:, jj, :, 0:8], scalar1=fdw(j, 0), scalar2=None, op0=ALU.mult),
                                 reads=[bab, B("vecs")], writes=[ba])
                            for k in (1, 2):
                                S.op("dve", lambda e, jj=jj, j=j, a3=a3, k=k: e.scalar_tensor_tensor(
                                    out=a3, in0=abs_[:, jj, :, k:k + 8], scalar=fdw(j, k), in1=a3, op0=ALU.mult, op1=ALU.add),
                                    reads=[bab, B("vecs"), ba], writes=[ba])
                        S.op("act", lambda e, a_=a_: e.activation(out=a_[:, 0:w], in_=a_[:, 0:w], func=AF.Silu), reads=[ba], writes=[ba])
                        S.op("dve", lambda e, a_=a_, jj=jj, pb=pb, g_t=g_t: e.tensor_tensor(out=g_t[:, jj, 0:w], in0=a_[:, 0:w], in1=ps[:, pb, 0:w], op=ALU.mult),
                             reads=[ba, bpb], writes=[B("g", gsl, jj)])
                    if pending[0] is not None:
                        down(*pending[0])
                    pending[0] = (t0, w, kind, gsl, g_t)
                if pending[0] is not None:
                    down(*pending[0])
                    pending[0] = None
            S.emit()

    def conv(self, i, jl, tiles):
        S, B, nc, ps = self.S, self.B, self.nc, self.ps
        has_s = any(k == "s" for (_, _, k) in tiles)
        with ExitStack() as ph0:
            A0 = lambda n, s, d=F32: ph0.enter_context(nc.sbuf_tensor(self.un(n), list(s), d))
            ubuf = A0("c_ubuf", [128, KC, 30 + NPT], BF16)
            ubs = A0("c_ubs", [128, KC, 16, 38], BF16)
            self.rtmp = A0("c_rtmp", [128, 128])
            self.coefs(i, 0)
            with ExitStack() as ph:
                A = lambda n, s, d=F32: ph.enter_context(nc.sbuf_tensor(self.un(n), list(s), d))
                w_in = A("c_win", [128, KC, 2 * D], BF16)
                hT = A("c_hT", [128, KC, 512], BF16)
                tmp = [A("c_t%d" % k, [128, 512]) for k in range(2)]
                sq = [A("c_sq%d" % k, [128, 512], BF16) for k in range(2)]
                rs = A("c_rs", [128, 512])
                sg = [A("c_sg%d" % k, [128, 512]) for k in range(2)]
                u32 = [A("c_u%d" % k, [128, 512]) for k in range(2)]
                sst = [A("c_sst%d" % k, [128, 16, 30]) for k in range(2)]
                for hh in range(2):
                    S.dma("pool", lambda e, hh=hh: e.dma_start(out=w_in[:, :, hh * D:(hh + 1) * D],
                                                               in_=self.d_conv_w_in[jl, :, hh * D:(hh + 1) * D].rearrange("(c p) n -> p c n", p=128)),
                          writes=[B("cwin")])
                S.op("pool", lambda e: e.memset(ubuf[:, :, 0:30], 0.0), writes=[B("ub", c, 0) for c in range(KC)])
                if has_s:
                    for c in range(KC):
                        st_ = sst[c % 2]
                        S.dma("sp", lambda e, c=c, st_=st_: e.dma_start(out=st_[:], in_=self.d_sconvT[jl, c * 128:(c + 1) * 128, :].rearrange("p (s r) -> p s r", r=30)), writes=[B("sst", c % 2)])
                        S.op("pool", lambda e, c=c, st_=st_: e.tensor_copy(out=ubs[:, c, :, 0:30], in_=st_[:]), reads=[B("sst", c % 2)], writes=[B("ubs", c)])
                        S.dma("sp", lambda e, c=c, st_=st_: e.dma_start(out=self.o_convT[jl, c * 128:(c + 1) * 128, 160:512].rearrange("p (s r) -> p s r", r=22),
                                                                       in_=st_[:, :, 8:30]), reads=[B("sst", c % 2)])
                for (t0, w, kind) in tiles:
                    self.norm_mod(i, 0, t0, w, kind, hT, 0, tmp, sq, rs, "ch")
                    hrd = [B("ch", k, t) for k in range(KC) for t in range(0, (w + 127) // 128)]
                    for c in range(KC):
                        pa, pb = 2 * (c % 2), 2 * (c % 2) + 1
                        bpa, bpb = B("ps", pa), B("ps", pb)
                        for (pp, off) in ((pa, 0), (pb, D)):
                            for k in range(KC):
                                S.op("pe", lambda e, k=k, c=c, pp=pp, off=off: e.matmul(ps[:, pp, 0:w], lhsT=w_in[:, k, off + c * 128:off + (c + 1) * 128],
                                                                                       rhs=hT[:, k, 0:w], start=(k == 0), stop=(k == KC - 1)),
                                     reads=[B("cwin")] + hrd, writes=[B("ps", pp)])
                        s_ = sg[c % 2]
                        u_ = u32[c % 2]
                        S.op("act", lambda e, s_=s_, pb=pb: e.activation(out=s_[:, 0:w], in_=ps[:, pb, 0:w], func=AF.Sigmoid), reads=[bpb], writes=[B("sg", c % 2)])
                        S.op("dve", lambda e, s_=s_, u_=u_, pa=pa: e.tensor_tensor(out=u_[:, 0:w], in0=ps[:, pa, 0:w], in1=s_[:, 0:w], op=ALU.mult),
                             reads=[bpa, B("sg", c % 2)], writes=[B("u32", c % 2)])
                        if kind == "p":
                            ubw = [B("ub", c, t) for t in range((30 + t0) // 128, (30 + t0 + w + 127) // 128 + 1)]
                            if t0 == 0 and self.halo:
                                S.op("dve", lambda e, u_=u_: e.tensor_scalar(out=u_[:, 0:128], in0=u_[:, 0:128], scalar1=self.flag[:, 0:1], scalar2=None, op0=ALU.mult),
                                     reads=[B("u32", c % 2), B("flag")], writes=[B("u32", c % 2)])
                            S.op("pool", lambda e, u_=u_, c=c: e.tensor_copy(out=ubuf[:, c, 30 + t0:30 + t0 + w], in_=u_[:, 0:w]), reads=[B("u32", c % 2)], writes=ubw)
                            if t0 + w == NPT and self.halo:
                                S.dma("sp", lambda e, u_=u_, c=c: e.dma_start(out=self.o_convT[jl, c * 128:(c + 1) * 128, 0:32], in_=u_[:, w - 32:w]), reads=[B("u32", c % 2)])
                        else:
                            S.op("pool", lambda e, u_=u_, c=c: e.tensor_copy(out=ubs[:, c, :, 30:38], in_=u_[:, 0:128].rearrange("p (s j) -> p s j", j=8)),
                                 reads=[B("u32", c % 2)], writes=[B("ubs", c)])
                            S.dma("sp", lambda e, u_=u_, c=c: e.dma_start(out=self.o_convT[jl, c * 128:(c + 1) * 128, 32:160], in_=u_[:, 0:128]), reads=[B("u32", c % 2)])
                S.emit()
            if os.environ.get("KCUT") == "1":
                return
            with ExitStack() as ph:
                A = lambda n, s, d=F32: ph.enter_context(nc.sbuf_tensor(self.un(n), list(s), d))
                w_out = A("c_wout", [128, KC, D], BF16)
                identb = A("c_idb", [128, 128], BF16)
                diag = A("c_diag", [128, 2, 31, 128], BF16)
                W2 = 256
                y32 = A("c_y32", [128, KC, W2])
                ybf = [A("c_ybf%d" % k, [128, W2], BF16) for k in range(2)]
                ysq = [A("c_ysq%d" % k, [128, W2], BF16) for k in range(2)]
                mean = A("c_mean", [128, W2])
                rstd = A("c_rstd", [128, W2])
                zt = A("c_zt", [128, KC, W2], BF16)
                zt32 = [A("c_z32%d" % k, [128, W2]) for k in range(2)]
                uodd = A("c_uodd", [128, KC, W2 + 32], BF16)
                ubso = A("c_ubso", [128, KC, 16, 38], BF16)
                if has_s:
                    for c in range(KC):
                        S.op("pool", lambda e, c=c: e.tensor_copy(out=ubso[:, c, :, 0:36], in_=ubs[:, c, :, 1:37]), reads=[B("ubs", c)], writes=[B("ubso", c)])
                S.dma("pool", lambda e: e.dma_start(out=w_out[:], in_=self.d_conv_w_out[jl].rearrange("(c p) n -> p c n", p=128)), writes=[B("cwout")])
                S.op("pool", lambda e: e.tensor_copy(out=identb[:], in_=self.ident[:]), reads=[B("ident")], writes=[B("idb")])
                lng = lambda c: self.vecs[:, V_CLNG + jl * KC + c:V_CLNG + jl * KC + c + 1]
                lnb = lambda c: self.vecs[:, V_CLNB + jl * KC + c:V_CLNB + jl * KC + c + 1]
                sub = []
                for (t0, w, kind) in tiles:
                    for q0 in range(0, w, W2):
                        sub.append((t0 + q0, min(W2, w - q0), kind))
                for (t0, w, kind) in sub:
                    bps, bpq = B("ps", 6), B("ps", 7)
                    if kind == "p":
                        for c in range(KC):
                            S.op("pool", lambda e, c=c: e.tensor_copy(out=uodd[:, c, 0:w + 29], in_=ubuf[:, c, t0 + 1:t0 + w + 30]),
                                 reads=[B("ub", c, t) for t in range(t0 // 128, (t0 + w + 31 + 127) // 128 + 1)], writes=[B("uodd", c)])
                    for c in range(KC):
                        bank = c % 2
                        bp = B("ps", bank)
                        for k in range(31):
                            S.op("dve", lambda e, c=c, k=k: e.tensor_scalar(
                                out=diag[:, c % 2, k, :], in0=identb[:], scalar1=self.vecs[:, V_CDW + (jl * KC + c) * 31 + k:V_CDW + (jl * KC + c) * 31 + k + 1],
                                scalar2=None, op0=ALU.mult), reads=[B("idb"), B("vecs")], writes=[B("diag", c % 2, k)])
                        for k in range(31):
                            if kind == "p":
                                if k % 2 == 0:
                                    rhs = ubuf[:, c, t0 + k:t0 + k + w]
                                    rd = [B("ub", c, t) for t in range((t0 + k) // 128, (t0 + k + w + 127) // 128 + 1)]
                                else:
                                    rhs = uodd[:, c, k - 1:k - 1 + w]
                                    rd = [B("uodd", c)]
                            else:
                                if k % 2 == 0:
                                    rhs = ubs[:, c, :, k:k + 8]
                                    rd = [B("ubs", c)]
                                else:
                                    rhs = ubso[:, c, :, k - 1:k + 7]
                                    rd = [B("ubso", c)]
                            oap = ps[:, bank, 0:w] if kind == "p" else ps[:, bank, 0:128].rearrange("p (s j) -> p s j", j=8)
                            S.op("pe", lambda e, c=c, k=k, rhs=rhs, bank=bank, oap=oap: e.matmul(oap, lhsT=diag[:, c % 2, k, :], rhs=rhs, start=(k == 0), stop=(k == 30)),
                                 reads=rd + [B("diag", c % 2, k)], writes=[bp])
                        S.op("act", lambda e, c=c, bank=bank: e.activation(out=y32[:, c, 0:w], in_=ps[:, bank, 0:w], func=AF.Copy), reads=[bp], writes=[B("y32", c)])
                        S.op("dve", lambda e, c=c, bank=bank: e.tensor_copy(out=ybf[c % 2][:, 0:w], in_=y32[:, c, 0:w]), reads=[B("y32", c)], writes=[B("ybf", c % 2)])
                        S.op("act", lambda e, c=c, bank=bank: e.activation(out=ysq[c % 2][:, 0:w], in_=y32[:, c, 0:w], func=AF.Square), reads=[B("y32", c)], writes=[B("ysq", c % 2)])
                        S.op("pe", lambda e, c=c: e.matmul(ps[:, 6, 0:w], lhsT=self.ones[:], rhs=ybf[c % 2][:, 0:w], start=(c == 0), stop=(c == KC - 1)),
                             reads=[B("ybf", c % 2), B("ones")], writes=[bps])
                        S.op("pe", lambda e, c=c: e.matmul(ps[:, 7, 0:w], lhsT=self.ones[:], rhs=ysq[c % 2][:, 0:w], start=(c == 0), stop=(c == KC - 1)),
                             reads=[B("ysq", c % 2), B("ones")], writes=[bpq])
                    S.op("act", lambda e: e.activation(out=mean[:, 0:w], in_=ps[:, 6, 0:w], func=AF.Copy, scale=1.0 / D), reads=[bps], writes=[B("mean")])
                    S.op("dve", lambda e: e.tensor_tensor(out=rstd[:, 0:w], in0=mean[:, 0:w], in1=mean[:, 0:w], op=ALU.mult), reads=[B("mean")], writes=[B("rstd")])
                    S.op("dve", lambda e: e.scalar_tensor_tensor(out=rstd[:, 0:w], in0=ps[:, 7, 0:w], scalar=1.0 / D, in1=rstd[:, 0:w], op0=ALU.mult, op1=ALU.subtract),
                         reads=[bpq, B("rstd")], writes=[B("rstd")])
                    S.op("act", lambda e: e.activation(out=rstd[:, 0:w], in_=rstd[:, 0:w], func=AF.Sqrt, bias=EPS, scale=1.0), reads=[B("rstd")], writes=[B("rstd")])
                    S.op("dve", lambda e: e.reciprocal(out=rstd[:, 0:w], in_=rstd[:, 0:w]), reads=[B("rstd")], writes=[B("rstd")])
                    for c in range(KC):
                        z_ = zt32[c % 2]
                        bz = B("z32", c % 2)
                        S.op("dve", lambda e, c=c, z_=z_: e.tensor_tensor(out=z_[:, 0:w], in0=y32[:, c, 0:w], in1=mean[:, 0:w], op=ALU.subtract),
                             reads=[B("y32", c), B("mean")], writes=[bz])
                        S.op("dve", lambda e, c=c, z_=z_: e.tensor_tensor(out=z_[:, 0:w], in0=z_[:, 0:w], in1=rstd[:, 0:w], op=ALU.mult),
                             reads=[bz, B("rstd")], writes=[bz])
                        S.op("act", lambda e, c=c, z_=z_: e.activation(out=zt[:, c, 0:w], in_=z_[:, 0:w], func=AF.Silu, bias=lnb(c), scale=lng(c)),
                             reads=[bz, B("vecs")], writes=[B("zt", c)])
                    for c in range(KC):
                        bank = 2 + (c % 2)
                        bp = B("ps", bank)
                        for k in range(KC):
                            S.op("pe", lambda e, c=c, k=k, bank=bank: e.matmul(ps[:, bank, 0:w], lhsT=w_out[:, k, c * 128:(c + 1) * 128], rhs=zt[:, k, 0:w],
                                                                              start=(k == 0), stop=(k == KC - 1)), reads=[B("cwout"), B("zt", k)], writes=[bp])
                        self.resid(i, 0, c, t0, w, kind, ps[:, bank, 0:w], [bp])
                S.emit()


    def nsa_proj(self, i, ntiles, aux):
        S, B, nc, ps = self.S, self.B, self.nc, self.ps
        with ExitStack() as ph:
            A = lambda n, s, d=F32: ph.enter_context(nc.sbuf_tensor(self.un(n), list(s), d))
            w_kv = A("n_wkv", [128, KC, 1536], BF16)
            KcT = A("n_KcT", [64, 2, 4, 2048], BF16)
            w1 = A("n_w1", [64, 64, 256], BF16)
            w2 = A("n_w2", [128, 2, 2, 64], BF16)
            hT = A("n_hT", [128, KC, 128], BF16)
            rows = A("n_rows", [128, 1536])
            vst = A("n_vst", [128, 2, 256], BF16)
            kst = A("n_kst", [64, 2, 4, 128], BF16)
            pe2 = A("n_pe2", [64, 2, 128])
            HT = A("n_HT", [128, 2, 160], BF16)
            tmp = [A("n_t%d" % k, [128, 128]) for k in range(2)]
            sq = [A("n_sq%d" % k, [128, 128], BF16) for k in range(2)]
            rs = A("n_rs", [128, 128])
            self.coefs(i, 0)
            for hh in range(3):
                S.dma("pool", lambda e, hh=hh: e.dma_start(out=w_kv[:, :, hh * 512:(hh + 1) * 512],
                                                           in_=self.d_nsa_w_in[:, 1024 + hh * 512:1024 + (hh + 1) * 512].rearrange("(c p) n -> p c n", p=128)), writes=[B("wkv")])
            S.dma("sp", lambda e: e.dma_start(out=pe2[:], in_=self.d_pe2.rearrange("d (x t) -> d x t", x=2)), writes=[B("pe2")])
            for X in range(2):
                S.dma("pool", lambda e, X=X: e.dma_start(out=w2[:, X, :, :], in_=self.d_w2[X].rearrange("(c p) d -> p c d", p=128)), writes=[B("w2")])
            S.op("pool", lambda e: e.memset(HT[:], 0.0), writes=[B("HT")])
            for t in range(ntiles):
                l0 = t * 128 if aux else (t + 1) * 128
                pt = t if aux else 16 + t
                tok0 = t * 128
                self.norm_mod(i, 0, l0, 128, "p", hT, 0, tmp, sq, rs, "nh")
                hrd = [B("nh", k, 0) for k in range(KC)]
                for gq in range(3):
                    for k in range(KC):
                        S.op("pe", lambda e, gq=gq, k=k: e.matmul(ps[:, gq, 0:512], lhsT=hT[:, k, :], rhs=w_kv[:, k, gq * 512:(gq + 1) * 512],
                                                                 start=(k == 0), stop=(k == KC - 1)), reads=hrd + [B("wkv")], writes=[B("ps", gq)])
                    S.op("act", lambda e, gq=gq: e.activation(out=rows[:, gq * 512:(gq + 1) * 512], in_=ps[:, gq, 0:512], func=AF.Copy),
                         reads=[B("ps", gq)], writes=[B("rows", gq)])
                if not aux:
                    S.dma("sp", lambda e: e.dma_start(out=self.o_rows[tok0:tok0 + 128, :], in_=rows[:, 0:1024]), reads=[B("rows", 0), B("rows", 1)])
                    if t >= 12:
                        S.dma("sp", lambda e: e.dma_start(out=self.o_win[(t - 12) * 128:(t - 11) * 128, :], in_=rows[:, 1024:1536]), reads=[B("rows", 2)])
                S.op("dve", lambda e: e.tensor_copy(out=vst[:, 0, :], in_=rows[:, 768:1024]), reads=[B("rows", 1)], writes=[B("vst", 0)])
                S.op("dve", lambda e: e.tensor_copy(out=vst[:, 1, :], in_=rows[:, 1280:1536]), reads=[B("rows", 2)], writes=[B("vst", 1)])
                S.dma("sp", lambda e: e.dma_start(out=self.sc_Vs[pt], in_=vst[:, 0, :]), reads=[B("vst", 0)], writes=[B("scVs", pt)])
                S.dma("sp", lambda e: e.dma_start(out=self.sc_wV[pt], in_=vst[:, 1, :]), reads=[B("vst", 1)], writes=[B("scwV", pt)])
                for gi, off in enumerate((0, 256, 512, 1024)):
                    bank = 3 + gi % 2
                    for kvh in range(4):
                        for k in range(KC):
                            S.op("pe", lambda e, kvh=kvh, k=k, bank=bank, off=off: e.matmul(
                                ps[0:64, bank, kvh * 128:(kvh + 1) * 128], lhsT=w_kv[:, k, off + kvh * 64:off + kvh * 64 + 64], rhs=hT[:, k, :],
                                start=(k == 0), stop=(k == KC - 1)), reads=hrd + [B("wkv")], writes=[B("ps", bank)])
                    src3 = ps[0:64, bank, 0:512].rearrange("p (k t) -> p k t", k=4)
                    if gi < 2:
                        S.op("dve", lambda e, gi=gi, src3=src3: e.tensor_tensor(out=KcT[:, gi, :, tok0:tok0 + 128], in0=src3,
                                                                                 in1=bc(pe2[:, gi, :].unsqueeze(1), [64, 4, 128]), op=ALU.add),
                             reads=[B("ps", bank), B("pe2")], writes=[B("KcT", gi)])
                    else:
                        S.op("act", lambda e, gi=gi, src3=src3: e.activation(out=kst[:, gi - 2, :, :], in_=src3, func=AF.Copy),
                             reads=[B("ps", bank)], writes=[B("kst", gi - 2)])
                S.dma("sp", lambda e: e.dma_start(out=self.sc_Ks[:, :, pt * 128:(pt + 1) * 128], in_=kst[:, 0, :, :]), reads=[B("kst", 0)], writes=[B("scKs", pt)])
                S.dma("sp", lambda e: e.dma_start(out=self.sc_wK[pt].rearrange("d (k t) -> d k t", k=4), in_=kst[:, 1, :, :]), reads=[B("kst", 1)], writes=[B("scwK", pt)])
            slot0 = 0 if aux else 32
            for X in range(2):
                S.dma("pool", lambda e, X=X: e.dma_start(out=w1[:], in_=self.d_w1[X].rearrange("(i d) j -> d i j", d=64)), writes=[B("w1")])
                kv4 = KcT[:, X, :, :].rearrange("d k (n i) -> d k n i", i=64)
                for jc in range(2):
                    for ii in range(64):
                        S.op("pe", lambda e, jc=jc, ii=ii, kv4=kv4: e.matmul(
                            ps[:, 5, 0:128].rearrange("p (k n) -> p k n", k=4), lhsT=w1[:, ii, jc * 128:(jc + 1) * 128], rhs=kv4[:, :, :, ii],
                            start=(ii == 0), stop=(ii == 63)), reads=[B("w1"), B("KcT", X)], writes=[B("ps", 5)])
                    S.op("act", lambda e, jc=jc: e.activation(out=HT[:, jc, 32:160], in_=ps[:, 5, 0:128], func=AF.Silu), reads=[B("ps", 5)], writes=[B("HT")])
                if X == 0:
                    for jc in range(2):
                        S.op("pe", lambda e, jc=jc: e.matmul(ps[0:64, 6, 0:128], lhsT=w2[:, 0, jc, :], rhs=HT[:, jc, 32:160], start=(jc == 0), stop=(jc == 1)),
                             reads=[B("w2"), B("HT")], writes=[B("ps", 6)])
                    S.op("act", lambda e: e.activation(out=self.ckT[:, :, slot0:slot0 + 32], in_=ps[0:64, 6, 0:128].rearrange("p (k n) -> p k n", k=4), func=AF.Copy),
                         reads=[B("ps", 6)], writes=[B("ckT")])
                else:
                    for kvh in range(4):
                        for jc in range(2):
                            if slot0 == 0:
                                oap, lap = ps[0:32, 7, kvh * 64:(kvh + 1) * 64], HT[:, jc, 32 + kvh * 32:64 + kvh * 32]
                            else:
                                oap, lap = ps[0:64, 7, kvh * 64:(kvh + 1) * 64], HT[:, jc, kvh * 32:kvh * 32 + 64]
                            S.op("pe", lambda e, jc=jc, oap=oap, lap=lap: e.matmul(oap, lhsT=lap, rhs=w2[:, 1, jc, :], start=(jc == 0), stop=(jc == 1)),
                                 reads=[B("w2"), B("HT")], writes=[B("ps", 7)])
                    S.op("act", lambda e: e.activation(out=self.cv[slot0:slot0 + 32, :, :], in_=ps[slot0:slot0 + 32, 7, 0:256].rearrange("p (k d) -> p k d", k=4), func=AF.Copy),
                         reads=[B("ps", 7)], writes=[B("cv")])
            S.emit()

    def nsa_attn(self, i):
        S, B, nc, ps = self.S, self.B, self.nc, self.ps
        BIG = 240000.0
        with ExitStack() as ph:
            A = lambda n, s, d=F32: ph.enter_context(nc.sbuf_tensor(self.un(n), list(s), d))
            KsA = A("a_KsA", [128, 4, 4096], BF16)
            Vs = A("a_Vs", [128, 32, 4, 65], BF16)
            KwR = A("a_KwR", [64, 5, 4, 128], BF16)
            VwR = A("a_VwR", [128, 5, 4, 65], BF16)
            w_q = A("a_wq", [128, KC, 1072], BF16)
            w_o = A("a_wo", [128, KC, D], BF16)
            hT = A("a_hT", [128, KC, 128], BF16)
            QA = [A("a_QA%d" % k, [128, 4, 4, 128], BF16) for k in range(1)] * 2
            PT_ = [A("a_PT%d" % k, [128, 512], BF16) for k in range(2)]
            gates = [A("a_gates%d" % k, [128, 48]) for k in range(1)] * 2
            mk = A("a_mk", [128, 4, 64])
            tri = A("a_tri", [128, 2, 128], BF16)
            e4 = A("a_e4", [128, 16, 64])
            sm = A("a_sm", [128, 16])
            imp = A("a_imp", [128, 4, 64])
            imp2 = A("a_imp2", [128, 4, 64])
            wk = A("a_wk", [128, 4, 64])
            m8 = A("a_m8", [128, 4, 16])
            mbs = A("a_mbs", [128, 4, 128])
            pT = A("a_pT", [64, 4, 128], BF16)
            ocmp = [A("a_ocmp%d" % k, [128, 16, 64], BF16) for k in range(1)] * 2
            otot = A("a_otot", [128, 4, 4, 64])
            ototT = A("a_ototT", [128, KC, 128], BF16)
            ctmp = A("a_ctmp", [128, 4, 64])
            cf = A("a_cf", [128, 12])
            tmp = [imp2[:, 0:2, :].rearrange("p a b -> p (a b)"), wk[:, 0:2, :].rearrange("p a b -> p (a b)")]
            oTap = e4[0:65, 0:8, :].rearrange("p a b -> p (a b)")
            sq = [A("a_sq%d" % k, [128, 128], BF16) for k in range(2)]
            rs = A("a_rs", [128, 128])
            self.coefs(i, 0)
            S.dma("sp", lambda e: e.dma_start(out=KsA[0:64, :, :], in_=self.sc_Ks), reads=[B("scKs", t) for t in range(32)], writes=[B("KsA")])
            for kvh in range(4):
                S.dma("pool", lambda e, kvh=kvh: e.dma_start(out=KsA[64:128, kvh, :], in_=self.d_onehot), writes=[B("KsA")])
            S.op("pool", lambda e: e.memset(Vs[:], 1.0), writes=[B("Vs")])
            S.op("pool", lambda e: e.memset(VwR[:], 1.0), writes=[B("VwR", r_) for r_ in range(5)])
            for t in range(32):
                S.dma("sp", lambda e, t=t: e.dma_start(out=Vs[:, t, :, 0:64], in_=self.sc_Vs[t].rearrange("p (k d) -> p k d", k=4)), reads=[B("scVs", t)], writes=[B("Vs")])
            S.dma("pool", lambda e: e.dma_start(out=w_q[:, :, 0:1024], in_=self.d_nsa_w_in[:, 0:1024].rearrange("(c p) n -> p c n", p=128)), writes=[B("wq")])
            S.dma("pool", lambda e: e.dma_start(out=w_q[:, :, 1024:1072], in_=self.d_nsa_w_in[:, 2560:2608].rearrange("(c p) n -> p c n", p=128)), writes=[B("wq")])
            S.dma("pool", lambda e: e.dma_start(out=w_o[:], in_=self.d_nsa_w_out.rearrange("(c p) n -> p c n", p=128)), writes=[B("wo")])
            S.dma("pool", lambda e: e.dma_start(out=tri[:], in_=self.d_tri.rearrange("p (a t) -> p a t", a=2)), writes=[B("tri")])
            S.op("pool", lambda e: e.memset(mbs[:], 0.0), writes=[B("mbs")])

            def load_ring(pt):
                r_ = pt % 5
                S.dma("sp", lambda e: e.dma_start(out=KwR[:, r_, :, :], in_=self.sc_wK[pt].rearrange("d (k t) -> d k t", k=4)), reads=[B("scwK", pt)], writes=[B("KwR", r_)])
                S.dma("sp", lambda e: e.dma_start(out=VwR[:, r_, :, 0:64], in_=self.sc_wV[pt].rearrange("p (k d) -> p k d", k=4)), reads=[B("scwV", pt)], writes=[B("VwR", r_)])
            for pt in range(11, 15):
                load_ring(pt)

            def chain(qt):
                sl = 0
                l0 = qt * 128
                qa = QA[sl]
                bqa = B("QA", sl)
                load_ring(15 + qt)
                S.dma("sp", lambda e: e.dma_start(out=mk[:], in_=self.d_masks[qt].rearrange("p (a n) -> p a n", a=4)), writes=[B("mk")])
                self.norm_mod(i, 0, l0, 128, "p", hT, 0, tmp, sq, rs, "ah")
                hrd = [B("ah", k, 0) for k in range(KC)]
                for k in range(KC):
                    S.op("pe", lambda e, k=k: e.matmul(ps[:, 6, 0:48], lhsT=hT[:, k, :], rhs=w_q[:, k, 1024:1072], start=(k == 0), stop=(k == KC - 1)),
                         reads=hrd + [B("wq")], writes=[B("ps", 6)])
                S.op("act", lambda e: e.activation(out=gates[sl][:], in_=ps[:, 6, 0:48], func=AF.Sigmoid), reads=[B("ps", 6)], writes=[B("gates", sl)])
                for kvh in range(4):
                    bank = 4 + kvh % 2
                    for g in range(4):
                        hd = kvh * 4 + g
                        for k in range(KC):
                            S.op("pe", lambda e, g=g, k=k, hd=hd, bank=bank: e.matmul(ps[0:64, bank, g * 128:(g + 1) * 128], lhsT=w_q[:, k, hd * 64:(hd + 1) * 64], rhs=hT[:, k, :],
                                                                                     start=(k == 0), stop=(k == KC - 1)), reads=hrd + [B("wq")], writes=[B("ps", bank)])
                    S.op("act", lambda e, kvh=kvh, bank=bank: e.activation(out=qa[0:64, kvh, :, :], in_=ps[0:64, bank, 0:512].rearrange("p (g t) -> p g t", g=4), func=AF.Copy),
                         reads=[B("ps", bank)], writes=[bqa])
                for kvh in range(4):
                    for g in range(4):
                        hd = kvh * 4 + g
                        S.op("pe", lambda e, g=g, kvh=kvh, hd=hd: e.matmul(ps[:, 4 + hd // 8, (hd % 8) * 64:(hd % 8 + 1) * 64], lhsT=qa[0:64, kvh, g, :], rhs=self.ckT[:, kvh, :], start=True, stop=True),
                             reads=[bqa, B("ckT")], writes=[B("ps", 4 + hd // 8)])
                for hb in range(2):
                    S.op("dve", lambda e, hb=hb: e.scalar_tensor_tensor(out=e4[:, hb * 8:(hb + 1) * 8, :], in0=ps[:, 4 + hb, 0:512].rearrange("p (g n) -> p g n", g=8), scalar=0.125,
                                                                       in1=bc(mk[:, 0, :].unsqueeze(1), [128, 8, 64]), op0=ALU.mult, op1=ALU.add),
                         reads=[B("ps", 4 + hb), B("mk")], writes=[B("e4")])
                S.op("act", lambda e: e.activation(out=e4[:], in_=e4[:], func=AF.Exp), reads=[B("e4")], writes=[B("e4")])
                S.op("dve", lambda e: e.tensor_reduce(out=sm[:], in_=e4[:], axis=AX.X, op=ALU.add), reads=[B("e4")], writes=[B("sm")])
                S.op("dve", lambda e: e.tensor_scalar(out=sm[:], in0=sm[:], scalar1=1e-30, scalar2=None, op0=ALU.add), reads=[B("sm")], writes=[B("sm")])
                S.op("dve", lambda e: e.reciprocal(out=sm[:], in_=sm[:]), reads=[B("sm")], writes=[B("sm")])
                S.op("dve", lambda e: e.tensor_tensor(out=e4[:], in0=e4[:], in1=bc(sm[:].unsqueeze(2), [128, 16, 64]), op=ALU.mult),
                     reads=[B("sm"), B("e4")], writes=[B("e4")])
                for kvh in range(4):
                    S.op("dve", lambda e, kvh=kvh: e.tensor_reduce(out=imp[:, kvh, :], in_=e4[:, kvh * 4:(kvh + 1) * 4, :].rearrange("p g n -> p n g"), axis=AX.X, op=ALU.add),
                         reads=[B("e4")], writes=[B("imp")])
                S.op("dve", lambda e: e.tensor_tensor(out=imp[:], in0=imp[:], in1=bc(mk[:, 1, :].unsqueeze(1), [128, 4, 64]), op=ALU.mult), reads=[B("imp"), B("mk")], writes=[B("imp")])
                S.op("dve", lambda e: e.tensor_tensor(out=imp[:], in0=imp[:], in1=bc(mk[:, 2, :].unsqueeze(1), [128, 4, 64]), op=ALU.add), reads=[B("imp"), B("mk")], writes=[B("imp")])
                for kvh in range(4):
                    S.op("dve", lambda e, kvh=kvh: e.max(out=m8[:, kvh, 0:8], in_=imp[:, kvh, :]), reads=[B("imp")], writes=[B("m8")])
                    S.op("dve", lambda e, kvh=kvh: e.match_replace(out=wk[:, kvh, :], in_to_replace=m8[:, kvh, 0:8], in_values=imp[:, kvh, :], imm_value=-3e30), reads=[B("imp"), B("m8")], writes=[B("wk")])
                    S.op("dve", lambda e, kvh=kvh: e.max(out=m8[:, kvh, 8:16], in_=wk[:, kvh, :]), reads=[B("wk")], writes=[B("m8")])
                    S.op("dve", lambda e, kvh=kvh: e.match_replace(out=wk[:, kvh, :], in_to_replace=m8[:, kvh, 8:16], in_values=wk[:, kvh, :], imm_value=-3e30), reads=[B("wk"), B("m8")], writes=[B("wk")])
                S.op("dve", lambda e: e.tensor_tensor(out=imp2[:], in0=imp[:], in1=wk[:], op=ALU.subtract), reads=[B("wk"), B("imp")], writes=[B("imp2")])
                S.op("dve", lambda e: e.tensor_scalar(out=imp2[:], in0=imp2[:], scalar1=1.0, scalar2=None, op0=ALU.min), reads=[B("imp2")], writes=[B("imp2")])
                S.op("dve", lambda e: e.tensor_tensor(out=imp2[:], in0=imp2[:], in1=bc(mk[:, 3, :].unsqueeze(1), [128, 4, 64]), op=ALU.mult), reads=[B("imp2"), B("mk")], writes=[B("imp2")])
                S.op("dve", lambda e: e.tensor_scalar(out=mbs[:, :, 64:128], in0=imp2[:], scalar1=-1.0, scalar2=BIG, op0=ALU.add, op1=ALU.mult),
                     reads=[B("imp2")], writes=[B("mbs")])
                for kvh in range(4):
                    S.op("pe", lambda e, kvh=kvh: e.transpose(out=ps[:, 4, kvh * 128:(kvh + 1) * 128], in_=mbs[:, kvh, :], identity=self.ident[:]), reads=[B("mbs"), B("ident")], writes=[B("ps", 4)])
                for kvh in range(4):
                    S.op("act", lambda e, kvh=kvh: e.activation(out=qa[64:128, kvh, :, :], in_=bc(ps[64:128, 4, kvh * 128:(kvh + 1) * 128].unsqueeze(1), [64, 4, 128]), func=AF.Copy),
                         reads=[B("ps", 4)], writes=[bqa])
                for kvh in range(4):
                    for g in range(4):
                        S.op("pe", lambda e, g=g, kvh=kvh: e.transpose(out=ps[0:64, 5, g * 128:(g + 1) * 128], in_=e4[:, kvh * 4 + g, :], identity=self.ident[:]),
                             reads=[B("e4"), B("ident")], writes=[B("ps", 5)])
                    S.op("act", lambda e: e.activation(out=pT[:], in_=ps[0:64, 5, 0:512].rearrange("p (g t) -> p g t", g=4), func=AF.Copy), reads=[B("ps", 5)], writes=[B("pT")])
                    for g in range(4):
                        S.op("pe", lambda e, g=g, kvh=kvh: e.matmul(ps[:, 6, g * 64:(g + 1) * 64], lhsT=pT[:, g, :], rhs=self.cv[:, kvh, :], start=True, stop=True),
                             reads=[B("pT"), B("cv")], writes=[B("ps", 6)])
                    S.op("act", lambda e, kvh=kvh: e.activation(out=ocmp[sl][:, kvh * 4:(kvh + 1) * 4, :], in_=ps[:, 6, 0:256].rearrange("p (g d) -> p g d", g=4), func=AF.Copy),
                         reads=[B("ps", 6)], writes=[B("ocmp", sl)])

            tcnt = [0]

            def attend(qt):
                sl = 0
                l0 = qt * 128
                dk = 15 + qt
                qa_all = QA[sl]
                bqa = B("QA", sl)
                gt = gates[sl]
                tl = []
                for kvh in range(4):
                    for kt in range(dk + 1):
                        tl.append((kvh, 0, kt, kt == 0, kt == dk))
                    for wi, kt in enumerate(range(dk - 4, dk + 1)):
                        tl.append((kvh, 1, kt, wi == 0, wi == 4))

                def emit_S(n):
                    kvh, br, kt, first, last = tl[n]
                    sb_ = n % 2
                    if br == 0:
                        lhs, rhs, rk = KsA[:, kvh, kt * 128:(kt + 1) * 128], qa_all[:, kvh, :, :].rearrange("p g t -> p (g t)"), B("KsA")
                    else:
                        lhs, rhs, rk = KwR[:, kt % 5, kvh, :], qa_all[0:64, kvh, :, :].rearrange("p g t -> p (g t)"), B("KwR", kt % 5)
                    S.op("pe", lambda e, lhs=lhs, rhs=rhs, sb_=sb_: e.matmul(ps[:, sb_, 0:512], lhsT=lhs, rhs=rhs, start=True, stop=True), reads=[rk, bqa], writes=[B("ps", sb_)])

                def emit_rest(n):
                    kvh, br, kt, first, last = tl[n]
                    sb_ = n % 2
                    pt_ = PT_[sb_]
                    pt3 = pt_[:].rearrange("p (g t) -> p g t", g=4)
                    S.op("act", lambda e, sb_=sb_, pt_=pt_: e.activation(out=pt_[:], in_=ps[:, sb_, 0:512], func=AF.Exp, scale=0.125), reads=[B("ps", sb_)], writes=[B("PT", sb_)])
                    if (br == 0 and last) or (br == 1 and (first or last)):
                        ti_ = 1 if (br == 1 and first) else 0
                        S.op("dve", lambda e, pt3=pt3, ti_=ti_: e.tensor_tensor(out=pt3, in0=pt3, in1=bc(tri[:, ti_, :].unsqueeze(1), [128, 4, 128]), op=ALU.mult),
                             reads=[B("PT", sb_), B("tri")], writes=[B("PT", sb_)])
                    if br == 1 and kt <= 15:
                        S.op("dve", lambda e, pt_=pt_: e.tensor_scalar(out=pt_[:], in0=pt_[:], scalar1=self.flag[:, 0:1], scalar2=None, op0=ALU.mult),
                             reads=[B("PT", sb_), B("flag")], writes=[B("PT", sb_)])
                    if br == 0:
                        vv, rv = Vs[:, kt, kvh, :], B("Vs")
                    else:
                        vv, rv = VwR[:, kt % 5, kvh, :], B("VwR", kt % 5)
                    S.op("pe", lambda e, vv=vv, pt_=pt_, br=br, first=first, last=last: e.matmul(ps[0:65, 2 + br, 0:512], lhsT=vv, rhs=pt_[:], start=first, stop=last),
                         reads=[B("PT", sb_), rv], writes=[B("ps", 2 + br)])
                    if not last:
                        return
                    bk = 2 + br
                    S.op("act", lambda e, bk=bk: e.activation(out=oTap, in_=ps[0:65, bk, 0:512], func=AF.Copy), reads=[B("ps", bk)], writes=[B("e4")])
                    for g in range(4):
                        S.op("pe", lambda e, bk=bk, g=g: e.transpose(out=ps[:, bk + 2, g * 65:(g + 1) * 65], in_=oTap[:, g * 128:(g + 1) * 128], identity=self.ident[0:65, 0:65]),
                             reads=[B("e4"), B("ident")], writes=[B("ps", bk + 2)])
                    if br == 0:
                        return
                    osel = ps[:, 4, 0:260].rearrange("p (g d) -> p g d", g=4)
                    owin = ps[:, 5, 0:260].rearrange("p (g d) -> p g d", g=4)
                    oc = ocmp[sl][:, kvh * 4:(kvh + 1) * 4, :]
                    S.op("dve", lambda e, osel=osel: e.tensor_scalar(out=cf[:, 0:4], in0=osel[:, :, 64], scalar1=1e-30, scalar2=None, op0=ALU.add), reads=[B("ps", 4)], writes=[B("cf")])
                    S.op("dve", lambda e, owin=owin: e.tensor_scalar(out=cf[:, 4:8], in0=owin[:, :, 64], scalar1=1e-30, scalar2=None, op0=ALU.add), reads=[B("ps", 5)], writes=[B("cf")])
                    S.op("dve", lambda e: e.reciprocal(out=cf[:, 0:8], in_=cf[:, 0:8]), reads=[B("cf")], writes=[B("cf")])
                    S.op("dve", lambda e, kvh=kvh: e.tensor_tensor(out=cf[:, 0:4], in0=cf[:, 0:4], in1=gt[:, 16 + kvh * 4:20 + kvh * 4], op=ALU.mult), reads=[B("cf"), B("gates", sl)], writes=[B("cf")])
                    S.op("dve", lambda e, kvh=kvh: e.tensor_tensor(out=cf[:, 4:8], in0=cf[:, 4:8], in1=gt[:, 32 + kvh * 4:36 + kvh * 4], op=ALU.mult), reads=[B("cf"), B("gates", sl)], writes=[B("cf")])
                    ot = otot[:, kvh, :, :]
                    S.op("dve", lambda e, ot=ot, oc=oc, kvh=kvh: e.tensor_tensor(out=ot, in0=oc, in1=bc(gt[:, kvh * 4:kvh * 4 + 4].unsqueeze(2), [128, 4, 64]), op=ALU.mult),
                         reads=[B("ocmp", sl), B("gates", sl)], writes=[B("otot")])
                    for (src, c0, bk2) in ((osel, 0, 4), (owin, 4, 5)):
                        S.op("dve", lambda e, src=src, c0=c0: e.tensor_tensor(out=ctmp[:], in0=src[:, :, 0:64], in1=bc(cf[:, c0:c0 + 4].unsqueeze(2), [128, 4, 64]), op=ALU.mult),
                             reads=[B("ps", bk2), B("cf")], writes=[B("ctmp")])
                        S.op("dve", lambda e, ot=ot: e.tensor_tensor(out=ot, in0=ot, in1=ctmp[:], op=ALU.add), reads=[B("ctmp"), B("otot")], writes=[B("otot")])

                emit_S(0)
                for n in range(len(tl)):
                    if n + 1 < len(tl):
                        emit_S(n + 1)
                    emit_rest(n)
                of = otot[:].rearrange("p k g d -> p (k g d)")
                for hh in range(2):
                    for c4 in range(4):
                        c = hh * 4 + c4
                        S.op("pe", lambda e, c=c, c4=c4, hh=hh, of=of: e.transpose(out=ps[:, hh, c4 * 128:(c4 + 1) * 128], in_=of[:, c * 128:(c + 1) * 128], identity=self.ident[:]),
                             reads=[B("otot"), B("ident")], writes=[B("ps", hh)])
                    S.op("act", lambda e, hh=hh: e.activation(out=ototT[:, hh * 4:(hh + 1) * 4, :], in_=ps[:, hh, 0:512].rearrange("p (c t) -> p c t", c=4), func=AF.Copy),
                         reads=[B("ps", hh)], writes=[B("ototT")])
                for c in range(KC):
                    bank = 6 + (c % 2)
                    for k in range(KC):
                        S.op("pe", lambda e, c=c, k=k, bank=bank: e.matmul(ps[:, bank, 0:128], lhsT=w_o[:, k, c * 128:(c + 1) * 128], rhs=ototT[:, k, :], start=(k == 0), stop=(k == KC - 1)),
                             reads=[B("wo"), B("ototT")], writes=[B("ps", bank)])
                    self.resid(i, 0, c, l0, 128, "p", ps[:, bank, 0:128], [B("ps", bank)])

            for qt in range(17):
                chain(qt)
                attend(qt)
            S.emit()

    def nsa_sample_prep(self):
        S, B, nc, ps = self.S, self.B, self.nc, self.ps
        with ExitStack() as ph:
            A = lambda n, s, d=F32: ph.enter_context(nc.sbuf_tensor(self.un(n), list(s), d))
            w1 = A("s_w1", [64, 64, 256], BF16)
            w2 = A("s_w2", [128, 2, 2, 64], BF16)
            KcT = A("s_KcT", [64, 4, 2048], BF16)
            pgt = [A("s_pg%d" % k, [128, 1024]) for k in range(4)]
            kst = [A("s_kst%d" % k, [64, 4, 128], BF16) for k in range(2)]
            vcs = [A("s_vcs%d" % k, [64, 4, 128], BF16) for k in range(2)]
            vst = [A("s_vst%d" % k, [128, 256], BF16) for k in range(2)]
            pe2 = A("s_pe2", [64, 2, 128])
            HT = A("s_HT", [128, 2, 160], BF16)
            idx = A("s_idx", [128, 256], I32)
            iot = A("s_iot", [128, 1])
            with ExitStack() as ph1:
                ptb = ph1.enter_context(nc.sbuf_tensor(self.un("s_ptb"), [128, 256], I32))
                ptf = ph1.enter_context(nc.sbuf_tensor(self.un("s_ptf"), [128, 256], F32))
                S.dma("sp", lambda e: e.dma_start(out=ptb[:], in_=self.d_pt.partition_broadcast(128)), writes=[B("ptb")])
                S.dma("sp", lambda e: e.dma_start(out=iot[:], in_=self.d_iota), writes=[B("iot")])
                S.op("dve", lambda e: e.tensor_copy(out=ptf[:], in_=ptb[:]), reads=[B("ptb")], writes=[B("ptf")])
                S.op("dve", lambda e: e.tensor_scalar(out=ptf[:], in0=ptf[:], scalar1=128.0, scalar2=iot[:, 0:1], op0=ALU.mult, op1=ALU.add),
                     reads=[B("ptf"), B("iot")], writes=[B("ptf")])
                S.op("dve", lambda e: e.tensor_copy(out=idx[:], in_=ptf[:]), reads=[B("ptf")], writes=[B("idx")])
                S.emit()
            S.dma("sp", lambda e: e.dma_start(out=pe2[:], in_=self.d_pe2.rearrange("d (x t) -> d x t", x=2)), writes=[B("pe2")])
            for X in range(2):
                S.dma("pool", lambda e, X=X: e.dma_start(out=w2[:, X, :, :], in_=self.d_w2[X].rearrange("(c p) d -> p c d", p=128)), writes=[B("w2")])
            S.op("pool", lambda e: e.memset(HT[:], 0.0), writes=[B("HT")])
            S.dma("sp", lambda e: e.dma_start(out=self.o_swinp, in_=self.d_swin[:, 8:512, :]))

            def compress(X, sq_):
                r0 = 32 * (sq_ % 2)
                kv4 = KcT[:].rearrange("d k (n i) -> d k n i", i=64)
                for jc in range(2):
                    for ii in range(64):
                        S.op("pe", lambda e, jc=jc, ii=ii, kv4=kv4: e.matmul(
                            ps[:, 5, 0:128].rearrange("p (k n) -> p k n", k=4), lhsT=w1[:, ii, jc * 128:(jc + 1) * 128], rhs=kv4[:, :, :, ii],
                            start=(ii == 0), stop=(ii == 63)), reads=[B("w1"), B("KcT")], writes=[B("ps", 5)])
                    S.op("act", lambda e, jc=jc: e.activation(out=HT[:, jc, 32:160], in_=ps[:, 5, 0:128], func=AF.Silu), reads=[B("ps", 5)], writes=[B("HT")])
                if X == 0:
                    for jc in range(2):
                        S.op("pe", lambda e, jc=jc: e.matmul(ps[0:64, 6, 0:128], lhsT=w2[:, 0, jc, :], rhs=HT[:, jc, 32:160], start=(jc == 0), stop=(jc == 1)),
                             reads=[B("w2"), B("HT")], writes=[B("ps", 6)])
                    S.op("act", lambda e: e.activation(out=self.CK_all[:, :, sq_ * 32:(sq_ + 1) * 32], in_=ps[0:64, 6, 0:128].rearrange("p (k n) -> p k n", k=4), func=AF.Copy),
                         reads=[B("ps", 6)], writes=[B("CKall")])
                else:
                    for kvh in range(4):
                        for jc in range(2):
                            if r0 == 0:
                                oap, lap = ps[0:32, 7, kvh * 64:(kvh + 1) * 64], HT[:, jc, 32 + kvh * 32:64 + kvh * 32]
                            else:
                                oap, lap = ps[0:64, 7, kvh * 64:(kvh + 1) * 64], HT[:, jc, kvh * 32:kvh * 32 + 64]
                            S.op("pe", lambda e, jc=jc, oap=oap, lap=lap: e.matmul(oap, lhsT=lap, rhs=w2[:, 1, jc, :], start=(jc == 0), stop=(jc == 1)),
                                 reads=[B("w2"), B("HT")], writes=[B("ps", 7)])
                    S.op("act", lambda e: e.activation(out=self.CV_all[r0:r0 + 32, sq_ // 2, :, :], in_=ps[r0:r0 + 32, 7, 0:256].rearrange("p (k d) -> p k d", k=4), func=AF.Copy),
                         reads=[B("ps", 7)], writes=[B("CVall")])

            S.dma("pool", lambda e: e.dma_start(out=w1[:], in_=self.d_w1[0].rearrange("(i d) j -> d i j", d=64)), writes=[B("w1")])
            n = 0
            for sq_ in range(16):
                for pg in range(16):
                    pb = n % 4
                    kb = n % 2
                    n += 1
                    pt_ = pgt[pb]
                    col = sq_ * 16 + pg
                    S.dma("pool", lambda e, pt_=pt_, col=col: e.indirect_dma_start(out=pt_[:], out_offset=None, in_=self.d_cache,
                          in_offset=bass.IndirectOffsetOnAxis(ap=idx[:, col:col + 1], axis=0)), reads=[B("idx")], writes=[B("pg", pb)])
                    for X in range(3):
                        bank = X % 2
                        for kvh in range(4):
                            S.op("pe", lambda e, X=X, kvh=kvh, bank=bank, pt_=pt_: e.transpose(out=ps[0:64, bank, kvh * 128:(kvh + 1) * 128],
                                                                                              in_=pt_[:, X * 256 + kvh * 64:X * 256 + kvh * 64 + 64], identity=self.ident[:]),
                                 reads=[B("pg", pb), B("ident")], writes=[B("ps", bank)])
                        src3 = ps[0:64, bank, 0:512].rearrange("p (k t) -> p k t", k=4)
                        if X == 0:
                            S.op("dve", lambda e, src3=src3, pg=pg: e.tensor_tensor(out=KcT[:, :, pg * 128:(pg + 1) * 128], in0=src3,
                                                                                    in1=bc(pe2[:, 0, :].unsqueeze(1), [64, 4, 128]), op=ALU.add),
                                 reads=[B("ps", bank), B("pe2")], writes=[B("KcT")])
                        elif X == 1:
                            S.op("dve", lambda e, src3=src3, kb=kb: e.tensor_tensor(out=vcs[kb][:], in0=src3, in1=bc(pe2[:, 1, :].unsqueeze(1), [64, 4, 128]), op=ALU.add),
                                 reads=[B("ps", bank), B("pe2")], writes=[B("vcs", kb)])
                        else:
                            S.op("act", lambda e, src3=src3, kb=kb: e.activation(out=kst[kb][:], in_=src3, func=AF.Copy), reads=[B("ps", bank)], writes=[B("kst", kb)])
                    S.op("pool", lambda e, pt_=pt_, kb=kb: e.tensor_copy(out=vst[kb][:], in_=pt_[:, 768:1024]), reads=[B("pg", pb)], writes=[B("vst", kb)])
                    S.dma("sp", lambda e, kb=kb, sq_=sq_, pg=pg: e.dma_start(out=self.sc_sKs[sq_, :, :, pg * 128:(pg + 1) * 128], in_=kst[kb][:]), reads=[B("kst", kb)], writes=[B("scsK", sq_)])
                    S.dma("sp", lambda e, kb=kb, sq_=sq_, pg=pg: e.dma_start(out=self.sc_sVc[sq_, :, :, pg * 128:(pg + 1) * 128], in_=vcs[kb][:]), reads=[B("vcs", kb)], writes=[B("scsC", sq_)])
                    S.dma("sp", lambda e, kb=kb, sq_=sq_, pg=pg: e.dma_start(out=self.sc_sVs[sq_, pg], in_=vst[kb][:]), reads=[B("vst", kb)], writes=[B("scsV", sq_)])
                compress(0, sq_)
            S.dma("pool", lambda e: e.dma_start(out=w1[:], in_=self.d_w1[1].rearrange("(i d) j -> d i j", d=64)), writes=[B("w1")])
            for sq_ in range(16):
                S.dma("sp", lambda e, sq_=sq_: e.dma_start(out=KcT[:], in_=self.sc_sVc[sq_]), reads=[B("scsC", sq_)], writes=[B("KcT")])
                compress(1, sq_)
            S.emit()

    def nsa_sample_attn(self, i):
        S, B, nc, ps = self.S, self.B, self.nc, self.ps
        BIG = 240000.0
        l0 = NPT
        with ExitStack() as ph0:
            A0 = lambda n, s, d=F32: ph0.enter_context(nc.sbuf_tensor(self.un(n), list(s), d))
            QA = A0("b_QA", [128, 4, 4, 128], BF16)
            KnA = A0("b_KnA", [128, 4, 128], BF16)
            KwN = A0("b_KwN", [64, 4, 128], BF16)
            Vns = A0("b_Vns", [128, 4, 65], BF16)
            Vnw = A0("b_Vnw", [128, 4, 65], BF16)
            gates = A0("b_gates", [128, 48])
            self.rtmp = A0("b_rtmp", [128, 128])
            self.coefs(i, 0)
            with ExitStack() as ph:
                A = lambda n, s, d=F32: ph.enter_context(nc.sbuf_tensor(self.un(n), list(s), d))
                w_kv = A("b_wkv", [128, KC, 1536], BF16)
                w_q = A("b_wq", [128, KC, 1072], BF16)
                hT = A("b_hT", [128, KC, 128], BF16)
                rows = A("b_rows", [128, 1536])
                tmp = [A("b_t%d" % k, [128, 128]) for k in range(2)]
                sq = [A("b_sq%d" % k, [128, 128], BF16) for k in range(2)]
                rs = A("b_rs", [128, 128])
                for hh in range(3):
                    S.dma("pool", lambda e, hh=hh: e.dma_start(out=w_kv[:, :, hh * 512:(hh + 1) * 512],
                                                               in_=self.d_nsa_w_in[:, 1024 + hh * 512:1024 + (hh + 1) * 512].rearrange("(c p) n -> p c n", p=128)), writes=[B("wkv")])
                S.dma("pool", lambda e: e.dma_start(out=w_q[:, :, 0:1024], in_=self.d_nsa_w_in[:, 0:1024].rearrange("(c p) n -> p c n", p=128)), writes=[B("wq")])
                S.dma("pool", lambda e: e.dma_start(out=w_q[:, :, 1024:1072], in_=self.d_nsa_w_in[:, 2560:2608].rearrange("(c p) n -> p c n", p=128)), writes=[B("wq")])
                for kvh in range(4):
                    S.dma("pool", lambda e, kvh=kvh: e.dma_start(out=KnA[64:128, kvh, :], in_=self.d_ohnew), writes=[B("KnA")])
                S.op("pool", lambda e: e.memset(Vns[:], 1.0), writes=[B("Vns")])
                S.op("pool", lambda e: e.memset(Vnw[:], 1.0), writes=[B("Vnw")])
                self.norm_mod(i, 0, l0, 128, "s", hT, 0, tmp, sq, rs, "bh")
                hrd = [B("bh", k, 0) for k in range(KC)]
                for gq in range(3):
                    for k in range(KC):
                        S.op("pe", lambda e, gq=gq, k=k: e.matmul(ps[:, gq, 0:512], lhsT=hT[:, k, :], rhs=w_kv[:, k, gq * 512:(gq + 1) * 512],
                                                                 start=(k == 0), stop=(k == KC - 1)), reads=hrd + [B("wkv")], writes=[B("ps", gq)])
                    S.op("act", lambda e, gq=gq: e.activation(out=rows[:, gq * 512:(gq + 1) * 512], in_=ps[:, gq, 0:512], func=AF.Copy),
                         reads=[B("ps", gq)], writes=[B("rows", gq)])
                S.dma("sp", lambda e: e.dma_start(out=self.o_rows[2048:2176, :], in_=rows[:, 0:1024]), reads=[B("rows", 0), B("rows", 1)])
                S.dma("sp", lambda e: e.dma_start(out=self.o_win[512:640, :], in_=rows[:, 1024:1536]), reads=[B("rows", 2)])
                S.op("dve", lambda e: e.tensor_copy(out=Vns[:, :, 0:64], in_=rows[:, 768:1024].rearrange("p (k d) -> p k d", k=4)), reads=[B("rows", 1)], writes=[B("Vns")])
                S.op("dve", lambda e: e.tensor_copy(out=Vnw[:, :, 0:64], in_=rows[:, 1280:1536].rearrange("p (k d) -> p k d", k=4)), reads=[B("rows", 2)], writes=[B("Vnw")])
                for gi, off in enumerate((512, 1024)):
                    bank = 3 + gi
                    for kvh in range(4):
                        for k in range(KC):
                            S.op("pe", lambda e, kvh=kvh, k=k, bank=bank, off=off: e.matmul(
                                ps[0:64, bank, kvh * 128:(kvh + 1) * 128], lhsT=w_kv[:, k, off + kvh * 64:off + kvh * 64 + 64], rhs=hT[:, k, :],
                                start=(k == 0), stop=(k == KC - 1)), reads=hrd + [B("wkv")], writes=[B("ps", bank)])
                    src3 = ps[0:64, bank, 0:512].rearrange("p (k t) -> p k t", k=4)
                    dst = KnA[0:64, :, :] if gi == 0 else KwN[:]
                    S.op("act", lambda e, src3=src3, dst=dst: e.activation(out=dst, in_=src3, func=AF.Copy), reads=[B("ps", bank)], writes=[B("KnA") if gi == 0 else B("KwN")])
                for k in range(KC):
                    S.op("pe", lambda e, k=k: e.matmul(ps[:, 6, 0:48], lhsT=hT[:, k, :], rhs=w_q[:, k, 1024:1072], start=(k == 0), stop=(k == KC - 1)),
                         reads=hrd + [B("wq")], writes=[B("ps", 6)])
                S.op("act", lambda e: e.activation(out=gates[:], in_=ps[:, 6, 0:48], func=AF.Sigmoid), reads=[B("ps", 6)], writes=[B("gates")])
                for kvh in range(4):
                    bank = 5 + kvh % 2
                    for g in range(4):
                        hd = kvh * 4 + g
                        for k in range(KC):
                            S.op("pe", lambda e, g=g, k=k, hd=hd, bank=bank: e.matmul(ps[0:64, bank, g * 128:(g + 1) * 128], lhsT=w_q[:, k, hd * 64:(hd + 1) * 64], rhs=hT[:, k, :],
                                                                                     start=(k == 0), stop=(k == KC - 1)), reads=hrd + [B("wq")], writes=[B("ps", bank)])
                    S.op("act", lambda e, kvh=kvh, bank=bank: e.activation(out=QA[0:64, kvh, :, :], in_=ps[0:64, bank, 0:512].rearrange("p (g t) -> p g t", g=4), func=AF.Copy),
                         reads=[B("ps", bank)], writes=[B("QAs", kvh)])
                S.emit()
            with ExitStack() as ph:
                A = lambda n, s, d=F32: ph.enter_context(nc.sbuf_tensor(self.un(n), list(s), d))
                w_o = A("b_wo", [128, KC, D], BF16)
                KsA = A("b_KsA", [128, 4, 2048], BF16)
                Vs = A("b_Vs", [128, 16, 4, 65], BF16)
                KwS = A("b_KwS", [64, 4, 512], BF16)
                VwS = A("b_VwS", [128, 4, 4, 65], BF16)
                wst = A("b_wst", [128, 4, 512])
                e4s = A("b_e4s", [128, 4, 512])
                psg = A("b_psg", [128, 512])
                pTs = A("b_pTs", [64, 8, 128], BF16)
                pad = [A("b_pad%d" % k, [128, 4, 128], BF16) for k in range(2)]
                PTn = A("b_PTn", [128, 512], BF16)
                mnew = A("b_mnew", [128, 128], BF16)
                tri = A("b_tri", [128, 2, 128], BF16)
                vbs = A("b_vbs", [128, 512])
                mk = A("b_mk", [128, 4, 64])
                sm = A("b_sm", [128, 8])
                imp = A("b_imp", [128, 64])
                imp2 = A("b_imp2", [128, 64])
                wk = A("b_wk", [128, 64])
                m8 = A("b_m8", [128, 16])
                mbs = A("b_mbs", [128, 128])
                ocmp = A("b_ocmp", [128, 4, 4, 64])
                oacc = A("b_oacc", [128, 2, 4, 260])
                otot = A("b_otot", [128, 4, 4, 64])
                ototT = A("b_ototT", [128, KC, 128], BF16)
                ctmp = A("b_ctmp", [128, 4, 64])
                cf = A("b_cf", [128, 12])
                S.dma("pool", lambda e: e.dma_start(out=w_o[:], in_=self.d_nsa_w_out.rearrange("(c p) n -> p c n", p=128)), writes=[B("wo")])
                for kvh in range(4):
                    S.dma("pool", lambda e, kvh=kvh: e.dma_start(out=KsA[64:128, kvh, :], in_=self.d_onehot[:, 0:2048]), writes=[B("KsAs")])
                S.dma("pool", lambda e: e.dma_start(out=mnew[:], in_=self.d_mnew), writes=[B("mnew")])
                S.dma("pool", lambda e: e.dma_start(out=tri[:], in_=self.d_tri.rearrange("p (a t) -> p a t", a=2)), writes=[B("tri")])
                S.dma("sp", lambda e: e.dma_start(out=vbs[:], in_=self.d_vbs), writes=[B("vbs")])
                S.dma("sp", lambda e: e.dma_start(out=mk[:], in_=self.d_masks[17].rearrange("p (a n) -> p a n", a=4)), writes=[B("mk")])
                S.op("pool", lambda e: e.memset(Vs[:], 1.0), writes=[B("Vss")])
                S.op("pool", lambda e: e.memset(VwS[:], 1.0), writes=[B("VwS")])
                S.op("pool", lambda e: e.memset(mbs[:], 0.0), writes=[B("mbs")])
                S.op("pool", lambda e: e.memset(imp[:], 0.0), writes=[B("imp")])
                S.op("pool", lambda e: e.memset(oacc[:], 0.0), writes=[B("oacc")])
                for kvh in range(4):
                    bqa = B("QAs", kvh)
                    for g in range(4):
                        S.op("pe", lambda e, g=g, kvh=kvh: e.matmul(ps[:, g, 0:512], lhsT=QA[0:64, kvh, g, :], rhs=self.CK_all[:, kvh, :], start=True, stop=True),
                             reads=[bqa, B("CKall")], writes=[B("ps", g)])
                        S.op("dve", lambda e, g=g: e.scalar_tensor_tensor(out=e4s[:, g, :], in0=ps[:, g, 0:512], scalar=0.125, in1=vbs[:], op0=ALU.mult, op1=ALU.add),
                             reads=[B("ps", g), B("vbs")], writes=[B("e4s")])
                    S.op("act", lambda e: e.activation(out=e4s[:], in_=e4s[:], func=AF.Exp), reads=[B("e4s")], writes=[B("e4s")])
                    S.op("dve", lambda e: e.tensor_reduce(out=sm[:, 0:4], in_=e4s[:], axis=AX.X, op=ALU.add), reads=[B("e4s")], writes=[B("sm")])
                    S.op("dve", lambda e: e.tensor_scalar(out=sm[:, 0:4], in0=sm[:, 0:4], scalar1=1e-30, scalar2=None, op0=ALU.add), reads=[B("sm")], writes=[B("sm")])
                    S.op("dve", lambda e: e.reciprocal(out=sm[:, 0:4], in_=sm[:, 0:4]), reads=[B("sm")], writes=[B("sm")])
                    S.op("dve", lambda e: e.tensor_tensor(out=e4s[:], in0=e4s[:], in1=bc(sm[:, 0:4].unsqueeze(2), [128, 4, 512]), op=ALU.mult),
                         reads=[B("sm"), B("e4s")], writes=[B("e4s")])
                    S.op("dve", lambda e: e.tensor_tensor(out=psg[:], in0=e4s[:, 0, :], in1=e4s[:, 1, :], op=ALU.add), reads=[B("e4s")], writes=[B("psg")])
                    for g in (2, 3):
                        S.op("dve", lambda e, g=g: e.tensor_tensor(out=psg[:], in0=psg[:], in1=e4s[:, g, :], op=ALU.add), reads=[B("e4s"), B("psg")], writes=[B("psg")])
                    S.op("dve", lambda e: e.tensor_reduce(out=imp[:, 0:32], in_=psg[:].rearrange("p (s n) -> p n s", n=32), axis=AX.X, op=ALU.add), reads=[B("psg")], writes=[B("imp")])
                    S.op("dve", lambda e: e.tensor_tensor(out=imp2[:], in0=imp[:], in1=mk[:, 1, :], op=ALU.mult), reads=[B("imp"), B("mk")], writes=[B("imp2")])
                    S.op("dve", lambda e: e.tensor_tensor(out=imp2[:], in0=imp2[:], in1=mk[:, 2, :], op=ALU.add), reads=[B("imp2"), B("mk")], writes=[B("imp2")])
                    S.op("dve", lambda e: e.max(out=m8[:, 0:8], in_=imp2[:]), reads=[B("imp2")], writes=[B("m8")])
                    S.op("dve", lambda e: e.match_replace(out=wk[:], in_to_replace=m8[:, 0:8], in_values=imp2[:], imm_value=-3e30), reads=[B("imp2"), B("m8")], writes=[B("wk")])
                    S.op("dve", lambda e: e.max(out=m8[:, 8:16], in_=wk[:]), reads=[B("wk")], writes=[B("m8")])
                    S.op("dve", lambda e: e.match_replace(out=wk[:], in_to_replace=m8[:, 8:16], in_values=wk[:], imm_value=-3e30), reads=[B("wk"), B("m8")], writes=[B("wk")])
                    S.op("dve", lambda e: e.tensor_tensor(out=imp2[:], in0=imp2[:], in1=wk[:], op=ALU.subtract), reads=[B("wk"), B("imp2")], writes=[B("imp2")])
                    S.op("dve", lambda e: e.tensor_scalar(out=imp2[:], in0=imp2[:], scalar1=1.0, scalar2=None, op0=ALU.min), reads=[B("imp2")], writes=[B("imp2")])
                    S.op("dve", lambda e: e.tensor_tensor(out=imp2[:], in0=imp2[:], in1=mk[:, 3, :], op=ALU.mult), reads=[B("imp2"), B("mk")], writes=[B("imp2")])
                    S.op("dve", lambda e: e.tensor_scalar(out=mbs[:, 64:128], in0=imp2[:], scalar1=-1.0, scalar2=BIG, op0=ALU.add, op1=ALU.mult),
                         reads=[B("imp2")], writes=[B("mbs")])
                    S.op("pe", lambda e: e.transpose(out=ps[:, 4, 0:128], in_=mbs[:], identity=self.ident[:]), reads=[B("mbs"), B("ident")], writes=[B("ps", 4)])
                    S.op("act", lambda e, kvh=kvh: e.activation(out=QA[64:128, kvh, :, :], in_=bc(ps[64:128, 4, 0:128].unsqueeze(1), [64, 4, 128]), func=AF.Copy),
                         reads=[B("ps", 4)], writes=[bqa])
                    for g in range(4):
                        for hh in range(2):
                            for c4 in range(4):
                                ch = hh * 4 + c4
                                S.op("pe", lambda e, g=g, ch=ch, c4=c4, hh=hh: e.transpose(out=ps[0:64, 6 + hh, c4 * 128:(c4 + 1) * 128], in_=e4s[:, g, ch * 64:(ch + 1) * 64], identity=self.ident[:]),
                                     reads=[B("e4s"), B("ident")], writes=[B("ps", 6 + hh)])
                            S.op("act", lambda e, hh=hh: e.activation(out=pTs[:, hh * 4:(hh + 1) * 4, :], in_=ps[0:64, 6 + hh, 0:512].rearrange("p (c t) -> p c t", c=4), func=AF.Copy),
                                 reads=[B("ps", 6 + hh)], writes=[B("pTs")])
                        for ch in range(8):
                            S.op("pe", lambda e, g=g, ch=ch, kvh=kvh: e.matmul(ps[:, 5, g * 64:(g + 1) * 64], lhsT=pTs[:, ch, :], rhs=self.CV_all[:, ch, kvh, :], start=(ch == 0), stop=(ch == 7)),
                                 reads=[B("pTs"), B("CVall")], writes=[B("ps", 5)])
                    S.op("act", lambda e, kvh=kvh: e.activation(out=ocmp[:, kvh, :, :], in_=ps[:, 5, 0:256].rearrange("p (g d) -> p g d", g=4), func=AF.Copy), reads=[B("ps", 5)], writes=[B("ocmp")])
                ti = 0
                for kvh in range(4):
                    bqa = B("QAs", kvh)
                    for br in range(2):
                        sb_ = ti % 2
                        ti += 1
                        if br == 0:
                            lhs, rhs = KnA[:, kvh, :], QA[:, kvh, :, :].rearrange("p g t -> p (g t)")
                            vv = Vns
                        else:
                            lhs, rhs = KwN[:, kvh, :], QA[0:64, kvh, :, :].rearrange("p g t -> p (g t)")
                            vv = Vnw
                        S.op("pe", lambda e, lhs=lhs, rhs=rhs, sb_=sb_: e.matmul(ps[:, sb_, 0:512], lhsT=lhs, rhs=rhs, start=True, stop=True),
                             reads=[B("KnA"), B("KwN"), bqa], writes=[B("ps", sb_)])
                        S.op("act", lambda e, sb_=sb_: e.activation(out=PTn[:], in_=ps[:, sb_, 0:512], func=AF.Exp, scale=0.125), reads=[B("ps", sb_)], writes=[B("PTn")])
                        S.op("dve", lambda e: e.tensor_tensor(out=PTn[:].rearrange("p (g t) -> p g t", g=4), in0=PTn[:].rearrange("p (g t) -> p g t", g=4),
                                                              in1=bc(mnew[:].unsqueeze(1), [128, 4, 128]), op=ALU.mult), reads=[B("PTn"), B("mnew")], writes=[B("PTn")])
                        for g in range(4):
                            S.op("pe", lambda e, g=g, vv=vv, kvh=kvh, br=br: e.matmul(ps[:, 2 + br, g * 65:(g + 1) * 65], lhsT=PTn[:, g * 128:(g + 1) * 128], rhs=vv[:, kvh, :], start=True, stop=True),
                                 reads=[B("PTn"), B("Vns"), B("Vnw")], writes=[B("ps", 2 + br)])
                        S.op("dve", lambda e, kvh=kvh, br=br: e.tensor_tensor(out=oacc[:, br, kvh, :], in0=oacc[:, br, kvh, :], in1=ps[:, 2 + br, 0:260], op=ALU.add),
                             reads=[B("ps", 2 + br), B("oacc")], writes=[B("oacc")])
                for sq_ in range(16):
                    S.dma("sp", lambda e, sq_=sq_: e.dma_start(out=KsA[0:64, :, :], in_=self.sc_sKs[sq_]), reads=[B("scsK", sq_)], writes=[B("KsAs")])
                    for pg in range(16):
                        S.dma("sp", lambda e, sq_=sq_, pg=pg: e.dma_start(out=Vs[:, pg, :, 0:64], in_=self.sc_sVs[sq_, pg].rearrange("p (k d) -> p k d", k=4)), reads=[B("scsV", sq_)], writes=[B("Vss")])
                    S.dma("sp", lambda e, sq_=sq_: e.dma_start(out=wst[:], in_=self.d_swin[sq_].rearrange("(t p) c -> p t c", p=128)), writes=[B("wst")])
                    for t in range(4):
                        for kvh in range(4):
                            S.op("pe", lambda e, t=t, kvh=kvh: e.transpose(out=ps[0:64, 4, kvh * 128:(kvh + 1) * 128], in_=wst[:, t, kvh * 64:(kvh + 1) * 64], identity=self.ident[:]),
                                 reads=[B("wst"), B("ident")], writes=[B("ps", 4)])
                        S.op("act", lambda e, t=t: e.activation(out=KwS[:, :, t * 128:(t + 1) * 128], in_=ps[0:64, 4, 0:512].rearrange("p (k t) -> p k t", k=4), func=AF.Copy),
                             reads=[B("ps", 4)], writes=[B("KwS")])
                        S.op("pool", lambda e, t=t: e.tensor_copy(out=VwS[:, t, :, 0:64], in_=wst[:, t, 256:512].rearrange("p (k d) -> p k d", k=4)), reads=[B("wst")], writes=[B("VwS")])
                    for k_ in range(2):
                        S.op("pool", lambda e, k_=k_: e.memset(pad[k_][:], 0.0), writes=[B("pad", k_)])
                    qs = slice(sq_ * 8, (sq_ + 1) * 8)
                    tl = []
                    for kvh in range(4):
                        for br in range(2):
                            nkt = 16 if br == 0 else 4
                            for kt in range(nkt):
                                tl.append((kvh, br, kt, kt == 0, kt == nkt - 1))

                    def emit_S(n):
                        kvh, br, kt, first, last = tl[n]
                        sb_ = n % 2
                        if br == 0:
                            lhs, rhs, rdk = KsA[:, kvh, kt * 128:(kt + 1) * 128], QA[:, kvh, :, qs], B("KsAs")
                        else:
                            lhs, rhs, rdk = KwS[:, kvh, kt * 128:(kt + 1) * 128], QA[0:64, kvh, :, qs], B("KwS")
                        S.op("pe", lambda e, lhs=lhs, rhs=rhs, sb_=sb_: e.matmul(ps[:, sb_, 0:32].rearrange("p (g t) -> p g t", g=4), lhsT=lhs, rhs=rhs, start=True, stop=True),
                             reads=[rdk, B("QAs", kvh)], writes=[B("ps", sb_)])

                    def emit_rest(n):
                        kvh, br, kt, first, last = tl[n]
                        sb_ = n % 2
                        pd = pad[sb_]
                        if br == 0:
                            vv, rdv = Vs[:, kt, kvh, :], B("Vss")
                        else:
                            vv, rdv = VwS[:, kt, kvh, :], B("VwS")
                        S.op("act", lambda e, sb_=sb_, pd=pd: e.activation(out=pd[:, :, qs], in_=ps[:, sb_, 0:32].rearrange("p (g t) -> p g t", g=4), func=AF.Exp, scale=0.125),
                             reads=[B("ps", sb_)], writes=[B("pad", sb_)])
                        if br == 1 and kt == 0:
                            S.op("dve", lambda e, pd=pd: e.tensor_tensor(out=pd[:, :, qs], in0=pd[:, :, qs], in1=bc(tri[:, 1, 0:8].unsqueeze(1), [128, 4, 8]), op=ALU.mult),
                                 reads=[B("pad", sb_), B("tri")], writes=[B("pad", sb_)])
                        for g in range(4):
                            S.op("pe", lambda e, g=g, pd=pd, vv=vv, br=br, first=first, last=last: e.matmul(ps[:, 2 + br, g * 65:(g + 1) * 65], lhsT=pd[:, g, :], rhs=vv,
                                                                                                             start=first, stop=last), reads=[B("pad", sb_), rdv], writes=[B("ps", 2 + br)])
                        if last:
                            S.op("dve", lambda e, kvh=kvh, br=br: e.tensor_tensor(out=oacc[:, br, kvh, :], in0=oacc[:, br, kvh, :], in1=ps[:, 2 + br, 0:260], op=ALU.add),
                                 reads=[B("ps", 2 + br), B("oacc")], writes=[B("oacc")])

                    emit_S(0)
                    for n in range(len(tl)):
                        if n + 1 < len(tl):
                            emit_S(n + 1)
                        emit_rest(n)
                for kvh in range(4):
                    osel = oacc[:, 0, kvh, :].rearrange("p (g d) -> p g d", g=4)
                    owin = oacc[:, 1, kvh, :].rearrange("p (g d) -> p g d", g=4)
                    S.op("dve", lambda e, osel=osel: e.tensor_scalar(out=cf[:, 0:4], in0=osel[:, :, 64], scalar1=1e-30, scalar2=None, op0=ALU.add), reads=[B("oacc")], writes=[B("cf")])
                    S.op("dve", lambda e, owin=owin: e.tensor_scalar(out=cf[:, 4:8], in0=owin[:, :, 64], scalar1=1e-30, scalar2=None, op0=ALU.add), reads=[B("oacc")], writes=[B("cf")])
                    S.op("dve", lambda e: e.reciprocal(out=cf[:, 0:8], in_=cf[:, 0:8]), reads=[B("cf")], writes=[B("cf")])
                    S.op("dve", lambda e, kvh=kvh: e.tensor_tensor(out=cf[:, 0:4], in0=cf[:, 0:4], in1=gates[:, 16 + kvh * 4:20 + kvh * 4], op=ALU.mult), reads=[B("cf"), B("gates")], writes=[B("cf")])
                    S.op("dve", lambda e, kvh=kvh: e.tensor_tensor(out=cf[:, 4:8], in0=cf[:, 4:8], in1=gates[:, 32 + kvh * 4:36 + kvh * 4], op=ALU.mult), reads=[B("cf"), B("gates")], writes=[B("cf")])
                    ot = otot[:, kvh, :, :]
                    S.op("dve", lambda e, ot=ot, kvh=kvh: e.tensor_tensor(out=ot, in0=ocmp[:, kvh, :, :], in1=bc(gates[:, kvh * 4:kvh * 4 + 4].unsqueeze(2), [128, 4, 64]), op=ALU.mult),
                         reads=[B("ocmp"), B("gates")], writes=[B("otot")])
                    for (src, c0) in ((osel, 0), (owin, 4)):
                        S.op("dve", lambda e, src=src, c0=c0: e.tensor_tensor(out=ctmp[:], in0=src[:, :, 0:64], in1=bc(cf[:, c0:c0 + 4].unsqueeze(2), [128, 4, 64]), op=ALU.mult),
                             reads=[B("oacc"), B("cf")], writes=[B("ctmp")])
                        S.op("dve", lambda e, ot=ot: e.tensor_tensor(out=ot, in0=ot, in1=ctmp[:], op=ALU.add), reads=[B("ctmp"), B("otot")], writes=[B("otot")])
                of = otot[:].rearrange("p k g d -> p (k g d)")
                for hh in range(2):
                    for c4 in range(4):
                        c = hh * 4 + c4
                        S.op("pe", lambda e, c=c, c4=c4, hh=hh, of=of: e.transpose(out=ps[:, hh, c4 * 128:(c4 + 1) * 128], in_=of[:, c * 128:(c + 1) * 128], identity=self.ident[:]),
                             reads=[B("otot"), B("ident")], writes=[B("ps", hh)])
                    S.op("act", lambda e, hh=hh: e.activation(out=ototT[:, hh * 4:(hh + 1) * 4, :], in_=ps[:, hh, 0:512].rearrange("p (c t) -> p c t", c=4), func=AF.Copy),
                         reads=[B("ps", hh)], writes=[B("ototT")])
                for c in range(KC):
                    bank = 6 + (c % 2)
                    for k in range(KC):
                        S.op("pe", lambda e, c=c, k=k, bank=bank: e.matmul(ps[:, bank, 0:128], lhsT=w_o[:, k, c * 128:(c + 1) * 128], rhs=ototT[:, k, :], start=(k == 0), stop=(k == KC - 1)),
                             reads=[B("wo"), B("ototT")], writes=[B("ps", bank)])
                    self.resid(i, 0, c, l0, 128, "s", ps[:, bank, 0:128], [B("ps", bank)])
                S.emit()

    def pool(self, i, tiles):
        S, B, nc, ps = self.S, self.B, self.nc, self.ps
        with ExitStack() as ph:
            A = lambda n, s, d=F32: ph.enter_context(nc.sbuf_tensor(self.un(n), list(s), d))
            pw = A("p_w", [128, 4, 2, 256], BF16)
            hp = A("p_h", [128, KC, 15 + 512])
            hps = A("p_hs", [128, KC, 16, 23])
            sa = A("p_sa", [128, KC, 15 + 512])
            sb_ = A("p_sb", [128, KC, 15 + 512])
            dT = A("p_d", [128, KC, 512], BF16)
            icn = A("p_icn", [128, 4, 128])
            gl = A("p_gl", [128, KC])
            tmp = [A("p_t%d" % k, [128, 512]) for k in range(2)]
            sq = [A("p_sq%d" % k, [128, 512], BF16) for k in range(2)]
            rs = A("p_rs", [128, 512])
            hdummy = A("p_hd", [128, KC, 512], BF16)
            hs32 = A("p_hs32", [128, KC, 128])

            self.rtmp = A("p_rtmp", [128, 128])
            self.coefs(i, 0)
            S.dma("pool", lambda e: e.dma_start(out=pw[:], in_=self.d_pool_w.rearrange("g (k p) n -> p g k n", p=128)), writes=[B("pw")])
            S.dma("sp", lambda e: e.dma_start(out=icn[:], in_=self.d_invcnt.rearrange("p (g t) -> p g t", g=4)), writes=[B("icn")])
            for c in range(KC):
                S.dma("sp", lambda e, c=c: e.dma_start(out=hps[:, c, :, 0:15], in_=self.d_spoolT[c * 128:(c + 1) * 128, :].rearrange("p (s r) -> p s r", r=15)), writes=[B("hps")])
            for c in range(KC):
                S.dma("sp", lambda e, c=c: e.dma_start(out=self.o_poolT[c * 128:(c + 1) * 128, 144:256].rearrange("p (s r) -> p s r", r=7),
                                                       in_=hps[:, c, :, 8:15]), reads=[B("hps")])
            S.op("dve", lambda e: e.tensor_tensor(out=gl[:], in0=self.coef[:, 2, :], in1=self.vecs[:, V_PSC:V_PSC + 8], op=ALU.mult),
                 reads=[B("coef"), B("vecs")], writes=[B("gl")])
            S.op("pool", lambda e: e.memset(hp[:, :, 0:15], 0.0), writes=[B("h32", c) for c in range(KC)])
            wins = (2, 4, 8, 16)
            for (t0, w, kind) in tiles:
                h32w = [B("h32", c) for c in range(KC)]
                if kind == "p":
                    self.norm_mod(i, 0, t0, w, kind, hdummy, 0, tmp, sq, rs, "phd", fp32_out=hp[:, :, 15:15 + 512])
                    if t0 == 0 and self.halo:
                        S.op("dve", lambda e: e.tensor_scalar(out=hp[:, :, 15:143], in0=hp[:, :, 15:143], scalar1=self.flag[:, 0:1], scalar2=None, op0=ALU.mult),
                             reads=h32w + [B("flag")], writes=h32w)
                    if t0 + w == NPT and self.halo:
                        S.dma("sp", lambda e, w=w: e.dma_start(out=self.o_poolT[:, 0:16].rearrange("(c p) t -> p c t", p=128), in_=hp[:, :, 15 + w - 16:15 + w]), reads=h32w)
                    H = lambda a, b: hp[:, :, a:b]
                    L = 15 + w
                    cur = hp
                    stages = {}
                    src = hp
                    for si, sh in enumerate((1, 2, 4, 8)):
                        dst = sa if si % 2 == 0 else sb_
                        c0 = 2 * si
                        S.op("dve" if si % 2 == 0 else "pool", lambda e, src=src, dst=dst, sh=sh, c0=c0, L=L: e.tensor_tensor(
                            out=dst[:, c0:KC, sh:L], in0=src[:, c0:KC, sh:L], in1=src[:, c0:KC, 0:L - sh], op=ALU.add),
                            reads=h32w + [B("psa"), B("psb")], writes=[B("psa") if si % 2 == 0 else B("psb")])
                        src = dst
                        stages[si] = dst
                    for g in range(4):
                        stg = stages[g]
                        for cc in (2 * g, 2 * g + 1):
                            S.op("dve", lambda e, stg=stg, cc=cc, g=g, w=w: e.tensor_scalar(out=stg[:, cc, 15:15 + w], in0=stg[:, cc, 15:15 + w], scalar1=1.0 / wins[g],
                                                                                            scalar2=None, op0=ALU.mult), reads=[B("psa"), B("psb")], writes=[B("psa"), B("psb")])
                            if t0 == 0:
                                S.op("dve", lambda e, stg=stg, cc=cc, g=g: e.tensor_tensor(out=stg[:, cc, 143:271], in0=stg[:, cc, 143:271], in1=icn[:, g, :], op=ALU.mult),
                                     reads=[B("psa"), B("psb"), B("icn")], writes=[B("psa"), B("psb")])
                            S.op("dve", lambda e, stg=stg, cc=cc, w=w: e.tensor_tensor(out=dT[:, cc, 0:w], in0=stg[:, cc, 15:15 + w], in1=hp[:, cc, 15:15 + w], op=ALU.subtract),
                                 reads=[B("psa"), B("psb")] + h32w, writes=[B("pd", cc)])
                    S.op("pool", lambda e, w=w: e.tensor_copy(out=hp[:, :, 0:15], in_=hp[:, :, w:w + 15]), reads=h32w + [B("psa"), B("psb")], writes=h32w)
                else:
                    hview = hps[:, :, :, 15:23]
                    self.norm_mod(i, 0, t0, w, kind, hdummy, 0, tmp, sq, rs, "phd", fp32_out=hs32)
                    for c in range(KC):
                        S.op("pool", lambda e, c=c: e.tensor_copy(out=hps[:, c, :, 15:23], in_=hs32[:, c, :].rearrange("p (s j) -> p s j", j=8)), reads=h32w, writes=[B("hps")])
                    for c in range(KC):
                        S.dma("sp", lambda e, c=c: e.dma_start(out=self.o_poolT[c * 128:(c + 1) * 128, 16:144].rearrange("p (s j) -> p s j", j=8), in_=hps[:, c, :, 15:23]), reads=[B("hps")])
                    for g in range(4):
                        wn = wins[g]
                        for cc in (2 * g, 2 * g + 1):
                            acc = sb_[:, cc, 0:128].rearrange("p (s j) -> p s j", j=8)
                            S.op("dve", lambda e, cc=cc, acc=acc: e.tensor_tensor(out=acc, in0=hps[:, cc, :, 15:23], in1=hps[:, cc, :, 14:22], op=ALU.add),
                                 reads=[B("hps")], writes=[B("psb")])
                            for r in range(2, wn):
                                S.op("dve", lambda e, cc=cc, acc=acc, r=r: e.tensor_tensor(out=acc, in0=acc, in1=hps[:, cc, :, 15 - r:23 - r], op=ALU.add),
                                     reads=[B("hps"), B("psb")], writes=[B("psb")])
                            S.op("dve", lambda e, cc=cc, acc=acc, wn=wn: e.scalar_tensor_tensor(
                                out=dT[:, cc, 0:128].rearrange("p (s j) -> p s j", j=8), in0=acc, scalar=1.0 / wn, in1=hps[:, cc, :, 15:23], op0=ALU.mult, op1=ALU.subtract),
                                reads=[B("hps"), B("psb")], writes=[B("pd", cc)])
                for g in range(4):
                    for o in range(2):
                        c = 2 * g + o
                        bank = c % 2
                        bp = B("ps", bank)
                        for k in range(2):
                            S.op("pe", lambda e, g=g, o=o, k=k, bank=bank, w=w: e.matmul(ps[:, bank, 0:w], lhsT=pw[:, g, k, o * 128:(o + 1) * 128], rhs=dT[:, 2 * g + k, 0:w],
                                                                                        start=(k == 0), stop=(k == 1)), reads=[B("pw"), B("pd", 2 * g + k)], writes=[bp])
                        if kind == "p":
                            self.resid(i, 0, c, t0, w, kind, ps[:, bank, 0:w], [bp, B("gl")], extra_scale=gl[:, c:c + 1])
                        else:
                            self.resid(i, 0, c, t0, w, kind, ps[:, bank, 0:w], [bp, B("vecs")], extra_scale=self.vecs[:, V_PSC + c:V_PSC + c + 1])
            S.emit()

    def final(self, tiles):
        S, B, nc, ps = self.S, self.B, self.nc, self.ps
        with ExitStack() as ph:
            A = lambda n, s, d=F32: ph.enter_context(nc.sbuf_tensor(self.un(n), list(s), d))
            sq = [A("y_sq%d" % k, [128, 512], BF16) for k in range(2)]
            rs = A("y_rs", [128, 512])
            yo = [A("y_o%d" % k, [128, 512]) for k in range(2)]
            for (t0, w, kind) in tiles:
                bp = B("ps", 7)
                for c in range(KC):
                    s_ = sq[c % 2]
                    S.op("act", lambda e, c=c, s_=s_: e.activation(out=s_[:, 0:w], in_=self.xT[:, c, t0:t0 + w], func=AF.Square),
                         reads=self.xb(c, t0, w), writes=[B("sq", c % 2)])
                    S.op("pe", lambda e, c=c, s_=s_: e.matmul(ps[:, 7, 0:w], lhsT=self.ones[:], rhs=s_[:, 0:w], start=(c == 0), stop=(c == KC - 1)),
                         reads=[B("sq", c % 2), B("ones")], writes=[bp])
                S.op("act", lambda e: e.activation(out=rs[:, 0:w], in_=ps[:, 7, 0:w], func=AF.Sqrt, bias=EPS, scale=1.0 / D), reads=[bp], writes=[B("rs")])
                S.op("dve", lambda e: e.reciprocal(out=rs[:, 0:w], in_=rs[:, 0:w]), reads=[B("rs")], writes=[B("rs")])
                for c in range(KC):
                    y_ = yo[c % 2]
                    S.op("dve", lambda e, c=c, y_=y_: e.scalar_tensor_tensor(out=y_[:, 0:w], in0=self.xT[:, c, t0:t0 + w], scalar=self.vecs[:, V_FG + c:V_FG + c + 1],
                                                                              in1=rs[:, 0:w], op0=ALU.mult, op1=ALU.mult),
                         reads=self.xb(c, t0, w) + [B("rs"), B("vecs")], writes=[B("yo", c % 2)])
                    S.dma("sp", lambda e, c=c, y_=y_: e.dma_start(out=self.o_yT[c * 128:(c + 1) * 128, t0:t0 + w], in_=y_[:, 0:w]), reads=[B("yo", c % 2)])
            S.emit()

    def main(self):
        PT = [(0, 512, "p"), (512, 512, "p"), (1024, 512, "p"), (1536, 512, "p"), (2048, 128, "p"), (2176, 128, "s")]
        self.halo = True
        st = self.stage
        if os.environ.get("KNOS"):
            PT = PT[:int(os.environ["KNOS"])]
        if st >= 2:
            AT = [(0, 512, "p"), (512, 512, "p"), (1024, 512, "p"), (1536, 512, "p")]
            self.halo = False
            self.ada(0)
            self.conv(0, 0, AT)
            self.ffn(0, AT)
            self.ada(1)
            self.nsa_proj(1, 16, True)
            self.load_x()
            self.halo = True
        if st >= -1 and st < 2:
            self.ada(0)
        if st >= 0:
            self.conv(0, 0, PT)
        if st >= 1:
            self.ffn(0, PT)
        if st >= 2:
            self.nsa_proj(1, 16, False)
            self.nsa_attn(1)
            if not os.environ.get("KNOSAMP"):
                with ExitStack() as phs:
                    self.CK_all = phs.enter_context(self.nc.sbuf_tensor(self.un("CK_all"), [64, 4, 512], BF16))
                    self.CV_all = phs.enter_context(self.nc.sbuf_tensor(self.un("CV_all"), [64, 8, 4, 64], BF16))
                    self.nsa_sample_prep()
                    self.nsa_sample_attn(1)
            self.ffn(1, PT)
        if st >= 3:
            self.ada(2)
            self.pool(2, PT)
            self.ffn(2, PT)
            self.ada(3)
            self.conv(3, 1, PT)
            self.ffn(3, PT)
        self.final(PT)


_CACHE = {}


def kernel(x_prompt, x_sample, cache_nsa_kv, state_nsa_win, state_conv, state_pool, state_ffn, page_table,
           c_prompt, c_sample, ada_w, ada_b, norm1_g, norm2_g, final_g, conv_w_in, conv_w_dw, conv_ln_g,
           conv_ln_b, conv_w_out, nsa_w_in, nsa_cmp_pe, nsa_cmp_w1, nsa_cmp_w2, nsa_w_out, pool_w, pool_scale,
           ffn_w_up, ffn_w_dw, ffn_w_down):
    stage = int(os.environ.get("KSTAGE", "3"))
    f32 = np.float32
    A = lambda a: np.ascontiguousarray(np.asarray(a), dtype=f32)
    x_prompt, x_sample = A(x_prompt), A(x_sample)
    vecs = np.zeros((128, NV), f32)

    def fm(v):
        v = A(v)
        sh = v.shape[:-1]
        n = v.shape[-1] // 128
        return np.moveaxis(v.reshape(sh + (n, 128)), -1, 0)
    vecs[:, V_N1:V_N1 + 32] = fm(norm1_g).reshape(128, 32)
    vecs[:, V_N2:V_N2 + 32] = fm(norm2_g).reshape(128, 32)
    vecs[:, V_FG:V_FG + 8] = fm(final_g).reshape(128, 8)
    vecs[:, V_ADAB:V_ADAB + 192] = fm(ada_b).reshape(128, 192)
    vecs[:, V_CDW:V_CDW + 496] = np.transpose(fm(conv_w_dw), (0, 1, 3, 2)).reshape(128, 496)
    vecs[:, V_CLNG:V_CLNG + 16] = fm(conv_ln_g).reshape(128, 16)
    vecs[:, V_CLNB:V_CLNB + 16] = fm(conv_ln_b).reshape(128, 16)
    vecs[:, V_PSC:V_PSC + 8] = fm(pool_scale).reshape(128, 8)
    vecs[:, V_FDW:V_FDW + 264] = np.transpose(fm(ffn_w_dw), (0, 1, 3, 2)).reshape(128, 264)
    ident = np.eye(128, dtype=f32)
    pe = A(nsa_cmp_pe)[0]
    pe2 = np.concatenate([np.tile(pe[X].T, (1, 2)) for X in range(2)], axis=1)
    onehot = (np.arange(4096)[None, :] // 64 == np.arange(64)[:, None]).astype(f32)
    kk, qq = np.arange(128)[:, None], np.arange(128)[None, :]
    tri = np.concatenate([(kk <= qq), (kk >= qq)], axis=1).astype(f32)

    def masks_for(half):
        off = 32 * (1 - half)
        p0 = half * 2048 - 128
        M = np.zeros((18, 128, 4, 64), f32)
        nn = (np.arange(64) - off)[None, :]
        for qt in range(17):
            posc = (p0 + 128 * qt + np.arange(128))[:, None]
            validb = (nn >= 0) & (posc >= 0)
            cur = posc // 64
            cmpvalid = validb & (64 * (nn + 1) - 1 <= posc)
            future = (~validb) | (nn > cur)
            forced = validb & ((nn == 0) | (nn == cur) | (nn == cur - 1)) & ~future
            M[qt, :, 0] = np.where(cmpvalid, 0.0, -1e30)
            M[qt, :, 1] = (~forced & ~future)
            M[qt, :, 2] = forced * 1e4 + future * (-1e30)
            M[qt, :, 3] = ~future
        return M.reshape(18, 128, 256)
    mask_h = [masks_for(0), masks_for(1)]
    sl = np.arange(64)
    for M_ in mask_h:
        Ms = M_.reshape(18, 128, 4, 64)
        Ms[17, :, 0] = 0.0
        Ms[17, :, 1] = ((sl >= 1) & (sl <= 30))[None, :]
        Ms[17, :, 2] = (np.isin(sl, (0, 31, 32)) * 1e4 + (sl >= 33) * (-1e30))[None, :]
        Ms[17, :, 3] = (sl <= 32)[None, :]
    qs_, qj_ = np.arange(128) // 8, np.arange(128) % 8
    vbs = np.where(qs_[:, None] == (np.arange(512) // 32)[None, :], 0.0, -1e30).astype(f32)
    mnew = ((qs_[:, None] == qs_[None, :]) & (qj_[:, None] <= qj_[None, :])).astype(f32)
    ohnew = np.zeros((64, 128), f32)
    ohnew[32, :] = 1.0
    cache2d = A(cache_nsa_kv)[0].reshape(2560 * 128, 1024)
    iota = np.arange(128, dtype=f32).reshape(128, 1)
    swin_all = A(state_nsa_win)[0].reshape(128, 512, 512)
    ptab = np.ascontiguousarray(np.asarray(page_table), dtype=np.int32)
    shared = dict(vecs=vecs, ident=ident, ada_w=A(ada_w), conv_w_in=A(conv_w_in), conv_w_out=A(conv_w_out),
                  ffn_w_up=A(ffn_w_up), ffn_w_down=A(ffn_w_down), pool_w=A(pool_w)[0],
                  nsa_w_in=A(nsa_w_in)[0], nsa_w_out=A(nsa_w_out)[0], cmp_w1=A(nsa_cmp_w1)[0], cmp_w2=A(nsa_cmp_w2)[0],
                  pe2=np.ascontiguousarray(pe2), onehot=onehot, tri=tri, cache=cache2d, iota=iota, vbs=vbs, mnew=mnew, ohnew=ohnew)
    in_maps = []
    for c in range(NCORES):
        b, half = c // 2, c % 2
        xT = np.zeros((D, TOT), f32)
        p0 = half * 2048 - 128
        lo = max(p0, 0)
        xT[:, lo - p0:NPT] = x_prompt[b, lo:p0 + NPT].T
        ss = slice(16 * c, 16 * c + 16)
        xT[:, NPT:] = x_sample[ss].reshape(128, D).T
        xA = np.ascontiguousarray(x_prompt[b, 0:2048].T) if half == 1 else np.zeros((D, 2048), f32)
        cT = np.concatenate([A(c_prompt)[b][:, None], A(c_sample)[ss].T], axis=1)
        invcnt = np.zeros((128, 4, 128), f32)
        for g, wn in enumerate((2, 4, 8, 16)):
            pos = (p0 + 128 + np.arange(128)).astype(f32)
            invcnt[:, g, :] = (wn / np.minimum(wn, np.maximum(pos, 0) + 1))[None, :]
        m = dict(shared)
        m.update(xT=xT, xA=xA, cT=np.ascontiguousarray(cT), flag=np.full((128, 1), float(half), f32),
                 sconvT=np.ascontiguousarray(np.transpose(A(state_conv)[:, ss], (0, 3, 1, 2)).reshape(2, D, 480)),
                 spoolT=np.ascontiguousarray(np.transpose(A(state_pool)[0, ss], (2, 0, 1)).reshape(D, 240)),
                 sffnT=np.ascontiguousarray(np.transpose(A(state_ffn)[:, ss], (0, 3, 1, 2)).reshape(4, DFF, 32)),
                 invcnt=invcnt.reshape(128, 512), masks=mask_h[half],
                 ptab=np.ascontiguousarray(ptab[ss].reshape(1, 256)), swin=np.ascontiguousarray(swin_all[ss]))
        in_maps.append(m)
    if stage not in _CACHE:
        _CACHE[stage] = K(stage).build()
    nc = _CACHE[stage]
    res = run_bass_kernel_spmd(nc, in_maps, core_ids=list(range(NCORES)))
    R = res.results
    y_prompt = np.zeros((4, 4096, D), f32)
    y_sample = np.zeros((128, 8, D), f32)
    p_conv = np.zeros((2, 4, 30, D), f32)
    s_conv = np.zeros((2, 128, 30, D), f32)
    p_pool = np.zeros((1, 4, 15, D), f32)
    s_pool = np.zeros((1, 128, 15, D), f32)
    p_ffn = np.zeros((4, 4, 2, DFF), f32)
    s_ffn = np.zeros((4, 128, 2, DFF), f32)
    p_rows = np.zeros((1, 4, 4096, 4, 4, 64), f32)
    p_win = np.zeros((1, 4, 512, 2, 4, 64), f32)
    s_rows = np.zeros((1, 128, 8, 4, 4, 64), f32)
    s_win = np.zeros((1, 128, 512, 2, 4, 64), f32)
    sc_in, sp_in = A(state_conv), A(state_pool)
    for c in range(NCORES):
        b, half = c // 2, c % 2
        r = R[c]
        yT = r["o_yT"]
        y_prompt[b, half * 2048:(half + 1) * 2048] = yT[:, 128:NPT].T
        ss = slice(16 * c, 16 * c + 16)
        y_sample[ss] = yT[:, NPT:].T.reshape(16, 8, D)
        p_rows[0, b, half * 2048:(half + 1) * 2048] = r["o_rows"][0:2048].reshape(2048, 4, 4, 64)
        if half == 1:
            p_win[0, b] = r["o_win"][0:512].reshape(512, 2, 4, 64)
        s_rows[0, ss] = r["o_rows"][2048:2176].reshape(16, 8, 4, 4, 64)
        s_win[0, ss, 0:504] = r["o_swinp"].reshape(16, 504, 2, 4, 64)
        s_win[0, ss, 504:512] = r["o_win"][512:640].reshape(16, 8, 2, 4, 64)
        cv = r["o_convT"]
        fv = r["o_ffnT"]
        pv = r["o_poolT"]
        if half == 1:
            p_conv[:, b] = np.transpose(cv[:, :, 2:32], (0, 2, 1))
            p_ffn[:, b] = np.transpose(fv[:, :, 0:2], (0, 2, 1))
            p_pool[0, b] = pv[:, 1:16].T
        s_conv[:, ss, 0:22] = np.transpose(cv[:, :, 160:512].reshape(2, D, 16, 22), (0, 2, 3, 1))
        s_conv[:, ss, 22:30] = np.transpose(cv[:, :, 32:160].reshape(2, D, 16, 8), (0, 2, 3, 1))
        s_ffn[:, ss] = np.transpose(fv[:, :, 2:34].reshape(4, DFF, 16, 2), (0, 2, 3, 1))
        s_pool[0, ss, 0:7] = np.transpose(pv[:, 144:256].reshape(D, 16, 7), (1, 2, 0))
        s_pool[0, ss, 7:15] = np.transpose(pv[:, 16:144].reshape(D, 16, 8), (1, 2, 0))
    return (y_prompt, y_sample, p_conv, p_rows, p_win, p_pool, p_ffn, s_conv, s_rows, s_win, s_pool, s_ffn)
```

```python
import os
from contextlib import ExitStack
import numpy as np
import concourse.bass as bass
import concourse.mybir as mybir
from concourse.bass_utils import run_bass_kernel_spmd

F32 = mybir.dt.float32
BF16 = mybir.dt.bfloat16
I32 = mybir.dt.int32
AF = mybir.ActivationFunctionType
ALU = mybir.AluOpType
AX = mybir.AxisListType

D = 1024
KC = 8
NPT = 2176
NS = 128
TOT = NPT + NS
DFF = 2816
FC = 22
EPS = 1e-6
NCORES = 8

V_N1, V_N2, V_FG, V_ADAB, V_CDW, V_CLNG, V_CLNB, V_PSC, V_FDW = 0, 32, 64, 72, 264, 760, 776, 792, 800
NV = 800 + 264


import types


def freeze(fn):
    if fn is None or fn.__closure__ is None:
        return fn
    cells = []
    for c in fn.__closure__:
        try:
            cells.append(types.CellType(c.cell_contents))
        except ValueError:
            cells.append(c)
    return types.FunctionType(fn.__code__, fn.__globals__, fn.__name__, fn.__defaults__, tuple(cells))


class Buf:
    __slots__ = ("w", "r")

    def __init__(self):
        self.w = None
        self.r = {}


class Sched:
    COMPUTE = ("pe", "act", "dve", "pool")

    def __init__(self, nc, stack, ndma=8):
        self.nc = nc
        self.ops = []
        self.cnt = {k: 0 for k in self.COMPUTE}
        self.known = {}
        self.ndma = ndma
        self.dma_i = {"sp": 0, "pool": 0}
        self.dma_cnt = {}
        self.bufs = {}
        self.sems = {}
        for k in self.COMPUTE:
            self.sems[k] = stack.enter_context(nc.semaphore("s_" + k))
        for q in ("sp", "pool"):
            for i in range(ndma):
                key = "d%s%d" % (q, i)
                self.sems[key] = stack.enter_context(nc.semaphore("s_" + key))
                self.dma_cnt[key] = 0

    def buf(self, *key):
        b = self.bufs.get(key)
        if b is None:
            b = self.bufs[key] = Buf()
        return b

    def _deps(self, queue, reads, writes):
        need = {}

        def add(sv):
            if sv is None:
                return
            k, v = sv
            if need.get(k, 0) < v:
                need[k] = v
        for b in reads:
            add(b.w)
        for b in writes:
            add(b.w)
            for k, v in b.r.items():
                add((k, v))
        waits = []
        for k, v in need.items():
            if queue == "pe" and k == "pe":
                continue
            if self.known.get((queue, k), 0) >= v:
                continue
            self.known[(queue, k)] = v
            waits.append((k, v))
        return waits

    def op(self, queue, fn, reads=(), writes=()):
        fn = freeze(fn)
        waits = self._deps(queue, reads, writes)
        self.cnt[queue] += 1
        v = self.cnt[queue]
        self.ops.append((queue, fn, waits, (queue, 1)))
        for b in reads:
            if b.r.get(queue, 0) < v:
                b.r[queue] = v
        for b in writes:
            b.w = (queue, v)
            b.r = {}

    def dma(self, queue, fn, reads=(), writes=()):
        i = self.dma_i[queue]
        self.dma_i[queue] += 1
        key = "d%s%d" % (queue, i % self.ndma)
        fn = freeze(fn)
        prev = self.dma_cnt[key]
        waits = self._deps(queue, reads, writes)
        if prev and self.known.get((queue, key), 0) < prev:
            self.known[(queue, key)] = prev
            waits.append((key, prev))
        v = prev + 16
        self.dma_cnt[key] = v
        self.ops.append((queue, fn, waits, (key, 16)))
        for b in reads:
            b.r[key] = v
        for b in writes:
            b.w = (key, v)
            b.r = {}

    def emit(self, final=False):
        nc = self.nc
        if final:
            waits = [(k, v) for k, v in self.dma_cnt.items() if v]
            waits += [(k, self.cnt[k]) for k in self.COMPUTE if self.cnt[k]]
            self.ops.append(("sp", None, waits, None))
        if not final:
            allw = [(k, v) for k, v in self.dma_cnt.items() if v] + [(k, self.cnt[k]) for k in self.COMPUTE if self.cnt[k]]
            for q in ("sp", "pe", "act", "dve", "pool"):
                ws = [(k, v) for (k, v) in allw if self.known.get((q, k), 0) < v]
                for (k, v) in ws:
                    self.known[(q, k)] = v
                self.ops.append((q, None, ws, None))
        byq = {q: [] for q in ("sp", "pe", "act", "dve", "pool")}
        for o in self.ops:
            byq[o[0]].append(o)
        self.ops = []
        sems = self.sems

        def run(eng, lst):
            for (_, fn, waits, inc) in lst:
                for (k, v) in waits:
                    eng.wait_ge(sems[k], v)
                if fn is not None:
                    fn(eng).then_inc(sems[inc[0]], inc[1])

        with nc.Block() as block:
            @block.sync
            def _(e):
                run(e, byq["sp"])

            @block.tensor
            def _(e):
                run(e, byq["pe"])

            @block.scalar
            def _(e):
                run(e, byq["act"])

            @block.vector
            def _(e):
                run(e, byq["dve"])

            @block.gpsimd
            def _(e):
                run(e, byq["pool"])


def bc(ap, shape):
    return ap.to_broadcast(list(shape))


class K:
    def __init__(self, stage):
        self.stage = stage
        nc = self.nc = bass.Bass("TRN2", target_bir_lowering=False)
        dt = nc.dram_tensor

        def inp(name, shape, dtype=F32):
            return dt(name, list(shape), dtype, kind="ExternalInput").ap()

        def outp(name, shape):
            return dt(name, list(shape), F32, kind="ExternalOutput").ap()
        self.d_xT = inp("xT", [D, TOT])
        self.d_xA = inp("xA", [D, 2048])
        self.d_cT = inp("cT", [D, 17])
        self.d_flag = inp("flag", [128, 1])
        self.d_vecs = inp("vecs", [128, NV])
        self.d_ident = inp("ident", [128, 128])
        self.d_ada_w = inp("ada_w", [4, D, 6 * D])
        self.d_conv_w_in = inp("conv_w_in", [2, D, 2 * D])
        self.d_conv_w_out = inp("conv_w_out", [2, D, D])
        self.d_ffn_w_up = inp("ffn_w_up", [4, D, 2 * DFF])
        self.d_ffn_w_down = inp("ffn_w_down", [4, DFF, D])
        self.d_pool_w = inp("pool_w", [4, 256, 256])
        self.d_sconvT = inp("sconvT", [2, D, 16 * 30])
        self.d_spoolT = inp("spoolT", [D, 16 * 15])
        self.d_sffnT = inp("sffnT", [4, DFF, 16 * 2])
        self.d_invcnt = inp("invcnt", [128, 4 * 128])
        self.d_nsa_w_in = inp("nsa_w_in", [D, 2608])
        self.d_nsa_w_out = inp("nsa_w_out", [D, D])
        self.d_w1 = inp("cmp_w1", [2, 4096, 256])
        self.d_w2 = inp("cmp_w2", [2, 256, 64])
        self.d_pe2 = inp("pe2", [64, 256])
        self.d_onehot = inp("onehot", [64, 4096])
        self.d_tri = inp("tri", [128, 256])
        self.d_masks = inp("masks", [18, 128, 256])
        self.d_cache = inp("cache", [2560 * 128, 1024])
        self.d_pt = inp("ptab", [1, 256], I32)
        self.d_iota = inp("iota", [128, 1])
        self.d_swin = inp("swin", [16, 512, 512])
        self.d_vbs = inp("vbs", [128, 512])
        self.d_ohnew = inp("ohnew", [64, 128])
        self.d_mnew = inp("mnew", [128, 128])
        self.sc_sKs = dt("sc_sKs", [16, 64, 4, 2048], BF16).ap()
        self.sc_sVs = dt("sc_sVs", [16, 16, 128, 256], BF16).ap()
        self.sc_sVc = dt("sc_sVc", [16, 64, 4, 2048], BF16).ap()
        self.o_swinp = outp("o_swinp", [16, 504, 512])
        self.sc_Ks = dt("sc_Ks", [64, 4, 4096], BF16).ap()
        self.sc_Vs = dt("sc_Vs", [32, 128, 256], BF16).ap()
        self.sc_wK = dt("sc_wK", [32, 64, 512], BF16).ap()
        self.sc_wV = dt("sc_wV", [32, 128, 256], BF16).ap()
        self.o_rows = outp("o_rows", [2048 + 128, 1024])
        self.o_win = outp("o_win", [512 + 128, 512])
        self.o_yT = outp("o_yT", [D, TOT])
        self.o_convT = outp("o_convT", [2, D, 160 + 352])
        self.o_ffnT = outp("o_ffnT", [4, DFF, 2 + 32])
        self.o_poolT = outp("o_poolT", [D, 144 + 112])

    def build(self):
        nc = self.nc
        with ExitStack() as st:
            E = st.enter_context
            self.S = S = Sched(nc, st)
            self.B = S.buf
            sb = lambda n, s, d=F32: E(nc.sbuf_tensor(self.un(n), list(s), d))
            self.xT = sb("xT_s", [128, KC, TOT])
            self.modT = sb("modT", [128, 2, 48, 17])
            self.vecs = sb("vecs_s", [128, NV])
            self.ident = sb("ident_s", [128, 128])
            self.ones = sb("ones_s", [128, 128], BF16)
            self.flag = sb("flag_s", [128, 1])
            self.cmT = sb("cmT", [128, KC, 17], BF16)
            self.coef = sb("coef", [128, 6, KC])
            self.As = sb("As", [128, KC, NS])
            self.ckT = sb("ckT", [64, 4, 64], BF16)
            self.cv = sb("cv", [64, 4, 64], BF16)
            self.ps = E(nc.psum_tensor("ps", [128, 8, 512], F32))
            self.setup()
            S.emit()
            self.main()
            S.emit(final=True)
        return nc

    def setup(self):
        S, B = self.S, self.B
        xT, vecs, ident, flag = self.xT, self.vecs, self.ident, self.flag
        if self.stage >= 2:
            for c in range(KC):
                S.dma("sp", lambda e, c=c: e.dma_start(out=xT[:, c, 0:2048], in_=self.d_xA[c * 128:(c + 1) * 128, :]),
                      writes=[B("x", c, t) for t in range(16)])
        else:
            self.load_x()
        S.dma("sp", lambda e: e.dma_start(out=vecs[:], in_=self.d_vecs), writes=[B("vecs")])
        S.dma("sp", lambda e: e.dma_start(out=ident[:], in_=self.d_ident), writes=[B("ident")])
        S.dma("sp", lambda e: e.dma_start(out=flag[:], in_=self.d_flag), writes=[B("flag")])
        S.op("pool", lambda e: e.memset(self.ones[:], 1.0), writes=[B("ones")])
        with ExitStack() as ph:
            ct = ph.enter_context(self.nc.sbuf_tensor("ct", [128, KC, 17], F32))
            S.dma("sp", lambda e: e.dma_start(out=ct[:], in_=self.d_cT.rearrange("(c p) n -> p c n", p=128)), writes=[B("ct")])
            S.op("act", lambda e: e.activation(out=self.cmT[:], in_=ct[:], func=AF.Silu), reads=[B("ct")], writes=[B("cmT")])
            S.emit()

    def load_x(self):
        S, B = self.S, self.B
        for c in range(KC):
            S.dma("sp", lambda e, c=c: e.dma_start(out=self.xT[:, c, :], in_=self.d_xT[c * 128:(c + 1) * 128, :]),
                  writes=[B("x", c, t) for t in range(TOT // 128)])

    def un(self, n):
        self._uid = getattr(self, "_uid", 0) + 1
        return "%s_%d" % (n, self._uid)

    def xb(self, c, t0, w):
        return [self.B("x", c, t) for t in range(t0 // 128, (t0 + w + 127) // 128)]

    def ada(self, i):
        S, B, nc = self.S, self.B, self.nc
        slot = i % 2
        with ExitStack() as ph:
            wb = [ph.enter_context(nc.sbuf_tensor(self.un("adaw%d" % k), [128, KC, 1024], BF16)) for k in range(2)]
            for grp in range(6):
                w = wb[grp % 2]
                bw = B("adaw", grp % 2)
                S.dma("pool", lambda e, w=w, grp=grp: e.dma_start(
                    out=w[:], in_=self.d_ada_w[i, :, grp * 1024:(grp + 1) * 1024].rearrange("(c p) n -> p c n", p=128)), writes=[bw])
                for j in range(8):
                    bank = j % 2
                    bp = B("ps", bank)
                    for k in range(KC):
                        S.op("pe", lambda e, w=w, j=j, k=k, bank=bank: e.matmul(
                            self.ps[:, bank, 0:17], lhsT=w[:, k, j * 128:(j + 1) * 128], rhs=self.cmT[:, k, :],
                            start=(k == 0), stop=(k == KC - 1)), reads=[bw, B("cmT")], writes=[bp])
                    m = grp * 8 + j
                    S.op("act", lambda e, m=m, bank=bank: e.activation(
                        out=self.modT[:, slot, m, :], in_=self.ps[:, bank, 0:17], func=AF.Identity,
                        bias=self.vecs[:, V_ADAB + i * 48 + m:V_ADAB + i * 48 + m + 1], scale=1.0),
                        reads=[bp, B("vecs")], writes=[B("modT", slot)])
            S.emit()

    def coefs(self, i, sub):
        S, B = self.S, self.B
        slot = i % 2
        mod = self.modT
        ng = self.vecs[:, (V_N1 if sub == 0 else V_N2) + i * 8:(V_N1 if sub == 0 else V_N2) + i * 8 + 8]
        o = 3 * sub
        m0 = 24 * sub
        rd = [B("modT", slot), B("vecs")]
        S.op("dve", lambda e: e.scalar_tensor_tensor(out=self.coef[:, o, :], in0=mod[:, slot, m0 + 8:m0 + 16, 0], scalar=1.0,
                                                     in1=ng, op0=ALU.add, op1=ALU.mult), reads=rd, writes=[B("coef")])
        S.op("dve", lambda e: e.tensor_copy(out=self.coef[:, o + 1, :], in_=mod[:, slot, m0:m0 + 8, 0]), reads=rd, writes=[B("coef")])
        S.op("dve", lambda e: e.tensor_copy(out=self.coef[:, o + 2, :], in_=mod[:, slot, m0 + 16:m0 + 24, 0]), reads=rd, writes=[B("coef")])
        for c in range(KC):
            S.op("dve", lambda e, c=c: e.tensor_scalar(
                out=self.As[:, c, :].rearrange("p (s j) -> p s j", j=8),
                in0=bc(mod[:, slot, m0 + 8 + c, 1:17].unsqueeze(2), [128, 16, 8]), scalar1=1.0,
                scalar2=ng[:, c:c + 1], op0=ALU.add, op1=ALU.mult), reads=rd, writes=[B("As")])

    def mod_s(self, i, m):
        return bc(self.modT[:, i % 2, m * 8:(m + 1) * 8, 1:17].unsqueeze(3), [128, 8, 16, 8])

    def norm_mod(self, i, sub, t0, w, kind, hT, hcol, tmp, sq, rs, hname, fp32_out=None):
        S, B = self.S, self.B
        xT, ps = self.xT, self.ps
        bp = B("ps", 7)
        for c in range(KC):
            s_ = sq[c % 2]
            S.op("act", lambda e, c=c, s_=s_: e.activation(out=s_[:, 0:w], in_=xT[:, c, t0:t0 + w], func=AF.Square),
                 reads=self.xb(c, t0, w), writes=[B("sq", c % 2)])
            S.op("pe", lambda e, c=c, s_=s_: e.matmul(ps[:, 7, 0:w], lhsT=self.ones[:], rhs=s_[:, 0:w], start=(c == 0), stop=(c == KC - 1)),
                 reads=[B("sq", c % 2), B("ones")], writes=[bp])
        S.op("act", lambda e: e.activation(out=rs[:, 0:w], in_=ps[:, 7, 0:w], func=AF.Sqrt, bias=EPS, scale=1.0 / D), reads=[bp], writes=[B("rs")])
        S.op("dve", lambda e: e.reciprocal(out=rs[:, 0:w], in_=rs[:, 0:w]), reads=[B("rs")], writes=[B("rs")])
        o = 3 * sub
        for c in range(KC):
            t_ = tmp[c % 2]
            bt = B("nt", c % 2)
            hb = B(hname, c, hcol // 128) if hname else None
            S.op("dve", lambda e, c=c, t_=t_: e.tensor_tensor(out=t_[:, 0:w], in0=xT[:, c, t0:t0 + w], in1=rs[:, 0:w], op=ALU.mult),
                 reads=self.xb(c, t0, w) + [B("rs")], writes=[bt])
            wr = [B(hname, c, t) for t in range(hcol // 128, (hcol + w + 127) // 128)]
            if kind == "p":
                S.op("act", lambda e, c=c, t_=t_: e.activation(out=hT[:, c, hcol:hcol + w], in_=t_[:, 0:w], func=AF.Identity,
                                                              bias=self.coef[:, o + 1, c:c + 1], scale=self.coef[:, o, c:c + 1]),
                     reads=[bt, B("coef")], writes=wr)
                if fp32_out is not None:
                    S.op("dve", lambda e, c=c, t_=t_: e.tensor_scalar(out=fp32_out[:, c, hcol:hcol + w], in0=t_[:, 0:w],
                                                                       scalar1=self.coef[:, o, c:c + 1], scalar2=self.coef[:, o + 1, c:c + 1],
                                                                       op0=ALU.mult, op1=ALU.add), reads=[bt, B("coef")], writes=[B("h32", c)])
            else:
                S.op("dve", lambda e, c=c, t_=t_: e.tensor_tensor(out=t_[:, 0:w], in0=t_[:, 0:w], in1=self.As[:, c, :], op=ALU.mult),
                     reads=[bt, B("As")], writes=[bt])
                shs = self.modT[:, i % 2, 24 * sub + c, 1:17]
                S.op("dve", lambda e, c=c, t_=t_, shs=shs: e.tensor_tensor(
                    out=hT[:, c, hcol:hcol + w].rearrange("p (s j) -> p s j", j=8), in0=t_[:, 0:w].rearrange("p (s j) -> p s j", j=8),
                    in1=bc(shs.unsqueeze(2), [128, 16, 8]), op=ALU.add), reads=[bt, B("modT", i % 2)], writes=wr)
                if fp32_out is not None:
                    S.op("pool", lambda e, c=c, t_=t_, shs=shs: e.tensor_tensor(
                        out=fp32_out[:, c, hcol:hcol + w].rearrange("p (s j) -> p s j", j=8), in0=t_[:, 0:w].rearrange("p (s j) -> p s j", j=8),
                        in1=bc(shs.unsqueeze(2), [128, 16, 8]), op=ALU.add), reads=[bt, B("modT", i % 2)], writes=[B("h32", c)])

    def resid(self, i, sub, c, t0, w, kind, src_ap, rd, extra_scale=None):
        S, B = self.S, self.B
        xs = self.xT[:, c, t0:t0 + w]
        if kind == "p":
            g = extra_scale if extra_scale is not None else self.coef[:, 3 * sub + 2, c:c + 1]
            S.op("dve", lambda e: e.scalar_tensor_tensor(out=xs, in0=src_ap, scalar=g, in1=xs, op0=ALU.mult, op1=ALU.add),
                 reads=rd + [B("coef")] + self.xb(c, t0, w), writes=self.xb(c, t0, w))
        else:
            gs = self.modT[:, i % 2, 24 * sub + 16 + c, 1:17]
            tmp = self.rtmp
            S.op("dve", lambda e: e.tensor_tensor(out=tmp[:, 0:w].rearrange("p (s j) -> p s j", j=8), in0=src_ap.rearrange("p (s j) -> p s j", j=8),
                                                  in1=bc(gs.unsqueeze(2), [128, 16, 8]), op=ALU.mult),
                 reads=rd + [B("modT", i % 2)], writes=[B("rtmp")])
            if extra_scale is not None:
                S.op("dve", lambda e: e.tensor_scalar(out=tmp[:, 0:w], in0=tmp[:, 0:w], scalar1=extra_scale, scalar2=None, op0=ALU.mult),
                     reads=[B("rtmp")], writes=[B("rtmp")])
            S.op("dve", lambda e: e.tensor_tensor(out=xs, in0=xs, in1=tmp[:, 0:w], op=ALU.add),
                 reads=[B("rtmp")] + self.xb(c, t0, w), writes=self.xb(c, t0, w))

    def ffn(self, i, tiles):
        S, B, nc, ps = self.S, self.B, self.nc, self.ps
        G = 4
        with ExitStack() as ph:
            A = lambda n, s, d=F32: ph.enter_context(nc.sbuf_tensor(self.un(n), list(s), d))
            hT = A("f_hT", [128, KC, TOT], BF16)
            wa = [A("f_wa%d" % k, [128, KC, G * 128], BF16) for k in range(2)]
            wbb = [A("f_wb%d" % k, [128, KC, G * 128], BF16) for k in range(2)]
            wd = [A("f_wd%d" % k, [128, G, D], BF16) for k in range(2)]
            abuf = A("f_abuf", [128, G, 2 + 512])
            abs_ = A("f_abs", [128, G, 16, 10])
            ac = [A("f_ac%d" % k, [128, 512]) for k in range(2)]
            gb = [A("f_g%d" % k, [128, G, 512], BF16) for k in range(2)]
            tmp = ac
            sq = [A("f_sq%d" % k, [128, 512], BF16) for k in range(2)]
            rs = A("f_rs", [128, 512])
            self.rtmp = A("f_rtmp", [128, 128])
            self.coefs(i, 1)
            for (t0, w, kind) in tiles:
                self.norm_mod(i, 1, t0, w, kind, hT, t0, tmp, sq, rs, "fh")
            fdw = lambda j, k: self.vecs[:, V_FDW + (i * FC + j) * 3 + k:V_FDW + (i * FC + j) * 3 + k + 1]
            npass = (FC + G - 1) // G
            gi = 0
            def load_w(pp):
                j0 = pp * G
                ng = min(G, FC - j0)
                sl = pp % 2
                bw = B("fw", sl)
                S.dma("pool", lambda e, sl=sl, j0=j0, ng=ng: e.dma_start(
                    out=wa[sl][:, :, 0:ng * 128], in_=self.d_ffn_w_up[i, :, j0 * 128:(j0 + ng) * 128].rearrange("(c p) n -> p c n", p=128)), writes=[bw])
                S.dma("pool", lambda e, sl=sl, j0=j0, ng=ng: e.dma_start(
                    out=wbb[sl][:, :, 0:ng * 128], in_=self.d_ffn_w_up[i, :, DFF + j0 * 128:DFF + (j0 + ng) * 128].rearrange("(c p) n -> p c n", p=128)), writes=[bw])
                S.dma("pool", lambda e, sl=sl, j0=j0, ng=ng: e.dma_start(
                    out=wd[sl][:, 0:ng, :], in_=self.d_ffn_w_down[i, j0 * 128:(j0 + ng) * 128, :].rearrange("(g p) n -> p g n", p=128)), writes=[bw])
            load_w(0)
            for p_ in range(npass):
                j0 = p_ * G
                ng = min(G, FC - j0)
                sl = p_ % 2
                bw = B("fw", sl)
                if p_ + 1 < npass:
                    load_w(p_ + 1)
                if any(k == "s" for (_, _, k) in tiles):
                    for g_ in range(ng):
                        S.dma("sp", lambda e, j0=j0, g_=g_: e.dma_start(
                            out=abs_[:, g_, :, 0:2], in_=self.d_sffnT[i, (j0 + g_) * 128:(j0 + g_ + 1) * 128, :].rearrange("p (s r) -> p s r", r=2)),
                            writes=[B("abs", g_)])
                pending = [None]

                def down(t0, w, kind, gsl, g_t):
                    for c in range(KC):
                        bank = 4 + (c % 2)
                        bp = B("ps", bank)
                        for jj in range(ng):
                            S.op("pe", lambda e, c=c, jj=jj, bank=bank, g_t=g_t: e.matmul(ps[:, bank, 0:w], lhsT=wd[sl][:, jj, c * 128:(c + 1) * 128],
                                                                                         rhs=g_t[:, jj, 0:w], start=(jj == 0), stop=(jj == ng - 1)),
                                 reads=[bw, B("g", gsl, jj)], writes=[bp])
                        self.resid(i, 1, c, t0, w, kind, ps[:, bank, 0:w], [bp])
                for ti, (t0, w, kind) in enumerate(tiles):
                    gsl = gi % 2
                    gi += 1
                    g_t = gb[gsl]
                    for jj in range(ng):
                        if jj == 1 and pending[0] is not None:
                            down(*pending[0])
                            pending[0] = None
                        j = j0 + jj
                        pa, pb = 2 * (jj % 2), 2 * (jj % 2) + 1
                        bpa, bpb = B("ps", pa), B("ps", pb)
                        hrd = [B("fh", k, t) for k in range(KC) for t in range(t0 // 128, (t0 + w + 127) // 128)]
                        for k in range(KC):
                            S.op("pe", lambda e, k=k, jj=jj, pa=pa: e.matmul(ps[:, pa, 0:w], lhsT=wa[sl][:, k, jj * 128:(jj + 1) * 128],
                                                                            rhs=hT[:, k, t0:t0 + w], start=(k == 0), stop=(k == KC - 1)),
                                 reads=[bw] + hrd, writes=[bpa])
                        for k in range(KC):
                            S.op("pe", lambda e, k=k, jj=jj, pb=pb: e.matmul(ps[:, pb, 0:w], lhsT=wbb[sl][:, k, jj * 128:(jj + 1) * 128],
                                                                            rhs=hT[:, k, t0:t0 + w], start=(k == 0), stop=(k == KC - 1)),
                                 reads=[bw] + hrd, writes=[bpb])
                        a_ = ac[jj % 2]
                        ba = B("ac", jj % 2)
                        if kind == "p":
                            bab = B("abuf", jj)
                            if ti == 0:
                                S.op("pool", lambda e, jj=jj: e.memset(abuf[:, jj, 0:2], 0.0), writes=[bab])
                            S.op("act", lambda e, jj=jj, pa=pa: e.activation(out=abuf[:, jj, 2:2 + w], in_=ps[:, pa, 0:w], func=AF.Copy),
                                 reads=[bpa], writes=[bab])
                            if ti == 0 and self.halo:
                                S.op("dve", lambda e, jj=jj: e.tensor_scalar(out=abuf[:, jj, 2:130], in0=abuf[:, jj, 2:130], scalar1=self.flag[:, 0:1],
                                                                            scalar2=None, op0=ALU.mult), reads=[bab, B("flag")], writes=[bab])
                            if t0 + w == NPT and self.halo:
                                S.dma("sp", lambda e, jj=jj, j=j: e.dma_start(out=self.o_ffnT[i, j * 128:(j + 1) * 128, 0:2], in_=abuf[:, jj, w:w + 2]), reads=[bab])
                            S.op("dve", lambda e, jj=jj, j=j, a_=a_: e.tensor_scalar(out=a_[:, 0:w], in0=abuf[:, jj, 0:w], scalar1=fdw(j, 0), scalar2=None, op0=ALU.mult),
                                 reads=[bab, B("vecs")], writes=[ba])
                            for k in (1, 2):
                                S.op("dve", lambda e, jj=jj, j=j, a_=a_, k=k: e.scalar_tensor_tensor(
                                    out=a_[:, 0:w], in0=abuf[:, jj, k:k + w], scalar=fdw(j, k), in1=a_[:, 0:w], op0=ALU.mult, op1=ALU.add),
                                    reads=[bab, B("vecs"), ba], writes=[ba])
                            S.op("pool", lambda e, jj=jj: e.tensor_copy(out=abuf[:, jj, 0:2], in_=abuf[:, jj, w:w + 2]), reads=[bab], writes=[bab])
                        else:
                            bab = B("abs", jj)
                            S.op("act", lambda e, jj=jj, pa=pa: e.activation(out=abs_[:, jj, :, 2:10], in_=ps[:, pa, 0:128].rearrange("p (s j) -> p s j", j=8), func=AF.Copy),
                                 reads=[bpa], writes=[bab])
                            S.dma("sp", lambda e, jj=jj, j=j: e.dma_start(
                                out=self.o_ffnT[i, j * 128:(j + 1) * 128, 2:34].rearrange("p (s r) -> p s r", r=2), in_=abs_[:, jj, :, 8:10]), reads=[bab])
                            a3 = a_[:, 0:128].rearrange("p (s j) -> p s j", j=8)
                            S.op("dve", lambda e, jj=jj, j=j, a3=a3: e.tensor_scalar(out=a3, in0=abs_[:, jj, :, 0:8], scalar1=fdw(j, 0), scalar2=None, op0=ALU.mult),
                                 reads=[bab, B("vecs")], writes=[ba])
                            for k in (1, 2):
                                S.op("dve", lambda e, jj=jj, j=j, a3=a3, k=k: e.scalar_tensor_tensor(
                                    out=a3, in0=abs_[:, jj, :, k:k + 8], scalar=fdw(j, k), in1=a3, op0=ALU.mult, op1=ALU.add),
                                    reads=[bab, B("vecs"), ba], writes=[ba])
                        S.op("act", lambda e, a_=a_: e.activation(out=a_[:, 0:w], in_=a_[:, 0:w], func=AF.Silu), reads=[ba], writes=[ba])
                        S.op("dve", lambda e, a_=a_, jj=jj, pb=pb, g_t=g_t: e.tensor_tensor(out=g_t[:, jj, 0:w], in0=a_[:, 0:w], in1=ps[:, pb, 0:w], op=ALU.mult),
                             reads=[ba, bpb], writes=[B("g", gsl, jj)])
                    if pending[0] is not None:
                        down(*pending[0])
                    pending[0] = (t0, w, kind, gsl, g_t)
                if pending[0] is not None:
                    down(*pending[0])
                    pending[0] = None
            S.emit()

    def conv(self, i, jl, tiles):
        S, B, nc, ps = self.S, self.B, self.nc, self.ps
        has_s = any(k == "s" for (_, _, k) in tiles)
        with ExitStack() as ph0:
            A0 = lambda n, s, d=F32: ph0.enter_context(nc.sbuf_tensor(self.un(n), list(s), d))
            ubuf = A0("c_ubuf", [128, KC, 30 + NPT], BF16)
            ubs = A0("c_ubs", [128, KC, 16, 38], BF16)
            self.rtmp = A0("c_rtmp", [128, 128])
            self.coefs(i, 0)
            with ExitStack() as ph:
                A = lambda n, s, d=F32: ph.enter_context(nc.sbuf_tensor(self.un(n), list(s), d))
                w_in = A("c_win", [128, KC, 2 * D], BF16)
                hT = A("c_hT", [128, KC, 512], BF16)
                tmp = [A("c_t%d" % k, [128, 512]) for k in range(2)]
                sq = [A("c_sq%d" % k, [128, 512], BF16) for k in range(2)]
                rs = A("c_rs", [128, 512])
                sg = [A("c_sg%d" % k, [128, 512]) for k in range(2)]
                u32 = [A("c_u%d" % k, [128, 512]) for k in range(2)]
                sst = [A("c_sst%d" % k, [128, 16, 30]) for k in range(2)]
                for hh in range(2):
                    S.dma("pool", lambda e, hh=hh: e.dma_start(out=w_in[:, :, hh * D:(hh + 1) * D],
                                                               in_=self.d_conv_w_in[jl, :, hh * D:(hh + 1) * D].rearrange("(c p) n -> p c n", p=128)),
                          writes=[B("cwin")])
                S.op("pool", lambda e: e.memset(ubuf[:, :, 0:30], 0.0), writes=[B("ub", c, 0) for c in range(KC)])
                if has_s:
                    for c in range(KC):
                        st_ = sst[c % 2]
                        S.dma("sp", lambda e, c=c, st_=st_: e.dma_start(out=st_[:], in_=self.d_sconvT[jl, c * 128:(c + 1) * 128, :].rearrange("p (s r) -> p s r", r=30)), writes=[B("sst", c % 2)])
                        S.op("pool", lambda e, c=c, st_=st_: e.tensor_copy(out=ubs[:, c, :, 0:30], in_=st_[:]), reads=[B("sst", c % 2)], writes=[B("ubs", c)])
                        S.dma("sp", lambda e, c=c, st_=st_: e.dma_start(out=self.o_convT[jl, c * 128:(c + 1) * 128, 160:512].rearrange("p (s r) -> p s r", r=22),
                                                                       in_=st_[:, :, 8:30]), reads=[B("sst", c % 2)])
                for (t0, w, kind) in tiles:
                    self.norm_mod(i, 0, t0, w, kind, hT, 0, tmp, sq, rs, "ch")
                    hrd = [B("ch", k, t) for k in range(KC) for t in range(0, (w + 127) // 128)]
                    for c in range(KC):
                        pa, pb = 2 * (c % 2), 2 * (c % 2) + 1
                        bpa, bpb = B("ps", pa), B("ps", pb)
                        for (pp, off) in ((pa, 0), (pb, D)):
                            for k in range(KC):
                                S.op("pe", lambda e, k=k, c=c, pp=pp, off=off: e.matmul(ps[:, pp, 0:w], lhsT=w_in[:, k, off + c * 128:off + (c + 1) * 128],
                                                                                       rhs=hT[:, k, 0:w], start=(k == 0), stop=(k == KC - 1)),
                                     reads=[B("cwin")] + hrd, writes=[B("ps", pp)])
                        s_ = sg[c % 2]
                        u_ = u32[c % 2]
                        S.op("act", lambda e, s_=s_, pb=pb: e.activation(out=s_[:, 0:w], in_=ps[:, pb, 0:w], func=AF.Sigmoid), reads=[bpb], writes=[B("sg", c % 2)])
                        S.op("dve", lambda e, s_=s_, u_=u_, pa=pa: e.tensor_tensor(out=u_[:, 0:w], in0=ps[:, pa, 0:w], in1=s_[:, 0:w], op=ALU.mult),
                             reads=[bpa, B("sg", c % 2)], writes=[B("u32", c % 2)])
                        if kind == "p":
                            ubw = [B("ub", c, t) for t in range((30 + t0) // 128, (30 + t0 + w + 127) // 128 + 1)]
                            if t0 == 0 and self.halo:
                                S.op("dve", lambda e, u_=u_: e.tensor_scalar(out=u_[:, 0:128], in0=u_[:, 0:128], scalar1=self.flag[:, 0:1], scalar2=None, op0=ALU.mult),
                                     reads=[B("u32", c % 2), B("flag")], writes=[B("u32", c % 2)])
                            S.op("pool", lambda e, u_=u_, c=c: e.tensor_copy(out=ubuf[:, c, 30 + t0:30 + t0 + w], in_=u_[:, 0:w]), reads=[B("u32", c % 2)], writes=ubw)
                            if t0 + w == NPT and self.halo:
                                S.dma("sp", lambda e, u_=u_, c=c: e.dma_start(out=self.o_convT[jl, c * 128:(c + 1) * 128, 0:32], in_=u_[:, w - 32:w]), reads=[B("u32", c % 2)])
                        else:
                            S.op("pool", lambda e, u_=u_, c=c: e.tensor_copy(out=ubs[:, c, :, 30:38], in_=u_[:, 0:128].rearrange("p (s j) -> p s j", j=8)),
                                 reads=[B("u32", c % 2)], writes=[B("ubs", c)])
                            S.dma("sp", lambda e, u_=u_, c=c: e.dma_start(out=self.o_convT[jl, c * 128:(c + 1) * 128, 32:160], in_=u_[:, 0:128]), reads=[B("u32", c % 2)])
                S.emit()
            if os.environ.get("KCUT") == "1":
                return
            with ExitStack() as ph:
                A = lambda n, s, d=F32: ph.enter_context(nc.sbuf_tensor(self.un(n), list(s), d))
                w_out = A("c_wout", [128, KC, D], BF16)
                identb = A("c_idb", [128, 128], BF16)
                diag = A("c_diag", [128, 2, 31, 128], BF16)
                W2 = 256
                y32 = A("c_y32", [128, KC, W2])
                ybf = [A("c_ybf%d" % k, [128, W2], BF16) for k in range(2)]
                ysq = [A("c_ysq%d" % k, [128, W2], BF16) for k in range(2)]
                mean = A("c_mean", [128, W2])
                rstd = A("c_rstd", [128, W2])
                zt = A("c_zt", [128, KC, W2], BF16)
                zt32 = [A("c_z32%d" % k, [128, W2]) for k in range(2)]
                uodd = A("c_uodd", [128, KC, W2 + 32], BF16)
                ubso = A("c_ubso", [128, KC, 16, 38], BF16)
                if has_s:
                    for c in range(KC):
                        S.op("pool", lambda e, c=c: e.tensor_copy(out=ubso[:, c, :, 0:36], in_=ubs[:, c, :, 1:37]), reads=[B("ubs", c)], writes=[B("ubso", c)])
                S.dma("pool", lambda e: e.dma_start(out=w_out[:], in_=self.d_conv_w_out[jl].rearrange("(c p) n -> p c n", p=128)), writes=[B("cwout")])
                S.op("pool", lambda e: e.tensor_copy(out=identb[:], in_=self.ident[:]), reads=[B("ident")], writes=[B("idb")])
                lng = lambda c: self.vecs[:, V_CLNG + jl * KC + c:V_CLNG + jl * KC + c + 1]
                lnb = lambda c: self.vecs[:, V_CLNB + jl * KC + c:V_CLNB + jl * KC + c + 1]
                sub = []
                for (t0, w, kind) in tiles:
                    for q0 in range(0, w, W2):
                        sub.append((t0 + q0, min(W2, w - q0), kind))
                for (t0, w, kind) in sub:
                    bps, bpq = B("ps", 6), B("ps", 7)
                    if kind == "p":
                        for c in range(KC):
                            S.op("pool", lambda e, c=c: e.tensor_copy(out=uodd[:, c, 0:w + 29], in_=ubuf[:, c, t0 + 1:t0 + w + 30]),
                                 reads=[B("ub", c, t) for t in range(t0 // 128, (t0 + w + 31 + 127) // 128 + 1)], writes=[B("uodd", c)])
                    for c in range(KC):
                        bank = c % 2
                        bp = B("ps", bank)
                        for k in range(31):
                            S.op("dve", lambda e, c=c, k=k: e.tensor_scalar(
                                out=diag[:, c % 2, k, :], in0=identb[:], scalar1=self.vecs[:, V_CDW + (jl * KC + c) * 31 + k:V_CDW + (jl * KC + c) * 31 + k + 1],
                                scalar2=None, op0=ALU.mult), reads=[B("idb"), B("vecs")], writes=[B("diag", c % 2, k)])
                        for k in range(31):
                            if kind == "p":
                                if k % 2 == 0:
                                    rhs = ubuf[:, c, t0 + k:t0 + k + w]
                                    rd = [B("ub", c, t) for t in range((t0 + k) // 128, (t0 + k + w + 127) // 128 + 1)]
                                else:
                                    rhs = uodd[:, c, k - 1:k - 1 + w]
                                    rd = [B("uodd", c)]
                            else:
                                if k % 2 == 0:
                                    rhs = ubs[:, c, :, k:k + 8]
                                    rd = [B("ubs", c)]
                                else:
                                    rhs = ubso[:, c, :, k - 1:k + 7]
                                    rd = [B("ubso", c)]
                            oap = ps[:, bank, 0:w] if kind == "p" else ps[:, bank, 0:128].rearrange("p (s j) -> p s j", j=8)
                            S.op("pe", lambda e, c=c, k=k, rhs=rhs, bank=bank, oap=oap: e.matmul(oap, lhsT=diag[:, c % 2, k, :], rhs=rhs, start=(k == 0), stop=(k == 30)),
                                 reads=rd + [B("diag", c % 2, k)], writes=[bp])
                        S.op("act", lambda e, c=c, bank=bank: e.activation(out=y32[:, c, 0:w], in_=ps[:, bank, 0:w], func=AF.Copy), reads=[bp], writes=[B("y32", c)])
                        S.op("dve", lambda e, c=c, bank=bank: e.tensor_copy(out=ybf[c % 2][:, 0:w], in_=y32[:, c, 0:w]), reads=[B("y32", c)], writes=[B("ybf", c % 2)])
                        S.op("act", lambda e, c=c, bank=bank: e.activation(out=ysq[c % 2][:, 0:w], in_=y32[:, c, 0:w], func=AF.Square), reads=[B("y32", c)], writes=[B("ysq", c % 2)])
                        S.op("pe", lambda e, c=c: e.matmul(ps[:, 6, 0:w], lhsT=self.ones[:], rhs=ybf[c % 2][:, 0:w], start=(c == 0), stop=(c == KC - 1)),
                             reads=[B("ybf", c % 2), B("ones")], writes=[bps])
                        S.op("pe", lambda e, c=c: e.matmul(ps[:, 7, 0:w], lhsT=self.ones[:], rhs=ysq[c % 2][:, 0:w], start=(c == 0), stop=(c == KC - 1)),
                             reads=[B("ysq", c % 2), B("ones")], writes=[bpq])
                    S.op("act", lambda e: e.activation(out=mean[:, 0:w], in_=ps[:, 6, 0:w], func=AF.Copy, scale=1.0 / D), reads=[bps], writes=[B("mean")])
                    S.op("dve", lambda e: e.tensor_tensor(out=rstd[:, 0:w], in0=mean[:, 0:w], in1=mean[:, 0:w], op=ALU.mult), reads=[B("mean")], writes=[B("rstd")])
                    S.op("dve", lambda e: e.scalar_tensor_tensor(out=rstd[:, 0:w], in0=ps[:, 7, 0:w], scalar=1.0 / D, in1=rstd[:, 0:w], op0=ALU.mult, op1=ALU.subtract),
                         reads=[bpq, B("rstd")], writes=[B("rstd")])
                    S.op("act", lambda e: e.activation(out=rstd[:, 0:w], in_=rstd[:, 0:w], func=AF.Sqrt, bias=EPS, scale=1.0), reads=[B("rstd")], writes=[B("rstd")])
                    S.op("dve", lambda e: e.reciprocal(out=rstd[:, 0:w], in_=rstd[:, 0:w]), reads=[B("rstd")], writes=[B("rstd")])
                    for c in range(KC):
                        z_ = zt32[c % 2]
                        bz = B("z32", c % 2)
                        S.op("dve", lambda e, c=c, z_=z_: e.tensor_tensor(out=z_[:, 0:w], in0=y32[:, c, 0:w], in1=mean[:, 0:w], op=ALU.subtract),
                             reads=[B("y32", c), B("mean")], writes=[bz])
                        S.op("dve", lambda e, c=c, z_=z_: e.tensor_tensor(out=z_[:, 0:w], in0=z_[:, 0:w], in1=rstd[:, 0:w], op=ALU.mult),
                             reads=[bz, B("rstd")], writes=[bz])
                        S.op("act", lambda e, c=c, z_=z_: e.activation(out=zt[:, c, 0:w], in_=z_[:, 0:w], func=AF.Silu, bias=lnb(c), scale=lng(c)),
                             reads=[bz, B("vecs")], writes=[B("zt", c)])
                    for c in range(KC):
                        bank = 2 + (c % 2)
                        bp = B("ps", bank)
                        for k in range(KC):
                            S.op("pe", lambda e, c=c, k=k, bank=bank: e.matmul(ps[:, bank, 0:w], lhsT=w_out[:, k, c * 128:(c + 1) * 128], rhs=zt[:, k, 0:w],
                                                                              start=(k == 0), stop=(k == KC - 1)), reads=[B("cwout"), B("zt", k)], writes=[bp])
                        self.resid(i, 0, c, t0, w, kind, ps[:, bank, 0:w], [bp])
                S.emit()


    def nsa_proj(self, i, ntiles, aux):
        S, B, nc, ps = self.S, self.B, self.nc, self.ps
        with ExitStack() as ph:
            A = lambda n, s, d=F32: ph.enter_context(nc.sbuf_tensor(self.un(n), list(s), d))
            w_kv = A("n_wkv", [128, KC, 1536], BF16)
            KcT = A("n_KcT", [64, 2, 4, 2048], BF16)
            w1 = A("n_w1", [64, 64, 256], BF16)
            w2 = A("n_w2", [128, 2, 2, 64], BF16)
            hT = A("n_hT", [128, KC, 128], BF16)
            rows = A("n_rows", [128, 1536])
            vst = A("n_vst", [128, 2, 256], BF16)
            kst = A("n_kst", [64, 2, 4, 128], BF16)
            pe2 = A("n_pe2", [64, 2, 128])
            HT = A("n_HT", [128, 2, 160], BF16)
            tmp = [A("n_t%d" % k, [128, 128]) for k in range(2)]
            sq = [A("n_sq%d" % k, [128, 128], BF16) for k in range(2)]
            rs = A("n_rs", [128, 128])
            self.coefs(i, 0)
            for hh in range(3):
                S.dma("pool", lambda e, hh=hh: e.dma_start(out=w_kv[:, :, hh * 512:(hh + 1) * 512],
                                                           in_=self.d_nsa_w_in[:, 1024 + hh * 512:1024 + (hh + 1) * 512].rearrange("(c p) n -> p c n", p=128)), writes=[B("wkv")])
            S.dma("sp", lambda e: e.dma_start(out=pe2[:], in_=self.d_pe2.rearrange("d (x t) -> d x t", x=2)), writes=[B("pe2")])
            for X in range(2):
                S.dma("pool", lambda e, X=X: e.dma_start(out=w2[:, X, :, :], in_=self.d_w2[X].rearrange("(c p) d -> p c d", p=128)), writes=[B("w2")])
            S.op("pool", lambda e: e.memset(HT[:], 0.0), writes=[B("HT")])
            for t in range(ntiles):
                l0 = t * 128 if aux else (t + 1) * 128
                pt = t if aux else 16 + t
                tok0 = t * 128
                self.norm_mod(i, 0, l0, 128, "p", hT, 0, tmp, sq, rs, "nh")
                hrd = [B("nh", k, 0) for k in range(KC)]
                for gq in range(3):
                    for k in range(KC):
                        S.op("pe", lambda e, gq=gq, k=k: e.matmul(ps[:, gq, 0:512], lhsT=hT[:, k, :], rhs=w_kv[:, k, gq * 512:(gq + 1) * 512],
                                                                 start=(k == 0), stop=(k == KC - 1)), reads=hrd + [B("wkv")], writes=[B("ps", gq)])
                    S.op("act", lambda e, gq=gq: e.activation(out=rows[:, gq * 512:(gq + 1) * 512], in_=ps[:, gq, 0:512], func=AF.Copy),
                         reads=[B("ps", gq)], writes=[B("rows", gq)])
                if not aux:
                    S.dma("sp", lambda e: e.dma_start(out=self.o_rows[tok0:tok0 + 128, :], in_=rows[:, 0:1024]), reads=[B("rows", 0), B("rows", 1)])
                    if t >= 12:
                        S.dma("sp", lambda e: e.dma_start(out=self.o_win[(t - 12) * 128:(t - 11) * 128, :], in_=rows[:, 1024:1536]), reads=[B("rows", 2)])
                S.op("dve", lambda e: e.tensor_copy(out=vst[:, 0, :], in_=rows[:, 768:1024]), reads=[B("rows", 1)], writes=[B("vst", 0)])
                S.op("dve", lambda e: e.tensor_copy(out=vst[:, 1, :], in_=rows[:, 1280:1536]), reads=[B("rows", 2)], writes=[B("vst", 1)])
                S.dma("sp", lambda e: e.dma_start(out=self.sc_Vs[pt], in_=vst[:, 0, :]), reads=[B("vst", 0)], writes=[B("scVs", pt)])
                S.dma("sp", lambda e: e.dma_start(out=self.sc_wV[pt], in_=vst[:, 1, :]), reads=[B("vst", 1)], writes=[B("scwV", pt)])
                for gi, off in enumerate((0, 256, 512, 1024)):
                    bank = 3 + gi % 2
                    for kvh in range(4):
                        for k in range(KC):
                            S.op("pe", lambda e, kvh=kvh, k=k, bank=bank, off=off: e.matmul(
                                ps[0:64, bank, kvh * 128:(kvh + 1) * 128], lhsT=w_kv[:, k, off + kvh * 64:off + kvh * 64 + 64], rhs=hT[:, k, :],
                                start=(k == 0), stop=(k == KC - 1)), reads=hrd + [B("wkv")], writes=[B("ps", bank)])
                    src3 = ps[0:64, bank, 0:512].rearrange("p (k t) -> p k t", k=4)
                    if gi < 2:
                        S.op("dve", lambda e, gi=gi, src3=src3: e.tensor_tensor(out=KcT[:, gi, :, tok0:tok0 + 128], in0=src3,
                                                                                 in1=bc(pe2[:, gi, :].unsqueeze(1), [64, 4, 128]), op=ALU.add),
                             reads=[B("ps", bank), B("pe2")], writes=[B("KcT", gi)])
                    else:
                        S.op("act", lambda e, gi=gi, src3=src3: e.activation(out=kst[:, gi - 2, :, :], in_=src3, func=AF.Copy),
                             reads=[B("ps", bank)], writes=[B("kst", gi - 2)])
                S.dma("sp", lambda e: e.dma_start(out=self.sc_Ks[:, :, pt * 128:(pt + 1) * 128], in_=kst[:, 0, :, :]), reads=[B("kst", 0)], writes=[B("scKs", pt)])
                S.dma("sp", lambda e: e.dma_start(out=self.sc_wK[pt].rearrange("d (k t) -> d k t", k=4), in_=kst[:, 1, :, :]), reads=[B("kst", 1)], writes=[B("scwK", pt)])
            slot0 = 0 if aux else 32
            for X in range(2):
                S.dma("pool", lambda e, X=X: e.dma_start(out=w1[:], in_=self.d_w1[X].rearrange("(i d) j -> d i j", d=64)), writes=[B("w1")])
                kv4 = KcT[:, X, :, :].rearrange("d k (n i) -> d k n i", i=64)
                for jc in range(2):
                    for ii in range(64):
                        S.op("pe", lambda e, jc=jc, ii=ii, kv4=kv4: e.matmul(
                            ps[:, 5, 0:128].rearrange("p (k n) -> p k n", k=4), lhsT=w1[:, ii, jc * 128:(jc + 1) * 128], rhs=kv4[:, :, :, ii],
                            start=(ii == 0), stop=(ii == 63)), reads=[B("w1"), B("KcT", X)], writes=[B("ps", 5)])
                    S.op("act", lambda e, jc=jc: e.activation(out=HT[:, jc, 32:160], in_=ps[:, 5, 0:128], func=AF.Silu), reads=[B("ps", 5)], writes=[B("HT")])
                if X == 0:
                    for jc in range(2):
                        S.op("pe", lambda e, jc=jc: e.matmul(ps[0:64, 6, 0:128], lhsT=w2[:, 0, jc, :], rhs=HT[:, jc, 32:160], start=(jc == 0), stop=(jc == 1)),
                             reads=[B("w2"), B("HT")], writes=[B("ps", 6)])
                    S.op("act", lambda e: e.activation(out=self.ckT[:, :, slot0:slot0 + 32], in_=ps[0:64, 6, 0:128].rearrange("p (k n) -> p k n", k=4), func=AF.Copy),
                         reads=[B("ps", 6)], writes=[B("ckT")])
                else:
                    for kvh in range(4):
                        for jc in range(2):
                            if slot0 == 0:
                                oap, lap = ps[0:32, 7, kvh * 64:(kvh + 1) * 64], HT[:, jc, 32 + kvh * 32:64 + kvh * 32]
                            else:
                                oap, lap = ps[0:64, 7, kvh * 64:(kvh + 1) * 64], HT[:, jc, kvh * 32:kvh * 32 + 64]
                            S.op("pe", lambda e, jc=jc, oap=oap, lap=lap: e.matmul(oap, lhsT=lap, rhs=w2[:, 1, jc, :], start=(jc == 0), stop=(jc == 1)),
                                 reads=[B("w2"), B("HT")], writes=[B("ps", 7)])
                    S.op("act", lambda e: e.activation(out=self.cv[slot0:slot0 + 32, :, :], in_=ps[slot0:slot0 + 32, 7, 0:256].rearrange("p (k d) -> p k d", k=4), func=AF.Copy),
                         reads=[B("ps", 7)], writes=[B("cv")])
            S.emit()

    def nsa_attn(self, i):
        S, B, nc, ps = self.S, self.B, self.nc, self.ps
        BIG = 240000.0
        with ExitStack() as ph:
            A = lambda n, s, d=F32: ph.enter_context(nc.sbuf_tensor(self.un(n), list(s), d))
            KsA = A("a_KsA", [128, 4, 4096], BF16)
            Vs = A("a_Vs", [128, 32, 4, 65], BF16)
            KwR = A("a_KwR", [64, 5, 4, 128], BF16)
            VwR = A("a_VwR", [128, 5, 4, 65], BF16)
            w_q = A("a_wq", [128, KC, 1072], BF16)
            w_o = A("a_wo", [128, KC, D], BF16)
            hT = A("a_hT", [128, KC, 128], BF16)
            QA = [A("a_QA%d" % k, [128, 4, 4, 128], BF16) for k in range(1)] * 2
            PT_ = [A("a_PT%d" % k, [128, 512], BF16) for k in range(2)]
            gates = [A("a_gates%d" % k, [128, 48]) for k in range(1)] * 2
            mk = A("a_mk", [128, 4, 64])
            tri = A("a_tri", [128, 2, 128], BF16)
            e4 = A("a_e4", [128, 16, 64])
            sm = A("a_sm", [128, 16])
            imp = A("a_imp", [128, 4, 64])
            imp2 = A("a_imp2", [128, 4, 64])
            wk = A("a_wk", [128, 4, 64])
            m8 = A("a_m8", [128, 4, 16])
            mbs = A("a_mbs", [128, 4, 128])
            pT = A("a_pT", [64, 4, 128], BF16)
            ocmp = [A("a_ocmp%d" % k, [128, 16, 64], BF16) for k in range(1)] * 2
            otot = A("a_otot", [128, 4, 4, 64])
            ototT = A("a_ototT", [128, KC, 128], BF16)
            ctmp = A("a_ctmp", [128, 4, 64])
            cf = A("a_cf", [128, 12])
            tmp = [imp2[:, 0:2, :].rearrange("p a b -> p (a b)"), wk[:, 0:2, :].rearrange("p a b -> p (a b)")]
            oTap = e4[0:65, 0:8, :].rearrange("p a b -> p (a b)")
            sq = [A("a_sq%d" % k, [128, 128], BF16) for k in range(2)]
            rs = A("a_rs", [128, 128])
            self.coefs(i, 0)
            S.dma("sp", lambda e: e.dma_start(out=KsA[0:64, :, :], in_=self.sc_Ks), reads=[B("scKs", t) for t in range(32)], writes=[B("KsA")])
            for kvh in range(4):
                S.dma("pool", lambda e, kvh=kvh: e.dma_start(out=KsA[64:128, kvh, :], in_=self.d_onehot), writes=[B("KsA")])
            S.op("pool", lambda e: e.memset(Vs[:], 1.0), writes=[B("Vs")])
            S.op("pool", lambda e: e.memset(VwR[:], 1.0), writes=[B("VwR", r_) for r_ in range(5)])
            for t in range(32):
                S.dma("sp", lambda e, t=t: e.dma_start(out=Vs[:, t, :, 0:64], in_=self.sc_Vs[t].rearrange("p (k d) -> p k d", k=4)), reads=[B("scVs", t)], writes=[B("Vs")])
            S.dma("pool", lambda e: e.dma_start(out=w_q[:, :, 0:1024], in_=self.d_nsa_w_in[:, 0:1024].rearrange("(c p) n -> p c n", p=128)), writes=[B("wq")])
            S.dma("pool", lambda e: e.dma_start(out=w_q[:, :, 1024:1072], in_=self.d_nsa_w_in[:, 2560:2608].rearrange("(c p) n -> p c n", p=128)), writes=[B("wq")])
            S.dma("pool", lambda e: e.dma_start(out=w_o[:], in_=self.d_nsa_w_out.rearrange("(c p) n -> p c n", p=128)), writes=[B("wo")])
            S.dma("pool", lambda e: e.dma_start(out=tri[:], in_=self.d_tri.rearrange("p (a t) -> p a t", a=2)), writes=[B("tri")])
            S.op("pool", lambda e: e.memset(mbs[:], 0.0), writes=[B("mbs")])

            def load_ring(pt):
                r_ = pt % 5
                S.dma("sp", lambda e: e.dma_start(out=KwR[:, r_, :, :], in_=self.sc_wK[pt].rearrange("d (k t) -> d k t", k=4)), reads=[B("scwK", pt)], writes=[B("KwR", r_)])
                S.dma("sp", lambda e: e.dma_start(out=VwR[:, r_, :, 0:64], in_=self.sc_wV[pt].rearrange("p (k d) -> p k d", k=4)), reads=[B("scwV", pt)], writes=[B("VwR", r_)])
            for pt in range(11, 15):
                load_ring(pt)

            def chain(qt):
                sl = 0
                l0 = qt * 128
                qa = QA[sl]
                bqa = B("QA", sl)
                load_ring(15 + qt)
                S.dma("sp", lambda e: e.dma_start(out=mk[:], in_=self.d_masks[qt].rearrange("p (a n) -> p a n", a=4)), writes=[B("mk")])
                self.norm_mod(i, 0, l0, 128, "p", hT, 0, tmp, sq, rs, "ah")
                hrd = [B("ah", k, 0) for k in range(KC)]
                for k in range(KC):
                    S.op("pe", lambda e, k=k: e.matmul(ps[:, 6, 0:48], lhsT=hT[:, k, :], rhs=w_q[:, k, 1024:1072], start=(k == 0), stop=(k == KC - 1)),
                         reads=hrd + [B("wq")], writes=[B("ps", 6)])
                S.op("act", lambda e: e.activation(out=gates[sl][:], in_=ps[:, 6, 0:48], func=AF.Sigmoid), reads=[B("ps", 6)], writes=[B("gates", sl)])
                for kvh in range(4):
                    bank = 4 + kvh % 2
                    for g in range(4):
                        hd = kvh * 4 + g
                        for k in range(KC):
                            S.op("pe", lambda e, g=g, k=k, hd=hd, bank=bank: e.matmul(ps[0:64, bank, g * 128:(g + 1) * 128], lhsT=w_q[:, k, hd * 64:(hd + 1) * 64], rhs=hT[:, k, :],
                                                                                     start=(k == 0), stop=(k == KC - 1)), reads=hrd + [B("wq")], writes=[B("ps", bank)])
                    S.op("act", lambda e, kvh=kvh, bank=bank: e.activation(out=qa[0:64, kvh, :, :], in_=ps[0:64, bank, 0:512].rearrange("p (g t) -> p g t", g=4), func=AF.Copy),
                         reads=[B("ps", bank)], writes=[bqa])
                for kvh in range(4):
                    for g in range(4):
                        hd = kvh * 4 + g
                        S.op("pe", lambda e, g=g, kvh=kvh, hd=hd: e.matmul(ps[:, 4 + hd // 8, (hd % 8) * 64:(hd % 8 + 1) * 64], lhsT=qa[0:64, kvh, g, :], rhs=self.ckT[:, kvh, :], start=True, stop=True),
                             reads=[bqa, B("ckT")], writes=[B("ps", 4 + hd // 8)])
                for hb in range(2):
                    S.op("dve", lambda e, hb=hb: e.scalar_tensor_tensor(out=e4[:, hb * 8:(hb + 1) * 8, :], in0=ps[:, 4 + hb, 0:512].rearrange("p (g n) -> p g n", g=8), scalar=0.125,
                                                                       in1=bc(mk[:, 0, :].unsqueeze(1), [128, 8, 64]), op0=ALU.mult, op1=ALU.add),
                         reads=[B("ps", 4 + hb), B("mk")], writes=[B("e4")])
                S.op("act", lambda e: e.activation(out=e4[:], in_=e4[:], func=AF.Exp), reads=[B("e4")], writes=[B("e4")])
                S.op("dve", lambda e: e.tensor_reduce(out=sm[:], in_=e4[:], axis=AX.X, op=ALU.add), reads=[B("e4")], writes=[B("sm")])
                S.op("dve", lambda e: e.tensor_scalar(out=sm[:], in0=sm[:], scalar1=1e-30, scalar2=None, op0=ALU.add), reads=[B("sm")], writes=[B("sm")])
                S.op("dve", lambda e: e.reciprocal(out=sm[:], in_=sm[:]), reads=[B("sm")], writes=[B("sm")])
                S.op("dve", lambda e: e.tensor_tensor(out=e4[:], in0=e4[:], in1=bc(sm[:].unsqueeze(2), [128, 16, 64]), op=ALU.mult),
                     reads=[B("sm"), B("e4")], writes=[B("e4")])
                for kvh in range(4):
                    S.op("dve", lambda e, kvh=kvh: e.tensor_reduce(out=imp[:, kvh, :], in_=e4[:, kvh * 4:(kvh + 1) * 4, :].rearrange("p g n -> p n g"), axis=AX.X, op=ALU.add),
                         reads=[B("e4")], writes=[B("imp")])
                S.op("dve", lambda e: e.tensor_tensor(out=imp[:], in0=imp[:], in1=bc(mk[:, 1, :].unsqueeze(1), [128, 4, 64]), op=ALU.mult), reads=[B("imp"), B("mk")], writes=[B("imp")])
                S.op("dve", lambda e: e.tensor_tensor(out=imp[:], in0=imp[:], in1=bc(mk[:, 2, :].unsqueeze(1), [128, 4, 64]), op=ALU.add), reads=[B("imp"), B("mk")], writes=[B("imp")])
                for kvh in range(4):
                    S.op("dve", lambda e, kvh=kvh: e.max(out=m8[:, kvh, 0:8], in_=imp[:, kvh, :]), reads=[B("imp")], writes=[B("m8")])
                    S.op("dve", lambda e, kvh=kvh: e.match_replace(out=wk[:, kvh, :], in_to_replace=m8[:, kvh, 0:8], in_values=imp[:, kvh, :], imm_value=-3e30), reads=[B("imp"), B("m8")], writes=[B("wk")])
                    S.op("dve", lambda e, kvh=kvh: e.max(out=m8[:, kvh, 8:16], in_=wk[:, kvh, :]), reads=[B("wk")], writes=[B("m8")])
                    S.op("dve", lambda e, kvh=kvh: e.match_replace(out=wk[:, kvh, :], in_to_replace=m8[:, kvh, 8:16], in_values=wk[:, kvh, :], imm_value=-3e30), reads=[B("wk"), B("m8")], writes=[B("wk")])
                S.op("dve", lambda e: e.tensor_tensor(out=imp2[:], in0=imp[:], in1=wk[:], op=ALU.subtract), reads=[B("wk"), B("imp")], writes=[B("imp2")])
                S.op("dve", lambda e: e.tensor_scalar(out=imp2[:], in0=imp2[:], scalar1=1.0, scalar2=None, op0=ALU.min), reads=[B("imp2")], writes=[B("imp2")])
                S.op("dve", lambda e: e.tensor_tensor(out=imp2[:], in0=imp2[:], in1=bc(mk[:, 3, :].unsqueeze(1), [128, 4, 64]), op=ALU.mult), reads=[B("imp2"), B("mk")], writes=[B("imp2")])
                S.op("dve", lambda e: e.tensor_scalar(out=mbs[:, :, 64:128], in0=imp2[:], scalar1=-1.0, scalar2=BIG, op0=ALU.add, op1=ALU.mult),
                     reads=[B("imp2")], writes=[B("mbs")])
                for kvh in range(4):
                    S.op("pe", lambda e, kvh=kvh: e.transpose(out=ps[:, 4, kvh * 128:(kvh + 1) * 128], in_=mbs[:, kvh, :], identity=self.ident[:]), reads=[B("mbs"), B("ident")], writes=[B("ps", 4)])
                for kvh in range(4):
                    S.op("act", lambda e, kvh=kvh: e.activation(out=qa[64:128, kvh, :, :], in_=bc(ps[64:128, 4, kvh * 128:(kvh + 1) * 128].unsqueeze(1), [64, 4, 128]), func=AF.Copy),
                         reads=[B("ps", 4)], writes=[bqa])
                for kvh in range(4):
                    for g in range(4):
                        S.op("pe", lambda e, g=g, kvh=kvh: e.transpose(out=ps[0:64, 5, g * 128:(g + 1) * 128], in_=e4[:, kvh * 4 + g, :], identity=self.ident[:]),
                             reads=[B("e4"), B("ident")], writes=[B("ps", 5)])
                    S.op("act", lambda e: e.activation(out=pT[:], in_=ps[0:64, 5, 0:512].rearrange("p (g t) -> p g t", g=4), func=AF.Copy), reads=[B("ps", 5)], writes=[B("pT")])
                    for g in range(4):
                        S.op("pe", lambda e, g=g, kvh=kvh: e.matmul(ps[:, 6, g * 64:(g + 1) * 64], lhsT=pT[:, g, :], rhs=self.cv[:, kvh, :], start=True, stop=True),
                             reads=[B("pT"), B("cv")], writes=[B("ps", 6)])
                    S.op("act", lambda e, kvh=kvh: e.activation(out=ocmp[sl][:, kvh * 4:(kvh + 1) * 4, :], in_=ps[:, 6, 0:256].rearrange("p (g d) -> p g d", g=4), func=AF.Copy),
                         reads=[B("ps", 6)], writes=[B("ocmp", sl)])

            tcnt = [0]

            def attend(qt):
                sl = 0
                l0 = qt * 128
                dk = 15 + qt
                qa_all = QA[sl]
                bqa = B("QA", sl)
                gt = gates[sl]
                tl = []
                for kvh in range(4):
                    for kt in range(dk + 1):
                        tl.append((kvh, 0, kt, kt == 0, kt == dk))
                    for wi, kt in enumerate(range(dk - 4, dk + 1)):
                        tl.append((kvh, 1, kt, wi == 0, wi == 4))

                def emit_S(n):
                    kvh, br, kt, first, last = tl[n]
                    sb_ = n % 2
                    if br == 0:
                        lhs, rhs, rk = KsA[:, kvh, kt * 128:(kt + 1) * 128], qa_all[:, kvh, :, :].rearrange("p g t -> p (g t)"), B("KsA")
                    else:
                        lhs, rhs, rk = KwR[:, kt % 5, kvh, :], qa_all[0:64, kvh, :, :].rearrange("p g t -> p (g t)"), B("KwR", kt % 5)
                    S.op("pe", lambda e, lhs=lhs, rhs=rhs, sb_=sb_: e.matmul(ps[:, sb_, 0:512], lhsT=lhs, rhs=rhs, start=True, stop=True), reads=[rk, bqa], writes=[B("ps", sb_)])

                def emit_rest(n):
                    kvh, br, kt, first, last = tl[n]
                    sb_ = n % 2
                    pt_ = PT_[sb_]
                    pt3 = pt_[:].rearrange("p (g t) -> p g t", g=4)
                    S.op("act", lambda e, sb_=sb_, pt_=pt_: e.activation(out=pt_[:], in_=ps[:, sb_, 0:512], func=AF.Exp, scale=0.125), reads=[B("ps", sb_)], writes=[B("PT", sb_)])
                    if (br == 0 and last) or (br == 1 and (first or last)):
                        ti_ = 1 if (br == 1 and first) else 0
                        S.op("dve", lambda e, pt3=pt3, ti_=ti_: e.tensor_tensor(out=pt3, in0=pt3, in1=bc(tri[:, ti_, :].unsqueeze(1), [128, 4, 128]), op=ALU.mult),
                             reads=[B("PT", sb_), B("tri")], writes=[B("PT", sb_)])
                    if br == 1 and kt <= 15:
                        S.op("dve", lambda e, pt_=pt_: e.tensor_scalar(out=pt_[:], in0=pt_[:], scalar1=self.flag[:, 0:1], scalar2=None, op0=ALU.mult),
                             reads=[B("PT", sb_), B("flag")], writes=[B("PT", sb_)])
                    if br == 0:
                        vv, rv = Vs[:, kt, kvh, :], B("Vs")
                    else:
                        vv, rv = VwR[:, kt % 5, kvh, :], B("VwR", kt % 5)
                    S.op("pe", lambda e, vv=vv, pt_=pt_, br=br, first=first, last=last: e.matmul(ps[0:65, 2 + br, 0:512], lhsT=vv, rhs=pt_[:], start=first, stop=last),
                         reads=[B("PT", sb_), rv], writes=[B("ps", 2 + br)])
                    if not last:
                        return
                    bk = 2 + br
                    S.op("act", lambda e, bk=bk: e.activation(out=oTap, in_=ps[0:65, bk, 0:512], func=AF.Copy), reads=[B("ps", bk)], writes=[B("e4")])
                    for g in range(4):
                        S.op("pe", lambda e, bk=bk, g=g: e.transpose(out=ps[:, bk + 2, g * 65:(g + 1) * 65], in_=oTap[:, g * 128:(g + 1) * 128], identity=self.ident[0:65, 0:65]),
                             reads=[B("e4"), B("ident")], writes=[B("ps", bk + 2)])
                    if br == 0:
                        return
                    osel = ps[:, 4, 0:260].rearrange("p (g d) -> p g d", g=4)
                    owin = ps[:, 5, 0:260].rearrange("p (g d) -> p g d", g=4)
                    oc = ocmp[sl][:, kvh * 4:(kvh + 1) * 4, :]
                    S.op("dve", lambda e, osel=osel: e.tensor_scalar(out=cf[:, 0:4], in0=osel[:, :, 64], scalar1=1e-30, scalar2=None, op0=ALU.add), reads=[B("ps", 4)], writes=[B("cf")])
                    S.op("dve", lambda e, owin=owin: e.tensor_scalar(out=cf[:, 4:8], in0=owin[:, :, 64], scalar1=1e-30, scalar2=None, op0=ALU.add), reads=[B("ps", 5)], writes=[B("cf")])
                    S.op("dve", lambda e: e.reciprocal(out=cf[:, 0:8], in_=cf[:, 0:8]), reads=[B("cf")], writes=[B("cf")])
                    S.op("dve", lambda e, kvh=kvh: e.tensor_tensor(out=cf[:, 0:4], in0=cf[:, 0:4], in1=gt[:, 16 + kvh * 4:20 + kvh * 4], op=ALU.mult), reads=[B("cf"), B("gates", sl)], writes=[B("cf")])
                    S.op("dve", lambda e, kvh=kvh: e.tensor_tensor(out=cf[:, 4:8], in0=cf[:, 4:8], in1=gt[:, 32 + kvh * 4:36 + kvh * 4], op=ALU.mult), reads=[B("cf"), B("gates", sl)], writes=[B("cf")])
                    ot = otot[:, kvh, :, :]
                    S.op("dve", lambda e, ot=ot, oc=oc, kvh=kvh: e.tensor_tensor(out=ot, in0=oc, in1=bc(gt[:, kvh * 4:kvh * 4 + 4].unsqueeze(2), [128, 4, 64]), op=ALU.mult),
                         reads=[B("ocmp", sl), B("gates", sl)], writes=[B("otot")])
                    for (src, c0, bk2) in ((osel, 0, 4), (owin, 4, 5)):
                        S.op("dve", lambda e, src=src, c0=c0: e.tensor_tensor(out=ctmp[:], in0=src[:, :, 0:64], in1=bc(cf[:, c0:c0 + 4].unsqueeze(2), [128, 4, 64]), op=ALU.mult),
                             reads=[B("ps", bk2), B("cf")], writes=[B("ctmp")])
                        S.op("dve", lambda e, ot=ot: e.tensor_tensor(out=ot, in0=ot, in1=ctmp[:], op=ALU.add), reads=[B("ctmp"), B("otot")], writes=[B("otot")])

                emit_S(0)
                for n in range(len(tl)):
                    if n + 1 < len(tl):
                        emit_S(n + 1)
                    emit_rest(n)
                of = otot[:].rearrange("p k g d -> p (k g d)")
                for hh in range(2):
                    for c4 in range(4):
                        c = hh * 4 + c4
                        S.op("pe", lambda e, c=c, c4=c4, hh=hh, of=of: e.transpose(out=ps[:, hh, c4 * 128:(c4 + 1) * 128], in_=of[:, c * 128:(c + 1) * 128], identity=self.ident[:]),
                             reads=[B("otot"), B("ident")], writes=[B("ps", hh)])
                    S.op("act", lambda e, hh=hh: e.activation(out=ototT[:, hh * 4:(hh + 1) * 4, :], in_=ps[:, hh, 0:512].rearrange("p (c t) -> p c t", c=4), func=AF.Copy),
                         reads=[B("ps", hh)], writes=[B("ototT")])
                for c in range(KC):
                    bank = 6 + (c % 2)
                    for k in range(KC):
                        S.op("pe", lambda e, c=c, k=k, bank=bank: e.matmul(ps[:, bank, 0:128], lhsT=w_o[:, k, c * 128:(c + 1) * 128], rhs=ototT[:, k, :], start=(k == 0), stop=(k == KC - 1)),
                             reads=[B("wo"), B("ototT")], writes=[B("ps", bank)])
                    self.resid(i, 0, c, l0, 128, "p", ps[:, bank, 0:128], [B("ps", bank)])

            for qt in range(17):
                chain(qt)
                attend(qt)
            S.emit()

    def nsa_sample_prep(self):
        S, B, nc, ps = self.S, self.B, self.nc, self.ps
        with ExitStack() as ph:
            A = lambda n, s, d=F32: ph.enter_context(nc.sbuf_tensor(self.un(n), list(s), d))
            w1 = A("s_w1", [64, 64, 256], BF16)
            w2 = A("s_w2", [128, 2, 2, 64], BF16)
            KcT = A("s_KcT", [64, 4, 2048], BF16)
            pgt = [A("s_pg%d" % k, [128, 1024]) for k in range(4)]
            kst = [A("s_kst%d" % k, [64, 4, 128], BF16) for k in range(2)]
            vcs = [A("s_vcs%d" % k, [64, 4, 128], BF16) for k in range(2)]
            vst = [A("s_vst%d" % k, [128, 256], BF16) for k in range(2)]
            pe2 = A("s_pe2", [64, 2, 128])
            HT = A("s_HT", [128, 2, 160], BF16)
            idx = A("s_idx", [128, 256], I32)
            iot = A("s_iot", [128, 1])
            with ExitStack() as ph1:
                ptb = ph1.enter_context(nc.sbuf_tensor(self.un("s_ptb"), [128, 256], I32))
                ptf = ph1.enter_context(nc.sbuf_tensor(self.un("s_ptf"), [128, 256], F32))
                S.dma("sp", lambda e: e.dma_start(out=ptb[:], in_=self.d_pt.partition_broadcast(128)), writes=[B("ptb")])
                S.dma("sp", lambda e: e.dma_start(out=iot[:], in_=self.d_iota), writes=[B("iot")])
                S.op("dve", lambda e: e.tensor_copy(out=ptf[:], in_=ptb[:]), reads=[B("ptb")], writes=[B("ptf")])
                S.op("dve", lambda e: e.tensor_scalar(out=ptf[:], in0=ptf[:], scalar1=128.0, scalar2=iot[:, 0:1], op0=ALU.mult, op1=ALU.add),
                     reads=[B("ptf"), B("iot")], writes=[B("ptf")])
                S.op("dve", lambda e: e.tensor_copy(out=idx[:], in_=ptf[:]), reads=[B("ptf")], writes=[B("idx")])
                S.emit()
            S.dma("sp", lambda e: e.dma_start(out=pe2[:], in_=self.d_pe2.rearrange("d (x t) -> d x t", x=2)), writes=[B("pe2")])
            for X in range(2):
                S.dma("pool", lambda e, X=X: e.dma_start(out=w2[:, X, :, :], in_=self.d_w2[X].rearrange("(c p) d -> p c d", p=128)), writes=[B("w2")])
            S.op("pool", lambda e: e.memset(HT[:], 0.0), writes=[B("HT")])
            S.dma("sp", lambda e: e.dma_start(out=self.o_swinp, in_=self.d_swin[:, 8:512, :]))

            def compress(X, sq_):
                r0 = 32 * (sq_ % 2)
                kv4 = KcT[:].rearrange("d k (n i) -> d k n i", i=64)
                for jc in range(2):
                    for ii in range(64):
                        S.op("pe", lambda e, jc=jc, ii=ii, kv4=kv4: e.matmul(
                            ps[:, 5, 0:128].rearrange("p (k n) -> p k n", k=4), lhsT=w1[:, ii, jc * 128:(jc + 1) * 128], rhs=kv4[:, :, :, ii],
                            start=(ii == 0), stop=(ii == 63)), reads=[B("w1"), B("KcT")], writes=[B("ps", 5)])
                    S.op("act", lambda e, jc=jc: e.activation(out=HT[:, jc, 32:160], in_=ps[:, 5, 0:128], func=AF.Silu), reads=[B("ps", 5)], writes=[B("HT")])
                if X == 0:
                    for jc in range(2):
                        S.op("pe", lambda e, jc=jc: e.matmul(ps[0:64, 6, 0:128], lhsT=w2[:, 0, jc, :], rhs=HT[:, jc, 32:160], start=(jc == 0), stop=(jc == 1)),
                             reads=[B("w2"), B("HT")], writes=[B("ps", 6)])
                    S.op("act", lambda e: e.activation(out=self.CK_all[:, :, sq_ * 32:(sq_ + 1) * 32], in_=ps[0:64, 6, 0:128].rearrange("p (k n) -> p k n", k=4), func=AF.Copy),
                         reads=[B("ps", 6)], writes=[B("CKall")])
                else:
                    for kvh in range(4):
                        for jc in range(2):
                            if r0 == 0:
                                oap, lap = ps[0:32, 7, kvh * 64:(kvh + 1) * 64], HT[:, jc, 32 + kvh * 32:64 + kvh * 32]
                            else:
                                oap, lap = ps[0:64, 7, kvh * 64:(kvh + 1) * 64], HT[:, jc, kvh * 32:kvh * 32 + 64]
                            S.op("pe", lambda e, jc=jc, oap=oap, lap=lap: e.matmul(oap, lhsT=lap, rhs=w2[:, 1, jc, :], start=(jc == 0), stop=(jc == 1)),
                                 reads=[B("w2"), B("HT")], writes=[B("ps", 7)])
                    S.op("act", lambda e: e.activation(out=self.CV_all[r0:r0 + 32, sq_ // 2, :, :], in_=ps[r0:r0 + 32, 7, 0:256].rearrange("p (k d) -> p k d", k=4), func=AF.Copy),
                         reads=[B("ps", 7)], writes=[B("CVall")])

            S.dma("pool", lambda e: e.dma_start(out=w1[:], in_=self.d_w1[0].rearrange("(i d) j -> d i j", d=64)), writes=[B("w1")])
            n = 0
            for sq_ in range(16):
                for pg in range(16):
                    pb = n % 4
                    kb = n % 2
                    n += 1
                    pt_ = pgt[pb]
                    col = sq_ * 16 + pg
                    S.dma("pool", lambda e, pt_=pt_, col=col: e.indirect_dma_start(out=pt_[:], out_offset=None, in_=self.d_cache,
                          in_offset=bass.IndirectOffsetOnAxis(ap=idx[:, col:col + 1], axis=0)), reads=[B("idx")], writes=[B("pg", pb)])
                    for X in range(3):
                        bank = X % 2
                        for kvh in range(4):
                            S.op("pe", lambda e, X=X, kvh=kvh, bank=bank, pt_=pt_: e.transpose(out=ps[0:64, bank, kvh * 128:(kvh + 1) * 128],
                                                                                              in_=pt_[:, X * 256 + kvh * 64:X * 256 + kvh * 64 + 64], identity=self.ident[:]),
                                 reads=[B("pg", pb), B("ident")], writes=[B("ps", bank)])
                        src3 = ps[0:64, bank, 0:512].rearrange("p (k t) -> p k t", k=4)
                        if X == 0:
                            S.op("dve", lambda e, src3=src3, pg=pg: e.tensor_tensor(out=KcT[:, :, pg * 128:(pg + 1) * 128], in0=src3,
                                                                                    in1=bc(pe2[:, 0, :].unsqueeze(1), [64, 4, 128]), op=ALU.add),
                                 reads=[B("ps", bank), B("pe2")], writes=[B("KcT")])
                        elif X == 1:
                            S.op("dve", lambda e, src3=src3, kb=kb: e.tensor_tensor(out=vcs[kb][:], in0=src3, in1=bc(pe2[:, 1, :].unsqueeze(1), [64, 4, 128]), op=ALU.add),
                                 reads=[B("ps", bank), B("pe2")], writes=[B("vcs", kb)])
                        else:
                            S.op("act", lambda e, src3=src3, kb=kb: e.activation(out=kst[kb][:], in_=src3, func=AF.Copy), reads=[B("ps", bank)], writes=[B("kst", kb)])
                    S.op("pool", lambda e, pt_=pt_, kb=kb: e.tensor_copy(out=vst[kb][:], in_=pt_[:, 768:1024]), reads=[B("pg", pb)], writes=[B("vst", kb)])
                    S.dma("sp", lambda e, kb=kb, sq_=sq_, pg=pg: e.dma_start(out=self.sc_sKs[sq_, :, :, pg * 128:(pg + 1) * 128], in_=kst[kb][:]), reads=[B("kst", kb)], writes=[B("scsK", sq_)])
                    S.dma("sp", lambda e, kb=kb, sq_=sq_, pg=pg: e.dma_start(out=self.sc_sVc[sq_, :, :, pg * 128:(pg + 1) * 128], in_=vcs[kb][:]), reads=[B("vcs", kb)], writes=[B("scsC", sq_)])
                    S.dma("sp", lambda e, kb=kb, sq_=sq_, pg=pg: e.dma_start(out=self.sc_sVs[sq_, pg], in_=vst[kb][:]), reads=[B("vst", kb)], writes=[B("scsV", sq_)])
                compress(0, sq_)
            S.dma("pool", lambda e: e.dma_start(out=w1[:], in_=self.d_w1[1].rearrange("(i d) j -> d i j", d=64)), writes=[B("w1")])
            for sq_ in range(16):
                S.dma("sp", lambda e, sq_=sq_: e.dma_start(out=KcT[:], in_=self.sc_sVc[sq_]), reads=[B("scsC", sq_)], writes=[B("KcT")])
                compress(1, sq_)
            S.emit()

    def nsa_sample_attn(self, i):
        S, B, nc, ps = self.S, self.B, self.nc, self.ps
        BIG = 240000.0
        l0 = NPT
        with ExitStack() as ph0:
            A0 = lambda n, s, d=F32: ph0.enter_context(nc.sbuf_tensor(self.un(n), list(s), d))
            QA = A0("b_QA", [128, 4, 4, 128], BF16)
            KnA = A0("b_KnA", [128, 4, 128], BF16)
            KwN = A0("b_KwN", [64, 4, 128], BF16)
            Vns = A0("b_Vns", [128, 4, 65], BF16)
            Vnw = A0("b_Vnw", [128, 4, 65], BF16)
            gates = A0("b_gates", [128, 48])
            self.rtmp = A0("b_rtmp", [128, 128])
            self.coefs(i, 0)
            with ExitStack() as ph:
                A = lambda n, s, d=F32: ph.enter_context(nc.sbuf_tensor(self.un(n), list(s), d))
                w_kv = A("b_wkv", [128, KC, 1536], BF16)
                w_q = A("b_wq", [128, KC, 1072], BF16)
                hT = A("b_hT", [128, KC, 128], BF16)
                rows = A("b_rows", [128, 1536])
                tmp = [A("b_t%d" % k, [128, 128]) for k in range(2)]
                sq = [A("b_sq%d" % k, [128, 128], BF16) for k in range(2)]
                rs = A("b_rs", [128, 128])
                for hh in range(3):
                    S.dma("pool", lambda e, hh=hh: e.dma_start(out=w_kv[:, :, hh * 512:(hh + 1) * 512],
                                                               in_=self.d_nsa_w_in[:, 1024 + hh * 512:1024 + (hh + 1) * 512].rearrange("(c p) n -> p c n", p=128)), writes=[B("wkv")])
                S.dma("pool", lambda e: e.dma_start(out=w_q[:, :, 0:1024], in_=self.d_nsa_w_in[:, 0:1024].rearrange("(c p) n -> p c n", p=128)), writes=[B("wq")])
                S.dma("pool", lambda e: e.dma_start(out=w_q[:, :, 1024:1072], in_=self.d_nsa_w_in[:, 2560:2608].rearrange("(c p) n -> p c n", p=128)), writes=[B("wq")])
                for kvh in range(4):
                    S.dma("pool", lambda e, kvh=kvh: e.dma_start(out=KnA[64:128, kvh, :], in_=self.d_ohnew), writes=[B("KnA")])
                S.op("pool", lambda e: e.memset(Vns[:], 1.0), writes=[B("Vns")])
                S.op("pool", lambda e: e.memset(Vnw[:], 1.0), writes=[B("Vnw")])
                self.norm_mod(i, 0, l0, 128, "s", hT, 0, tmp, sq, rs, "bh")
                hrd = [B("bh", k, 0) for k in range(KC)]
                for gq in range(3):
                    for k in range(KC):
                        S.op("pe", lambda e, gq=gq, k=k: e.matmul(ps[:, gq, 0:512], lhsT=hT[:, k, :], rhs=w_kv[:, k, gq * 512:(gq + 1) * 512],
                                                                 start=(k == 0), stop=(k == KC - 1)), reads=hrd + [B("wkv")], writes=[B("ps", gq)])
                    S.op("act", lambda e, gq=gq: e.activation(out=rows[:, gq * 512:(gq + 1) * 512], in_=ps[:, gq, 0:512], func=AF.Copy),
                         reads=[B("ps", gq)], writes=[B("rows", gq)])
                S.dma("sp", lambda e: e.dma_start(out=self.o_rows[2048:2176, :], in_=rows[:, 0:1024]), reads=[B("rows", 0), B("rows", 1)])
                S.dma("sp", lambda e: e.dma_start(out=self.o_win[512:640, :], in_=rows[:, 1024:1536]), reads=[B("rows", 2)])
                S.op("dve", lambda e: e.tensor_copy(out=Vns[:, :, 0:64], in_=rows[:, 768:1024].rearrange("p (k d) -> p k d", k=4)), reads=[B("rows", 1)], writes=[B("Vns")])
                S.op("dve", lambda e: e.tensor_copy(out=Vnw[:, :, 0:64], in_=rows[:, 1280:1536].rearrange("p (k d) -> p k d", k=4)), reads=[B("rows", 2)], writes=[B("Vnw")])
                for gi, off in enumerate((512, 1024)):
                    bank = 3 + gi
                    for kvh in range(4):
                        for k in range(KC):
                            S.op("pe", lambda e, kvh=kvh, k=k, bank=bank, off=off: e.matmul(
                                ps[0:64, bank, kvh * 128:(kvh + 1) * 128], lhsT=w_kv[:, k, off + kvh * 64:off + kvh * 64 + 64], rhs=hT[:, k, :],
                                start=(k == 0), stop=(k == KC - 1)), reads=hrd + [B("wkv")], writes=[B("ps", bank)])
                    src3 = ps[0:64, bank, 0:512].rearrange("p (k t) -> p k t", k=4)
                    dst = KnA[0:64, :, :] if gi == 0 else KwN[:]
                    S.op("act", lambda e, src3=src3, dst=dst: e.activation(out=dst, in_=src3, func=AF.Copy), reads=[B("ps", bank)], writes=[B("KnA") if gi == 0 else B("KwN")])
                for k in range(KC):
                    S.op("pe", lambda e, k=k: e.matmul(ps[:, 6, 0:48], lhsT=hT[:, k, :], rhs=w_q[:, k, 1024:1072], start=(k == 0), stop=(k == KC - 1)),
                         reads=hrd + [B("wq")], writes=[B("ps", 6)])
                S.op("act", lambda e: e.activation(out=gates[:], in_=ps[:, 6, 0:48], func=AF.Sigmoid), reads=[B("ps", 6)], writes=[B("gates")])
                for kvh in range(4):
                    bank = 5 + kvh % 2
                    for g in range(4):
                        hd = kvh * 4 + g
                        for k in range(KC):
                            S.op("pe", lambda e, g=g, k=k, hd=hd, bank=bank: e.matmul(ps[0:64, bank, g * 128:(g + 1) * 128], lhsT=w_q[:, k, hd * 64:(hd + 1) * 64], rhs=hT[:, k, :],
                                                                                     start=(k == 0), stop=(k == KC - 1)), reads=hrd + [B("wq")], writes=[B("ps", bank)])
                    S.op("act", lambda e, kvh=kvh, bank=bank: e.activation(out=QA[0:64, kvh, :, :], in_=ps[0:64, bank, 0:512].rearrange("p (g t) -> p g t", g=4), func=AF.Copy),
                         reads=[B("ps", bank)], writes=[B("QAs", kvh)])
                S.emit()
            with ExitStack() as ph:
                A = lambda n, s, d=F32: ph.enter_context(nc.sbuf_tensor(self.un(n), list(s), d))
                w_o = A("b_wo", [128, KC, D], BF16)
                KsA = A("b_KsA", [128, 4, 2048], BF16)
                Vs = A("b_Vs", [128, 16, 4, 65], BF16)
                KwS = A("b_KwS", [64, 4, 512], BF16)
                VwS = A("b_VwS", [128, 4, 4, 65], BF16)
                wst = A("b_wst", [128, 4, 512])
                e4s = A("b_e4s", [128, 4, 512])
                psg = A("b_psg", [128, 512])
                pTs = A("b_pTs", [64, 8, 128], BF16)
                pad = [A("b_pad%d" % k, [128, 4, 128], BF16) for k in range(2)]
                PTn = A("b_PTn", [128, 512], BF16)
                mnew = A("b_mnew", [128, 128], BF16)
                tri = A("b_tri", [128, 2, 128], BF16)
                vbs = A("b_vbs", [128, 512])
                mk = A("b_mk", [128, 4, 64])
                sm = A("b_sm", [128, 8])
                imp = A("b_imp", [128, 64])
                imp2 = A("b_imp2", [128, 64])
                wk = A("b_wk", [128, 64])
                m8 = A("b_m8", [128, 16])
                mbs = A("b_mbs", [128, 128])
                ocmp = A("b_ocmp", [128, 4, 4, 64])
                oacc = A("b_oacc", [128, 2, 4, 260])
                otot = A("b_otot", [128, 4, 4, 64])
                ototT = A("b_ototT", [128, KC, 128], BF16)
                ctmp = A("b_ctmp", [128, 4, 64])
                cf = A("b_cf", [128, 12])
                S.dma("pool", lambda e: e.dma_start(out=w_o[:], in_=self.d_nsa_w_out.rearrange("(c p) n -> p c n", p=128)), writes=[B("wo")])
                for kvh in range(4):
                    S.dma("pool", lambda e, kvh=kvh: e.dma_start(out=KsA[64:128, kvh, :], in_=self.d_onehot[:, 0:2048]), writes=[B("KsAs")])
                S.dma("pool", lambda e: e.dma_start(out=mnew[:], in_=self.d_mnew), writes=[B("mnew")])
                S.dma("pool", lambda e: e.dma_start(out=tri[:], in_=self.d_tri.rearrange("p (a t) -> p a t", a=2)), writes=[B("tri")])
                S.dma("sp", lambda e: e.dma_start(out=vbs[:], in_=self.d_vbs), writes=[B("vbs")])
                S.dma("sp", lambda e: e.dma_start(out=mk[:], in_=self.d_masks[17].rearrange("p (a n) -> p a n", a=4)), writes=[B("mk")])
                S.op("pool", lambda e: e.memset(Vs[:], 1.0), writes=[B("Vss")])
                S.op("pool", lambda e: e.memset(VwS[:], 1.0), writes=[B("VwS")])
                S.op("pool", lambda e: e.memset(mbs[:], 0.0), writes=[B("mbs")])
                S.op("pool", lambda e: e.memset(imp[:], 0.0), writes=[B("imp")])
                S.op("pool", lambda e: e.memset(oacc[:], 0.0), writes=[B("oacc")])
                for kvh in range(4):
                    bqa = B("QAs", kvh)
                    for g in range(4):
                        S.op("pe", lambda e, g=g, kvh=kvh: e.matmul(ps[:, g, 0:512], lhsT=QA[0:64, kvh, g, :], rhs=self.CK_all[:, kvh, :], start=True, stop=True),
                             reads=[bqa, B("CKall")], writes=[B("ps", g)])
                        S.op("dve", lambda e, g=g: e.scalar_tensor_tensor(out=e4s[:, g, :], in0=ps[:, g, 0:512], scalar=0.125, in1=vbs[:], op0=ALU.mult, op1=ALU.add),
                             reads=[B("ps", g), B("vbs")], writes=[B("e4s")])
                    S.op("act", lambda e: e.activation(out=e4s[:], in_=e4s[:], func=AF.Exp), reads=[B("e4s")], writes=[B("e4s")])
                    S.op("dve", lambda e: e.tensor_reduce(out=sm[:, 0:4], in_=e4s[:], axis=AX.X, op=ALU.add), reads=[B("e4s")], writes=[B("sm")])
                    S.op("dve", lambda e: e.tensor_scalar(out=sm[:, 0:4], in0=sm[:, 0:4], scalar1=1e-30, scalar2=None, op0=ALU.add), reads=[B("sm")], writes=[B("sm")])
                    S.op("dve", lambda e: e.reciprocal(out=sm[:, 0:4], in_=sm[:, 0:4]), reads=[B("sm")], writes=[B("sm")])
                    S.op("dve", lambda e: e.tensor_tensor(out=e4s[:], in0=e4s[:], in1=bc(sm[:, 0:4].unsqueeze(2), [128, 4, 512]), op=ALU.mult),
                         reads=[B("sm"), B("e4s")], writes=[B("e4s")])
                    S.op("dve", lambda e: e.tensor_tensor(out=psg[:], in0=e4s[:, 0, :], in1=e4s[:, 1, :], op=ALU.add), reads=[B("e4s")], writes=[B("psg")])
                    for g in (2, 3):
                        S.op("dve", lambda e, g=g: e.tensor_tensor(out=psg[:], in0=psg[:], in1=e4s[:, g, :], op=ALU.add), reads=[B("e4s"), B("psg")], writes=[B("psg")])
                    S.op("dve", lambda e: e.tensor_reduce(out=imp[:, 0:32], in_=psg[:].rearrange("p (s n) -> p n s", n=32), axis=AX.X, op=ALU.add), reads=[B("psg")], writes=[B("imp")])
                    S.op("dve", lambda e: e.tensor_tensor(out=imp2[:], in0=imp[:], in1=mk[:, 1, :], op=ALU.mult), reads=[B("imp"), B("mk")], writes=[B("imp2")])
                    S.op("dve", lambda e: e.tensor_tensor(out=imp2[:], in0=imp2[:], in1=mk[:, 2, :], op=ALU.add), reads=[B("imp2"), B("mk")], writes=[B("imp2")])
                    S.op("dve", lambda e: e.max(out=m8[:, 0:8], in_=imp2[:]), reads=[B("imp2")], writes=[B("m8")])
                    S.op("dve", lambda e: e.match_replace(out=wk[:], in_to_replace=m8[:, 0:8], in_values=imp2[:], imm_value=-3e30), reads=[B("imp2"), B("m8")], writes=[B("wk")])
                    S.op("dve", lambda e: e.max(out=m8[:, 8:16], in_=wk[:]), reads=[B("wk")], writes=[B("m8")])
                    S.op("dve", lambda e: e.match_replace(out=wk[:], in_to_replace=m8[:, 8:16], in_values=wk[:], imm_value=-3e30), reads=[B("wk"), B("m8")], writes=[B("wk")])
                    S.op("dve", lambda e: e.tensor_tensor(out=imp2[:], in0=imp2[:], in1=wk[:], op=ALU.subtract), reads=[B("wk"), B("imp2")], writes=[B("imp2")])
                    S.op("dve", lambda e: e.tensor_scalar(out=imp2[:], in0=imp2[:], scalar1=1.0, scalar2=None, op0=ALU.min), reads=[B("imp2")], writes=[B("imp2")])
                    S.op("dve", lambda e: e.tensor_tensor(out=imp2[:], in0=imp2[:], in1=mk[:, 3, :], op=ALU.mult), reads=[B("imp2"), B("mk")], writes=[B("imp2")])
                    S.op("dve", lambda e: e.tensor_scalar(out=mbs[:, 64:128], in0=imp2[:], scalar1=-1.0, scalar2=BIG, op0=ALU.add, op1=ALU.mult),
                         reads=[B("imp2")], writes=[B("mbs")])
                    S.op("pe", lambda e: e.transpose(out=ps[:, 4, 0:128], in_=mbs[:], identity=self.ident[:]), reads=[B("mbs"), B("ident")], writes=[B("ps", 4)])
                    S.op("act", lambda e, kvh=kvh: e.activation(out=QA[64:128, kvh, :, :], in_=bc(ps[64:128, 4, 0:128].unsqueeze(1), [64, 4, 128]), func=AF.Copy),
                         reads=[B("ps", 4)], writes=[bqa])
                    for g in range(4):
                        for hh in range(2):
                            for c4 in range(4):
                                ch = hh * 4 + c4
                                S.op("pe", lambda e, g=g, ch=ch, c4=c4, hh=hh: e.transpose(out=ps[0:64, 6 + hh, c4 * 128:(c4 + 1) * 128], in_=e4s[:, g, ch * 64:(ch + 1) * 64], identity=self.ident[:]),
                                     reads=[B("e4s"), B("ident")], writes=[B("ps", 6 + hh)])
                            S.op("act", lambda e, hh=hh: e.activation(out=pTs[:, hh * 4:(hh + 1) * 4, :], in_=ps[0:64, 6 + hh, 0:512].rearrange("p (c t) -> p c t", c=4), func=AF.Copy),
                                 reads=[B("ps", 6 + hh)], writes=[B("pTs")])
                        for ch in range(8):
                            S.op("pe", lambda e, g=g, ch=ch, kvh=kvh: e.matmul(ps[:, 5, g * 64:(g + 1) * 64], lhsT=pTs[:, ch, :], rhs=self.CV_all[:, ch, kvh, :], start=(ch == 0), stop=(ch == 7)),
                                 reads=[B("pTs"), B("CVall")], writes=[B("ps", 5)])
                    S.op("act", lambda e, kvh=kvh: e.activation(out=ocmp[:, kvh, :, :], in_=ps[:, 5, 0:256].rearrange("p (g d) -> p g d", g=4), func=AF.Copy), reads=[B("ps", 5)], writes=[B("ocmp")])
                ti = 0
                for kvh in range(4):
                    bqa = B("QAs", kvh)
                    for br in range(2):
                        sb_ = ti % 2
                        ti += 1
                        if br == 0:
                            lhs, rhs = KnA[:, kvh, :], QA[:, kvh, :, :].rearrange("p g t -> p (g t)")
                            vv = Vns
                        else:
                            lhs, rhs = KwN[:, kvh, :], QA[0:64, kvh, :, :].rearrange("p g t -> p (g t)")
                            vv = Vnw
                        S.op("pe", lambda e, lhs=lhs, rhs=rhs, sb_=sb_: e.matmul(ps[:, sb_, 0:512], lhsT=lhs, rhs=rhs, start=True, stop=True),
                             reads=[B("KnA"), B("KwN"), bqa], writes=[B("ps", sb_)])
                        S.op("act", lambda e, sb_=sb_: e.activation(out=PTn[:], in_=ps[:, sb_, 0:512], func=AF.Exp, scale=0.125), reads=[B("ps", sb_)], writes=[B("PTn")])
                        S.op("dve", lambda e: e.tensor_tensor(out=PTn[:].rearrange("p (g t) -> p g t", g=4), in0=PTn[:].rearrange("p (g t) -> p g t", g=4),
                                                              in1=bc(mnew[:].unsqueeze(1), [128, 4, 128]), op=ALU.mult), reads=[B("PTn"), B("mnew")], writes=[B("PTn")])
                        for g in range(4):
                            S.op("pe", lambda e, g=g, vv=vv, kvh=kvh, br=br: e.matmul(ps[:, 2 + br, g * 65:(g + 1) * 65], lhsT=PTn[:, g * 128:(g + 1) * 128], rhs=vv[:, kvh, :], start=True, stop=True),
                                 reads=[B("PTn"), B("Vns"), B("Vnw")], writes=[B("ps", 2 + br)])
                        S.op("dve", lambda e, kvh=kvh, br=br: e.tensor_tensor(out=oacc[:, br, kvh, :], in0=oacc[:, br, kvh, :], in1=ps[:, 2 + br, 0:260], op=ALU.add),
                             reads=[B("ps", 2 + br), B("oacc")], writes=[B("oacc")])
                for sq_ in range(16):
                    S.dma("sp", lambda e, sq_=sq_: e.dma_start(out=KsA[0:64, :, :], in_=self.sc_sKs[sq_]), reads=[B("scsK", sq_)], writes=[B("KsAs")])
                    for pg in range(16):
                        S.dma("sp", lambda e, sq_=sq_, pg=pg: e.dma_start(out=Vs[:, pg, :, 0:64], in_=self.sc_sVs[sq_, pg].rearrange("p (k d) -> p k d", k=4)), reads=[B("scsV", sq_)], writes=[B("Vss")])
                    S.dma("sp", lambda e, sq_=sq_: e.dma_start(out=wst[:], in_=self.d_swin[sq_].rearrange("(t p) c -> p t c", p=128)), writes=[B("wst")])
                    for t in range(4):
                        for kvh in range(4):
                            S.op("pe", lambda e, t=t, kvh=kvh: e.transpose(out=ps[0:64, 4, kvh * 128:(kvh + 1) * 128], in_=wst[:, t, kvh * 64:(kvh + 1) * 64], identity=self.ident[:]),
                                 reads=[B("wst"), B("ident")], writes=[B("ps", 4)])
                        S.op("act", lambda e, t=t: e.activation(out=KwS[:, :, t * 128:(t + 1) * 128], in_=ps[0:64, 4, 0:512].rearrange("p (k t) -> p k t", k=4), func=AF.Copy),
                             reads=[B("ps", 4)], writes=[B("KwS")])
                        S.op("pool", lambda e, t=t: e.tensor_copy(out=VwS[:, t, :, 0:64], in_=wst[:, t, 256:512].rearrange("p (k d) -> p k d", k=4)), reads=[B("wst")], writes=[B("VwS")])
                    for k_ in range(2):
                        S.op("pool", lambda e, k_=k_: e.memset(pad[k_][:], 0.0), writes=[B("pad", k_)])
                    qs = slice(sq_ * 8, (sq_ + 1) * 8)
                    tl = []
                    for kvh in range(4):
                        for br in range(2):
                            nkt = 16 if br == 0 else 4
                            for kt in range(nkt):
                                tl.append((kvh, br, kt, kt == 0, kt == nkt - 1))

                    def emit_S(n):
                        kvh, br, kt, first, last = tl[n]
                        sb_ = n % 2
                        if br == 0:
                            lhs, rhs, rdk = KsA[:, kvh, kt * 128:(kt + 1) * 128], QA[:, kvh, :, qs], B("KsAs")
                        else:
                            lhs, rhs, rdk = KwS[:, kvh, kt * 128:(kt + 1) * 128], QA[0:64, kvh, :, qs], B("KwS")
                        S.op("pe", lambda e, lhs=lhs, rhs=rhs, sb_=sb_: e.matmul(ps[:, sb_, 0:32].rearrange("p (g t) -> p g t", g=4), lhsT=lhs, rhs=rhs, start=True, stop=True),
                             reads=[rdk, B("QAs", kvh)], writes=[B("ps", sb_)])

                    def emit_rest(n):
                        kvh, br, kt, first, last = tl[n]
                        sb_ = n % 2
                        pd = pad[sb_]
                        if br == 0:
                            vv, rdv = Vs[:, kt, kvh, :], B("Vss")
                        else:
                            vv, rdv = VwS[:, kt, kvh, :], B("VwS")
                        S.op("act", lambda e, sb_=sb_, pd=pd: e.activation(out=pd[:, :, qs], in_=ps[:, sb_, 0:32].rearrange("p (g t) -> p g t", g=4), func=AF.Exp, scale=0.125),
                             reads=[B("ps", sb_)], writes=[B("pad", sb_)])
                        if br == 1 and kt == 0:
                            S.op("dve", lambda e, pd=pd: e.tensor_tensor(out=pd[:, :, qs], in0=pd[:, :, qs], in1=bc(tri[:, 1, 0:8].unsqueeze(1), [128, 4, 8]), op=ALU.mult),
                                 reads=[B("pad", sb_), B("tri")], writes=[B("pad", sb_)])
                        for g in range(4):
                            S.op("pe", lambda e, g=g, pd=pd, vv=vv, br=br, first=first, last=last: e.matmul(ps[:, 2 + br, g * 65:(g + 1) * 65], lhsT=pd[:, g, :], rhs=vv,
                                                                                                             start=first, stop=last), reads=[B("pad", sb_), rdv], writes=[B("ps", 2 + br)])
                        if last:
                            S.op("dve", lambda e, kvh=kvh, br=br: e.tensor_tensor(out=oacc[:, br, kvh, :], in0=oacc[:, br, kvh, :], in1=ps[:, 2 + br, 0:260], op=ALU.add),
                                 reads=[B("ps", 2 + br), B("oacc")], writes=[B("oacc")])

                    emit_S(0)
                    for n in range(len(tl)):
                        if n + 1 < len(tl):
                            emit_S(n + 1)
                        emit_rest(n)
                for kvh in range(4):
                    osel = oacc[:, 0, kvh, :].rearrange("p (g d) -> p g d", g=4)
                    owin = oacc[:, 1, kvh, :].rearrange("p (g d) -> p g d", g=4)
                    S.op("dve", lambda e, osel=osel: e.tensor_scalar(out=cf[:, 0:4], in0=osel[:, :, 64], scalar1=1e-30, scalar2=None, op0=ALU.add), reads=[B("oacc")], writes=[B("cf")])
                    S.op("dve", lambda e, owin=owin: e.tensor_scalar(out=cf[:, 4:8], in0=owin[:, :, 64], scalar1=1e-30, scalar2=None, op0=ALU.add), reads=[B("oacc")], writes=[B("cf")])
                    S.op("dve", lambda e: e.reciprocal(out=cf[:, 0:8], in_=cf[:, 0:8]), reads=[B("cf")], writes=[B("cf")])
                    S.op("dve", lambda e, kvh=kvh: e.tensor_tensor(out=cf[:, 0:4], in0=cf[:, 0:4], in1=gates[:, 16 + kvh * 4:20 + kvh * 4], op=ALU.mult), reads=[B("cf"), B("gates")], writes=[B("cf")])
                    S.op("dve", lambda e, kvh=kvh: e.tensor_tensor(out=cf[:, 4:8], in0=cf[:, 4:8], in1=gates[:, 32 + kvh * 4:36 + kvh * 4], op=ALU.mult), reads=[B("cf"), B("gates")], writes=[B("cf")])
                    ot = otot[:, kvh, :, :]
                    S.op("dve", lambda e, ot=ot, kvh=kvh: e.tensor_tensor(out=ot, in0=ocmp[:, kvh, :, :], in1=bc(gates[:, kvh * 4:kvh * 4 + 4].unsqueeze(2), [128, 4, 64]), op=ALU.mult),
                         reads=[B("ocmp"), B("gates")], writes=[B("otot")])
                    for (src, c0) in ((osel, 0), (owin, 4)):
                        S.op("dve", lambda e, src=src, c0=c0: e.tensor_tensor(out=ctmp[:], in0=src[:, :, 0:64], in1=bc(cf[:, c0:c0 + 4].unsqueeze(2), [128, 4, 64]), op=ALU.mult),
                             reads=[B("oacc"), B("cf")], writes=[B("ctmp")])
                        S.op("dve", lambda e, ot=ot: e.tensor_tensor(out=ot, in0=ot, in1=ctmp[:], op=ALU.add), reads=[B("ctmp"), B("otot")], writes=[B("otot")])
                of = otot[:].rearrange("p k g d -> p (k g d)")
                for hh in range(2):
                    for c4 in range(4):
                        c = hh * 4 + c4
                        S.op("pe", lambda e, c=c, c4=c4, hh=hh, of=of: e.transpose(out=ps[:, hh, c4 * 128:(c4 + 1) * 128], in_=of[:, c * 128:(c + 1) * 128], identity=self.ident[:]),
                             reads=[B("otot"), B("ident")], writes=[B("ps", hh)])
                    S.op("act", lambda e, hh=hh: e.activation(out=ototT[:, hh * 4:(hh + 1) * 4, :], in_=ps[:, hh, 0:512].rearrange("p (c t) -> p c t", c=4), func=AF.Copy),
                         reads=[B("ps", hh)], writes=[B("ototT")])
                for c in range(KC):
                    bank = 6 + (c % 2)
                    for k in range(KC):
                        S.op("pe", lambda e, c=c, k=k, bank=bank: e.matmul(ps[:, bank, 0:128], lhsT=w_o[:, k, c * 128:(c + 1) * 128], rhs=ototT[:, k, :], start=(k == 0), stop=(k == KC - 1)),
                             reads=[B("wo"), B("ototT")], writes=[B("ps", bank)])
                    self.resid(i, 0, c, l0, 128, "s", ps[:, bank, 0:128], [B("ps", bank)])
                S.emit()

    def pool(self, i, tiles):
        S, B, nc, ps = self.S, self.B, self.nc, self.ps
        with ExitStack() as ph:
            A = lambda n, s, d=F32: ph.enter_context(nc.sbuf_tensor(self.un(n), list(s), d))
            pw = A("p_w", [128, 4, 2, 256], BF16)
            hp = A("p_h", [128, KC, 15 + 512])
            hps = A("p_hs", [128, KC, 16, 23])
            sa = A("p_sa", [128, KC, 15 + 512])
            sb_ = A("p_sb", [128, KC, 15 + 512])
            dT = A("p_d", [128, KC, 512], BF16)
            icn = A("p_icn", [128, 4, 128])
            gl = A("p_gl", [128, KC])
            tmp = [A("p_t%d" % k, [128, 512]) for k in range(2)]
            sq = [A("p_sq%d" % k, [128, 512], BF16) for k in range(2)]
            rs = A("p_rs", [128, 512])
            hdummy = A("p_hd", [128, KC, 512], BF16)
            hs32 = A("p_hs32", [128, KC, 128])

            self.rtmp = A("p_rtmp", [128, 128])
            self.coefs(i, 0)
            S.dma("pool", lambda e: e.dma_start(out=pw[:], in_=self.d_pool_w.rearrange("g (k p) n -> p g k n", p=128)), writes=[B("pw")])
            S.dma("sp", lambda e: e.dma_start(out=icn[:], in_=self.d_invcnt.rearrange("p (g t) -> p g t", g=4)), writes=[B("icn")])
            for c in range(KC):
                S.dma("sp", lambda e, c=c: e.dma_start(out=hps[:, c, :, 0:15], in_=self.d_spoolT[c * 128:(c + 1) * 128, :].rearrange("p (s r) -> p s r", r=15)), writes=[B("hps")])
            for c in range(KC):
                S.dma("sp", lambda e, c=c: e.dma_start(out=self.o_poolT[c * 128:(c + 1) * 128, 144:256].rearrange("p (s r) -> p s r", r=7),
                                                       in_=hps[:, c, :, 8:15]), reads=[B("hps")])
            S.op("dve", lambda e: e.tensor_tensor(out=gl[:], in0=self.coef[:, 2, :], in1=self.vecs[:, V_PSC:V_PSC + 8], op=ALU.mult),
                 reads=[B("coef"), B("vecs")], writes=[B("gl")])
            S.op("pool", lambda e: e.memset(hp[:, :, 0:15], 0.0), writes=[B("h32", c) for c in range(KC)])
            wins = (2, 4, 8, 16)
            for (t0, w, kind) in tiles:
                h32w = [B("h32", c) for c in range(KC)]
                if kind == "p":
                    self.norm_mod(i, 0, t0, w, kind, hdummy, 0, tmp, sq, rs, "phd", fp32_out=hp[:, :, 15:15 + 512])
                    if t0 == 0 and self.halo:
                        S.op("dve", lambda e: e.tensor_scalar(out=hp[:, :, 15:143], in0=hp[:, :, 15:143], scalar1=self.flag[:, 0:1], scalar2=None, op0=ALU.mult),
                             reads=h32w + [B("flag")], writes=h32w)
                    if t0 + w == NPT and self.halo:
                        S.dma("sp", lambda e, w=w: e.dma_start(out=self.o_poolT[:, 0:16].rearrange("(c p) t -> p c t", p=128), in_=hp[:, :, 15 + w - 16:15 + w]), reads=h32w)
                    H = lambda a, b: hp[:, :, a:b]
                    L = 15 + w
                    cur = hp
                    stages = {}
                    src = hp
                    for si, sh in enumerate((1, 2, 4, 8)):
                        dst = sa if si % 2 == 0 else sb_
                        c0 = 2 * si
                        S.op("dve" if si % 2 == 0 else "pool", lambda e, src=src, dst=dst, sh=sh, c0=c0, L=L: e.tensor_tensor(
                            out=dst[:, c0:KC, sh:L], in0=src[:, c0:KC, sh:L], in1=src[:, c0:KC, 0:L - sh], op=ALU.add),
                            reads=h32w + [B("psa"), B("psb")], writes=[B("psa") if si % 2 == 0 else B("psb")])
                        src = dst
                        stages[si] = dst
                    for g in range(4):
                        stg = stages[g]
                        for cc in (2 * g, 2 * g + 1):
                            S.op("dve", lambda e, stg=stg, cc=cc, g=g, w=w: e.tensor_scalar(out=stg[:, cc, 15:15 + w], in0=stg[:, cc, 15:15 + w], scalar1=1.0 / wins[g],
                                                                                            scalar2=None, op0=ALU.mult), reads=[B("psa"), B("psb")], writes=[B("psa"), B("psb")])
                            if t0 == 0:
                                S.op("dve", lambda e, stg=stg, cc=cc, g=g: e.tensor_tensor(out=stg[:, cc, 143:271], in0=stg[:, cc, 143:271], in1=icn[:, g, :], op=ALU.mult),
                                     reads=[B("psa"), B("psb"), B("icn")], writes=[B("psa"), B("psb")])
                            S.op("dve", lambda e, stg=stg, cc=cc, w=w: e.tensor_tensor(out=dT[:, cc, 0:w], in0=stg[:, cc, 15:15 + w], in1=hp[:, cc, 15:15 + w], op=ALU.subtract),
                                 reads=[B("psa"), B("psb")] + h32w, writes=[B("pd", cc)])
                    S.op("pool", lambda e, w=w: e.tensor_copy(out=hp[:, :, 0:15], in_=hp[:, :, w:w + 15]), reads=h32w + [B("psa"), B("psb")], writes=h32w)
                else:
                    hview = hps[:, :, :, 15:23]
                    self.norm_mod(i, 0, t0, w, kind, hdummy, 0, tmp, sq, rs, "phd", fp32_out=hs32)
                    for c in range(KC):
                        S.op("pool", lambda e, c=c: e.tensor_copy(out=hps[:, c, :, 15:23], in_=hs32[:, c, :].rearrange("p (s j) -> p s j", j=8)), reads=h32w, writes=[B("hps")])
                    for c in range(KC):
                        S.dma("sp", lambda e, c=c: e.dma_start(out=self.o_poolT[c * 128:(c + 1) * 128, 16:144].rearrange("p (s j) -> p s j", j=8), in_=hps[:, c, :, 15:23]), reads=[B("hps")])
                    for g in range(4):
                        wn = wins[g]
                        for cc in (2 * g, 2 * g + 1):
                            acc = sb_[:, cc, 0:128].rearrange("p (s j) -> p s j", j=8)
                            S.op("dve", lambda e, cc=cc, acc=acc: e.tensor_tensor(out=acc, in0=hps[:, cc, :, 15:23], in1=hps[:, cc, :, 14:22], op=ALU.add),
                                 reads=[B("hps")], writes=[B("psb")])
                            for r in range(2, wn):
                                S.op("dve", lambda e, cc=cc, acc=acc, r=r: e.tensor_tensor(out=acc, in0=acc, in1=hps[:, cc, :, 15 - r:23 - r], op=ALU.add),
                                     reads=[B("hps"), B("psb")], writes=[B("psb")])
                            S.op("dve", lambda e, cc=cc, acc=acc, wn=wn: e.scalar_tensor_tensor(
                                out=dT[:, cc, 0:128].rearrange("p (s j) -> p s j", j=8), in0=acc, scalar=1.0 / wn, in1=hps[:, cc, :, 15:23], op0=ALU.mult, op1=ALU.subtract),
                                reads=[B("hps"), B("psb")], writes=[B("pd", cc)])
                for g in range(4):
                    for o in range(2):
                        c = 2 * g + o
                        bank = c % 2
                        bp = B("ps", bank)
                        for k in range(2):
                            S.op("pe", lambda e, g=g, o=o, k=k, bank=bank, w=w: e.matmul(ps[:, bank, 0:w], lhsT=pw[:, g, k, o * 128:(o + 1) * 128], rhs=dT[:, 2 * g + k, 0:w],
                                                                                        start=(k == 0), stop=(k == 1)), reads=[B("pw"), B("pd", 2 * g + k)], writes=[bp])
                        if kind == "p":
                            self.resid(i, 0, c, t0, w, kind, ps[:, bank, 0:w], [bp, B("gl")], extra_scale=gl[:, c:c + 1])
                        else:
                            self.resid(i, 0, c, t0, w, kind, ps[:, bank, 0:w], [bp, B("vecs")], extra_scale=self.vecs[:, V_PSC + c:V_PSC + c + 1])
            S.emit()

    def final(self, tiles):
        S, B, nc, ps = self.S, self.B, self.nc, self.ps
        with ExitStack() as ph:
            A = lambda n, s, d=F32: ph.enter_context(nc.sbuf_tensor(self.un(n), list(s), d))
            sq = [A("y_sq%d" % k, [128, 512], BF16) for k in range(2)]
            rs = A("y_rs", [128, 512])
            yo = [A("y_o%d" % k, [128, 512]) for k in range(2)]
            for (t0, w, kind) in tiles:
                bp = B("ps", 7)
                for c in range(KC):
                    s_ = sq[c % 2]
                    S.op("act", lambda e, c=c, s_=s_: e.activation(out=s_[:, 0:w], in_=self.xT[:, c, t0:t0 + w], func=AF.Square),
                         reads=self.xb(c, t0, w), writes=[B("sq", c % 2)])
                    S.op("pe", lambda e, c=c, s_=s_: e.matmul(ps[:, 7, 0:w], lhsT=self.ones[:], rhs=s_[:, 0:w], start=(c == 0), stop=(c == KC - 1)),
                         reads=[B("sq", c % 2), B("ones")], writes=[bp])
                S.op("act", lambda e: e.activation(out=rs[:, 0:w], in_=ps[:, 7, 0:w], func=AF.Sqrt, bias=EPS, scale=1.0 / D), reads=[bp], writes=[B("rs")])
                S.op("dve", lambda e: e.reciprocal(out=rs[:, 0:w], in_=rs[:, 0:w]), reads=[B("rs")], writes=[B("rs")])
                for c in range(KC):
                    y_ = yo[c % 2]
                    S.op("dve", lambda e, c=c, y_=y_: e.scalar_tensor_tensor(out=y_[:, 0:w], in0=self.xT[:, c, t0:t0 + w], scalar=self.vecs[:, V_FG + c:V_FG + c + 1],
                                                                              in1=rs[:, 0:w], op0=ALU.mult, op1=ALU.mult),
                         reads=self.xb(c, t0, w) + [B("rs"), B("vecs")], writes=[B("yo", c % 2)])
                    S.dma("sp", lambda e, c=c, y_=y_: e.dma_start(out=self.o_yT[c * 128:(c + 1) * 128, t0:t0 + w], in_=y_[:, 0:w]), reads=[B("yo", c % 2)])
            S.emit()

    def main(self):
        PT = [(0, 512, "p"), (512, 512, "p"), (1024, 512, "p"), (1536, 512, "p"), (2048, 128, "p"), (2176, 128, "s")]
        self.halo = True
        st = self.stage
        if os.environ.get("KNOS"):
            PT = PT[:int(os.environ["KNOS"])]
        if st >= 2:
            AT = [(0, 512, "p"), (512, 512, "p"), (1024, 512, "p"), (1536, 512, "p")]
            self.halo = False
            self.ada(0)
            self.conv(0, 0, AT)
            self.ffn(0, AT)
            self.ada(1)
            self.nsa_proj(1, 16, True)
            self.load_x()
            self.halo = True
        if st >= -1 and st < 2:
            self.ada(0)
        if st >= 0:
            self.conv(0, 0, PT)
        if st >= 1:
            self.ffn(0, PT)
        if st >= 2:
            self.nsa_proj(1, 16, False)
            self.nsa_attn(1)
            if not os.environ.get("KNOSAMP"):
                with ExitStack() as phs:
                    self.CK_all = phs.enter_context(self.nc.sbuf_tensor(self.un("CK_all"), [64, 4, 512], BF16))
                    self.CV_all = phs.enter_context(self.nc.sbuf_tensor(self.un("CV_all"), [64, 8, 4, 64], BF16))
                    self.nsa_sample_prep()
                    self.nsa_sample_attn(1)
            self.ffn(1, PT)
        if st >= 3:
            self.ada(2)
            self.pool(2, PT)
            self.ffn(2, PT)
            self.ada(3)
            self.conv(3, 1, PT)
            self.ffn(3, PT)
        self.final(PT)


_CACHE = {}


def kernel(x_prompt, x_sample, cache_nsa_kv, state_nsa_win, state_conv, state_pool, state_ffn, page_table,
           c_prompt, c_sample, ada_w, ada_b, norm1_g, norm2_g, final_g, conv_w_in, conv_w_dw, conv_ln_g,
           conv_ln_b, conv_w_out, nsa_w_in, nsa_cmp_pe, nsa_cmp_w1, nsa_cmp_w2, nsa_w_out, pool_w, pool_scale,
           ffn_w_up, ffn_w_dw, ffn_w_down):
    stage = int(os.environ.get("KSTAGE", "3"))
    f32 = np.float32
    A = lambda a: np.ascontiguousarray(np.asarray(a), dtype=f32)
    x_prompt, x_sample = A(x_prompt), A(x_sample)
    vecs = np.zeros((128, NV), f32)

    def fm(v):
        v = A(v)
        sh = v.shape[:-1]
        n = v.shape[-1] // 128
        return np.moveaxis(v.reshape(sh + (n, 128)), -1, 0)
    vecs[:, V_N1:V_N1 + 32] = fm(norm1_g).reshape(128, 32)
    vecs[:, V_N2:V_N2 + 32] = fm(norm2_g).reshape(128, 32)
    vecs[:, V_FG:V_FG + 8] = fm(final_g).reshape(128, 8)
    vecs[:, V_ADAB:V_ADAB + 192] = fm(ada_b).reshape(128, 192)
    vecs[:, V_CDW:V_CDW + 496] = np.transpose(fm(conv_w_dw), (0, 1, 3, 2)).reshape(128, 496)
    vecs[:, V_CLNG:V_CLNG + 16] = fm(conv_ln_g).reshape(128, 16)
    vecs[:, V_CLNB:V_CLNB + 16] = fm(conv_ln_b).reshape(128, 16)
    vecs[:, V_PSC:V_PSC + 8] = fm(pool_scale).reshape(128, 8)
    vecs[:, V_FDW:V_FDW + 264] = np.transpose(fm(ffn_w_dw), (0, 1, 3, 2)).reshape(128, 264)
    ident = np.eye(128, dtype=f32)
    pe = A(nsa_cmp_pe)[0]
    pe2 = np.concatenate([np.tile(pe[X].T, (1, 2)) for X in range(2)], axis=1)
    onehot = (np.arange(4096)[None, :] // 64 == np.arange(64)[:, None]).astype(f32)
    kk, qq = np.arange(128)[:, None], np.arange(128)[None, :]
    tri = np.concatenate([(kk <= qq), (kk >= qq)], axis=1).astype(f32)

    def masks_for(half):
        off = 32 * (1 - half)
        p0 = half * 2048 - 128
        M = np.zeros((18, 128, 4, 64), f32)
        nn = (np.arange(64) - off)[None, :]
        for qt in range(17):
            posc = (p0 + 128 * qt + np.arange(128))[:, None]
            validb = (nn >= 0) & (posc >= 0)
            cur = posc // 64
            cmpvalid = validb & (64 * (nn + 1) - 1 <= posc)
            future = (~validb) | (nn > cur)
            forced = validb & ((nn == 0) | (nn == cur) | (nn == cur - 1)) & ~future
            M[qt, :, 0] = np.where(cmpvalid, 0.0, -1e30)
            M[qt, :, 1] = (~forced & ~future)
            M[qt, :, 2] = forced * 1e4 + future * (-1e30)
            M[qt, :, 3] = ~future
        return M.reshape(18, 128, 256)
    mask_h = [masks_for(0), masks_for(1)]
    sl = np.arange(64)
    for M_ in mask_h:
        Ms = M_.reshape(18, 128, 4, 64)
        Ms[17, :, 0] = 0.0
        Ms[17, :, 1] = ((sl >= 1) & (sl <= 30))[None, :]
        Ms[17, :, 2] = (np.isin(sl, (0, 31, 32)) * 1e4 + (sl >= 33) * (-1e30))[None, :]
        Ms[17, :, 3] = (sl <= 32)[None, :]
    qs_, qj_ = np.arange(128) // 8, np.arange(128) % 8
    vbs = np.where(qs_[:, None] == (np.arange(512) // 32)[None, :], 0.0, -1e30).astype(f32)
    mnew = ((qs_[:, None] == qs_[None, :]) & (qj_[:, None] <= qj_[None, :])).astype(f32)
    ohnew = np.zeros((64, 128), f32)
    ohnew[32, :] = 1.0
    cache2d = A(cache_nsa_kv)[0].reshape(2560 * 128, 1024)
    iota = np.arange(128, dtype=f32).reshape(128, 1)
    swin_all = A(state_nsa_win)[0].reshape(128, 512, 512)
    ptab = np.ascontiguousarray(np.asarray(page_table), dtype=np.int32)
    shared = dict(vecs=vecs, ident=ident, ada_w=A(ada_w), conv_w_in=A(conv_w_in), conv_w_out=A(conv_w_out),
                  ffn_w_up=A(ffn_w_up), ffn_w_down=A(ffn_w_down), pool_w=A(pool_w)[0],
                  nsa_w_in=A(nsa_w_in)[0], nsa_w_out=A(nsa_w_out)[0], cmp_w1=A(nsa_cmp_w1)[0], cmp_w2=A(nsa_cmp_w2)[0],
                  pe2=np.ascontiguousarray(pe2), onehot=onehot, tri=tri, cache=cache2d, iota=iota, vbs=vbs, mnew=mnew, ohnew=ohnew)
    in_maps = []
    for c in range(NCORES):
        b, half = c // 2, c % 2
        xT = np.zeros((D, TOT), f32)
        p0 = half * 2048 - 128
        lo = max(p0, 0)
        xT[:, lo - p0:NPT] = x_prompt[b, lo:p0 + NPT].T
        ss = slice(16 * c, 16 * c + 16)
        xT[:, NPT:] = x_sample[ss].reshape(128, D).T
        xA = np.ascontiguousarray(x_prompt[b, 0:2048].T) if half == 1 else np.zeros((D, 2048), f32)
        cT = np.concatenate([A(c_prompt)[b][:, None], A(c_sample)[ss].T], axis=1)
        invcnt = np.zeros((128, 4, 128), f32)
        for g, wn in enumerate((2, 4, 8, 16)):
            pos = (p0 + 128 + np.arange(128)).astype(f32)
            invcnt[:, g, :] = (wn / np.minimum(wn, np.maximum(pos, 0) + 1))[None, :]
        m = dict(shared)
        m.update(xT=xT, xA=xA, cT=np.ascontiguousarray(cT), flag=np.full((128, 1), float(half), f32),
                 sconvT=np.ascontiguousarray(np.transpose(A(state_conv)[:, ss], (0, 3, 1, 2)).reshape(2, D, 480)),
                 spoolT=np.ascontiguousarray(np.transpose(A(state_pool)[0, ss], (2, 0, 1)).reshape(D, 240)),
                 sffnT=np.ascontiguousarray(np.transpose(A(state_ffn)[:, ss], (0, 3, 1, 2)).reshape(4, DFF, 32)),
                 invcnt=invcnt.reshape(128, 512), masks=mask_h[half],
                 ptab=np.ascontiguousarray(ptab[ss].reshape(1, 256)), swin=np.ascontiguousarray(swin_all[ss]))
        in_maps.append(m)
    if stage not in _CACHE:
        _CACHE[stage] = K(stage).build()
    nc = _CACHE[stage]
    res = run_bass_kernel_spmd(nc, in_maps, core_ids=list(range(NCORES)))
    R = res.results
    y_prompt = np.zeros((4, 4096, D), f32)
    y_sample = np.zeros((128, 8, D), f32)
    p_conv = np.zeros((2, 4, 30, D), f32)
    s_conv = np.zeros((2, 128, 30, D), f32)
    p_pool = np.zeros((1, 4, 15, D), f32)
    s_pool = np.zeros((1, 128, 15, D), f32)
    p_ffn = np.zeros((4, 4, 2, DFF), f32)
    s_ffn = np.zeros((4, 128, 2, DFF), f32)
    p_rows = np.zeros((1, 4, 4096, 4, 4, 64), f32)
    p_win = np.zeros((1, 4, 512, 2, 4, 64), f32)
    s_rows = np.zeros((1, 128, 8, 4, 4, 64), f32)
    s_win = np.zeros((1, 128, 512, 2, 4, 64), f32)
    sc_in, sp_in = A(state_conv), A(state_pool)
    for c in range(NCORES):
        b, half = c // 2, c % 2
        r = R[c]
        yT = r["o_yT"]
        y_prompt[b, half * 2048:(half + 1) * 2048] = yT[:, 128:NPT].T
        ss = slice(16 * c, 16 * c + 16)
        y_sample[ss] = yT[:, NPT:].T.reshape(16, 8, D)
        p_rows[0, b, half * 2048:(half + 1) * 2048] = r["o_rows"][0:2048].reshape(2048, 4, 4, 64)
        if half == 1:
            p_win[0, b] = r["o_win"][0:512].reshape(512, 2, 4, 64)
        s_rows[0, ss] = r["o_rows"][2048:2176].reshape(16, 8, 4, 4, 64)
        s_win[0, ss, 0:504] = r["o_swinp"].reshape(16, 504, 2, 4, 64)
        s_win[0, ss, 504:512] = r["o_win"][512:640].reshape(16, 8, 2, 4, 64)
        cv = r["o_convT"]
        fv = r["o_ffnT"]
        pv = r["o_poolT"]
        if half == 1:
            p_conv[:, b] = np.transpose(cv[:, :, 2:32], (0, 2, 1))
            p_ffn[:, b] = np.transpose(fv[:, :, 0:2], (0, 2, 1))
            p_pool[0, b] = pv[:, 1:16].T
        s_conv[:, ss, 0:22] = np.transpose(cv[:, :, 160:512].reshape(2, D, 16, 22), (0, 2, 3, 1))
        s_conv[:, ss, 22:30] = np.transpose(cv[:, :, 32:160].reshape(2, D, 16, 8), (0, 2, 3, 1))
        s_ffn[:, ss] = np.transpose(fv[:, :, 2:34].reshape(4, DFF, 16, 2), (0, 2, 3, 1))
        s_pool[0, ss, 0:7] = np.transpose(pv[:, 144:256].reshape(D, 16, 7), (1, 2, 0))
        s_pool[0, ss, 7:15] = np.transpose(pv[:, 16:144].reshape(D, 16, 8), (1, 2, 0))
    return (y_prompt, y_sample, p_conv, p_rows, p_win, p_pool, p_ffn, s_conv, s_rows, s_win, s_pool, s_ffn)
```
